# Optimizing a Trainium2 kernel written in Bass

```python
import math
import jax
import jax.numpy as jnp
from jax import lax
import numpy as np

D_MODEL = 2048
BATCH = 4
SEQ = 2048
DEPTH = 4

GRID_W = 64
CTX_LEN = 256
HEAD_DIM = 128
N_HEADS_TOTAL = D_MODEL // HEAD_DIM
MLA_HEADS = N_HEADS_TOTAL // 2
MLA_NOPE_DIM = 128
MLA_ROPE_DIM = 64
MLA_QK_DIM = MLA_NOPE_DIM + MLA_ROPE_DIM
MLA_V_DIM = HEAD_DIM
MLA_KV_RANK = 512
NA_HEADS = N_HEADS_TOTAL - MLA_HEADS
NA_DIM = HEAD_DIM
NA_KH = 8
NA_KW = 16
EV_IN_WIDTH = MLA_HEADS * MLA_QK_DIM + MLA_KV_RANK + MLA_ROPE_DIM + 3 * NA_HEADS * NA_DIM
ATTN_BLOCK = 128
ROPE_THETA = 10000.0
HGRN_WIDTH = D_MODEL // 2
HGRN_HEADS = HGRN_WIDTH // HEAD_DIM
HGRN_DK = HEAD_DIM
HGRN_DV = HEAD_DIM
HGRN_CHUNK = 64
FORGET_FLOOR = 1e-30
HYENA_WIDTH = D_MODEL - HGRN_WIDTH
HYENA_SHORT = 3
HYENA_EMB = 33
HYENA_BANDS = (HYENA_EMB - 1) // 2
HYENA_FILT_HIDDEN = 64
HYENA_DECAY_TARGET = 1e-2
HYENA_FAST_PCT = 0.3
HYENA_SLOW_PCT = 1.5
OD_IN_WIDTH = 5 * HGRN_WIDTH + 3 * HYENA_WIDTH
MLP_HIDDEN = 4 * D_MODEL
N_EVEN = (DEPTH + 1) // 2
N_ODD = DEPTH // 2
NORM_EPS = 1e-6
NEG_INF = -1e30
F32 = jnp.float32

kernel_name = "hybrid_mla_natten_hgrn2_hyena_dit"


def rmsnorm(x, g):
    xf = x.astype(F32)
    y = xf * lax.rsqrt(jnp.mean(xf * xf, axis=-1, keepdims=True) + NORM_EPS)
    return (y * g.astype(F32)).astype(x.dtype)


def modulate(x, shift, scale):
    return x * (1.0 + scale) + shift


def rope_1d(x, pos):
    d = x.shape[-1]
    inv_freq = ROPE_THETA ** (-jnp.arange(0, d, 2, dtype=F32) / d)
    ang = pos.astype(F32)[:, None] * inv_freq[None, :]
    cos, sin = jnp.cos(ang)[:, None, :], jnp.sin(ang)[:, None, :]
    x1, x2 = x[..., : d // 2].astype(F32), x[..., d // 2:].astype(F32)
    return jnp.concatenate([x1 * cos - x2 * sin, x2 * cos + x1 * sin], -1).astype(x.dtype)


def rope_2d(x, rows, cols):
    half = x.shape[-1] // 2
    return jnp.concatenate([rope_1d(x[..., :half], rows), rope_1d(x[..., half:], cols)], -1)


def softmax_attend(q, k, v, scale):
    s = jnp.einsum("bhqd,bhkd->bhqk", q, k).astype(F32) * scale
    p = jax.nn.softmax(s, axis=-1).astype(v.dtype)
    return jnp.einsum("bhqk,bhke->bhqe", p, v)


def blocked_attend(q, k, v, scale):
    b, h, n_q, d = q.shape
    n_blk = n_q // ATTN_BLOCK
    qb = q.reshape(b, h, n_blk, ATTN_BLOCK, d).transpose(2, 0, 1, 3, 4)
    ob = lax.map(lambda qi: softmax_attend(qi, k, v, scale), qb)
    return ob.transpose(1, 2, 0, 3, 4).reshape(b, h, n_q, v.shape[-1])


def merge_heads(*outs):
    return jnp.concatenate([o.transpose(0, 2, 1, 3).reshape(o.shape[0], o.shape[2], -1) for o in outs], -1)


def even_heads(p, kv_norm_g, w_ukv, grid_pos):
    b, n, _ = p.shape
    sizes = [MLA_HEADS * MLA_QK_DIM, MLA_KV_RANK, MLA_ROPE_DIM, NA_HEADS * NA_DIM, NA_HEADS * NA_DIM]
    cuts = [int(s) for s in np.cumsum(sizes)]
    q_mla, c_kv, k_pe, q_na, k_na, v_na = jnp.split(p, cuts, axis=-1)
    q = q_mla.reshape(b, n, MLA_HEADS, MLA_QK_DIM)
    q_nope, q_pe = q[..., :MLA_NOPE_DIM], q[..., MLA_NOPE_DIM:]
    k_pe = k_pe[:, :, None, :]
    if grid_pos is not None:
        q_pe = rope_2d(q_pe, *grid_pos)
        k_pe = rope_2d(k_pe, *grid_pos)
    kv = (rmsnorm(c_kv, kv_norm_g) @ w_ukv).reshape(b, n, MLA_HEADS, MLA_NOPE_DIM + MLA_V_DIM)
    k_nope, v = kv[..., :MLA_NOPE_DIM], kv[..., MLA_NOPE_DIM:]
    q = jnp.concatenate([q_nope, q_pe], -1)
    k = jnp.concatenate([k_nope, jnp.broadcast_to(k_pe, (b, n, MLA_HEADS, MLA_ROPE_DIM))], -1)
    bhld = lambda t: t.transpose(0, 2, 1, 3)
    na = lambda t: t.reshape(b, n, NA_HEADS, NA_DIM).transpose(0, 2, 1, 3)
    return bhld(q), bhld(k), bhld(v), na(q_na), na(k_na), na(v_na)


def neighbourhood_attend(q, k, v, k_ctx, v_ctx, rel_bias):
    b, h, n, d = q.shape
    n_rows = n // GRID_W
    kh = min(NA_KH, n_rows)
    r = np.arange(n_rows)
    col = np.arange(GRID_W)
    row_idx = np.clip(r - kh // 2, 0, n_rows - kh)[:, None] + np.arange(kh)[None, :]
    col_start = np.clip(col - NA_KW // 2, 0, GRID_W - NA_KW)
    col_mask = (col[None, :] >= col_start[:, None]) & (col[None, :] < col_start[:, None] + NA_KW)
    row_off = row_idx - r[:, None] + (NA_KH - 1)
    col_off = np.clip(col[None, :] - col[:, None], 1 - NA_KW, NA_KW - 1) + (NA_KW - 1)
    qg = q.reshape(b, h, n_rows, GRID_W, d)
    k_band = k.reshape(b, h, n_rows, GRID_W, d)[:, :, row_idx]
    v_band = v.reshape(b, h, n_rows, GRID_W, d)[:, :, row_idx]
    scale = d ** -0.5
    bias = rel_bias[:, row_off[:, None, :, None], col_off[None, :, None, :]].astype(F32)
    s_nb = jnp.einsum("bhrqd,bhrkwd->bhrqkw", qg, k_band).astype(F32) * scale + bias
    s_nb = jnp.where(col_mask[:, None, :], s_nb, NEG_INF)
    s_ctx = jnp.einsum("bhrqd,bhcd->bhrqc", qg, k_ctx).astype(F32) * scale
    s = jnp.concatenate([s_nb.reshape(b, h, n_rows, GRID_W, kh * GRID_W), s_ctx], -1)
    p = jax.nn.softmax(s, axis=-1).astype(v.dtype)
    p_nb = p[..., : kh * GRID_W].reshape(b, h, n_rows, GRID_W, kh, GRID_W)
    o = (jnp.einsum("bhrqkw,bhrkwd->bhrqd", p_nb, v_band)
         + jnp.einsum("bhrqc,bhcd->bhrqd", p[..., kh * GRID_W:], v_ctx))
    return o.reshape(b, h, n, d)


def even_mixer(a_lat, a_ctx, w_in, kv_norm_g, w_ukv, rel_bias, ctx_out):
    n = a_lat.shape[1]
    pos = jnp.arange(n)
    grid = (pos // GRID_W, pos % GRID_W)
    q_l, k_l, v_l, qn_l, kn_l, vn_l = even_heads(a_lat @ w_in, kv_norm_g, w_ukv, grid)
    q_c, k_c, v_c, qn_c, kn_c, vn_c = even_heads(a_ctx @ w_in, kv_norm_g, w_ukv, None)
    mla_scale = MLA_QK_DIM ** -0.5
    o_mla = blocked_attend(q_l, jnp.concatenate([k_c, k_l], 2), jnp.concatenate([v_c, v_l], 2), mla_scale)
    o_na = neighbourhood_attend(qn_l, kn_l, vn_l, kn_c, vn_c, rel_bias)
    y_lat = merge_heads(o_mla, o_na)
    if not ctx_out:
        return y_lat, None
    y_ctx = merge_heads(softmax_attend(q_c, k_c, v_c, mla_scale),
                        softmax_attend(qn_c, kn_c, vn_c, NA_DIM ** -0.5))
    return y_lat, y_ctx


def gla_chunked(q, v, log_f, s0):
    b, h, n, dk = q.shape
    dv = v.shape[-1]
    n_chunks = n // HGRN_CHUNK
    k = -jnp.expm1(log_f)
    chunks = lambda t: t.reshape(b, h, n_chunks, HGRN_CHUNK, t.shape[-1]).transpose(2, 0, 1, 3, 4)
    causal = jnp.tril(jnp.ones((HGRN_CHUNK, HGRN_CHUNK), dtype=bool))[:, :, None]

    def step(state, inp):
        qc, kc, vc, fc = inp
        cum = jnp.cumsum(fc, axis=2)
        diff = jnp.where(causal, cum[:, :, :, None, :] - cum[:, :, None, :, :], 0.0)
        decay = jnp.where(causal, jnp.exp(diff), 0.0)
        scores = jnp.einsum("bhtd,bhsd,bhtsd->bhts", qc, kc, decay)
        o = (jnp.einsum("bhtd,bhde->bhte", qc * jnp.exp(cum), state)
             + jnp.einsum("bhts,bhse->bhte", scores, vc))
        last = cum[:, :, -1:, :]
        state = (jnp.exp(last[:, :, 0, :])[..., None] * state
                 + jnp.einsum("bhsd,bhse->bhde", kc * jnp.exp(last - cum), vc))
        return state, o

    s_fin, o = lax.scan(step, s0, (chunks(q), chunks(k), chunks(v), chunks(log_f)))
    return o.transpose(1, 2, 0, 3, 4).reshape(b, h, n, dv), s_fin


def log_forget(z, lb):
    f = lb + (1.0 - lb) * jax.nn.sigmoid(z.astype(F32))
    return jnp.log(jnp.maximum(f, FORGET_FLOOR))


def hgrn2_bidir(p_lat, p_ctx, lb_fwd, lb_bwd, norm_g, ctx_out):
    def heads(t):
        b_, n_, _ = t.shape
        return t.astype(F32).reshape(b_, n_, HGRN_HEADS, -1).transpose(0, 2, 1, 3)

    def prep(p):
        q, i, zf, zb, g = jnp.split(p, 5, axis=-1)
        return (heads(jax.nn.silu(q)), heads(i), heads(log_forget(zf, lb_fwd)),
                heads(log_forget(zb, lb_bwd)), g)

    def readout(o, g):
        b_, h_, n_, dv = o.shape
        o = rmsnorm(o, norm_g.reshape(HGRN_HEADS, 1, HGRN_DV)).transpose(0, 2, 1, 3).reshape(b_, n_, h_ * dv)
        return (o * jax.nn.silu(g.astype(F32))).astype(g.dtype)

    rev = lambda t: jnp.flip(t, axis=2)
    q_l, i_l, lff_l, lfb_l, g_l = prep(p_lat)
    q_c, i_c, lff_c, lfb_c, g_c = prep(p_ctx)
    s0 = jnp.zeros((p_ctx.shape[0], HGRN_HEADS, HGRN_DK, HGRN_DV), F32)
    o_cf, s_cf = gla_chunked(q_c, i_c, lff_c, s0)
    o_lf, _ = gla_chunked(q_l, i_l, lff_l, s_cf)
    o_cb, s_cb = gla_chunked(rev(q_c), rev(i_c), rev(lfb_c), s0)
    o_lb, _ = gla_chunked(rev(q_l), rev(i_l), rev(lfb_l), s_cb)
    y_lat = readout(o_lf + rev(o_lb), g_l)
    y_ctx = readout(o_cf + rev(o_cb), g_c) if ctx_out else None
    return y_lat, y_ctx


def short_conv(u, w, bias):
    n = u.shape[1]
    pad = HYENA_SHORT // 2
    up = jnp.pad(u, ((0, 0), (pad, HYENA_SHORT - 1 - pad), (0, 0)))
    out = bias
    for j in range(HYENA_SHORT):
        out = out + up[:, j:j + n] * w[j]
    return out


def hyena_filter(n, w1, b1, w2, b2, w3, b3, freq, w_out):
    pos = jnp.arange(n, dtype=F32)
    t = pos / max(n - 1, 1)
    bands = jnp.linspace(1e-4, HYENA_BANDS - 1, HYENA_BANDS, dtype=F32)
    ang = (2.0 * math.pi / n) * pos[:, None] * bands[None, :]
    z = jnp.concatenate([t[:, None], jnp.cos(ang), -jnp.sin(ang)], -1)
    fr = freq.astype(F32)
    hdn = jnp.sin(fr * (z @ w1.astype(F32) + b1.astype(F32)))
    hdn = jnp.sin(fr * (hdn @ w2.astype(F32) + b2.astype(F32)))
    hdn = jnp.sin(fr * (hdn @ w3.astype(F32) + b3.astype(F32)))
    filt = hdn @ w_out.astype(F32)
    max_decay = math.log(HYENA_DECAY_TARGET) / HYENA_FAST_PCT
    min_decay = math.log(HYENA_DECAY_TARGET) / HYENA_SLOW_PCT
    deltas = jnp.abs(jnp.linspace(min_decay, max_decay, HYENA_WIDTH, dtype=F32))
    window = jnp.exp(-t[:, None] * deltas[None, :])
    return filt[:, :HYENA_WIDTH] * window, filt[:, HYENA_WIDTH:] * window


def long_conv_bidir(v, h_fwd, h_bwd, skip):
    n, ch = h_fwd.shape
    k = jnp.concatenate([h_fwd, jnp.zeros((1, ch), F32), h_bwd[:0:-1]], axis=0)
    vf = v.astype(F32)
    y = jnp.fft.irfft(jnp.fft.rfft(vf, n=2 * n, axis=1) * jnp.fft.rfft(k, axis=0)[None],
                      n=2 * n, axis=1)[:, :n]
    return (y + vf * skip.astype(F32)).astype(v.dtype)


def hyena(u, conv_w, conv_b, filt, skip):
    n = u.shape[1]
    x0, x1, v = jnp.split(short_conv(u, conv_w, conv_b), 3, axis=-1)
    h_fwd, h_bwd = hyena_filter(n, *filt)
    return x0 * long_conv_bidir(v * x1, h_fwd, h_bwd, skip)


def odd_mixer(a_lat, a_ctx, w_in, lb_fwd, lb_bwd, norm_g, conv_w, conv_b, filt, skip, ctx_out):
    split_at = 5 * HGRN_WIDTH
    p_lat = a_lat @ w_in
    p_ctx = a_ctx @ (w_in if ctx_out else w_in[:, :split_at])
    hg_lat, hg_ctx = hgrn2_bidir(p_lat[..., :split_at], p_ctx[..., :split_at], lb_fwd, lb_bwd, norm_g, ctx_out)
    y_lat = jnp.concatenate([hg_lat, hyena(p_lat[..., split_at:], conv_w, conv_b, filt, skip)], -1)
    if not ctx_out:
        return y_lat, None
    y_ctx = jnp.concatenate([hg_ctx, hyena(p_ctx[..., split_at:], conv_w, conv_b, filt, skip)], -1)
    return y_lat, y_ctx


def sq_relu_mlp(h, w1, w2):
    return jnp.square(jax.nn.relu(h @ w1)) @ w2


def setup_inputs(seed: int = 0) -> dict:
    key = jax.random.key(seed)
    ks = jax.random.split(key, 32)
    D = D_MODEL

    def nrm(k, shape, scale):
        return jax.random.normal(k, shape, F32) * scale

    return {
        "x": nrm(ks[0], (BATCH, SEQ, D), 1.0),
        "c": nrm(ks[1], (BATCH, D), 1.0),
        "ctx": nrm(ks[2], (BATCH, CTX_LEN, D), 1.0),
        "c_ctx": nrm(ks[3], (D,), 1.0),
        "ada_w": nrm(ks[4], (DEPTH, D, 6 * D), 0.5 * D ** -0.5),
        "ada_b": nrm(ks[5], (DEPTH, 6 * D), 0.02),
        "norm_mix_g": 1.0 + nrm(ks[6], (DEPTH, D), 0.02),
        "norm_mlp_g": 1.0 + nrm(ks[7], (DEPTH, D), 0.02),
        "w_out": nrm(ks[8], (DEPTH, D, D), D ** -0.5),
        "mlp_w1": nrm(ks[9], (DEPTH, D, MLP_HIDDEN), D ** -0.5),
        "mlp_w2": nrm(ks[10], (DEPTH, MLP_HIDDEN, D), MLP_HIDDEN ** -0.5),
        "final_norm_g": 1.0 + nrm(ks[11], (D,), 0.02),
        "ev_w_in": nrm(ks[12], (N_EVEN, D, EV_IN_WIDTH), D ** -0.5),
        "mla_kv_norm_g": 1.0 + nrm(ks[13], (N_EVEN, MLA_KV_RANK), 0.02),
        "mla_w_ukv": nrm(ks[14], (N_EVEN, MLA_KV_RANK, MLA_HEADS * (MLA_NOPE_DIM + MLA_V_DIM)), MLA_KV_RANK ** -0.5),
        "na_rel_bias": nrm(ks[15], (N_EVEN, NA_HEADS, 2 * NA_KH - 1, 2 * NA_KW - 1), 0.1),
        "od_w_in": nrm(ks[16], (N_ODD, D, OD_IN_WIDTH), D ** -0.5),
        "hgrn_lb_logits": nrm(ks[17], (2, N_ODD, HGRN_WIDTH), 0.5),
        "hgrn_norm_g": 1.0 + nrm(ks[18], (N_ODD, HGRN_WIDTH), 0.02),
        "hy_conv_w": nrm(ks[19], (N_ODD, HYENA_SHORT, 3 * HYENA_WIDTH), HYENA_SHORT ** -0.5),
        "hy_conv_b": nrm(ks[20], (N_ODD, 3 * HYENA_WIDTH), 0.02),
        "hy_filt_w1": nrm(ks[21], (N_ODD, HYENA_EMB, HYENA_FILT_HIDDEN), HYENA_EMB ** -0.5),
        "hy_filt_b1": nrm(ks[22], (N_ODD, HYENA_FILT_HIDDEN), 0.1),
        "hy_filt_w2": nrm(ks[23], (N_ODD, HYENA_FILT_HIDDEN, HYENA_FILT_HIDDEN), HYENA_FILT_HIDDEN ** -0.5),
        "hy_filt_b2": nrm(ks[24], (N_ODD, HYENA_FILT_HIDDEN), 0.1),
        "hy_filt_w3": nrm(ks[25], (N_ODD, HYENA_FILT_HIDDEN, HYENA_FILT_HIDDEN), HYENA_FILT_HIDDEN ** -0.5),
        "hy_filt_b3": nrm(ks[26], (N_ODD, HYENA_FILT_HIDDEN), 0.1),
        "hy_filt_freq": 1.0 + nrm(ks[27], (N_ODD, HYENA_FILT_HIDDEN), 0.1),
        "hy_filt_wout": nrm(ks[28], (N_ODD, HYENA_FILT_HIDDEN, 2 * HYENA_WIDTH), 0.1 * HYENA_FILT_HIDDEN ** -0.5),
        "hy_skip": nrm(ks[29], (N_ODD, HYENA_WIDTH), 0.5),
    }


def reference(x, c, ctx, c_ctx, ada_w, ada_b, norm_mix_g, norm_mlp_g, w_out, mlp_w1, mlp_w2,
              final_norm_g, ev_w_in, mla_kv_norm_g, mla_w_ukv, na_rel_bias, od_w_in,
              hgrn_lb_logits, hgrn_norm_g, hy_conv_w, hy_conv_b, hy_filt_w1, hy_filt_b1,
              hy_filt_w2, hy_filt_b2, hy_filt_w3, hy_filt_b3, hy_filt_freq, hy_filt_wout, hy_skip):
    lb_p = jax.nn.softmax(hgrn_lb_logits.astype(F32), axis=1)
    lower_bounds = jnp.cumsum(lb_p, axis=1) - lb_p[:, :1]
    h_lat, h_ctx = x, ctx
    for l in range(DEPTH):
        ctx_out = l < DEPTH - 1
        mod_lat = jax.nn.silu(c) @ ada_w[l] + ada_b[l]
        sh_a, sc_a, g_a, sh_m, sc_m, g_m = [t[:, None, :] for t in jnp.split(mod_lat, 6, axis=-1)]
        mod_ctx = jax.nn.silu(c_ctx) @ ada_w[l] + ada_b[l]
        csh_a, csc_a, cg_a, csh_m, csc_m, cg_m = jnp.split(mod_ctx, 6, axis=-1)
        a_lat = modulate(rmsnorm(h_lat, norm_mix_g[l]), sh_a, sc_a)
        a_ctx = modulate(rmsnorm(h_ctx, norm_mix_g[l]), csh_a, csc_a)
        if l % 2 == 0:
            e = l // 2
            y_lat, y_ctx = even_mixer(a_lat, a_ctx, ev_w_in[e], mla_kv_norm_g[e], mla_w_ukv[e],
                                      na_rel_bias[e], ctx_out)
        else:
            o = l // 2
            filt = (hy_filt_w1[o], hy_filt_b1[o], hy_filt_w2[o], hy_filt_b2[o], hy_filt_w3[o],
                    hy_filt_b3[o], hy_filt_freq[o], hy_filt_wout[o])
            y_lat, y_ctx = odd_mixer(a_lat, a_ctx, od_w_in[o], lower_bounds[0, o], lower_bounds[1, o],
                                     hgrn_norm_g[o], hy_conv_w[o], hy_conv_b[o], filt, hy_skip[o], ctx_out)
        h_lat = h_lat + g_a * (y_lat @ w_out[l])
        h_lat = h_lat + g_m * sq_relu_mlp(modulate(rmsnorm(h_lat, norm_mlp_g[l]), sh_m, sc_m),
                                          mlp_w1[l], mlp_w2[l])
        if ctx_out:
            h_ctx = h_ctx + cg_a * (y_ctx @ w_out[l])
            h_ctx = h_ctx + cg_m * sq_relu_mlp(modulate(rmsnorm(h_ctx, norm_mlp_g[l]), csh_m, csc_m),
                                               mlp_w1[l], mlp_w2[l])
    return rmsnorm(h_lat, final_norm_g)
```

```python
import math
import numpy as np
import ml_dtypes
import concourse.bass as bass
import concourse.mybir as mybir
from concourse.bass_utils import run_bass_kernel_spmd

F32 = mybir.dt.float32
BF16 = mybir.dt.bfloat16
AF = mybir.ActivationFunctionType
ALU = mybir.AluOpType
AX = mybir.AxisListType
NPBF = ml_dtypes.bfloat16

D = 2048
SEQ = 2048
CTX = 256
DEPTH = 4
TL = 1152
TM = 2304
KC = 16
HID = 8192
EPS = 1e-6
GRID_W = 64

ENGS = ("pe", "act", "dve", "pool", "sp")
DMA_RING = 12
SB_BASE = 16640
SB_END = 229376


def _dsize(dt):
    return 4 if dt == F32 else 2


class Op:
    __slots__ = ("eng", "fn", "waits", "idx", "is_dma", "dma_sem", "dma_val", "needs_inc")


class MK:
    def __init__(self, nc):
        self.nc = nc
        self.ops = {e: [] for e in ENGS}
        self.acc = {}
        self.waited = {e: {} for e in ENGS}
        self.dma_count = {e: 0 for e in ENGS}
        self.dma_ring_uses = {e: [0] * DMA_RING for e in ENGS}
        self.sb_base = {}
        self.sb_top = SB_BASE
        self.psum = [nc.alloc_psum_tensor("bank%d" % i, [128, 512], F32) for i in range(8)]
        self.n_dram = 0

    def sb(self, name, shape, dtype, at=None):
        per = _dsize(dtype)
        for s in shape[1:]:
            per *= int(s)
        if at is None:
            at = self.sb_top
            self.sb_top = (at + per + 31) // 32 * 32
        assert at % 32 == 0 and at + per <= SB_END, (name, at, per)
        t = self.nc.alloc_sbuf_tensor_at(name, list(shape), dtype, offset=at)
        self.sb_base[t.name] = (at, per)
        return t

    def mark(self):
        return self.sb_top

    def release(self, mark):
        self.sb_top = mark

    def dram(self, name, shape, dtype, kind="Internal"):
        return self.nc.dram_tensor(name, list(shape), dtype, kind=kind).ap()

    def _box(self, ap):
        t = ap.tensor
        space = str(ap.space)
        off = int(ap.offset)
        pairs = [(int(s), int(c)) for (s, c) in ap.ap]
        ds = _dsize(ap.dtype)
        if space == "PSUM":
            return ("P:" + t.name, 0, 128, 0, 1)
        if space == "SB":
            base, per = self.sb_base[t.name]
            P = per // ds
            plo = off // P
            flo = off % P
            pext = 0
            fext = 0
            for (s, c) in pairs:
                if c <= 1:
                    continue
                if s != 0 and s % P == 0:
                    pext += (c - 1) * (s // P)
                else:
                    fext += (c - 1) * abs(s)
            return ("SB", plo, plo + pext + 1, base + flo * ds, base + (flo + fext + 1) * ds)
        ext = 0
        for (s, c) in pairs:
            if c <= 1:
                continue
            ext += (c - 1) * abs(s)
        return ("D:" + t.name, 0, 1, off, off + ext + 1)

    @staticmethod
    def _overlap(a, b):
        return a[1] < b[2] and b[1] < a[2] and a[3] < b[4] and b[3] < a[4]

    @staticmethod
    def _contains(a, b):
        return a[1] <= b[1] and b[2] <= a[2] and a[3] <= b[3] and b[4] <= a[4]

    def _need(self, op, tok):
        e = op.eng
        if tok[0] == "e":
            _, x, idx = tok
            if x == e and e == "pe":
                return
            key = ("e", x)
            val = idx
        else:
            _, q, slot, useno = tok
            key = ("d", q, slot)
            val = useno
        w = self.waited[e]
        if w.get(key, -1) >= val:
            return
        w[key] = val
        op.waits.append((key, val))

    BUCKET = 2048

    def _split(self, b):
        if b[0] != "SB":
            return [b]
        out = []
        lo, hi = b[3], b[4]
        k = lo // self.BUCKET
        while k * self.BUCKET < hi:
            out.append((("SB", k), b[1], b[2], max(lo, k * self.BUCKET), min(hi, (k + 1) * self.BUCKET)))
            k += 1
        return out

    def _track(self, op, reads, writes, tok):
        rb = []
        wb = []
        for a in reads:
            b = self._box(a)
            if b[0].startswith("P:"):
                wb.append(b)
            else:
                rb.extend(self._split(b))
        for a in writes:
            wb.extend(self._split(self._box(a)))
        for b in rb:
            lst = self.acc.get(b[0])
            if lst:
                for rec in lst:
                    if rec[1] == "w" and self._overlap(rec[0], b):
                        self._need(op, rec[2])
        for b in wb:
            lst = self.acc.get(b[0])
            if lst:
                for rec in lst:
                    if self._overlap(rec[0], b):
                        self._need(op, rec[2])
        for b in rb:
            lst = self.acc.setdefault(b[0], [])
            if tok[0] == "e":
                lst[:] = [r for r in lst if not (r[1] == "r" and r[2][0] == "e" and r[2][1] == tok[1]
                                                 and self._contains(b, r[0]))]
            lst.append([b, "r", tok])
        for b in wb:
            lst = self.acc.setdefault(b[0], [])
            lst[:] = [r for r in lst if not self._contains(b, r[0])]
            lst.append([b, "w", tok])

    def op(self, eng, fn, reads, writes):
        o = Op()
        o.eng = eng
        o.fn = fn
        o.waits = []
        o.is_dma = False
        o.needs_inc = False
        o.idx = len(self.ops[eng])
        self._track(o, reads, writes, ("e", eng, o.idx))
        self.ops[eng].append(o)
        return o

    def dma(self, out, in_, q="sp", **kw):
        o = Op()
        o.eng = q
        o.waits = []
        o.is_dma = True
        o.needs_inc = False
        o.idx = len(self.ops[q])
        n = self.dma_count[q]
        self.dma_count[q] = n + 1
        slot = n % DMA_RING
        prev = self.dma_ring_uses[q][slot]
        if prev > 0:
            self._need(o, ("d", q, slot, prev))
        self.dma_ring_uses[q][slot] = prev + 1
        o.dma_sem = (q, slot)
        o.dma_val = prev + 1
        o.fn = lambda eng, out=out, in_=in_, kw=kw: eng.dma_start(out=out, in_=in_, **kw)
        self._track(o, [in_], [out], ("d", q, slot, prev + 1))
        self.ops[q].append(o)
        return o

    def matmul(self, out, lhsT, rhs, start=True, stop=True, **kw):
        return self.op("pe", lambda e: e.matmul(out, lhsT, rhs, start=start, stop=stop, **kw),
                       [lhsT, rhs], [out])

    def transpose(self, out, in_, ident):
        return self.op("pe", lambda e: e.transpose(out, in_, ident), [in_, ident], [out])

    def act(self, out, in_, func, bias=None, scale=None, accum_out=None):
        reads = [in_]
        kw = {}
        if bias is not None:
            kw["bias"] = bias
            if not isinstance(bias, (int, float)):
                reads.append(bias)
        if scale is not None:
            kw["scale"] = scale
            if not isinstance(scale, (int, float)):
                reads.append(scale)
        writes = [out]
        if accum_out is not None:
            kw["accum_out"] = accum_out
            writes.append(accum_out)
        return self.op("act", lambda e: e.activation(out, in_, func, **kw), reads, writes)

    def tt(self, out, in0, in1, op, eng="dve"):
        return self.op(eng, lambda e: e.tensor_tensor(out, in0, in1, op), [in0, in1], [out])

    def ts(self, out, in0, s1, s2, op0, op1=None, eng="dve"):
        reads = [in0]
        if not isinstance(s1, (int, float)):
            reads.append(s1)
        if s2 is not None and not isinstance(s2, (int, float)):
            reads.append(s2)
        if op1 is None:
            return self.op(eng, lambda e: e.tensor_scalar(out, in0, s1, None, op0), reads, [out])
        return self.op(eng, lambda e: e.tensor_scalar(out, in0, s1, s2, op0, op1), reads, [out])

    def stt(self, out, in0, scalar, in1, op0, op1):
        reads = [in0, in1]
        if not isinstance(scalar, (int, float)):
            reads.append(scalar)
        return self.op("dve", lambda e: e.scalar_tensor_tensor(out, in0, scalar, in1, op0, op1),
                       reads, [out])

    def copy(self, out, in_, eng="dve"):
        if eng == "act":
            return self.op("act", lambda e: e.copy(out, in_), [in_], [out])
        return self.op(eng, lambda e: e.tensor_copy(out, in_), [in_], [out])

    def memset(self, ap, val, eng="dve"):
        return self.op(eng, lambda e: e.memset(ap, val), [], [ap])

    def recip(self, out, in_):
        return self.op("dve", lambda e: e.reciprocal(out, in_), [in_], [out])

    def scan(self, out, d0, d1, init, op0, op1):
        reads = [d0, d1]
        if not isinstance(init, (int, float)):
            reads.append(init)
        return self.op("dve", lambda e: e.tensor_tensor_scan(out, d0, d1, init, op0, op1), reads, [out])

    def emit(self):
        nc = self.nc
        for e in ENGS:
            for o in self.ops[e]:
                for (key, val) in o.waits:
                    if key[0] == "e":
                        self.ops[key[1]][val].needs_inc = True
        cnt = {}
        for e in ENGS:
            c = 0
            arr = []
            for o in self.ops[e]:
                if o.needs_inc and not o.is_dma:
                    c += 1
                arr.append(c)
            cnt[e] = arr
        from contextlib import ExitStack
        with ExitStack() as st:
            esem = {e: st.enter_context(nc.semaphore("s_" + e)) for e in ENGS}
            dsem = {}
            for q in ENGS:
                for s in range(min(DMA_RING, self.dma_count[q])):
                    dsem[(q, s)] = st.enter_context(nc.semaphore("d_%s_%d" % (q, s)))
            block = st.enter_context(nc.Block())
            engmap = {"pe": "tensor", "act": "scalar", "dve": "vector", "pool": "gpsimd", "sp": "sync"}

            def make(e):
                def body(eng):
                    for o in self.ops[e]:
                        for (key, val) in o.waits:
                            if key[0] == "e":
                                eng.wait_ge(esem[key[1]], cnt[key[1]][val])
                            else:
                                eng.wait_ge(dsem[(key[1], key[2])], 16 * val)
                        ins = o.fn(eng)
                        if o.is_dma:
                            ins.then_inc(dsem[o.dma_sem], 16)
                        elif o.needs_inc:
                            ins.then_inc(esem[e], 1)
                    for s in range(min(DMA_RING, self.dma_count[e])):
                        eng.wait_ge(dsem[(e, s)], 16 * self.dma_ring_uses[e][s])
                return body

            for e in ENGS:
                if len(self.ops[e]) == 0:
                    continue
                getattr(block, engmap[e])(make(e))
        return nc


MOD_SH_A, MOD_SC_A, MOD_G_A, MOD_SH_M, MOD_SC_M, MOD_G_M, MOD_GSC_A, MOD_GSC_M = [16 * i for i in range(8)]


class Ctx:
    def __init__(self, nc):
        self.nc = nc
        self.m = MK(nc)
        self.wring = None
        self.wi = 0
        self.ev = 0

    def consts(self, ident_d, need_bf=True):
        m = self.m
        self.ident = m.sb("ident", [128, 128], F32)
        m.dma(self.ident[:], ident_d)
        self.ones_bf = m.sb("ones_bf", [128, 128], BF16)
        m.memset(self.ones_bf[:], 1.0)
        self.ident_bf = m.sb("ident_bf", [128, 128], BF16)
        m.copy(self.ident_bf[:], self.ident[:])

    def init_wring(self, nslot=5):
        m = self.m
        self.wring = [m.sb("wslot%d" % i, [128, 4096], BF16) for i in range(nslot)]
        self.wi = 0

    def wload(self, W2d, k0, kcb, c0, ncb):
        assert kcb * ncb <= 4096
        slot = self.wring[self.wi % len(self.wring)]
        self.wi += 1
        view = slot[:, 0:kcb * ncb].rearrange("p (k n) -> p k n", n=ncb)
        src = W2d[k0:k0 + 128 * kcb, c0:c0 + ncb].rearrange("(k p) n -> p k n", p=128)
        self.m.dma(view, src, q="pool")
        return view

    def evac_eng(self):
        self.ev += 1
        return "act" if self.ev % 2 == 0 else "dve"


def rstd_compute(cx, src, nch, T, nfeat, rstd, sq, ps_bank, tmp):
    m = cx.m
    ps = cx.m.psum[ps_bank]
    for t0 in range(0, T, 384):
        tw = min(384, T - t0)
        for k in range(nch):
            m.act(sq[:, k, 0:tw], src[:, k, t0:t0 + tw], AF.Square)
        for k in range(nch):
            m.matmul(ps[:, 0:tw], cx.ones_bf[:, 0:128], sq[:, k, 0:tw], start=(k == 0), stop=(k == nch - 1))
        m.act(tmp[:, 0:tw], ps[:, 0:tw], AF.Sqrt, bias=cx.eps_col[:, 0:1], scale=1.0 / nfeat)
        m.recip(rstd[:, t0:t0 + tw], tmp[:, 0:tw])


def norm_modulate(cx, hT, actT, rstd, modall, l, gsc_base, sh_base, tmp4):
    m = cx.m
    for k in range(KC):
        for (c0, c1, col) in ((0, 1024, 0), (1024, TL, 1)):
            gs = modall[:, l, gsc_base + k, col:col + 1]
            sh = modall[:, l, sh_base + k, col:col + 1]
            m.stt(tmp4[:, c0:c1], hT[:, k, c0:c1], gs, rstd[:, c0:c1], ALU.mult, ALU.mult)
            m.act(actT[:, k, c0:c1], tmp4[:, c0:c1], AF.Identity, bias=sh, scale=1.0)


def proj_fm(cx, W2d, ncols_total, col0, chunk_cols, inT, nkc, consume, T=TL, bank_sets=((0, 1, 2), (3, 4, 5))):
    m = cx.m
    ncb = 4096 // nkc
    nchunks = ncols_total // chunk_cols
    per_slot = ncb // chunk_cols
    ci = 0
    tbs = [(t0, min(384, T - t0)) for t0 in range(0, T, 384)]
    for s0 in range(0, nchunks, per_slot):
        nhere = min(per_slot, nchunks - s0)
        slot = cx.wload(W2d, 0, nkc, col0 + s0 * chunk_cols, nhere * chunk_cols)
        for j in range(nhere):
            banks = bank_sets[ci % len(bank_sets)]
            for k in range(nkc):
                for ti, (t0, tw) in enumerate(tbs):
                    m.matmul(m.psum[banks[ti]][0:chunk_cols, 0:tw],
                             slot[:, k, j * chunk_cols:(j + 1) * chunk_cols],
                             inT[:, k, t0:t0 + tw], start=(k == 0), stop=(k == nkc - 1))
            for ti, (t0, tw) in enumerate(tbs):
                consume(ci, ti, m.psum[banks[ti]][0:chunk_cols, 0:tw], t0, tw)
            ci += 1


def emit_mod(cx, out, cT_d, ada_w_d, ada_b_d, nch, nr):
    m = cx.m
    mk = m.mark()
    sc = m.sb("mod_sc", [128, KC, nr], F32)
    m.dma(sc[:], cT_d)
    m.act(sc[:], sc[:], AF.Silu)
    adabT = m.sb("mod_adabT", [128, DEPTH, nch], F32)
    btmp = m.sb("mod_btmp", [nch, 128], F32)
    for l in range(DEPTH):
        m.dma(btmp[:], ada_b_d[l].rearrange("(c p) -> c p", p=128))
        m.transpose(m.psum[6][:, 0:nch], btmp[:], cx.ident[0:nch, 0:nch])
        m.copy(adabT[:, l, :], m.psum[6][:, 0:nch])
    wsl = [m.sb("mod_w%d" % i, [128, KC, 512], F32) for i in range(3)]
    wi = 0
    for l in range(DEPTH):
        ps = m.psum[l % 2]
        for nb in range(nch // 4):
            slot = wsl[wi % 3]
            wi += 1
            src = ada_w_d[l][:, nb * 512:(nb + 1) * 512].rearrange("(k p) n -> p k n", p=128)
            m.dma(slot[:], src, q="sp")
            for jj in range(4):
                j = nb * 4 + jj
                for k in range(KC):
                    m.matmul(ps[:, nr * j:nr * j + nr], slot[:, k, jj * 128:(jj + 1) * 128], sc[:, k, :],
                             start=(k == 0), stop=(k == KC - 1))
        m.tt(out[:, l, :, :], ps[:, 0:nch * nr].rearrange("p (c t) -> p c t", t=nr),
             adabT[:, l, :].unsqueeze(2).to_broadcast([128, nch, nr]), ALU.add)
    m.release(mk)


def emit_gsc(cx, modall, gmix_d, gmlp_d):
    m = cx.m
    mk = m.mark()
    gmix = m.sb("mod_gmix", [128, DEPTH, KC], F32)
    gmlp = m.sb("mod_gmlp", [128, DEPTH, KC], F32)
    m.dma(gmix[:], gmix_d)
    m.dma(gmlp[:], gmlp_d)
    for l in range(DEPTH):
        m.stt(modall[:, l, MOD_GSC_A:MOD_GSC_A + 16, :], modall[:, l, MOD_SC_A:MOD_SC_A + 16, :], 1.0,
              gmix[:, l, :].unsqueeze(2).to_broadcast([128, KC, 2]), ALU.add, ALU.mult)
        m.stt(modall[:, l, MOD_GSC_M:MOD_GSC_M + 16, :], modall[:, l, MOD_SC_M:MOD_SC_M + 16, :], 1.0,
              gmlp[:, l, :].unsqueeze(2).to_broadcast([128, KC, 2]), ALU.add, ALU.mult)
    m.release(mk)


def build_mod_program():
    nc = bass.Bass("TRN2", target_bir_lowering=False)
    cx = Ctx(nc)
    m = cx.m
    ident_d = m.dram("ident", [128, 128], F32, "ExternalInput")
    cT_d = m.dram("cT", [128, KC, 5], F32, "ExternalInput")
    ada_w_d = m.dram("ada_w", [DEPTH, D, 1536], F32, "ExternalInput")
    ada_b_d = m.dram("ada_b", [DEPTH, 1536], F32, "ExternalInput")
    out_d = m.dram("modpart", [128, DEPTH, 12, 5], F32, "ExternalOutput")
    cx.consts(ident_d)
    outt = m.sb("modpart", [128, DEPTH, 12, 5], F32)
    emit_mod(cx, outt, cT_d, ada_w_d, ada_b_d, 12, 5)
    m.dma(out_d, outt[:])
    m.emit()
    return nc


EV_QNOPE, EV_CKV, EV_QNA, EV_KNA, EV_PE, EV_VNA, EV_NEXT = 0, 1024, 1536, 2560, 3584, 4736, 5760
SEGS = ((0, 1024, 0), (1024, TL, 1))


def proj_tm(cx, W2d, k0, nkc, col0, ncols, inT, consume, T, banks=(6, 7)):
    m = cx.m
    ncb = 4096 // nkc
    bi = 0
    for c0 in range(0, ncols, ncb):
        cw = min(ncb, ncols - c0)
        slot = cx.wload(W2d, k0, nkc, col0 + c0, cw)
        for t in range(T // 128):
            for n0 in range(0, cw, 512):
                nw = min(512, cw - n0)
                ps = m.psum[banks[bi % 2]]
                bi += 1
                for k in range(nkc):
                    m.matmul(ps[:, 0:nw], inT[:, k, t * 128:(t + 1) * 128], slot[:, k, n0:n0 + nw],
                             start=(k == 0), stop=(k == nkc - 1))
                consume(t, c0 + n0, nw, ps[:, 0:nw])


def rmw_consumer(cx, hT, modall, lidx, gate_base):
    m = cx.m

    def consume(ci, ti, ps, t0, tw):
        for (c0, c1, col) in SEGS:
            a, b = max(c0, t0), min(c1, t0 + tw)
            if a >= b:
                continue
            m.stt(hT[:, ci, a:b], ps[:, a - t0:b - t0], modall[:, lidx, gate_base + ci, col:col + 1],
                  hT[:, ci, a:b], ALU.mult, ALU.add)
    return consume


def emit_post(cx, lp, hT, actT, rstd, modall, io, T):
    m = cx.m
    if "load_y" in io:
        io["load_y"](actT)
    else:
        m.dma(actT[:], io["yT"])
    proj_fm(cx, io["w_out"], D, 0, 128, actT, KC, rmw_consumer(cx, hT, modall, lp, MOD_G_A), T=T)
    mk = m.mark()
    sq = m.sb("sq", [128, KC, 384], BF16)
    tmp4 = m.sb("tmp4", [128, TL], F32)
    tmp = m.sb("tmp", [128, 384], F32)
    rstd_compute(cx, hT, KC, T, D, rstd, sq, 6, tmp)
    norm_modulate(cx, hT, actT, rstd, modall, lp, MOD_GSC_M, MOD_SH_M, tmp4)
    m.release(mk)
    mk = m.mark()
    hid = [m.sb("hid%d" % i, [128, 4, TL], BF16) for i in range(2)]
    rtmp = [m.sb("rtmp%d" % i, [128, 384], F32) for i in range(2)]
    rmw = rmw_consumer(cx, hT, modall, lp, MOD_G_M)
    cnt = [0]
    tbs = [(t0, min(384, T - t0)) for t0 in range(0, T, 384)]
    for hb in range(HID // 512):
        hbuf = hid[hb % 2]

        def relu2(ci, ti, ps, t0, tw, hbuf=hbuf):
            r = rtmp[cnt[0] % 2]
            cnt[0] += 1
            m.act(r[:, 0:tw], ps, AF.Relu)
            m.tt(hbuf[:, ci, t0:t0 + tw], r[:, 0:tw], r[:, 0:tw], ALU.mult, eng="dve")
        proj_fm(cx, io["w1"], 512, hb * 512, 128, actT, KC, relu2, T=T)
        for half in range(2):
            slot = cx.wload(io["w2"], hb * 512, 4, half * 1024, 1024)
            for j in range(8):
                oc = half * 8 + j
                banks = ((0, 1, 2), (3, 4, 5))[oc % 2]
                for k in range(4):
                    for ti, (t0, tw) in enumerate(tbs):
                        m.matmul(m.psum[banks[ti]][:, 0:tw], slot[:, k, j * 128:(j + 1) * 128],
                                 hbuf[:, k, t0:t0 + tw], start=(k == 0), stop=(k == 3))
                for ti, (t0, tw) in enumerate(tbs):
                    rmw(oc, ti, m.psum[banks[ti]][:, 0:tw], t0, tw)
    m.release(mk)


def stager(cx, stage, dst_of_chunk, kind="copy"):
    m = cx.m

    def consume(ci, ti, ps, t0, tw):
        st = stage[ci % len(stage)]
        rows = ps.shape[0]
        if kind == "silu":
            m.act(st[0:rows, t0:t0 + tw], ps, AF.Silu)
        else:
            if cx.evac_eng() == "act":
                m.act(st[0:rows, t0:t0 + tw], ps, AF.Identity)
            else:
                m.copy(st[0:rows, t0:t0 + tw], ps)
        if t0 + tw >= TL:
            m.dma(dst_of_chunk(ci), st[0:rows, :])
    return consume


def emit_pre_even(cx, l, hT, actT, rstd, modall, io):
    m = cx.m
    e = l // 2
    mk = m.mark()
    sq = m.sb("sq", [128, KC, 384], BF16)
    tmp4 = m.sb("tmp4", [128, TL], F32)
    tmp = m.sb("tmp", [128, 384], F32)
    rstd_compute(cx, hT, KC, TL, D, rstd, sq, 6, tmp)
    norm_modulate(cx, hT, actT, rstd, modall, l, MOD_GSC_A, MOD_SH_A, tmp4)
    m.release(mk)
    mk = m.mark()
    stage = [m.sb("stg%d" % i, [128, TL], BF16) for i in range(4)]
    ckvT = m.sb("ckvT", [128, 4, TL], F32)
    kvnT = m.sb("kvnT", [128, 4, TL], BF16)
    cosT = m.sb("cosT", [64, TL], F32)
    sinT = m.sb("sinT", [64, TL], F32)
    rt = [m.sb("ropet%d" % i, [64, 384], F32) for i in range(2)]
    sq4 = m.sb("sq4", [128, 4, 384], BF16)
    tmp = m.sb("tmpb", [128, 384], F32)
    kvg = m.sb("kvg", [128, 4], F32)
    m.dma(cosT[:], io["cosT"])
    m.dma(sinT[:], io["sinT"])
    m.dma(kvg[:], io["kvg"])
    W = io["w_in"]
    proj_fm(cx, W, 1024, EV_QNOPE, 128, actT, KC, stager(cx, stage, lambda ci: io["qT"][ci, 0:128, :]))

    def ckv_c(ci, ti, ps, t0, tw):
        m.copy(ckvT[:, ci, t0:t0 + tw], ps, eng=cx.evac_eng())
    proj_fm(cx, W, 512, EV_CKV, 128, actT, KC, ckv_c)
    proj_fm(cx, W, 1024, EV_QNA, 128, actT, KC, stager(cx, stage, lambda ci: io["qnT"][ci * 128:(ci + 1) * 128, :]))
    proj_fm(cx, W, 1024, EV_KNA, 128, actT, KC, stager(cx, stage, lambda ci: io["knT"][ci * 128:(ci + 1) * 128, :]))

    keep = {}

    def rope_c(ci, ti, ps, t0, tw):
        if ci % 2 == 0:
            keep[ti] = ps
            return
        p = ci // 2
        st = stage[p % len(stage)]
        m.tt(rt[0][:, 0:tw], keep[ti], cosT[:, t0:t0 + tw], ALU.mult)
        m.tt(rt[1][:, 0:tw], ps, sinT[:, t0:t0 + tw], ALU.mult)
        m.tt(st[0:64, t0:t0 + tw], rt[0][:, 0:tw], rt[1][:, 0:tw], ALU.add)
        if t0 + tw >= TL:
            dst = io["qT"][p, 128:192, :] if p < 8 else io["kpeT"]
            m.dma(dst, st[0:64, :])
    proj_fm(cx, W, 18 * 64, EV_PE, 64, actT, KC, rope_c)

    def vna_c(t, c0, nw, ps):
        st = stage[(t + c0 // 256) % len(stage)]
        m.copy(st[:, 0:nw], ps, eng=cx.evac_eng())
        m.dma(io["vn"][t * 128:(t + 1) * 128, c0:c0 + nw], st[:, 0:nw])
    proj_tm(cx, W, 0, KC, EV_VNA, 1024, actT, vna_c, TL)

    rstd_compute(cx, ckvT, 4, TL, 512, rstd, sq4, 6, tmp)
    for k in range(4):
        m.stt(ckvT[:, k, :], ckvT[:, k, :], kvg[:, k:k + 1], rstd[:, :], ALU.mult, ALU.mult)
        m.copy(kvnT[:, k, :], ckvT[:, k, :], eng="act")
    proj_fm(cx, io["w_ukv"], 1024, 0, 128, kvnT, 4, stager(cx, stage, lambda ci: io["kT"][ci, :, :]))

    def v_c(t, c0, nw, ps):
        st = stage[(t + c0 // 512) % len(stage)]
        m.copy(st[:, 0:nw], ps, eng=cx.evac_eng())
        m.dma(io["v"][t * 128:(t + 1) * 128, c0:c0 + nw], st[:, 0:nw])
    proj_tm(cx, io["w_ukv"], 0, 4, 1024, 1024, kvnT, v_c, TL)
    m.release(mk)


def emit_pre_odd(cx, l, hT, actT, rstd, modall, io):
    m = cx.m
    mk = m.mark()
    sq = m.sb("sq", [128, KC, 384], BF16)
    tmp4 = m.sb("tmp4", [128, TL], F32)
    tmp = m.sb("tmp", [128, 384], F32)
    rstd_compute(cx, hT, KC, TL, D, rstd, sq, 6, tmp)
    norm_modulate(cx, hT, actT, rstd, modall, l, MOD_GSC_A, MOD_SH_A, tmp4)
    m.release(mk)
    mk = m.mark()
    stage = [m.sb("stg%d" % i, [128, TL], BF16) for i in range(4)]
    stage32 = [m.sb("stg32_%d" % i, [128, TL], F32) for i in range(2)]
    W = io["w_in"]
    proj_fm(cx, W, 1024, 0, 128, actT, KC,
            stager(cx, stage, lambda ci: io["qT"][ci * 128:(ci + 1) * 128, :], kind="silu"))

    def vi_c(t, c0, nw, ps):
        st = stage[(t + c0 // 256) % len(stage)]
        m.copy(st[:, 0:nw], ps, eng=cx.evac_eng())
        m.dma(io["vi"][t * 128:(t + 1) * 128, c0:c0 + nw], st[:, 0:nw])
    proj_tm(cx, W, 0, KC, 1024, 1024, actT, vi_c, TL)
    proj_fm(cx, W, 2048, 2048, 128, actT, KC,
            stager(cx, stage32, lambda ci: io["zT"][ci * 128:(ci + 1) * 128, :]))
    proj_fm(cx, W, 1024, 4096, 128, actT, KC,
            stager(cx, stage, lambda ci: io["gT"][ci * 128:(ci + 1) * 128, :], kind="silu"))
    proj_fm(cx, W, 3072, 5120, 128, actT, KC,
            stager(cx, stage, lambda ci: io["uT"][ci * 128:(ci + 1) * 128, :]))
    m.release(mk)


def emit_final(cx, hT, rstd, io):
    m = cx.m
    mk = m.mark()
    sq = m.sb("sq", [128, KC, 384], BF16)
    tmp = m.sb("tmp", [128, 384], F32)
    fg = m.sb("fg", [128, KC], F32)
    stage32 = [m.sb("stg32_%d" % i, [128, 1024], F32) for i in range(2)]
    m.dma(fg[:], io["fg"])
    rstd_compute(cx, hT, KC, 1024, D, rstd, sq, 6, tmp)
    for k in range(KC):
        st = stage32[k % 2]
        m.stt(st[:, :], hT[:, k, 0:1024], fg[:, k:k + 1], rstd[:, 0:1024], ALU.mult, ALU.mult)
        m.dma(io["outT"][:, k, :], st[:, :])
    m.release(mk)


def build_R_program(l):
    nc = bass.Bass("TRN2", target_bir_lowering=False)
    cx = Ctx(nc)
    m = cx.m
    dr = lambda n, s, d, k="ExternalInput": m.dram(n, s, d, k)
    ident_d = dr("ident", [128, 128], F32)
    hT_d = dr("hT", [128, KC, TL], F32)
    mod_d = dr("modraw", [128, DEPTH, 96, 2], F32)
    cx.consts(ident_d)
    cx.eps_col = m.sb("eps", [128, 1], F32)
    m.memset(cx.eps_col[:], EPS)
    modall = m.sb("modall", [128, DEPTH, 128, 2], F32)
    m.dma(modall[:, :, 0:96, :], mod_d)
    emit_gsc(cx, modall, dr("gmix", [128, DEPTH, KC], F32), dr("gmlp", [128, DEPTH, KC], F32))
    hT = m.sb("hT", [128, KC, TL], F32)
    m.dma(hT[:], hT_d)
    actT = m.sb("actT", [128, KC, TL], BF16)
    rstd = m.sb("rstd", [128, TL], F32)
    cx.init_wring(4)
    if l >= 1:
        io = {"yT": dr("yT", [128, KC, TL], BF16), "w_out": dr("w_out", [D, D], F32),
              "w1": dr("w1", [D, HID], F32), "w2": dr("w2", [HID, D], F32)}
        emit_post(cx, l - 1, hT, actT, rstd, modall, io, TL if l <= 3 else 1024)
    if l <= 3 and l % 2 == 0:
        io = {"w_in": dr("w_in", [D, EV_NEXT], F32), "w_ukv": dr("w_ukv", [512, 2048], F32),
              "cosT": dr("cosT", [64, TL], F32), "sinT": dr("sinT", [64, TL], F32),
              "kvg": dr("kvg", [128, 4], F32),
              "qT": dr("qT", [8, 192, TL], BF16, "ExternalOutput"),
              "kT": dr("kT", [8, 128, TL], BF16, "ExternalOutput"),
              "kpeT": dr("kpeT", [64, TL], BF16, "ExternalOutput"),
              "v": dr("v", [TL, 1024], BF16, "ExternalOutput"),
              "qnT": dr("qnT", [1024, TL], BF16, "ExternalOutput"),
              "knT": dr("knT", [1024, TL], BF16, "ExternalOutput"),
              "vn": dr("vn", [TL, 1024], BF16, "ExternalOutput")}
        emit_pre_even(cx, l, hT, actT, rstd, modall, io)
    elif l <= 3:
        io = {"w_in": dr("w_in", [D, 8192], F32),
              "qT": dr("qT", [1024, TL], BF16, "ExternalOutput"),
              "vi": dr("vi", [TL, 1024], BF16, "ExternalOutput"),
              "zT": dr("zT", [2048, TL], F32, "ExternalOutput"),
              "gT": dr("gT", [1024, TL], BF16, "ExternalOutput"),
              "uT": dr("uT", [3072, TL], BF16, "ExternalOutput")}
        emit_pre_odd(cx, l, hT, actT, rstd, modall, io)
    if l <= 3:
        hout = dr("hT_out", [128, KC, TL], F32, "ExternalOutput")
        m.dma(hout, hT[:])
    else:
        io = {"fg": dr("fg", [128, KC], F32), "outT": dr("outT", [128, KC, 1024], F32, "ExternalOutput")}
        emit_final(cx, hT, rstd, io)
    m.emit()
    return nc


def fm_vec(v):
    return np.ascontiguousarray(np.asarray(v).reshape(-1, 128).T)


def fm_tokens(tok):
    T = tok.shape[0]
    return np.ascontiguousarray(tok.reshape(T, KC, 128).transpose(2, 1, 0))


def rope_perm_idx():
    idx = np.zeros(64, dtype=np.int64)
    for j in range(64):
        jj = j % 32
        idx[j] = j + 16 if jj < 16 else j - 16
    return idx


def ev_w_in_ext(w):
    perm = rope_perm_idx()
    cols = []
    for h in range(8):
        cols.append(np.arange(h * 192, h * 192 + 128))
    cols.append(np.arange(1536, 2048))
    cols.append(np.arange(2112, 3136))
    cols.append(np.arange(3136, 4160))
    for h in range(8):
        base = h * 192 + 128
        cols.append(base + np.arange(64))
        cols.append(base + perm)
    cols.append(2048 + np.arange(64))
    cols.append(2048 + perm)
    cols.append(np.arange(4160, 5184))
    idx = np.concatenate(cols)
    assert idx.shape[0] == EV_NEXT
    return np.ascontiguousarray(w[:, idx])


def ukv_ext(w):
    kc = np.concatenate([np.arange(h * 256, h * 256 + 128) for h in range(8)])
    vc = np.concatenate([np.arange(h * 256 + 128, h * 256 + 256) for h in range(8)])
    return np.ascontiguousarray(w[:, np.concatenate([kc, vc])])


def rope_tables(rank):
    pos = np.arange(rank * 1024, rank * 1024 + 1024)
    rows = (pos // GRID_W).astype(np.float32)
    cols = (pos % GRID_W).astype(np.float32)
    inv_freq = (np.float32(10000.0) ** (-np.arange(0, 32, 2, dtype=np.float32) / np.float32(32))).astype(np.float32)
    cosT = np.ones((64, TL), dtype=np.float32)
    sinT = np.zeros((64, TL), dtype=np.float32)
    for j in range(64):
        p = rows if j < 32 else cols
        jj = j % 32
        ang = (p * inv_freq[jj % 16]).astype(np.float32)
        cosT[j, :1024] = np.cos(ang)
        s = np.sin(ang)
        sinT[j, :1024] = -s if jj < 16 else s
    return cosT, sinT


NA_CLS = {(0, 0): 0, (1, 0): 1, (2, 0): 2, (3, 0): 3, (4, 0): 4, (5, 1): 5, (5, 0): 6, (6, 0): 7, (7, 0): 8}


def na_row_info(r):
    rs = min(max(r - 4, 0), 24)
    base = 2 * (rs // 2)
    cls = NA_CLS[(r - base, rs - base)]
    ntiles = 5 if rs - base == 1 else 4
    return base // 2, cls, ntiles


def na_tables():
    ridx = np.zeros((9, 5, 128, 64), dtype=np.int64)
    cidx = np.zeros((9, 5, 128, 64), dtype=np.int64)
    mask = np.zeros((9, 5, 128, 64), dtype=np.float32)
    p = np.arange(128)
    qc = np.arange(64)
    kcol = (p % 64)[:, None]
    col_start = np.clip(qc - 8, 0, 48)[None, :]
    col_ok = (kcol >= col_start) & (kcol < col_start + 16)
    coff = np.clip(kcol - qc[None, :], -15, 15) + 15
    for (dr, off), c in NA_CLS.items():
        for j in range(5):
            krel = 2 * j + p // 64
            inband = (krel >= off) & (krel < off + 8)
            roff = np.clip(krel - dr + 7, 0, 14)
            ridx[c, j] = roff[:, None]
            cidx[c, j] = coff
            ok = inband[:, None] & col_ok
            mask[c, j] = np.where(ok, 0.0, -30000.0)
    return ridx, cidx, mask


def dense_attn(cx, parts, vtile, key_tiles, q0, qn, scale, yT, pbuf, rsb):
    m = cx.m
    nk = len(key_tiles)
    for qb in range(q0, q0 + qn, 512):
        qw = min(512, q0 + qn - qb)
        O = m.psum[4]
        Sm = m.psum[5]

        def s_mm(i):
            sb = m.psum[i % 4]
            kt = key_tiles[i]
            for pi, (qp, kp) in enumerate(parts):
                m.matmul(sb[:, 0:qw], kp[:, kt * 128:(kt + 1) * 128], qp[:, qb:qb + qw],
                         start=(pi == 0), stop=(pi == len(parts) - 1))
        s_mm(0)
        for i in range(nk):
            if i + 1 < nk:
                s_mm(i + 1)
            P = pbuf[i % len(pbuf)]
            m.act(P[:, 0:qw], m.psum[i % 4][:, 0:qw], AF.Exp, scale=scale)
            m.matmul(O[:, 0:qw], vtile(key_tiles[i]), P[:, 0:qw], start=(i == 0), stop=(i == nk - 1))
            m.matmul(Sm[:, 0:qw], cx.ones_bf[:, 0:128], P[:, 0:qw], start=(i == 0), stop=(i == nk - 1))
        m.recip(rsb[:, 0:qw], Sm[:, 0:qw])
        m.tt(yT[:, qb:qb + qw], O[:, 0:qw], rsb[:, 0:qw], ALU.mult)


def emit_M_even(cx, io, ctx_out):
    m = cx.m
    mk = m.mark()
    LT = list(range(2, 18))
    CTt = [0, 1]
    allk = CTt + LT
    pbuf = [m.sb("pbuf%d" % i, [128, 512], BF16) for i in range(3)]
    rsb = m.sb("rsb", [128, 512], F32)
    ystage = [m.sb("ystage%d" % i, [128, TM], BF16) for i in range(2)]
    vsb = m.sb("vsb", [128, 18, 512], BF16)
    m.dma(vsb[:], io["mv"].rearrange("(t p) c -> p t c", p=128))
    kpe = m.sb("kpe", [64, TM], BF16)
    m.dma(kpe[:], io["mkpe"])
    qn_ = [m.sb("qnope%d" % i, [128, TM], BF16) for i in range(2)]
    qp_ = [m.sb("qpe%d" % i, [64, TM], BF16) for i in range(2)]
    kn_ = [m.sb("knope%d" % i, [128, TM], BF16) for i in range(2)]
    mla_scale = 192.0 ** -0.5
    c0 = 0 if ctx_out else 256
    for h in range(4):
        qn, qp, kn = qn_[h % 2], qp_[h % 2], kn_[h % 2]
        m.dma(qn[:], io["mq"][h, 0:128, :])
        m.dma(qp[:], io["mq"][h, 128:192, :])
        m.dma(kn[:], io["mk"][h])
        ys = ystage[h % 2]
        parts = [(qn, kn), (qp, kpe)]
        vt = lambda kt, h=h: vsb[:, kt, h * 128:(h + 1) * 128]
        if ctx_out:
            dense_attn(cx, parts, vt, CTt, 0, 256, mla_scale, ys, pbuf, rsb)
        dense_attn(cx, parts, vt, allk, 256, 2048, mla_scale, ys, pbuf, rsb)
        m.dma(io["myT"][h, :, c0:TM], ys[:, c0:TM])
    m.dma(vsb[:], io["mvn"].rearrange("(t p) c -> p t c", p=128))
    mask = m.sb("namask", [128, 9 * 5 * 64], F32)
    m.dma(mask[:], io["namask"])
    bias_ = [m.sb("nabias%d" % i, [128, 9 * 5 * 64], F32) for i in range(2)]
    lg_ = [m.sb("nalg%d" % i, [128, 320], F32) for i in range(2)]
    P_ = [m.sb("naP%d" % i, [128, 448], BF16) for i in range(2)]
    nrs_ = [m.sb("nars%d" % i, [128, 64], F32) for i in range(2)]
    na_scale = 128.0 ** -0.5
    it = 0
    for h in range(4):
        qn, kn = qn_[h % 2], kn_[h % 2]
        m.dma(qn[:], io["mqn"][h * 128:(h + 1) * 128, :])
        m.dma(kn[:], io["mkn"][h * 128:(h + 1) * 128, :])
        bias = bias_[h % 2]
        m.dma(bias[:], io["nabias"][h])
        m.stt(bias[:], bias[:], 1.0, mask[:], ALU.mult, ALU.add)
        bv = bias[:].rearrange("p (c x) -> p c x", c=9)
        ys = ystage[h % 2]
        vt = lambda kt, h=h: vsb[:, kt, h * 128:(h + 1) * 128]
        if ctx_out:
            dense_attn(cx, [(qn, kn)], vt, CTt, 0, 256, na_scale, ys, pbuf, rsb)
        for r in range(32):
            j0, cls, nt = na_row_info(r)
            q0 = 256 + 64 * r
            S = m.psum[it % 2]
            O = m.psum[2 + it % 2]
            lg, P, nrs = lg_[it % 2], P_[it % 2], nrs_[it % 2]
            it += 1
            tiles = [2 + j0 + j for j in range(nt)]
            for j, kt in enumerate(tiles):
                m.matmul(S[:, 64 * j:64 * j + 64], kn[:, kt * 128:(kt + 1) * 128], qn[:, q0:q0 + 64])
            for j, kt in enumerate(CTt):
                m.matmul(S[:, 320 + 64 * j:384 + 64 * j], kn[:, kt * 128:(kt + 1) * 128], qn[:, q0:q0 + 64])
            m.stt(lg[:, 0:64 * nt], S[:, 0:64 * nt], na_scale, bv[:, cls, 0:64 * nt], ALU.mult, ALU.add)
            m.act(P[:, 0:64 * nt], lg[:, 0:64 * nt], AF.Exp)
            m.act(P[:, 320:448], S[:, 320:448], AF.Exp, scale=na_scale)
            srcs = [(kt, P[:, 64 * j:64 * j + 64]) for j, kt in enumerate(tiles)]
            srcs += [(kt, P[:, 320 + 64 * j:384 + 64 * j]) for j, kt in enumerate(CTt)]
            for i, (kt, pp) in enumerate(srcs):
                m.matmul(O[:, 0:64], vt(kt), pp, start=(i == 0), stop=(i == len(srcs) - 1), skip_group_check=True)
                m.matmul(O[:, 64:128], cx.ones_bf[:, 0:128], pp, start=False, stop=(i == len(srcs) - 1),
                         skip_group_check=True)
            m.recip(nrs[:, :], O[:, 64:128])
            m.tt(ys[:, q0:q0 + 64], O[:, 0:64], nrs[:, :], ALU.mult)
        m.dma(io["myT"][4 + h, :, c0:TM], ys[:, c0:TM])
    m.release(mk)


def build_Meven_program(ctx_out):
    nc = bass.Bass("TRN2", target_bir_lowering=False)
    cx = Ctx(nc)
    m = cx.m
    dr = lambda n, s, d, k="ExternalInput": m.dram(n, s, d, k)
    cx.consts(dr("ident", [128, 128], F32))
    io = {"mq": dr("mq", [4, 192, TM], BF16), "mk": dr("mk", [4, 128, TM], BF16),
          "mkpe": dr("mkpe", [64, TM], BF16), "mv": dr("mv", [TM, 512], BF16),
          "mqn": dr("mqn", [512, TM], BF16), "mkn": dr("mkn", [512, TM], BF16),
          "mvn": dr("mvn", [TM, 512], BF16),
          "nabias": dr("nabias", [4, 128, 2880], F32), "namask": dr("namask", [128, 2880], F32),
          "myT": dr("myT", [8, 128, TM], BF16, "ExternalOutput")}
    emit_M_even(cx, io, ctx_out)
    m.emit()
    return nc


def canon_cols(a0, a1):
    return np.ascontiguousarray(np.concatenate([a0[..., 1024:], a1[..., 1024:], a0[..., :1024], a1[..., :1024]], -1))


def canon_rows(a0, a1):
    return np.ascontiguousarray(np.concatenate([a0[1024:], a1[1024:], a0[:1024], a1[:1024]], 0))


_NA_TAB = None


def na_bias_host(rel_bias_e):
    global _NA_TAB
    if _NA_TAB is None:
        _NA_TAB = na_tables()
    ridx, cidx, mask = _NA_TAB
    g = rel_bias_e[:, ridx, cidx]
    g = np.ascontiguousarray(g.transpose(0, 3, 1, 2, 4).reshape(8, 128, 2880)).astype(np.float32)
    mk = np.ascontiguousarray(mask.transpose(2, 0, 1, 3).reshape(128, 2880))
    return g, mk


def assemble_Meven(o0, o1, r, nab, namask):
    hs = slice(4 * r, 4 * r + 4)
    cs = slice(512 * r, 512 * r + 512)
    return {"ident": np.eye(128, dtype=np.float32),
            "mq": canon_cols(o0["qT"][hs], o1["qT"][hs]),
            "mk": canon_cols(o0["kT"][hs], o1["kT"][hs]),
            "mkpe": canon_cols(o0["kpeT"], o1["kpeT"]),
            "mv": canon_rows(o0["v"][:, cs], o1["v"][:, cs]),
            "mqn": canon_cols(o0["qnT"][cs], o1["qnT"][cs]),
            "mkn": canon_cols(o0["knT"][cs], o1["knT"][cs]),
            "mvn": canon_rows(o0["vn"][:, cs], o1["vn"][:, cs]),
            "nabias": np.ascontiguousarray(nab[hs]), "namask": namask}


def assemble_y(m0, m1, r):
    full = np.zeros((16, 128, TM), dtype=m0["myT"].dtype)
    for rr, mm in ((0, m0), (1, m1)):
        full[4 * rr:4 * rr + 4] = mm["myT"][0:4]
        full[8 + 4 * rr:8 + 4 * rr + 4] = mm["myT"][4:8]
    loc = np.concatenate([full[:, :, 256 + 1024 * r:256 + 1024 * (r + 1)], full[:, :, 128 * r:128 * (r + 1)]], -1)
    return np.ascontiguousarray(loc.transpose(1, 0, 2))


NCH = 36


def hgrn_sigma(d, c):
    if d == 0:
        return c
    return 3 - c if c < 4 else 35 - (c - 4)


def emit_hgrn(cx, io, ctx_out, ystage):
    m = cx.m
    mk = m.mark()
    T = TM
    BIG = 2.0e17
    vtok = m.sb("hg_vtok", [128, 18, 512], BF16)
    m.dma(vtok[:], io["hv"].rearrange("(t p) c -> p t c", p=128))
    lbl = m.sb("hg_lbl", [128, 2, 2, 4], F32)
    m.dma(lbl[:], io["lbl"])
    lb = m.sb("hg_lb", [128, 2, 4], F32)
    oml = m.sb("hg_oml", [128, 2, 4], F32)
    if io["odd_idx"] == 0:
        m.memset(lb[:], 0.0)
    else:
        m.tt(lb[:], lbl[:, :, 1, :], lbl[:, :, 0, :], ALU.subtract)
        m.act(lb[:], lb[:], AF.Sigmoid)
    m.ts(oml[:], lb[:], -1.0, 1.0, ALU.mult, ALU.add)
    ng = m.sb("hg_ng", [128, 4], F32)
    m.dma(ng[:], io["ng"])
    masks = m.sb("hg_masks", [128, 4, 128], F32)
    m.dma(masks[:], io["hmasks"])
    ones_f = m.sb("hg_ones", [128, T], BF16)
    m.memset(ones_f[:], 1.0)
    qb = m.sb("hg_q", [128, T], BF16)
    gb = m.sb("hg_g", [128, T], BF16)
    Qi = [m.sb("hg_Qi%d" % d, [128, T], BF16) for d in range(2)]
    Ki = [m.sb("hg_Ki%d" % d, [128, T], BF16) for d in range(2)]
    Qo = [m.sb("hg_Qo%d" % d, [128, T], BF16) for d in range(2)]
    Ko = [m.sb("hg_Ko%d" % d, [128, T], BF16) for d in range(2)]
    Qs = [m.sb("hg_Qs%d" % d, [128, T], BF16) for d in range(2)]
    Sbf = [m.sb("hg_Sbf%d" % d, [128, NCH, 128], BF16) for d in range(2)]
    KlT = m.sb("hg_KlT", [128, T], BF16)
    Kltok = m.sb("hg_Kltok", [128, 18, 128], BF16)
    z = m.sb("hg_z", [128, T], F32)
    lf = m.sb("hg_lf", [128, T], F32)
    kk = m.sb("hg_k", [128, T], F32)
    ee = m.sb("hg_e", [128, T], F32)
    gaddr = m.mark()
    G = m.sb("hg_G", [128, T], F32)
    Gx = m.sb("hg_Gx", [128, T], F32)
    Dfull = m.sb("hg_Dfull", [128, 128 * NCH], F32, at=gaddr)
    Sst = m.sb("hg_Sst", [128, 128 * NCH], F32)
    Dsc = m.sb("hg_Dsc", [128, NCH], F32)
    Dtmp = m.sb("hg_Dtmp", [128, NCH], F32)
    Am = [m.sb("hg_Am%d" % i, [128, 128], BF16) for i in range(4)]
    T1 = [m.sb("hg_T1_%d" % i, [128, 128], F32) for i in range(2)]
    T2 = [m.sb("hg_T2_%d" % i, [128, 128], F32) for i in range(2)]
    oT = lf
    rstd = ee
    sq1 = m.sb("hg_sq", [128, 1, 384], BF16)
    tmp = m.sb("hg_tmp", [128, 384], F32)

    v64 = lambda t_: t_[:].rearrange("p (c j) -> p c j", j=64)
    v32 = lambda t_: t_[:].rearrange("p (c j) -> p c j", j=32)
    bc64 = lambda t_, j: v64(t_)[:, :, j:j + 1].to_broadcast([128, NCH, 64])
    bc32 = lambda t_, j: v32(t_)[:, :, j:j + 1].to_broadcast([128, 2 * NCH, 32])
    c0 = 0 if ctx_out else 256
    ai = 0
    for h in range(4):
        m.dma(qb[:], io["hq"][h * 128:(h + 1) * 128, :])
        m.dma(gb[:], io["hg"][h * 128:(h + 1) * 128, :])
        for d in range(2):
            m.dma(z[:], io["hz"][d, h * 128:(h + 1) * 128, :])
            m.act(z[:], z[:], AF.Sigmoid)
            m.ts(z[:], z[:], oml[:, d, h:h + 1], lb[:, d, h:h + 1], ALU.mult, ALU.add)
            m.ts(z[:], z[:], 1e-30, None, ALU.max, eng="pool")
            m.act(lf[:], z[:], AF.Ln)
            m.ts(kk[:], z[:], -1.0, 1.0, ALU.mult, ALU.add, eng="pool")
            m.scan(G[:], ones_f[:], lf[:], 0.0, ALU.mult, ALU.add)
            m.tt(Gx[:], G[:], lf[:], ALU.subtract, eng="pool")
            A = G if d == 0 else Gx
            sgn = 1.0 if d == 0 else -1.0
            m.tt(v32(z), v32(A), bc32(A, 15 if d == 0 else 16), ALU.subtract, eng="pool")
            m.act(ee[:], z[:], AF.Exp, scale=sgn)
            m.stt(Qi[d][:], ee[:], BIG, qb[:], ALU.min, ALU.mult)
            m.act(ee[:], z[:], AF.Exp, scale=-sgn)
            m.stt(Ki[d][:], ee[:], BIG, kk[:], ALU.min, ALU.mult)
            m.tt(v64(z), v64(A), bc64(A, 31 if d == 0 else 32), ALU.subtract, eng="pool")
            m.act(ee[:], z[:], AF.Exp, scale=sgn)
            m.stt(Qo[d][:], ee[:], 1.0, qb[:], ALU.min, ALU.mult)
            m.act(ee[:], z[:], AF.Exp, scale=-sgn)
            m.stt(Ko[d][:], ee[:], 1.0, kk[:], ALU.min, ALU.mult)
            if d == 0:
                m.tt(v64(z), v64(G), bc64(Gx, 0), ALU.subtract, eng="pool")
            else:
                m.tt(v64(z), v64(Gx), bc64(G, 63), ALU.subtract, eng="pool")
            m.act(ee[:], z[:], AF.Exp, scale=sgn)
            m.stt(Qs[d][:], ee[:], 1.0, qb[:], ALU.min, ALU.mult)
            if d == 0:
                m.tt(v64(z), v64(G), bc64(G, 63), ALU.subtract, eng="pool")
            else:
                m.tt(v64(z), v64(Gx), bc64(Gx, 0), ALU.subtract, eng="pool")
            m.act(ee[:], z[:], AF.Exp, scale=-sgn)
            m.stt(KlT[:], ee[:], 1.0, kk[:], ALU.min, ALU.mult)
            m.tt(Dtmp[:], v64(G)[:, :, 63], v64(Gx)[:, :, 0], ALU.subtract)
            m.act(Dtmp[:], Dtmp[:], AF.Exp)
            if d == 0:
                m.copy(Dsc[:, 1:NCH], Dtmp[:, 1:NCH], eng="pool")
                m.memset(Dsc[:, 0:1], 0.0, eng="pool")
            else:
                for c in range(NCH):
                    s = hgrn_sigma(1, c)
                    if s == 0:
                        m.memset(Dsc[:, 0:1], 0.0, eng="pool")
                    else:
                        m.copy(Dsc[:, s:s + 1], Dtmp[:, c:c + 1], eng="pool")
            m.copy(Dfull[:].rearrange("p (e s) -> p e s", s=NCH),
                   Dsc[:].unsqueeze(1).to_broadcast([128, 128, NCH]), eng="pool")
            for t4 in range(0, 18, 4):
                nt = min(4, 18 - t4)
                pb = m.psum[6 + (t4 // 4) % 2][:].bitcast(BF16)
                for i in range(nt):
                    m.transpose(pb[:, i * 128:(i + 1) * 128], KlT[:, (t4 + i) * 128:(t4 + i + 1) * 128], cx.ident_bf[:])
                m.copy(Kltok[:, t4:t4 + nt, :], pb[:, 0:nt * 128].rearrange("p (t d) -> p t d", d=128),
                       eng=cx.evac_eng())
            S3 = Sst[:].rearrange("p (e s) -> p e s", s=NCH)
            for c in range(NCH):
                t, j = c // 2, c % 2
                ps = m.psum[c % 4]
                m.matmul(ps[:, 0:128], Kltok[64 * j:64 * j + 64, t, :], vtok[64 * j:64 * j + 64, t, h * 128:(h + 1) * 128])
                s = hgrn_sigma(d, c)
                m.copy(S3[:, :, s], ps[:, 0:128], eng=cx.evac_eng())
            m.scan(Sst[:], Dfull[:], Sst[:], 0.0, ALU.mult, ALU.add)
            m.copy(Sbf[d][:], Sst[:].rearrange("p (e s) -> p s e", s=NCH), eng="pool")
        for t in range(c0 // 128, 18):
            psO = m.psum[4 + t % 2]
            ams = []
            for d in range(2):
                tsl = slice(t * 128, (t + 1) * 128)
                psA1 = m.psum[(2 * d) % 4]
                psA2 = m.psum[(2 * d + 1) % 4]
                m.matmul(psA1[:, 0:128], Ki[d][:, tsl], Qi[d][:, tsl])
                m.matmul(psA2[:, 0:128], Ko[d][:, tsl], Qo[d][:, tsl])
                am = Am[ai % 4]
                ai += 1
                m.tt(T1[d][:], psA1[:, 0:128], masks[:, 2 * d, :], ALU.mult)
                m.tt(T2[d][:], psA2[:, 0:128], masks[:, 2 * d + 1, :], ALU.mult)
                m.tt(am[:], T1[d][:], T2[d][:], ALU.add, eng="pool")
                ams.append(am)
            mms = []
            for d in range(2):
                mms.append((psO[:, 0:128], vtok[:, t, h * 128:(h + 1) * 128], ams[d][:]))
                for j in range(2):
                    c = 2 * t + j
                    s = hgrn_sigma(d, c)
                    if s >= 1:
                        mms.append((psO[:, 64 * j:64 * j + 64], Sbf[d][:, s - 1, :], Qs[d][:, c * 64:(c + 1) * 64]))
            for i, (o_, l_, r_) in enumerate(mms):
                m.matmul(o_, l_, r_, start=(i == 0), stop=(i == len(mms) - 1), skip_group_check=True)
            m.copy(oT[:, t * 128:(t + 1) * 128], psO[:, 0:128], eng=cx.evac_eng())
        rstd_compute(cx, oT[:].rearrange("p (o t) -> p o t", o=1)[:, :, c0:T], 1, T - c0, 128, rstd, sq1, 7, tmp)
        ys = ystage[h % 2]
        m.stt(oT[:, c0:T], oT[:, c0:T], ng[:, h:h + 1], rstd[:, 0:T - c0], ALU.mult, ALU.mult)
        m.tt(ys[:, c0:T], oT[:, c0:T], gb[:, c0:T], ALU.mult, eng="pool")
        m.dma(io["myT"][h, :, c0:T], ys[:, c0:T])
    m.release(mk)


def sin_reduced(cx, out, x, rows, w, t1):
    m = cx.m
    PI = math.pi
    for _ in range(2):
        m.ts(t1[0:rows, 0:w], x, PI, -2 * PI, ALU.is_gt, ALU.mult)
        m.tt(x, x, t1[0:rows, 0:w], ALU.add)
        m.ts(t1[0:rows, 0:w], x, -PI, 2 * PI, ALU.is_lt, ALU.mult)
        m.tt(x, x, t1[0:rows, 0:w], ALU.add)
    m.act(out, x, AF.Sin)


def emit_hyena(cx, io, ctx_out, ystage):
    m = cx.m
    mk = m.mark()
    T = TM
    cw = m.sb("hy_cw", [128, 3, 3, 4], F32)
    cb = m.sb("hy_cb", [128, 3, 4], F32)
    skip = m.sb("hy_skip", [128, 4], F32)
    m.dma(cw[:], io["cw"])
    m.dma(cb[:], io["cb"])
    m.dma(skip[:], io["skip"])
    fw1 = m.sb("hy_w1", [33, 64], F32)
    fw2 = m.sb("hy_w2", [64, 64], F32)
    fw3 = m.sb("hy_w3", [64, 64], F32)
    fb = m.sb("hy_fb", [64, 4], F32)
    fwo = m.sb("hy_wo", [64, 2, 512], F32)
    m.dma(fw1[:], io["fw1"])
    m.dma(fw2[:], io["fw2"])
    m.dma(fw3[:], io["fw3"])
    m.dma(fb[:], io["fb"])
    m.dma(fwo[:], io["fwo"])
    delt = m.sb("hy_delt", [128, 512], F32)
    m.dma(delt[:], io["delt"])
    X0 = [m.sb("hy_X0_%d" % i, [128, T], BF16) for i in range(4)]
    VX = [m.sb("hy_VX_%d" % i, [128, T], BF16) for i in range(4)]
    vxtok = m.sb("hy_vxtok", [128, 18, 512], BF16)
    mB = m.mark()
    ub = [m.sb("hy_u%d" % i, [128, T], BF16) for i in range(2)]
    acc = [m.sb("hy_acc%d" % i, [128, T], F32) for i in range(2)]
    segs = ((0, 256), (256, T))
    for cc in range(4):
        def conv(g, a, u):
            m.dma(u[:], io["hu"][g, cc * 128:(cc + 1) * 128, :])
            m.ts(a[:], u[:], cw[:, 1, g, cc:cc + 1], cb[:, g, cc:cc + 1], ALU.mult, ALU.add)
            for (s0, s1) in segs:
                m.stt(a[:, s0 + 1:s1], u[:, s0:s1 - 1], cw[:, 0, g, cc:cc + 1], a[:, s0 + 1:s1], ALU.mult, ALU.add)
                m.stt(a[:, s0:s1 - 1], u[:, s0 + 1:s1], cw[:, 2, g, cc:cc + 1], a[:, s0:s1 - 1], ALU.mult, ALU.add)
        conv(1, acc[0], ub[0])
        conv(2, acc[1], ub[1])
        m.tt(VX[cc][:], acc[0][:], acc[1][:], ALU.mult, eng="pool")
        conv(0, acc[0], ub[0])
        m.copy(X0[cc][:], acc[0][:], eng="act")
    for t in range(18):
        pb = m.psum[6 + t % 2][:].bitcast(BF16)
        for cc in range(4):
            m.transpose(pb[:, cc * 128:(cc + 1) * 128], VX[cc][:, t * 128:(t + 1) * 128], cx.ident_bf[:])
        m.copy(vxtok[:, t, :], pb[:, 0:512], eng=cx.evac_eng())
    m.release(mB)
    for (name, n, tok0) in (("c", 256, 0), ("l", 2048, 256)):
        if name == "c" and not ctx_out:
            continue
        m.release(mB)
        ntile = n // 128
        nfc = n // 128
        Yc = m.sb("hy_Yc", [128, nfc, 512], BF16)
        Ys = m.sb("hy_Ys", [128, nfc, 512], BF16)
        p1 = m.sb("hy_p1", [128, 512], F32)
        p2 = m.sb("hy_p2", [128, 512], F32)
        mC = m.mark()
        hs = m.sb("hy_hs", [128, ntile, 512], BF16)
        hd = m.sb("hy_hd", [128, ntile, 512], BF16)
        mD = m.mark()
        zT = m.sb("hy_zT", [33, n], F32)
        m.dma(zT[:], io["zT_" + name])
        tcol = m.sb("hy_tcol", [128, ntile], F32)
        m.dma(tcol[:], io["tcol_" + name])
        hA = m.sb("hy_hA", [64, n], F32)
        hB = m.sb("hy_hB", [64, n], F32)
        t1 = m.sb("hy_t1", [64, 512], F32)
        layers = ((fw1, 33, zT, hA, 0), (fw2, 64, hA, hB, 1), (fw3, 64, hB, hA, 2))
        for (wt, kdim, src_, dst, bi) in layers:
            for b0 in range(0, n, 512):
                bw = min(512, n - b0)
                ps = m.psum[(b0 // 512) % 2]
                m.matmul(ps[0:64, 0:bw], wt[0:kdim, :], src_[0:kdim, b0:b0 + bw])
                m.ts(dst[:, b0:b0 + bw], ps[0:64, 0:bw], fb[:, bi:bi + 1], fb[:, 3:4], ALU.add, ALU.mult)
                sin_reduced(cx, dst[:, b0:b0 + bw], dst[:, b0:b0 + bw], 64, bw, t1)
        h3 = hA
        win = m.sb("hy_win", [128, 512], F32)
        hf = m.sb("hy_hf", [128, 512], F32)
        hb = m.sb("hy_hb", [128, 512], F32)
        ntc = m.sb("hy_ntc", [128, ntile], F32)
        m.ts(ntc[:], tcol[:], -1.0, None, ALU.mult)
        for t in range(ntile):
            m.act(win[:], delt[:], AF.Exp, scale=ntc[:, t:t + 1])
            m.matmul(m.psum[2][:, 0:512], h3[0:64, t * 128:(t + 1) * 128], fwo[0:64, 0, :])
            m.matmul(m.psum[3][:, 0:512], h3[0:64, t * 128:(t + 1) * 128], fwo[0:64, 1, :])
            m.tt(hf[:], m.psum[2][:, 0:512], win[:], ALU.mult)
            m.tt(hb[:], m.psum[3][:, 0:512], win[:], ALU.mult)
            if t == 0:
                m.memset(hb[0:1, :], 0.0)
            m.tt(hs[:, t, :], hf[:], hb[:], ALU.add, eng="pool")
            m.tt(hd[:, t, :], hf[:], hb[:], ALU.subtract, eng="pool")
        m.release(mD)
        nfb = max(1, nfc // 4)
        fcb = nfc // nfb
        Cb = [m.sb("hy_Cb%d" % i, [128, ntile, fcb * 128], BF16) for i in range(1)]
        Sb = [m.sb("hy_Sb%d" % i, [128, ntile, fcb * 128], BF16) for i in range(1)]
        Kc = m.sb("hy_Kc", [128, 512], F32)
        Ks = m.sb("hy_Ks", [128, 512], F32)
        dF = io["dftF_" + name]
        tt0 = tok0 // 128
        for fbk in range(nfb):
            C_, S_ = Cb[0], Sb[0]
            m.dma(C_[:], dF[:, 0, fbk * fcb * 128:(fbk + 1) * fcb * 128].rearrange("(k p) f -> p k f", p=128))
            m.dma(S_[:], dF[:, 1, fbk * fcb * 128:(fbk + 1) * fcb * 128].rearrange("(k p) f -> p k f", p=128))
            for fi in range(fcb):
                fc = fbk * fcb + fi
                fsl = slice(fi * 128, (fi + 1) * 128)
                for t in range(ntile):
                    m.matmul(m.psum[0][:, 0:512], C_[:, t, fsl], hs[:, t, :], start=(t == 0), stop=(t == ntile - 1))
                for t in range(ntile):
                    m.matmul(m.psum[1][:, 0:512], S_[:, t, fsl], hd[:, t, :], start=(t == 0), stop=(t == ntile - 1))
                m.copy(Kc[:], m.psum[0][:, 0:512], eng="act")
                m.copy(Ks[:], m.psum[1][:, 0:512], eng="act")
                pc, ps_ = m.psum[2 + 2 * (fc % 2)], m.psum[3 + 2 * (fc % 2)]
                for t in range(ntile):
                    m.matmul(pc[:, 0:512], C_[:, t, fsl], vxtok[:, tt0 + t, :], start=(t == 0), stop=(t == ntile - 1))
                for t in range(ntile):
                    m.matmul(ps_[:, 0:512], S_[:, t, fsl], vxtok[:, tt0 + t, :], start=(t == 0), stop=(t == ntile - 1))
                m.tt(p1[:], pc[:, 0:512], Kc[:], ALU.mult)
                m.tt(p2[:], ps_[:, 0:512], Ks[:], ALU.mult)
                m.tt(Yc[:, fc, :], p1[:], p2[:], ALU.subtract, eng="pool")
                m.tt(p1[:], pc[:, 0:512], Ks[:], ALU.mult)
                m.tt(p2[:], ps_[:, 0:512], Kc[:], ALU.mult)
                m.tt(Ys[:, fc, :], p1[:], p2[:], ALU.add, eng="pool")
        m.release(mC)
        dI = io["dftI_" + name]
        tbw = min(512, n)
        Ci = [m.sb("hy_Ci%d" % i, [128, nfc, tbw], BF16) for i in range(2)]
        Si = [m.sb("hy_Si%d" % i, [128, nfc, tbw], BF16) for i in range(2)]
        for tb in range(n // tbw):
            C_, S_ = Ci[tb % 2], Si[tb % 2]
            m.dma(C_[:], dI[:, 0, tb * tbw:(tb + 1) * tbw].rearrange("(k p) t -> p k t", p=128))
            m.dma(S_[:], dI[:, 1, tb * tbw:(tb + 1) * tbw].rearrange("(k p) t -> p k t", p=128))
            for cc in range(4):
                po = m.psum[cc % 2]
                csl = slice(cc * 128, (cc + 1) * 128)
                for fc in range(nfc):
                    m.matmul(po[:, 0:tbw], Yc[:, fc, csl], C_[:, fc, :], start=(fc == 0), stop=False)
                for fc in range(nfc):
                    m.matmul(po[:, 0:tbw], Ys[:, fc, csl], S_[:, fc, :], start=False, stop=(fc == nfc - 1))
                a0 = tok0 + tb * tbw
                ys = ystage[cc % 2]
                m.stt(p1[:, 0:tbw], VX[cc][:, a0:a0 + tbw], skip[:, cc:cc + 1], po[:, 0:tbw], ALU.mult, ALU.add)
                m.tt(ys[:, 0:tbw], p1[:, 0:tbw], X0[cc][:, a0:a0 + tbw], ALU.mult, eng="pool")
                m.dma(io["myT"][4 + cc, :, a0:a0 + tbw], ys[:, 0:tbw])
    m.release(mk)


def build_Modd_program(odd_idx, ctx_out):
    nc = bass.Bass("TRN2", target_bir_lowering=False)
    cx = Ctx(nc)
    m = cx.m
    dr = lambda n, s, d, k="ExternalInput": m.dram(n, s, d, k)
    cx.consts(dr("ident", [128, 128], F32))
    cx.eps_col = m.sb("eps", [128, 1], F32)
    m.memset(cx.eps_col[:], EPS)
    io = {"odd_idx": odd_idx,
          "hq": dr("hq", [512, TM], BF16), "hz": dr("hz", [2, 512, TM], F32), "hv": dr("hv", [TM, 512], BF16),
          "hg": dr("hg", [512, TM], BF16), "hu": dr("hu", [3, 512, TM], BF16),
          "lbl": dr("lbl", [128, 2, 2, 4], F32), "ng": dr("ng", [128, 4], F32),
          "hmasks": dr("hmasks", [128, 4, 128], F32),
          "cw": dr("cw", [128, 3, 3, 4], F32), "cb": dr("cb", [128, 3, 4], F32), "skip": dr("skip", [128, 4], F32),
          "fw1": dr("fw1", [33, 64], F32), "fw2": dr("fw2", [64, 64], F32), "fw3": dr("fw3", [64, 64], F32),
          "fb": dr("fb", [64, 4], F32), "fwo": dr("fwo", [64, 2, 512], F32), "delt": dr("delt", [128, 512], F32),
          "zT_l": dr("zT_l", [33, 2048], F32), "tcol_l": dr("tcol_l", [128, 16], F32),
          "dftF_l": dr("dftF_l", [2048, 2, 2048], BF16), "dftI_l": dr("dftI_l", [2048, 2, 2048], BF16),
          "myT": dr("myT", [8, 128, TM], BF16, "ExternalOutput")}
    if ctx_out:
        io.update({"zT_c": dr("zT_c", [33, 256], F32), "tcol_c": dr("tcol_c", [128, 2], F32),
                   "dftF_c": dr("dftF_c", [256, 2, 256], BF16), "dftI_c": dr("dftI_c", [256, 2, 256], BF16)})
    ystage = [m.sb("ystage%d" % i, [128, TM], BF16) for i in range(2)]
    emit_hgrn(cx, io, ctx_out, ystage)
    emit_hyena(cx, io, ctx_out, ystage)
    m.emit()
    return nc


_HY_CONST = {}


def hyena_consts(n):
    if n in _HY_CONST:
        return _HY_CONST[n]
    pos = np.arange(n, dtype=np.float32)
    t = (pos / np.float32(max(n - 1, 1))).astype(np.float32)
    bands = np.linspace(1e-4, 15, 16, dtype=np.float32)
    ang = (np.float32(2.0 * math.pi / n) * pos[:, None] * bands[None, :]).astype(np.float32)
    z = np.concatenate([t[:, None], np.cos(ang), -np.sin(ang)], -1).astype(np.float32)
    zT = np.ascontiguousarray(z.T)
    tcol = np.ascontiguousarray(t.reshape(-1, 128).T)
    f = np.arange(n, dtype=np.float64) + 0.5
    M = np.arange(n, dtype=np.float64)[:, None] * (2.0 * math.pi * f[None, :] / (2.0 * n))
    c, s = np.cos(M), np.sin(M)
    dftF = np.ascontiguousarray(np.stack([c, s], 1)).astype(NPBF)
    dftI = np.ascontiguousarray(np.stack([c.T / n, s.T / n], 1)).astype(NPBF)
    _HY_CONST[n] = (zT, tcol, dftF, dftI)
    return _HY_CONST[n]


def hgrn_masks():
    i = np.arange(128)
    s, t = i[:, None], i[None, :]
    same32 = (s // 32) == (t // 32)
    same64 = (s // 64) == (t // 64)
    first = lambda x: (x % 64) < 32
    m1f = same32 & (s <= t)
    m2f = same64 & first(s) & ~first(t)
    m1b = same32 & (s >= t)
    m2b = same64 & ~first(s) & first(t)
    return np.ascontiguousarray(np.stack([m1f, m2f, m1b, m2b], 1)).astype(np.float32)


def hyena_deltas(r):
    mx = math.log(1e-2) / 0.3
    mn = math.log(1e-2) / 1.5
    d = np.abs(np.linspace(mn, mx, 1024, dtype=np.float32))
    return np.ascontiguousarray(np.broadcast_to(d[512 * r:512 * r + 512][None, :], (128, 512))).astype(np.float32)


def assemble_Modd(o0, o1, r, inp, o, ctx_out):
    cs = slice(512 * r, 512 * r + 512)
    hz = np.stack([canon_cols(o0["zT"][d * 1024 + 512 * r:d * 1024 + 512 * r + 512],
                              o1["zT"][d * 1024 + 512 * r:d * 1024 + 512 * r + 512]) for d in range(2)], 0)
    hu = np.stack([canon_cols(o0["uT"][g * 1024 + 512 * r:g * 1024 + 512 * r + 512],
                              o1["uT"][g * 1024 + 512 * r:g * 1024 + 512 * r + 512]) for g in range(3)], 0)
    lbl = np.stack([np.stack([fm_vec(inp["hgrn_lb_logits"][d, oo, cs]) for oo in range(2)], 1) for d in range(2)], 1)
    cwv = inp["hy_conv_w"][o]
    cw = np.stack([np.stack([fm_vec(cwv[tap, g * 1024 + 512 * r:g * 1024 + 512 * r + 512]) for g in range(3)], 1)
                   for tap in range(3)], 1)
    cbv = inp["hy_conv_b"][o]
    cb = np.stack([fm_vec(cbv[g * 1024 + 512 * r:g * 1024 + 512 * r + 512]) for g in range(3)], 1)
    wo = inp["hy_filt_wout"][o]
    zl, tl, fl, il = hyena_consts(2048)
    d = {"ident": np.eye(128, dtype=np.float32),
         "hq": canon_cols(o0["qT"][cs], o1["qT"][cs]), "hz": np.ascontiguousarray(hz),
         "hv": canon_rows(o0["vi"][:, cs], o1["vi"][:, cs]),
         "hg": canon_cols(o0["gT"][cs], o1["gT"][cs]), "hu": np.ascontiguousarray(hu),
         "lbl": np.ascontiguousarray(lbl).astype(np.float32), "ng": fm_vec(inp["hgrn_norm_g"][o][cs]),
         "hmasks": hgrn_masks(),
         "cw": np.ascontiguousarray(cw).astype(np.float32), "cb": np.ascontiguousarray(cb).astype(np.float32),
         "skip": fm_vec(inp["hy_skip"][o][cs]),
         "fw1": np.ascontiguousarray(inp["hy_filt_w1"][o]), "fw2": np.ascontiguousarray(inp["hy_filt_w2"][o]),
         "fw3": np.ascontiguousarray(inp["hy_filt_w3"][o]),
         "fb": np.ascontiguousarray(np.stack([inp["hy_filt_b1"][o], inp["hy_filt_b2"][o], inp["hy_filt_b3"][o],
                                              inp["hy_filt_freq"][o]], 1)).astype(np.float32),
         "fwo": np.ascontiguousarray(np.stack([wo[:, cs], wo[:, 1024 + 512 * r:1024 + 512 * r + 512]], 1)),
         "delt": hyena_deltas(r), "zT_l": zl, "tcol_l": tl, "dftF_l": fl, "dftI_l": il}
    if ctx_out:
        zc, tc, fc_, ic = hyena_consts(256)
        d.update({"zT_c": zc, "tcol_c": tc, "dftF_c": fc_, "dftI_c": ic})
    return d


def build_fused_program(depth=DEPTH):
    nc = bass.Bass("TRN2", target_bir_lowering=False)
    cx = Ctx(nc)
    m = cx.m
    ext = lambda n, s, d: m.dram(n, s, d, "ExternalInput")
    itn = lambda n, s, d: m.dram(n, s, d, "Internal")
    ident_d = ext("ident", [128, 128], F32)
    cx.consts(ident_d)
    cx.eps_col = m.sb("eps", [128, 1], F32)
    m.memset(cx.eps_col[:], EPS)
    h0_d = ext("h0", [2, 128, KC, TL], F32)
    cT_d = ext("cT", [128, KC, 2], F32)
    ada_w_d = ext("ada_w", [DEPTH, D, 6 * D], F32)
    ada_b_d = ext("ada_b", [DEPTH, 6 * D], F32)
    gmix_d = ext("gmix", [128, DEPTH, KC], F32)
    gmlp_d = ext("gmlp", [128, DEPTH, KC], F32)
    w_out_d = ext("w_out", [DEPTH, D, D], F32)
    w1_d = ext("mlp_w1", [DEPTH, D, HID], F32)
    w2_d = ext("mlp_w2", [DEPTH, HID, D], F32)
    evw_d = ext("ev_w_in", [2, D, EV_NEXT], F32)
    ukv_d = ext("w_ukv", [2, 512, 2048], F32)
    odw_d = ext("od_w_in", [2, D, 8192], F32)
    cos_d = ext("cosT", [2, 64, TL], F32)
    sin_d = ext("sinT", [2, 64, TL], F32)
    kvg_d = ext("kvg", [2, 128, 4], F32)
    fg_d = ext("fg", [128, KC], F32)
    nab_d = ext("nabias", [2, 8, 128, 2880], F32)
    namask_d = ext("namask", [128, 2880], F32)
    lbl_d = ext("lbl", [2, 128, 2, 2, 4], F32)
    ng_d = ext("ng", [2, 2, 128, 4], F32)
    hmasks_d = ext("hmasks", [128, 4, 128], F32)
    cw_d = ext("cw", [2, 2, 128, 3, 3, 4], F32)
    cb_d = ext("cb", [2, 2, 128, 3, 4], F32)
    skip_d = ext("skip", [2, 2, 128, 4], F32)
    fw1_d = ext("fw1", [2, 33, 64], F32)
    fw2_d = ext("fw2", [2, 64, 64], F32)
    fw3_d = ext("fw3", [2, 64, 64], F32)
    fb_d = ext("fb", [2, 64, 4], F32)
    fwo_d = ext("fwo", [2, 2, 64, 2, 512], F32)
    delt_d = ext("delt", [2, 128, 512], F32)
    hyc = {"zT_l": ext("zT_l", [33, 2048], F32), "tcol_l": ext("tcol_l", [128, 16], F32),
           "dftF_l": ext("dftF_l", [2048, 2, 2048], BF16), "dftI_l": ext("dftI_l", [2048, 2, 2048], BF16),
           "zT_c": ext("zT_c", [33, 256], F32), "tcol_c": ext("tcol_c", [128, 2], F32),
           "dftF_c": ext("dftF_c", [256, 2, 256], BF16), "dftI_c": ext("dftI_c", [256, 2, 256], BF16)}
    outT_d = m.dram("outT", [2, 128, KC, 1024], F32, "ExternalOutput")
    hT_d = [itn("hT_%d" % v, [128, KC, TL], F32) for v in range(2)]
    RE = [{"qT": itn("re_qT%d" % v, [8, 192, TL], BF16), "kT": itn("re_kT%d" % v, [8, 128, TL], BF16),
           "kpeT": itn("re_kpeT%d" % v, [64, TL], BF16), "v": itn("re_v%d" % v, [TL, 1024], BF16),
           "qnT": itn("re_qnT%d" % v, [1024, TL], BF16), "knT": itn("re_knT%d" % v, [1024, TL], BF16),
           "vn": itn("re_vn%d" % v, [TL, 1024], BF16)} for v in range(2)]
    RO = [{"qT": itn("ro_qT%d" % v, [1024, TL], BF16), "vi": itn("ro_vi%d" % v, [TL, 1024], BF16),
           "zT": itn("ro_zT%d" % v, [2048, TL], F32), "gT": itn("ro_gT%d" % v, [1024, TL], BF16),
           "uT": itn("ro_uT%d" % v, [3072, TL], BF16)} for v in range(2)]
    ME = {"mq": itn("me_mq", [4, 192, TM], BF16), "mk": itn("me_mk", [4, 128, TM], BF16),
          "mkpe": itn("me_mkpe", [64, TM], BF16), "mv": itn("me_mv", [TM, 512], BF16),
          "mqn": itn("me_mqn", [512, TM], BF16), "mkn": itn("me_mkn", [512, TM], BF16),
          "mvn": itn("me_mvn", [TM, 512], BF16)}
    MO = {"hq": itn("mo_hq", [512, TM], BF16), "hz": itn("mo_hz", [2, 512, TM], F32),
          "hv": itn("mo_hv", [TM, 512], BF16), "hg": itn("mo_hg", [512, TM], BF16),
          "hu": itn("mo_hu", [3, 512, TM], BF16)}
    myT_d = [itn("myT_%d" % v, [8, 128, TM], BF16) for v in range(2)]

    def ccols(dst, s0, s1):
        if len(dst.shape) == 3:
            m.dma(dst[:, :, 0:128], s0[:, :, 1024:TL])
            m.dma(dst[:, :, 128:256], s1[:, :, 1024:TL])
            m.dma(dst[:, :, 256:1280], s0[:, :, 0:1024])
            m.dma(dst[:, :, 1280:TM], s1[:, :, 0:1024])
        else:
            m.dma(dst[:, 0:128], s0[:, 1024:TL])
            m.dma(dst[:, 128:256], s1[:, 1024:TL])
            m.dma(dst[:, 256:1280], s0[:, 0:1024])
            m.dma(dst[:, 1280:TM], s1[:, 0:1024])

    def crows(dst, s0, s1):
        m.dma(dst[0:128], s0[1024:TL])
        m.dma(dst[128:256], s1[1024:TL])
        m.dma(dst[256:1280], s0[0:1024])
        m.dma(dst[1280:TM], s1[0:1024])

    modall = m.sb("modall", [128, DEPTH, 128, 2], F32)
    emit_mod(cx, modall[:, :, 0:96, :], cT_d, ada_w_d, ada_b_d, 96, 2)
    emit_gsc(cx, modall, gmix_d, gmlp_d)
    base_mark = m.mark()
    for l in range(depth + 1):
        last = (l == depth)
        for v in range(2):
            m.release(base_mark)
            hT = m.sb("hT", [128, KC, TL], F32)
            m.dma(hT[:], h0_d[v] if l == 0 else hT_d[v])
            actT = m.sb("actT", [128, KC, TL], BF16)
            rstd = m.sb("rstd", [128, TL], F32)
            cx.init_wring(4)
            if l >= 1:
                def load_y(actT_, v=v):
                    for f in range(16):
                        own, loc = (f // 4, f % 4) if f < 8 else ((f - 8) // 4, 4 + (f - 8) % 4)
                        srcT = myT_d[own][loc]
                        m.dma(actT_[:, f, 0:1024], srcT[:, 256 + 1024 * v:256 + 1024 * (v + 1)])
                        m.dma(actT_[:, f, 1024:TL], srcT[:, 128 * v:128 * (v + 1)])
                io = {"load_y": load_y, "w_out": w_out_d[l - 1], "w1": w1_d[l - 1], "w2": w2_d[l - 1]}
                emit_post(cx, l - 1, hT, actT, rstd, modall, io, TL if not last or depth < DEPTH else 1024)
            if not last and l % 2 == 0:
                e = l // 2
                io = dict(RE[v])
                io.update({"w_in": evw_d[e], "w_ukv": ukv_d[e], "cosT": cos_d[v], "sinT": sin_d[v], "kvg": kvg_d[e]})
                emit_pre_even(cx, l, hT, actT, rstd, modall, io)
            elif not last:
                io = dict(RO[v])
                io.update({"w_in": odw_d[l // 2]})
                emit_pre_odd(cx, l, hT, actT, rstd, modall, io)
            if not last:
                m.dma(hT_d[v], hT[:])
            else:
                emit_final(cx, hT, rstd, {"fg": fg_d, "outT": outT_d[v]})
        if last:
            break
        ctx_out = l < DEPTH - 1
        for v in range(2):
            m.release(base_mark)
            hs = slice(4 * v, 4 * v + 4)
            cs = slice(512 * v, 512 * v + 512)
            if l % 2 == 0:
                e = l // 2
                ccols(ME["mq"], RE[0]["qT"][hs], RE[1]["qT"][hs])
                ccols(ME["mk"], RE[0]["kT"][hs], RE[1]["kT"][hs])
                ccols(ME["mkpe"], RE[0]["kpeT"], RE[1]["kpeT"])
                crows(ME["mv"], RE[0]["v"][:, cs], RE[1]["v"][:, cs])
                ccols(ME["mqn"], RE[0]["qnT"][cs], RE[1]["qnT"][cs])
                ccols(ME["mkn"], RE[0]["knT"][cs], RE[1]["knT"][cs])
                crows(ME["mvn"], RE[0]["vn"][:, cs], RE[1]["vn"][:, cs])
                io = dict(ME)
                io.update({"nabias": nab_d[e, hs], "namask": namask_d, "myT": myT_d[v]})
                emit_M_even(cx, io, ctx_out)
            else:
                o = l // 2
                ccols(MO["hq"], RO[0]["qT"][cs], RO[1]["qT"][cs])
                for d in range(2):
                    zs = slice(d * 1024 + 512 * v, d * 1024 + 512 * v + 512)
                    ccols(MO["hz"][d], RO[0]["zT"][zs], RO[1]["zT"][zs])
                crows(MO["hv"], RO[0]["vi"][:, cs], RO[1]["vi"][:, cs])
                ccols(MO["hg"], RO[0]["gT"][cs], RO[1]["gT"][cs])
                for g in range(3):
                    us = slice(g * 1024 + 512 * v, g * 1024 + 512 * v + 512)
                    ccols(MO["hu"][g], RO[0]["uT"][us], RO[1]["uT"][us])
                io = dict(MO)
                io.update(hyc)
                io.update({"odd_idx": o, "lbl": lbl_d[v], "ng": ng_d[o, v], "hmasks": hmasks_d,
                           "cw": cw_d[o, v], "cb": cb_d[o, v], "skip": skip_d[o, v],
                           "fw1": fw1_d[o], "fw2": fw2_d[o], "fw3": fw3_d[o], "fb": fb_d[o],
                           "fwo": fwo_d[o, v], "delt": delt_d[v], "myT": myT_d[v]})
                ystage = [m.sb("ystage%d" % i, [128, TM], BF16) for i in range(2)]
                emit_hgrn(cx, io, ctx_out, ystage)
                emit_hyena(cx, io, ctx_out, ystage)
    m.emit()
    return nc


_FUSED = {}


def host_inputs(inp, b):
    x, ctx, c, c_ctx = inp["x"], inp["ctx"], inp["c"], inp["c_ctx"]
    h0 = np.stack([fm_tokens(np.concatenate([x[b, r * 1024:(r + 1) * 1024], ctx[b, r * 128:(r + 1) * 128]], 0))
                   for r in range(2)], 0)
    d = {"h0": np.ascontiguousarray(h0), "cT": np.ascontiguousarray(np.stack([fm_vec(c[b]), fm_vec(c_ctx)], -1))}
    return d


def host_shared(inp):
    sh = {"ident": np.eye(128, dtype=np.float32), "ada_w": inp["ada_w"], "ada_b": inp["ada_b"],
          "gmix": np.ascontiguousarray(np.stack([fm_vec(inp["norm_mix_g"][l]) for l in range(DEPTH)], 1)),
          "gmlp": np.ascontiguousarray(np.stack([fm_vec(inp["norm_mlp_g"][l]) for l in range(DEPTH)], 1)),
          "w_out": inp["w_out"], "mlp_w1": inp["mlp_w1"], "mlp_w2": inp["mlp_w2"],
          "ev_w_in": np.stack([ev_w_in_ext(inp["ev_w_in"][e]) for e in range(2)], 0),
          "w_ukv": np.stack([ukv_ext(inp["mla_w_ukv"][e]) for e in range(2)], 0),
          "od_w_in": inp["od_w_in"],
          "kvg": np.stack([fm_vec(inp["mla_kv_norm_g"][e]) for e in range(2)], 0),
          "fg": fm_vec(inp["final_norm_g"])}
    rt = [rope_tables(r) for r in range(2)]
    sh["cosT"] = np.stack([rt[0][0], rt[1][0]], 0)
    sh["sinT"] = np.stack([rt[0][1], rt[1][1]], 0)
    nabs = [na_bias_host(inp["na_rel_bias"][e]) for e in range(2)]
    sh["nabias"] = np.ascontiguousarray(np.stack([nabs[0][0], nabs[1][0]], 0))
    sh["namask"] = nabs[0][1]
    f32 = lambda a: np.ascontiguousarray(a).astype(np.float32)
    lg = inp["hgrn_lb_logits"]
    sh["lbl"] = f32(np.stack([np.stack([np.stack([fm_vec(lg[d, oo, 512 * v:512 * v + 512]) for oo in range(2)], 1)
                                        for d in range(2)], 1) for v in range(2)], 0))
    sh["ng"] = f32(np.stack([np.stack([fm_vec(inp["hgrn_norm_g"][o][512 * v:512 * v + 512]) for v in range(2)], 0)
                             for o in range(2)], 0))
    sh["hmasks"] = hgrn_masks()
    cwl, cbl, skl, fwol = [], [], [], []
    for o in range(2):
        cwv, cbv, wo = inp["hy_conv_w"][o], inp["hy_conv_b"][o], inp["hy_filt_wout"][o]
        cwl.append(np.stack([np.stack([np.stack([fm_vec(cwv[tap, g * 1024 + 512 * v:g * 1024 + 512 * v + 512])
                                                 for g in range(3)], 1) for tap in range(3)], 1) for v in range(2)], 0))
        cbl.append(np.stack([np.stack([fm_vec(cbv[g * 1024 + 512 * v:g * 1024 + 512 * v + 512]) for g in range(3)], 1)
                             for v in range(2)], 0))
        skl.append(np.stack([fm_vec(inp["hy_skip"][o][512 * v:512 * v + 512]) for v in range(2)], 0))
        fwol.append(np.stack([np.stack([wo[:, 512 * v:512 * v + 512], wo[:, 1024 + 512 * v:1024 + 512 * v + 512]], 1)
                              for v in range(2)], 0))
    sh["cw"], sh["cb"], sh["skip"], sh["fwo"] = f32(np.stack(cwl, 0)), f32(np.stack(cbl, 0)), f32(np.stack(skl, 0)), f32(np.stack(fwol, 0))
    sh["fw1"], sh["fw2"], sh["fw3"] = f32(inp["hy_filt_w1"]), f32(inp["hy_filt_w2"]), f32(inp["hy_filt_w3"])
    sh["fb"] = f32(np.stack([inp["hy_filt_b1"], inp["hy_filt_b2"], inp["hy_filt_b3"], inp["hy_filt_freq"]], -1))
    sh["delt"] = np.stack([hyena_deltas(v) for v in range(2)], 0)
    zl, tl, fl, il = hyena_consts(2048)
    zc, tc, fc_, ic = hyena_consts(256)
    sh.update({"zT_l": zl, "tcol_l": tl, "dftF_l": fl, "dftI_l": il, "zT_c": zc, "tcol_c": tc, "dftF_c": fc_, "dftI_c": ic})
    return sh


def kernel(**inp):
    inp = {k: np.asarray(v) for k, v in inp.items()}
    if "nc" not in _FUSED:
        _FUSED["nc"] = build_fused_program()
    sh = host_shared(inp)
    maps = []
    for b in range(4):
        d = dict(sh)
        d.update(host_inputs(inp, b))
        maps.append(d)
    res = run_bass_kernel_spmd(_FUSED["nc"], maps, core_ids=[0, 1, 2, 3])
    out = np.zeros((4, SEQ, D), dtype=np.float32)
    for b in range(4):
        oT = np.asarray(res.results[b]["outT"])
        for r in range(2):
            out[b, r * 1024:(r + 1) * 1024] = oT[r].transpose(2, 1, 0).reshape(1024, D)
    return out
```

```python
import math
import numpy as np
import ml_dtypes
import concourse.bass as bass
import concourse.mybir as mybir
from concourse.bass_utils import run_bass_kernel_spmd

F32 = mybir.dt.float32
BF16 = mybir.dt.bfloat16
AF = mybir.ActivationFunctionType
ALU = mybir.AluOpType
AX = mybir.AxisListType
NPBF = ml_dtypes.bfloat16

D = 2048
SEQ = 2048
CTX = 256
DEPTH = 4
TL = 1152
TM = 2304
KC = 16
HID = 8192
EPS = 1e-6
GRID_W = 64

ENGS = ("pe", "act", "dve", "pool", "sp")
DMA_RING = 12
SB_BASE = 16640
SB_END = 229376


def _dsize(dt):
    return 4 if dt == F32 else 2


class Op:
    __slots__ = ("eng", "fn", "waits", "idx", "is_dma", "dma_sem", "dma_val", "needs_inc")


class MK:
    def __init__(self, nc):
        self.nc = nc
        self.ops = {e: [] for e in ENGS}
        self.acc = {}
        self.waited = {e: {} for e in ENGS}
        self.dma_count = {e: 0 for e in ENGS}
        self.dma_ring_uses = {e: [0] * DMA_RING for e in ENGS}
        self.sb_base = {}
        self.sb_top = SB_BASE
        self.psum = [nc.alloc_psum_tensor("bank%d" % i, [128, 512], F32) for i in range(8)]
        self.n_dram = 0

    def sb(self, name, shape, dtype, at=None):
        per = _dsize(dtype)
        for s in shape[1:]:
            per *= int(s)
        if at is None:
            at = self.sb_top
            self.sb_top = (at + per + 31) // 32 * 32
        assert at % 32 == 0 and at + per <= SB_END, (name, at, per)
        t = self.nc.alloc_sbuf_tensor_at(name, list(shape), dtype, offset=at)
        self.sb_base[t.name] = (at, per)
        return t

    def mark(self):
        return self.sb_top

    def release(self, mark):
        self.sb_top = mark

    def dram(self, name, shape, dtype, kind="Internal"):
        return self.nc.dram_tensor(name, list(shape), dtype, kind=kind).ap()

    def _box(self, ap):
        t = ap.tensor
        space = str(ap.space)
        off = int(ap.offset)
        pairs = [(int(s), int(c)) for (s, c) in ap.ap]
        ds = _dsize(ap.dtype)
        if space == "PSUM":
            return ("P:" + t.name, 0, 128, 0, 1)
        if space == "SB":
            base, per = self.sb_base[t.name]
            P = per // ds
            plo = off // P
            flo = off % P
            pext = 0
            fext = 0
            for (s, c) in pairs:
                if c <= 1:
                    continue
                if s != 0 and s % P == 0:
                    pext += (c - 1) * (s // P)
                else:
                    fext += (c - 1) * abs(s)
            return ("SB", plo, plo + pext + 1, base + flo * ds, base + (flo + fext + 1) * ds)
        ext = 0
        for (s, c) in pairs:
            if c <= 1:
                continue
            ext += (c - 1) * abs(s)
        return ("D:" + t.name, 0, 1, off, off + ext + 1)

    @staticmethod
    def _overlap(a, b):
        return a[1] < b[2] and b[1] < a[2] and a[3] < b[4] and b[3] < a[4]

    @staticmethod
    def _contains(a, b):
        return a[1] <= b[1] and b[2] <= a[2] and a[3] <= b[3] and b[4] <= a[4]

    def _need(self, op, tok):
        e = op.eng
        if tok[0] == "e":
            _, x, idx = tok
            if x == e and e == "pe":
                return
            key = ("e", x)
            val = idx
        else:
            _, q, slot, useno = tok
            key = ("d", q, slot)
            val = useno
        w = self.waited[e]
        if w.get(key, -1) >= val:
            return
        w[key] = val
        op.waits.append((key, val))

    BUCKET = 2048

    def _split(self, b):
        if b[0] != "SB":
            return [b]
        out = []
        lo, hi = b[3], b[4]
        k = lo // self.BUCKET
        while k * self.BUCKET < hi:
            out.append((("SB", k), b[1], b[2], max(lo, k * self.BUCKET), min(hi, (k + 1) * self.BUCKET)))
            k += 1
        return out

    def _track(self, op, reads, writes, tok):
        rb = []
        wb = []
        for a in reads:
            b = self._box(a)
            if b[0].startswith("P:"):
                wb.append(b)
            else:
                rb.extend(self._split(b))
        for a in writes:
            wb.extend(self._split(self._box(a)))
        for b in rb:
            lst = self.acc.get(b[0])
            if lst:
                for rec in lst:
                    if rec[1] == "w" and self._overlap(rec[0], b):
                        self._need(op, rec[2])
        for b in wb:
            lst = self.acc.get(b[0])
            if lst:
                for rec in lst:
                    if self._overlap(rec[0], b):
                        self._need(op, rec[2])
        for b in rb:
            lst = self.acc.setdefault(b[0], [])
            if tok[0] == "e":
                lst[:] = [r for r in lst if not (r[1] == "r" and r[2][0] == "e" and r[2][1] == tok[1]
                                                 and self._contains(b, r[0]))]
            lst.append([b, "r", tok])
        for b in wb:
            lst = self.acc.setdefault(b[0], [])
            lst[:] = [r for r in lst if not self._contains(b, r[0])]
            lst.append([b, "w", tok])

    def op(self, eng, fn, reads, writes):
        o = Op()
        o.eng = eng
        o.fn = fn
        o.waits = []
        o.is_dma = False
        o.needs_inc = False
        o.idx = len(self.ops[eng])
        self._track(o, reads, writes, ("e", eng, o.idx))
        self.ops[eng].append(o)
        return o

    def dma(self, out, in_, q="sp", **kw):
        o = Op()
        o.eng = q
        o.waits = []
        o.is_dma = True
        o.needs_inc = False
        o.idx = len(self.ops[q])
        n = self.dma_count[q]
        self.dma_count[q] = n + 1
        slot = n % DMA_RING
        prev = self.dma_ring_uses[q][slot]
        if prev > 0:
            self._need(o, ("d", q, slot, prev))
        self.dma_ring_uses[q][slot] = prev + 1
        o.dma_sem = (q, slot)
        o.dma_val = prev + 1
        o.fn = lambda eng, out=out, in_=in_, kw=kw: eng.dma_start(out=out, in_=in_, **kw)
        self._track(o, [in_], [out], ("d", q, slot, prev + 1))
        self.ops[q].append(o)
        return o

    def matmul(self, out, lhsT, rhs, start=True, stop=True, **kw):
        return self.op("pe", lambda e: e.matmul(out, lhsT, rhs, start=start, stop=stop, **kw),
                       [lhsT, rhs], [out])

    def transpose(self, out, in_, ident):
        return self.op("pe", lambda e: e.transpose(out, in_, ident), [in_, ident], [out])

    def act(self, out, in_, func, bias=None, scale=None, accum_out=None):
        reads = [in_]
        kw = {}
        if bias is not None:
            kw["bias"] = bias
            if not isinstance(bias, (int, float)):
                reads.append(bias)
        if scale is not None:
            kw["scale"] = scale
            if not isinstance(scale, (int, float)):
                reads.append(scale)
        writes = [out]
        if accum_out is not None:
            kw["accum_out"] = accum_out
            writes.append(accum_out)
        return self.op("act", lambda e: e.activation(out, in_, func, **kw), reads, writes)

    def tt(self, out, in0, in1, op, eng="dve"):
        return self.op(eng, lambda e: e.tensor_tensor(out, in0, in1, op), [in0, in1], [out])

    def ts(self, out, in0, s1, s2, op0, op1=None, eng="dve"):
        reads = [in0]
        if not isinstance(s1, (int, float)):
            reads.append(s1)
        if s2 is not None and not isinstance(s2, (int, float)):
            reads.append(s2)
        if op1 is None:
            return self.op(eng, lambda e: e.tensor_scalar(out, in0, s1, None, op0), reads, [out])
        return self.op(eng, lambda e: e.tensor_scalar(out, in0, s1, s2, op0, op1), reads, [out])

    def stt(self, out, in0, scalar, in1, op0, op1):
        reads = [in0, in1]
        if not isinstance(scalar, (int, float)):
            reads.append(scalar)
        return self.op("dve", lambda e: e.scalar_tensor_tensor(out, in0, scalar, in1, op0, op1),
                       reads, [out])

    def copy(self, out, in_, eng="dve"):
        if eng == "act":
            return self.op("act", lambda e: e.copy(out, in_), [in_], [out])
        return self.op(eng, lambda e: e.tensor_copy(out, in_), [in_], [out])

    def memset(self, ap, val, eng="dve"):
        return self.op(eng, lambda e: e.memset(ap, val), [], [ap])

    def recip(self, out, in_):
        return self.op("dve", lambda e: e.reciprocal(out, in_), [in_], [out])

    def scan(self, out, d0, d1, init, op0, op1):
        reads = [d0, d1]
        if not isinstance(init, (int, float)):
            reads.append(init)
        return self.op("dve", lambda e: e.tensor_tensor_scan(out, d0, d1, init, op0, op1), reads, [out])

    def emit(self):
        nc = self.nc
        for e in ENGS:
            for o in self.ops[e]:
                for (key, val) in o.waits:
                    if key[0] == "e":
                        self.ops[key[1]][val].needs_inc = True
        cnt = {}
        for e in ENGS:
            c = 0
            arr = []
            for o in self.ops[e]:
                if o.needs_inc and not o.is_dma:
                    c += 1
                arr.append(c)
            cnt[e] = arr
        from contextlib import ExitStack
        with ExitStack() as st:
            esem = {e: st.enter_context(nc.semaphore("s_" + e)) for e in ENGS}
            dsem = {}
            for q in ENGS:
                for s in range(min(DMA_RING, self.dma_count[q])):
                    dsem[(q, s)] = st.enter_context(nc.semaphore("d_%s_%d" % (q, s)))
            block = st.enter_context(nc.Block())
            engmap = {"pe": "tensor", "act": "scalar", "dve": "vector", "pool": "gpsimd", "sp": "sync"}

            def make(e):
                def body(eng):
                    for o in self.ops[e]:
                        ws = [((esem[key[1]], cnt[key[1]][val]) if key[0] == "e" else
                               (dsem[(key[1], key[2])], 16 * val)) for (key, val) in o.waits]
                        attach = None
                        if ws and not o.is_dma:
                            attach = ws.pop()
                        for (s_, v_) in ws:
                            eng.wait_ge(s_, v_)
                        ins = o.fn(eng)
                        if attach is not None:
                            ins._wait_ge(attach[0], attach[1])
                        if o.is_dma:
                            ins.then_inc(dsem[o.dma_sem], 16)
                        elif o.needs_inc:
                            ins.then_inc(esem[e], 1)
                    for s in range(min(DMA_RING, self.dma_count[e])):
                        eng.wait_ge(dsem[(e, s)], 16 * self.dma_ring_uses[e][s])
                return body

            for e in ENGS:
                if len(self.ops[e]) == 0:
                    continue
                getattr(block, engmap[e])(make(e))
        return nc


MOD_SH_A, MOD_SC_A, MOD_G_A, MOD_SH_M, MOD_SC_M, MOD_G_M, MOD_GSC_A, MOD_GSC_M = [16 * i for i in range(8)]


class Ctx:
    def __init__(self, nc):
        self.nc = nc
        self.m = MK(nc)
        self.wring = None
        self.wi = 0
        self.ev = 0

    def consts(self, ident_d, need_bf=True):
        m = self.m
        self.ident = m.sb("ident", [128, 128], F32)
        m.dma(self.ident[:], ident_d)
        self.ones_bf = m.sb("ones_bf", [128, 128], BF16)
        m.memset(self.ones_bf[:], 1.0)
        self.ident_bf = m.sb("ident_bf", [128, 128], BF16)
        m.copy(self.ident_bf[:], self.ident[:])

    def init_wring(self, nslot=5):
        m = self.m
        self.wring = [m.sb("wslot%d" % i, [128, 4096], BF16) for i in range(nslot)]
        self.wi = 0

    def wload(self, Ws, si, kcb, ncb, nuse=None):
        assert kcb * ncb <= 4096
        slot = self.wring[self.wi % len(self.wring)]
        self.wi += 1
        view = slot[:, 0:kcb * ncb].rearrange("p (k n) -> p k n", n=ncb)
        src = Ws[si].rearrange("p (k n) -> p k n", n=ncb)
        if nuse is not None and nuse < ncb:
            self.m.dma(view[:, :, 0:nuse], src[:, :, 0:nuse], q="pool")
        else:
            self.m.dma(view, src, q="pool")
        return view

    def evac_eng(self):
        self.ev += 1
        return "act" if self.ev % 2 == 0 else "dve"


def rstd_compute(cx, src, nch, T, nfeat, rstd, sq, ps_bank, tmp):
    m = cx.m
    ps = cx.m.psum[ps_bank]
    for t0 in range(0, T, 384):
        tw = min(384, T - t0)
        for k in range(nch):
            m.act(sq[:, k, 0:tw], src[:, k, t0:t0 + tw], AF.Square)
        for k in range(nch):
            m.matmul(ps[:, 0:tw], cx.ones_bf[:, 0:128], sq[:, k, 0:tw], start=(k == 0), stop=(k == nch - 1))
        m.act(tmp[:, 0:tw], ps[:, 0:tw], AF.Sqrt, bias=cx.eps_col[:, 0:1], scale=1.0 / nfeat)
        m.recip(rstd[:, t0:t0 + tw], tmp[:, 0:tw])


def norm_modulate(cx, hT, actT, rstd, modall, l, gsc_base, sh_base, tmp4):
    m = cx.m
    for k in range(KC):
        for (c0, c1, col) in ((0, 1024, 0), (1024, TL, 1)):
            gs = modall[:, l, gsc_base + k, col:col + 1]
            sh = modall[:, l, sh_base + k, col:col + 1]
            m.stt(tmp4[:, c0:c1], hT[:, k, c0:c1], gs, rstd[:, c0:c1], ALU.mult, ALU.mult)
            m.act(actT[:, k, c0:c1], tmp4[:, c0:c1], AF.Identity, bias=sh, scale=1.0)


def proj_fm(cx, W2d, ncols_total, col0, chunk_cols, inT, nkc, consume, T=TL, bank_sets=((0, 1, 2), (3, 4, 5))):
    m = cx.m
    ncb = 4096 // nkc
    nchunks = ncols_total // chunk_cols
    per_slot = ncb // chunk_cols
    ci = 0
    tbs = [(t0, min(384, T - t0)) for t0 in range(0, T, 384)]
    assert col0 % ncb == 0
    for s0 in range(0, nchunks, per_slot):
        nhere = min(per_slot, nchunks - s0)
        slot = cx.wload(W2d, (col0 + s0 * chunk_cols) // ncb, nkc, ncb, nhere * chunk_cols)
        for j in range(nhere):
            banks = bank_sets[ci % len(bank_sets)]
            for k in range(nkc):
                for ti, (t0, tw) in enumerate(tbs):
                    m.matmul(m.psum[banks[ti]][0:chunk_cols, 0:tw],
                             slot[:, k, j * chunk_cols:(j + 1) * chunk_cols],
                             inT[:, k, t0:t0 + tw], start=(k == 0), stop=(k == nkc - 1))
            for ti, (t0, tw) in enumerate(tbs):
                consume(ci, ti, m.psum[banks[ti]][0:chunk_cols, 0:tw], t0, tw)
            ci += 1


def emit_mod(cx, out, cT_d, ada_w_d, ada_b_d, nch, nr):
    m = cx.m
    mk = m.mark()
    sc = m.sb("mod_sc", [128, KC, nr], F32)
    m.dma(sc[:], cT_d)
    m.act(sc[:], sc[:], AF.Silu)
    adabT = m.sb("mod_adabT", [128, DEPTH, nch], F32)
    btmp = m.sb("mod_btmp", [nch, 128], F32)
    for l in range(DEPTH):
        m.dma(btmp[:], ada_b_d[l].rearrange("(c p) -> c p", p=128))
        m.transpose(m.psum[6][:, 0:nch], btmp[:], cx.ident[0:nch, 0:nch])
        m.copy(adabT[:, l, :], m.psum[6][:, 0:nch])
    wsl = [m.sb("mod_w%d" % i, [128, KC, 512], F32) for i in range(3)]
    wi = 0
    for l in range(DEPTH):
        ps = m.psum[l % 2]
        for nb in range(nch // 4):
            slot = wsl[wi % 3]
            wi += 1
            m.dma(slot[:], ada_w_d[l, nb].rearrange("p (k n) -> p k n", n=512), q="sp")
            for jj in range(4):
                j = nb * 4 + jj
                for k in range(KC):
                    m.matmul(ps[:, nr * j:nr * j + nr], slot[:, k, jj * 128:(jj + 1) * 128], sc[:, k, :],
                             start=(k == 0), stop=(k == KC - 1))
        m.tt(out[:, l, :, :], ps[:, 0:nch * nr].rearrange("p (c t) -> p c t", t=nr),
             adabT[:, l, :].unsqueeze(2).to_broadcast([128, nch, nr]), ALU.add)
    m.release(mk)


def emit_gsc(cx, modall, gmix_d, gmlp_d):
    m = cx.m
    mk = m.mark()
    gmix = m.sb("mod_gmix", [128, DEPTH, KC], F32)
    gmlp = m.sb("mod_gmlp", [128, DEPTH, KC], F32)
    m.dma(gmix[:], gmix_d)
    m.dma(gmlp[:], gmlp_d)
    for l in range(DEPTH):
        m.stt(modall[:, l, MOD_GSC_A:MOD_GSC_A + 16, :], modall[:, l, MOD_SC_A:MOD_SC_A + 16, :], 1.0,
              gmix[:, l, :].unsqueeze(2).to_broadcast([128, KC, 2]), ALU.add, ALU.mult)
        m.stt(modall[:, l, MOD_GSC_M:MOD_GSC_M + 16, :], modall[:, l, MOD_SC_M:MOD_SC_M + 16, :], 1.0,
              gmlp[:, l, :].unsqueeze(2).to_broadcast([128, KC, 2]), ALU.add, ALU.mult)
    m.release(mk)


def build_mod_program():
    nc = bass.Bass("TRN2", target_bir_lowering=False)
    cx = Ctx(nc)
    m = cx.m
    ident_d = m.dram("ident", [128, 128], F32, "ExternalInput")
    cT_d = m.dram("cT", [128, KC, 5], F32, "ExternalInput")
    ada_w_d = m.dram("ada_w", [DEPTH, D, 1536], F32, "ExternalInput")
    ada_b_d = m.dram("ada_b", [DEPTH, 1536], F32, "ExternalInput")
    out_d = m.dram("modpart", [128, DEPTH, 12, 5], F32, "ExternalOutput")
    cx.consts(ident_d)
    outt = m.sb("modpart", [128, DEPTH, 12, 5], F32)
    emit_mod(cx, outt, cT_d, ada_w_d, ada_b_d, 12, 5)
    m.dma(out_d, outt[:])
    m.emit()
    return nc


EV_QNOPE, EV_CKV, EV_QNA, EV_KNA, EV_PE, EV_VNA, EV_NEXT = 0, 1024, 1536, 2560, 3584, 4864, 5888
SEGS = ((0, 1024, 0), (1024, TL, 1))


def proj_tm(cx, W2d, k0, nkc, col0, ncols, inT, consume, T, banks=(6, 7)):
    m = cx.m
    ncb = 4096 // nkc
    bi = 0
    assert col0 % ncb == 0 and k0 == 0
    for c0 in range(0, ncols, ncb):
        cw = min(ncb, ncols - c0)
        slot = cx.wload(W2d, (col0 + c0) // ncb, nkc, ncb, cw)
        for t in range(T // 128):
            for n0 in range(0, cw, 512):
                nw = min(512, cw - n0)
                ps = m.psum[banks[bi % 2]]
                bi += 1
                for k in range(nkc):
                    m.matmul(ps[:, 0:nw], inT[:, k, t * 128:(t + 1) * 128], slot[:, k, n0:n0 + nw],
                             start=(k == 0), stop=(k == nkc - 1))
                consume(t, c0 + n0, nw, ps[:, 0:nw])


def rmw_consumer(cx, hT, modall, lidx, gate_base):
    m = cx.m

    def consume(ci, ti, ps, t0, tw):
        for (c0, c1, col) in SEGS:
            a, b = max(c0, t0), min(c1, t0 + tw)
            if a >= b:
                continue
            m.stt(hT[:, ci, a:b], ps[:, a - t0:b - t0], modall[:, lidx, gate_base + ci, col:col + 1],
                  hT[:, ci, a:b], ALU.mult, ALU.add)
    return consume


def emit_post(cx, lp, hT, actT, rstd, modall, io, T):
    m = cx.m
    if "load_y" in io:
        io["load_y"](actT)
    else:
        m.dma(actT[:], io["yT"])
    proj_fm(cx, io["w_out"], D, 0, 128, actT, KC, rmw_consumer(cx, hT, modall, lp, MOD_G_A), T=T)
    mk = m.mark()
    sq = m.sb("sq", [128, KC, 384], BF16)
    tmp4 = m.sb("tmp4", [128, TL], F32)
    tmp = m.sb("tmp", [128, 384], F32)
    rstd_compute(cx, hT, KC, T, D, rstd, sq, 6, tmp)
    norm_modulate(cx, hT, actT, rstd, modall, lp, MOD_GSC_M, MOD_SH_M, tmp4)
    m.release(mk)
    mk = m.mark()
    hid = [m.sb("hid%d" % i, [128, 4, TL], BF16) for i in range(2)]
    rtmp = [m.sb("rtmp%d" % i, [128, 384], F32) for i in range(2)]
    rmw = rmw_consumer(cx, hT, modall, lp, MOD_G_M)
    cnt = [0]
    tbs = [(t0, min(384, T - t0)) for t0 in range(0, T, 384)]
    for hb in range(HID // 512):
        hbuf = hid[hb % 2]

        def relu2(ci, ti, ps, t0, tw, hbuf=hbuf):
            r = rtmp[cnt[0] % 2]
            cnt[0] += 1
            m.act(r[:, 0:tw], ps, AF.Relu)
            m.tt(hbuf[:, ci, t0:t0 + tw], r[:, 0:tw], r[:, 0:tw], ALU.mult, eng="dve")
        proj_fm(cx, io["w1"], 512, hb * 512, 128, actT, KC, relu2, T=T)
        for half in range(2):
            slot = cx.wload(io["w2"], hb * 2 + half, 4, 1024)
            for j in range(8):
                oc = half * 8 + j
                banks = ((0, 1, 2), (3, 4, 5))[oc % 2]
                for k in range(4):
                    for ti, (t0, tw) in enumerate(tbs):
                        m.matmul(m.psum[banks[ti]][:, 0:tw], slot[:, k, j * 128:(j + 1) * 128],
                                 hbuf[:, k, t0:t0 + tw], start=(k == 0), stop=(k == 3))
                for ti, (t0, tw) in enumerate(tbs):
                    rmw(oc, ti, m.psum[banks[ti]][:, 0:tw], t0, tw)
    m.release(mk)


def stager(cx, stage, dst_of_chunk, kind="copy"):
    m = cx.m

    def consume(ci, ti, ps, t0, tw):
        st = stage[ci % len(stage)]
        rows = ps.shape[0]
        if kind == "silu":
            m.act(st[0:rows, t0:t0 + tw], ps, AF.Silu)
        else:
            if cx.evac_eng() == "act":
                m.act(st[0:rows, t0:t0 + tw], ps, AF.Identity)
            else:
                m.copy(st[0:rows, t0:t0 + tw], ps)
        if t0 + tw >= TL:
            m.dma(dst_of_chunk(ci), st[0:rows, :])
    return consume


def emit_pre_even(cx, l, hT, actT, rstd, modall, io):
    m = cx.m
    e = l // 2
    mk = m.mark()
    sq = m.sb("sq", [128, KC, 384], BF16)
    tmp4 = m.sb("tmp4", [128, TL], F32)
    tmp = m.sb("tmp", [128, 384], F32)
    rstd_compute(cx, hT, KC, TL, D, rstd, sq, 6, tmp)
    norm_modulate(cx, hT, actT, rstd, modall, l, MOD_GSC_A, MOD_SH_A, tmp4)
    m.release(mk)
    mk = m.mark()
    stage = [m.sb("stg%d" % i, [128, TL], BF16) for i in range(4)]
    ckvT = m.sb("ckvT", [128, 4, TL], F32)
    kvnT = m.sb("kvnT", [128, 4, TL], BF16)
    cosT = m.sb("cosT", [64, TL], F32)
    sinT = m.sb("sinT", [64, TL], F32)
    rt = [m.sb("ropet%d" % i, [64, 384], F32) for i in range(2)]
    sq4 = m.sb("sq4", [128, 4, 384], BF16)
    tmp = m.sb("tmpb", [128, 384], F32)
    kvg = m.sb("kvg", [128, 4], F32)
    m.dma(cosT[:], io["cosT"])
    m.dma(sinT[:], io["sinT"])
    m.dma(kvg[:], io["kvg"])
    W = io["w_in"]
    proj_fm(cx, W, 1024, EV_QNOPE, 128, actT, KC, stager(cx, stage, lambda ci: io["qT"][ci, 0:128, :]))

    def ckv_c(ci, ti, ps, t0, tw):
        m.copy(ckvT[:, ci, t0:t0 + tw], ps, eng=cx.evac_eng())
    proj_fm(cx, W, 512, EV_CKV, 128, actT, KC, ckv_c)
    proj_fm(cx, W, 1024, EV_QNA, 128, actT, KC, stager(cx, stage, lambda ci: io["qnT"][ci * 128:(ci + 1) * 128, :]))
    proj_fm(cx, W, 1024, EV_KNA, 128, actT, KC, stager(cx, stage, lambda ci: io["knT"][ci * 128:(ci + 1) * 128, :]))

    keep = {}

    def rope_c(ci, ti, ps, t0, tw):
        if ci % 2 == 0:
            keep[ti] = ps
            return
        p = ci // 2
        st = stage[p % len(stage)]
        m.tt(rt[0][:, 0:tw], keep[ti], cosT[:, t0:t0 + tw], ALU.mult)
        m.tt(rt[1][:, 0:tw], ps, sinT[:, t0:t0 + tw], ALU.mult)
        m.tt(st[0:64, t0:t0 + tw], rt[0][:, 0:tw], rt[1][:, 0:tw], ALU.add)
        if t0 + tw >= TL:
            dst = io["qT"][p, 128:192, :] if p < 8 else io["kpeT"]
            m.dma(dst, st[0:64, :])
    proj_fm(cx, W, 18 * 64, EV_PE, 64, actT, KC, rope_c)

    def vna_c(t, c0, nw, ps):
        st = stage[(t + c0 // 256) % len(stage)]
        m.copy(st[:, 0:nw], ps, eng=cx.evac_eng())
        m.dma(io["vn"][t * 128:(t + 1) * 128, c0:c0 + nw], st[:, 0:nw])
    proj_tm(cx, W, 0, KC, EV_VNA, 1024, actT, vna_c, TL)

    rstd_compute(cx, ckvT, 4, TL, 512, rstd, sq4, 6, tmp)
    for k in range(4):
        m.stt(ckvT[:, k, :], ckvT[:, k, :], kvg[:, k:k + 1], rstd[:, :], ALU.mult, ALU.mult)
        m.copy(kvnT[:, k, :], ckvT[:, k, :], eng="act")
    proj_fm(cx, io["w_ukv"], 1024, 0, 128, kvnT, 4, stager(cx, stage, lambda ci: io["kT"][ci, :, :]))

    def v_c(t, c0, nw, ps):
        st = stage[(t + c0 // 512) % len(stage)]
        m.copy(st[:, 0:nw], ps, eng=cx.evac_eng())
        m.dma(io["v"][t * 128:(t + 1) * 128, c0:c0 + nw], st[:, 0:nw])
    proj_tm(cx, io["w_ukv"], 0, 4, 1024, 1024, kvnT, v_c, TL)
    m.release(mk)


def emit_pre_odd(cx, l, hT, actT, rstd, modall, io):
    m = cx.m
    mk = m.mark()
    sq = m.sb("sq", [128, KC, 384], BF16)
    tmp4 = m.sb("tmp4", [128, TL], F32)
    tmp = m.sb("tmp", [128, 384], F32)
    rstd_compute(cx, hT, KC, TL, D, rstd, sq, 6, tmp)
    norm_modulate(cx, hT, actT, rstd, modall, l, MOD_GSC_A, MOD_SH_A, tmp4)
    m.release(mk)
    mk = m.mark()
    stage = [m.sb("stg%d" % i, [128, TL], BF16) for i in range(4)]
    stage32 = [m.sb("stg32_%d" % i, [128, TL], F32) for i in range(2)]
    W = io["w_in"]
    proj_fm(cx, W, 1024, 0, 128, actT, KC,
            stager(cx, stage, lambda ci: io["qT"][ci * 128:(ci + 1) * 128, :], kind="silu"))

    def vi_c(t, c0, nw, ps):
        st = stage[(t + c0 // 256) % len(stage)]
        m.copy(st[:, 0:nw], ps, eng=cx.evac_eng())
        m.dma(io["vi"][t * 128:(t + 1) * 128, c0:c0 + nw], st[:, 0:nw])
    proj_tm(cx, W, 0, KC, 1024, 1024, actT, vi_c, TL)
    proj_fm(cx, W, 2048, 2048, 128, actT, KC,
            stager(cx, stage32, lambda ci: io["zT"][ci * 128:(ci + 1) * 128, :]))
    proj_fm(cx, W, 1024, 4096, 128, actT, KC,
            stager(cx, stage, lambda ci: io["gT"][ci * 128:(ci + 1) * 128, :], kind="silu"))
    proj_fm(cx, W, 3072, 5120, 128, actT, KC,
            stager(cx, stage, lambda ci: io["uT"][ci * 128:(ci + 1) * 128, :]))
    m.release(mk)


def emit_final(cx, hT, rstd, io):
    m = cx.m
    mk = m.mark()
    sq = m.sb("sq", [128, KC, 384], BF16)
    tmp = m.sb("tmp", [128, 384], F32)
    fg = m.sb("fg", [128, KC], F32)
    stage32 = [m.sb("stg32_%d" % i, [128, 1024], F32) for i in range(2)]
    m.dma(fg[:], io["fg"])
    rstd_compute(cx, hT, KC, 1024, D, rstd, sq, 6, tmp)
    for k in range(KC):
        st = stage32[k % 2]
        m.stt(st[:, :], hT[:, k, 0:1024], fg[:, k:k + 1], rstd[:, 0:1024], ALU.mult, ALU.mult)
        m.dma(io["outT"][:, k, :], st[:, :])
    m.release(mk)


def build_R_program(l):
    nc = bass.Bass("TRN2", target_bir_lowering=False)
    cx = Ctx(nc)
    m = cx.m
    dr = lambda n, s, d, k="ExternalInput": m.dram(n, s, d, k)
    ident_d = dr("ident", [128, 128], F32)
    hT_d = dr("hT", [128, KC, TL], F32)
    mod_d = dr("modraw", [128, DEPTH, 96, 2], F32)
    cx.consts(ident_d)
    cx.eps_col = m.sb("eps", [128, 1], F32)
    m.memset(cx.eps_col[:], EPS)
    modall = m.sb("modall", [128, DEPTH, 128, 2], F32)
    m.dma(modall[:, :, 0:96, :], mod_d)
    emit_gsc(cx, modall, dr("gmix", [128, DEPTH, KC], F32), dr("gmlp", [128, DEPTH, KC], F32))
    hT = m.sb("hT", [128, KC, TL], F32)
    m.dma(hT[:], hT_d)
    actT = m.sb("actT", [128, KC, TL], BF16)
    rstd = m.sb("rstd", [128, TL], F32)
    cx.init_wring(4)
    if l >= 1:
        io = {"yT": dr("yT", [128, KC, TL], BF16), "w_out": dr("w_out", [D, D], F32),
              "w1": dr("w1", [D, HID], F32), "w2": dr("w2", [HID, D], F32)}
        emit_post(cx, l - 1, hT, actT, rstd, modall, io, TL if l <= 3 else 1024)
    if l <= 3 and l % 2 == 0:
        io = {"w_in": dr("w_in", [D, EV_NEXT], F32), "w_ukv": dr("w_ukv", [512, 2048], F32),
              "cosT": dr("cosT", [64, TL], F32), "sinT": dr("sinT", [64, TL], F32),
              "kvg": dr("kvg", [128, 4], F32),
              "qT": dr("qT", [8, 192, TL], BF16, "ExternalOutput"),
              "kT": dr("kT", [8, 128, TL], BF16, "ExternalOutput"),
              "kpeT": dr("kpeT", [64, TL], BF16, "ExternalOutput"),
              "v": dr("v", [TL, 1024], BF16, "ExternalOutput"),
              "qnT": dr("qnT", [1024, TL], BF16, "ExternalOutput"),
              "knT": dr("knT", [1024, TL], BF16, "ExternalOutput"),
              "vn": dr("vn", [TL, 1024], BF16, "ExternalOutput")}
        emit_pre_even(cx, l, hT, actT, rstd, modall, io)
    elif l <= 3:
        io = {"w_in": dr("w_in", [D, 8192], F32),
              "qT": dr("qT", [1024, TL], BF16, "ExternalOutput"),
              "vi": dr("vi", [TL, 1024], BF16, "ExternalOutput"),
              "zT": dr("zT", [2048, TL], F32, "ExternalOutput"),
              "gT": dr("gT", [1024, TL], BF16, "ExternalOutput"),
              "uT": dr("uT", [3072, TL], BF16, "ExternalOutput")}
        emit_pre_odd(cx, l, hT, actT, rstd, modall, io)
    if l <= 3:
        hout = dr("hT_out", [128, KC, TL], F32, "ExternalOutput")
        m.dma(hout, hT[:])
    else:
        io = {"fg": dr("fg", [128, KC], F32), "outT": dr("outT", [128, KC, 1024], F32, "ExternalOutput")}
        emit_final(cx, hT, rstd, io)
    m.emit()
    return nc


def fm_vec(v):
    return np.ascontiguousarray(np.asarray(v).reshape(-1, 128).T)


def fm_tokens(tok):
    T = tok.shape[0]
    return np.ascontiguousarray(tok.reshape(T, KC, 128).transpose(2, 1, 0))


def rope_perm_idx():
    idx = np.zeros(64, dtype=np.int64)
    for j in range(64):
        jj = j % 32
        idx[j] = j + 16 if jj < 16 else j - 16
    return idx


def ev_w_in_ext(w):
    perm = rope_perm_idx()
    cols = []
    for h in range(8):
        cols.append(np.arange(h * 192, h * 192 + 128))
    cols.append(np.arange(1536, 2048))
    cols.append(np.arange(2112, 3136))
    cols.append(np.arange(3136, 4160))
    for h in range(8):
        base = h * 192 + 128
        cols.append(base + np.arange(64))
        cols.append(base + perm)
    cols.append(2048 + np.arange(64))
    cols.append(2048 + perm)
    cols.append(np.zeros(128, dtype=np.int64))
    cols.append(np.arange(4160, 5184))
    idx = np.concatenate(cols)
    assert idx.shape[0] == EV_NEXT
    return np.ascontiguousarray(w[:, idx])


def ukv_ext(w):
    kc = np.concatenate([np.arange(h * 256, h * 256 + 128) for h in range(8)])
    vc = np.concatenate([np.arange(h * 256 + 128, h * 256 + 256) for h in range(8)])
    return np.ascontiguousarray(w[:, np.concatenate([kc, vc])])


def rope_tables(rank):
    pos = np.arange(rank * 1024, rank * 1024 + 1024)
    rows = (pos // GRID_W).astype(np.float32)
    cols = (pos % GRID_W).astype(np.float32)
    inv_freq = (np.float32(10000.0) ** (-np.arange(0, 32, 2, dtype=np.float32) / np.float32(32))).astype(np.float32)
    cosT = np.ones((64, TL), dtype=np.float32)
    sinT = np.zeros((64, TL), dtype=np.float32)
    for j in range(64):
        p = rows if j < 32 else cols
        jj = j % 32
        ang = (p * inv_freq[jj % 16]).astype(np.float32)
        cosT[j, :1024] = np.cos(ang)
        s = np.sin(ang)
        sinT[j, :1024] = -s if jj < 16 else s
    return cosT, sinT


NA_CLS = {(0, 0): 0, (1, 0): 1, (2, 0): 2, (3, 0): 3, (4, 0): 4, (5, 1): 5, (5, 0): 6, (6, 0): 7, (7, 0): 8}


def na_row_info(r):
    rs = min(max(r - 4, 0), 24)
    base = 2 * (rs // 2)
    cls = NA_CLS[(r - base, rs - base)]
    ntiles = 5 if rs - base == 1 else 4
    return base // 2, cls, ntiles


def na_tables():
    ridx = np.zeros((9, 5, 128, 64), dtype=np.int64)
    cidx = np.zeros((9, 5, 128, 64), dtype=np.int64)
    mask = np.zeros((9, 5, 128, 64), dtype=np.float32)
    p = np.arange(128)
    qc = np.arange(64)
    kcol = (p % 64)[:, None]
    col_start = np.clip(qc - 8, 0, 48)[None, :]
    col_ok = (kcol >= col_start) & (kcol < col_start + 16)
    coff = np.clip(kcol - qc[None, :], -15, 15) + 15
    for (dr, off), c in NA_CLS.items():
        for j in range(5):
            krel = 2 * j + p // 64
            inband = (krel >= off) & (krel < off + 8)
            roff = np.clip(krel - dr + 7, 0, 14)
            ridx[c, j] = roff[:, None]
            cidx[c, j] = coff
            ok = inband[:, None] & col_ok
            mask[c, j] = np.where(ok, 0.0, -30000.0)
    return ridx, cidx, mask


def dense_attn(cx, parts, vtile, key_tiles, q0, qn, scale, yT, pbuf, rsb):
    m = cx.m
    nk = len(key_tiles)
    for qb in range(q0, q0 + qn, 512):
        qw = min(512, q0 + qn - qb)
        O = m.psum[4]
        Sm = m.psum[5]

        def s_mm(i):
            sb = m.psum[i % 4]
            kt = key_tiles[i]
            for pi, (qp, kp) in enumerate(parts):
                m.matmul(sb[:, 0:qw], kp[:, kt * 128:(kt + 1) * 128], qp[:, qb:qb + qw],
                         start=(pi == 0), stop=(pi == len(parts) - 1))
        s_mm(0)
        for i in range(nk):
            if i + 1 < nk:
                s_mm(i + 1)
            P = pbuf[i % len(pbuf)]
            m.act(P[:, 0:qw], m.psum[i % 4][:, 0:qw], AF.Exp, scale=scale)
            m.matmul(O[:, 0:qw], vtile(key_tiles[i]), P[:, 0:qw], start=(i == 0), stop=(i == nk - 1))
            m.matmul(Sm[:, 0:qw], cx.ones_bf[:, 0:128], P[:, 0:qw], start=(i == 0), stop=(i == nk - 1))
        m.recip(rsb[:, 0:qw], Sm[:, 0:qw])
        m.tt(yT[:, qb:qb + qw], O[:, 0:qw], rsb[:, 0:qw], ALU.mult)


def emit_M_even(cx, io, ctx_out):
    m = cx.m
    mk = m.mark()
    LT = list(range(2, 18))
    CTt = [0, 1]
    allk = CTt + LT
    pbuf = [m.sb("pbuf%d" % i, [128, 512], BF16) for i in range(3)]
    rsb = m.sb("rsb", [128, 512], F32)
    ystage = [m.sb("ystage%d" % i, [128, TM], BF16) for i in range(2)]
    vsb = m.sb("vsb", [128, 18, 512], BF16)
    m.dma(vsb[:], io["mv"].rearrange("(t p) c -> p t c", p=128))
    kpe = m.sb("kpe", [64, TM], BF16)
    m.dma(kpe[:], io["mkpe"])
    qn_ = [m.sb("qnope%d" % i, [128, TM], BF16) for i in range(2)]
    qp_ = [m.sb("qpe%d" % i, [64, TM], BF16) for i in range(2)]
    kn_ = [m.sb("knope%d" % i, [128, TM], BF16) for i in range(2)]
    mla_scale = 192.0 ** -0.5
    c0 = 0 if ctx_out else 256
    for h in range(4):
        qn, qp, kn = qn_[h % 2], qp_[h % 2], kn_[h % 2]
        m.dma(qn[:], io["mq"][h, 0:128, :])
        m.dma(qp[:], io["mq"][h, 128:192, :])
        m.dma(kn[:], io["mk"][h])
        ys = ystage[h % 2]
        parts = [(qn, kn), (qp, kpe)]
        vt = lambda kt, h=h: vsb[:, kt, h * 128:(h + 1) * 128]
        if ctx_out:
            dense_attn(cx, parts, vt, CTt, 0, 256, mla_scale, ys, pbuf, rsb)
        dense_attn(cx, parts, vt, allk, 256, 2048, mla_scale, ys, pbuf, rsb)
        m.dma(io["myT"][h, :, c0:TM], ys[:, c0:TM])
    m.dma(vsb[:], io["mvn"].rearrange("(t p) c -> p t c", p=128))
    mask = m.sb("namask", [128, 9 * 5 * 64], F32)
    m.dma(mask[:], io["namask"])
    bias_ = [m.sb("nabias%d" % i, [128, 9 * 5 * 64], F32) for i in range(2)]
    lg_ = [m.sb("nalg%d" % i, [128, 320], F32) for i in range(2)]
    P_ = [m.sb("naP%d" % i, [128, 448], BF16) for i in range(2)]
    nrs_ = [m.sb("nars%d" % i, [128, 64], F32) for i in range(2)]
    na_scale = 128.0 ** -0.5
    it = 0
    for h in range(4):
        qn, kn = qn_[h % 2], kn_[h % 2]
        m.dma(qn[:], io["mqn"][h * 128:(h + 1) * 128, :])
        m.dma(kn[:], io["mkn"][h * 128:(h + 1) * 128, :])
        bias = bias_[h % 2]
        m.dma(bias[:], io["nabias"][h])
        m.stt(bias[:], bias[:], 1.0, mask[:], ALU.mult, ALU.add)
        bv = bias[:].rearrange("p (c x) -> p c x", c=9)
        ys = ystage[h % 2]
        vt = lambda kt, h=h: vsb[:, kt, h * 128:(h + 1) * 128]
        if ctx_out:
            dense_attn(cx, [(qn, kn)], vt, CTt, 0, 256, na_scale, ys, pbuf, rsb)
        for r in range(32):
            j0, cls, nt = na_row_info(r)
            q0 = 256 + 64 * r
            S = m.psum[it % 2]
            O = m.psum[2 + it % 2]
            lg, P, nrs = lg_[it % 2], P_[it % 2], nrs_[it % 2]
            it += 1
            tiles = [2 + j0 + j for j in range(nt)]
            for j, kt in enumerate(tiles):
                m.matmul(S[:, 64 * j:64 * j + 64], kn[:, kt * 128:(kt + 1) * 128], qn[:, q0:q0 + 64])
            for j, kt in enumerate(CTt):
                m.matmul(S[:, 320 + 64 * j:384 + 64 * j], kn[:, kt * 128:(kt + 1) * 128], qn[:, q0:q0 + 64])
            m.stt(lg[:, 0:64 * nt], S[:, 0:64 * nt], na_scale, bv[:, cls, 0:64 * nt], ALU.mult, ALU.add)
            m.act(P[:, 0:64 * nt], lg[:, 0:64 * nt], AF.Exp)
            m.act(P[:, 320:448], S[:, 320:448], AF.Exp, scale=na_scale)
            srcs = [(kt, P[:, 64 * j:64 * j + 64]) for j, kt in enumerate(tiles)]
            srcs += [(kt, P[:, 320 + 64 * j:384 + 64 * j]) for j, kt in enumerate(CTt)]
            for i, (kt, pp) in enumerate(srcs):
                m.matmul(O[:, 0:64], vt(kt), pp, start=(i == 0), stop=(i == len(srcs) - 1), skip_group_check=True)
                m.matmul(O[:, 64:128], cx.ones_bf[:, 0:128], pp, start=False, stop=(i == len(srcs) - 1),
                         skip_group_check=True)
            m.recip(nrs[:, :], O[:, 64:128])
            m.tt(ys[:, q0:q0 + 64], O[:, 0:64], nrs[:, :], ALU.mult)
        m.dma(io["myT"][4 + h, :, c0:TM], ys[:, c0:TM])
    m.release(mk)


def build_Meven_program(ctx_out):
    nc = bass.Bass("TRN2", target_bir_lowering=False)
    cx = Ctx(nc)
    m = cx.m
    dr = lambda n, s, d, k="ExternalInput": m.dram(n, s, d, k)
    cx.consts(dr("ident", [128, 128], F32))
    io = {"mq": dr("mq", [4, 192, TM], BF16), "mk": dr("mk", [4, 128, TM], BF16),
          "mkpe": dr("mkpe", [64, TM], BF16), "mv": dr("mv", [TM, 512], BF16),
          "mqn": dr("mqn", [512, TM], BF16), "mkn": dr("mkn", [512, TM], BF16),
          "mvn": dr("mvn", [TM, 512], BF16),
          "nabias": dr("nabias", [4, 128, 2880], F32), "namask": dr("namask", [128, 2880], F32),
          "myT": dr("myT", [8, 128, TM], BF16, "ExternalOutput")}
    emit_M_even(cx, io, ctx_out)
    m.emit()
    return nc


def canon_cols(a0, a1):
    return np.ascontiguousarray(np.concatenate([a0[..., 1024:], a1[..., 1024:], a0[..., :1024], a1[..., :1024]], -1))


def canon_rows(a0, a1):
    return np.ascontiguousarray(np.concatenate([a0[1024:], a1[1024:], a0[:1024], a1[:1024]], 0))


_NA_TAB = None


def na_bias_host(rel_bias_e):
    global _NA_TAB
    if _NA_TAB is None:
        _NA_TAB = na_tables()
    ridx, cidx, mask = _NA_TAB
    g = rel_bias_e[:, ridx, cidx]
    g = np.ascontiguousarray(g.transpose(0, 3, 1, 2, 4).reshape(8, 128, 2880)).astype(np.float32)
    mk = np.ascontiguousarray(mask.transpose(2, 0, 1, 3).reshape(128, 2880))
    return g, mk


def assemble_Meven(o0, o1, r, nab, namask):
    hs = slice(4 * r, 4 * r + 4)
    cs = slice(512 * r, 512 * r + 512)
    return {"ident": np.eye(128, dtype=np.float32),
            "mq": canon_cols(o0["qT"][hs], o1["qT"][hs]),
            "mk": canon_cols(o0["kT"][hs], o1["kT"][hs]),
            "mkpe": canon_cols(o0["kpeT"], o1["kpeT"]),
            "mv": canon_rows(o0["v"][:, cs], o1["v"][:, cs]),
            "mqn": canon_cols(o0["qnT"][cs], o1["qnT"][cs]),
            "mkn": canon_cols(o0["knT"][cs], o1["knT"][cs]),
            "mvn": canon_rows(o0["vn"][:, cs], o1["vn"][:, cs]),
            "nabias": np.ascontiguousarray(nab[hs]), "namask": namask}


def assemble_y(m0, m1, r):
    full = np.zeros((16, 128, TM), dtype=m0["myT"].dtype)
    for rr, mm in ((0, m0), (1, m1)):
        full[4 * rr:4 * rr + 4] = mm["myT"][0:4]
        full[8 + 4 * rr:8 + 4 * rr + 4] = mm["myT"][4:8]
    loc = np.concatenate([full[:, :, 256 + 1024 * r:256 + 1024 * (r + 1)], full[:, :, 128 * r:128 * (r + 1)]], -1)
    return np.ascontiguousarray(loc.transpose(1, 0, 2))


NCH = 36


def hgrn_sigma(d, c):
    if d == 0:
        return c
    return 3 - c if c < 4 else 35 - (c - 4)


def emit_hgrn(cx, io, ctx_out, ystage):
    m = cx.m
    mk = m.mark()
    T = TM
    BIG = 2.0e17
    vtok = m.sb("hg_vtok", [128, 18, 512], BF16)
    m.dma(vtok[:], io["hv"].rearrange("(t p) c -> p t c", p=128))
    lbl = m.sb("hg_lbl", [128, 2, 2, 4], F32)
    m.dma(lbl[:], io["lbl"])
    lb = m.sb("hg_lb", [128, 2, 4], F32)
    oml = m.sb("hg_oml", [128, 2, 4], F32)
    if io["odd_idx"] == 0:
        m.memset(lb[:], 0.0)
    else:
        m.tt(lb[:], lbl[:, :, 1, :], lbl[:, :, 0, :], ALU.subtract)
        m.act(lb[:], lb[:], AF.Sigmoid)
    m.ts(oml[:], lb[:], -1.0, 1.0, ALU.mult, ALU.add)
    ng = m.sb("hg_ng", [128, 4], F32)
    m.dma(ng[:], io["ng"])
    masks = m.sb("hg_masks", [128, 4, 128], F32)
    m.dma(masks[:], io["hmasks"])
    ones_f = m.sb("hg_ones", [128, T], BF16)
    m.memset(ones_f[:], 1.0)
    qb = m.sb("hg_q", [128, T], BF16)
    gb = m.sb("hg_g", [128, T], BF16)
    Qi = [m.sb("hg_Qi%d" % d, [128, T], BF16) for d in range(2)]
    Ki = [m.sb("hg_Ki%d" % d, [128, T], BF16) for d in range(2)]
    Qo = [m.sb("hg_Qo%d" % d, [128, T], BF16) for d in range(2)]
    Ko = [m.sb("hg_Ko%d" % d, [128, T], BF16) for d in range(2)]
    Qs = [m.sb("hg_Qs%d" % d, [128, T], BF16) for d in range(2)]
    Sbf = [m.sb("hg_Sbf%d" % d, [128, NCH, 128], BF16) for d in range(2)]
    KlT = m.sb("hg_KlT", [128, T], BF16)
    Kltok = m.sb("hg_Kltok", [128, 18, 128], BF16)
    z = m.sb("hg_z", [128, T], F32)
    lf = m.sb("hg_lf", [128, T], F32)
    kk = m.sb("hg_k", [128, T], F32)
    ee = m.sb("hg_e", [128, T], F32)
    gaddr = m.mark()
    G = m.sb("hg_G", [128, T], F32)
    Gx = m.sb("hg_Gx", [128, T], F32)
    Dfull = m.sb("hg_Dfull", [128, 128 * NCH], F32, at=gaddr)
    Sst = m.sb("hg_Sst", [128, 128 * NCH], F32)
    Dsc = m.sb("hg_Dsc", [128, NCH], F32)
    Dtmp = m.sb("hg_Dtmp", [128, NCH], F32)
    Am = [m.sb("hg_Am%d" % i, [128, 128], BF16) for i in range(4)]
    T1 = [m.sb("hg_T1_%d" % i, [128, 128], F32) for i in range(2)]
    T2 = [m.sb("hg_T2_%d" % i, [128, 128], F32) for i in range(2)]
    oT = lf
    rstd = ee
    sq1 = m.sb("hg_sq", [128, 1, 384], BF16)
    tmp = m.sb("hg_tmp", [128, 384], F32)

    v64 = lambda t_: t_[:].rearrange("p (c j) -> p c j", j=64)
    v32 = lambda t_: t_[:].rearrange("p (c j) -> p c j", j=32)
    bc64 = lambda t_, j: v64(t_)[:, :, j:j + 1].to_broadcast([128, NCH, 64])
    bc32 = lambda t_, j: v32(t_)[:, :, j:j + 1].to_broadcast([128, 2 * NCH, 32])
    c0 = 0 if ctx_out else 256
    ai = 0
    for h in range(4):
        m.dma(qb[:], io["hq"][h * 128:(h + 1) * 128, :])
        m.dma(gb[:], io["hg"][h * 128:(h + 1) * 128, :])
        for d in range(2):
            m.dma(z[:], io["hz"][d, h * 128:(h + 1) * 128, :])
            m.act(z[:], z[:], AF.Sigmoid)
            m.ts(z[:], z[:], oml[:, d, h:h + 1], lb[:, d, h:h + 1], ALU.mult, ALU.add)
            m.ts(z[:], z[:], 1e-30, None, ALU.max, eng="pool")
            m.act(lf[:], z[:], AF.Ln)
            m.ts(kk[:], z[:], -1.0, 1.0, ALU.mult, ALU.add, eng="pool")
            m.scan(G[:], ones_f[:], lf[:], 0.0, ALU.mult, ALU.add)
            m.tt(Gx[:], G[:], lf[:], ALU.subtract, eng="pool")
            A = G if d == 0 else Gx
            sgn = 1.0 if d == 0 else -1.0
            m.tt(v32(z), v32(A), bc32(A, 15 if d == 0 else 16), ALU.subtract, eng="pool")
            m.act(ee[:], z[:], AF.Exp, scale=sgn)
            m.stt(Qi[d][:], ee[:], BIG, qb[:], ALU.min, ALU.mult)
            m.act(ee[:], z[:], AF.Exp, scale=-sgn)
            m.stt(Ki[d][:], ee[:], BIG, kk[:], ALU.min, ALU.mult)
            m.tt(v64(z), v64(A), bc64(A, 31 if d == 0 else 32), ALU.subtract, eng="pool")
            m.act(ee[:], z[:], AF.Exp, scale=sgn)
            m.stt(Qo[d][:], ee[:], 1.0, qb[:], ALU.min, ALU.mult)
            m.act(ee[:], z[:], AF.Exp, scale=-sgn)
            m.stt(Ko[d][:], ee[:], 1.0, kk[:], ALU.min, ALU.mult)
            if d == 0:
                m.tt(v64(z), v64(G), bc64(Gx, 0), ALU.subtract, eng="pool")
            else:
                m.tt(v64(z), v64(Gx), bc64(G, 63), ALU.subtract, eng="pool")
            m.act(ee[:], z[:], AF.Exp, scale=sgn)
            m.stt(Qs[d][:], ee[:], 1.0, qb[:], ALU.min, ALU.mult)
            if d == 0:
                m.tt(v64(z), v64(G), bc64(G, 63), ALU.subtract, eng="pool")
            else:
                m.tt(v64(z), v64(Gx), bc64(Gx, 0), ALU.subtract, eng="pool")
            m.act(ee[:], z[:], AF.Exp, scale=-sgn)
            m.stt(KlT[:], ee[:], 1.0, kk[:], ALU.min, ALU.mult)
            m.tt(Dtmp[:], v64(G)[:, :, 63], v64(Gx)[:, :, 0], ALU.subtract)
            m.act(Dtmp[:], Dtmp[:], AF.Exp)
            if d == 0:
                m.copy(Dsc[:, 1:NCH], Dtmp[:, 1:NCH], eng="pool")
                m.memset(Dsc[:, 0:1], 0.0, eng="pool")
            else:
                for c in range(NCH):
                    s = hgrn_sigma(1, c)
                    if s == 0:
                        m.memset(Dsc[:, 0:1], 0.0, eng="pool")
                    else:
                        m.copy(Dsc[:, s:s + 1], Dtmp[:, c:c + 1], eng="pool")
            m.copy(Dfull[:].rearrange("p (e s) -> p e s", s=NCH),
                   Dsc[:].unsqueeze(1).to_broadcast([128, 128, NCH]), eng="pool")
            for t4 in range(0, 18, 4):
                nt = min(4, 18 - t4)
                pb = m.psum[6 + (t4 // 4) % 2][:].bitcast(BF16)
                for i in range(nt):
                    m.transpose(pb[:, i * 128:(i + 1) * 128], KlT[:, (t4 + i) * 128:(t4 + i + 1) * 128], cx.ident_bf[:])
                m.copy(Kltok[:, t4:t4 + nt, :], pb[:, 0:nt * 128].rearrange("p (t d) -> p t d", d=128),
                       eng=cx.evac_eng())
            S3 = Sst[:].rearrange("p (e s) -> p e s", s=NCH)
            for c in range(NCH):
                t, j = c // 2, c % 2
                ps = m.psum[c % 4]
                m.matmul(ps[:, 0:128], Kltok[64 * j:64 * j + 64, t, :], vtok[64 * j:64 * j + 64, t, h * 128:(h + 1) * 128])
                s = hgrn_sigma(d, c)
                m.copy(S3[:, :, s], ps[:, 0:128], eng=cx.evac_eng())
            m.scan(Sst[:], Dfull[:], Sst[:], 0.0, ALU.mult, ALU.add)
            m.copy(Sbf[d][:], Sst[:].rearrange("p (e s) -> p s e", s=NCH), eng="pool")
        for t in range(c0 // 128, 18):
            psO = m.psum[4 + t % 2]
            ams = []
            for d in range(2):
                tsl = slice(t * 128, (t + 1) * 128)
                psA1 = m.psum[(2 * d) % 4]
                psA2 = m.psum[(2 * d + 1) % 4]
                m.matmul(psA1[:, 0:128], Ki[d][:, tsl], Qi[d][:, tsl])
                m.matmul(psA2[:, 0:128], Ko[d][:, tsl], Qo[d][:, tsl])
                am = Am[ai % 4]
                ai += 1
                m.tt(T1[d][:], psA1[:, 0:128], masks[:, 2 * d, :], ALU.mult)
                m.tt(T2[d][:], psA2[:, 0:128], masks[:, 2 * d + 1, :], ALU.mult)
                m.tt(am[:], T1[d][:], T2[d][:], ALU.add, eng="pool")
                ams.append(am)
            mms = []
            for d in range(2):
                mms.append((psO[:, 0:128], vtok[:, t, h * 128:(h + 1) * 128], ams[d][:]))
                for j in range(2):
                    c = 2 * t + j
                    s = hgrn_sigma(d, c)
                    if s >= 1:
                        mms.append((psO[:, 64 * j:64 * j + 64], Sbf[d][:, s - 1, :], Qs[d][:, c * 64:(c + 1) * 64]))
            for i, (o_, l_, r_) in enumerate(mms):
                m.matmul(o_, l_, r_, start=(i == 0), stop=(i == len(mms) - 1), skip_group_check=True)
            m.copy(oT[:, t * 128:(t + 1) * 128], psO[:, 0:128], eng=cx.evac_eng())
        rstd_compute(cx, oT[:].rearrange("p (o t) -> p o t", o=1)[:, :, c0:T], 1, T - c0, 128, rstd, sq1, 7, tmp)
        ys = ystage[h % 2]
        m.stt(oT[:, c0:T], oT[:, c0:T], ng[:, h:h + 1], rstd[:, 0:T - c0], ALU.mult, ALU.mult)
        m.tt(ys[:, c0:T], oT[:, c0:T], gb[:, c0:T], ALU.mult, eng="pool")
        m.dma(io["myT"][h, :, c0:T], ys[:, c0:T])
    m.release(mk)


def sin_reduced(cx, out, x, rows, w, t1):
    m = cx.m
    PI = math.pi
    for _ in range(2):
        m.ts(t1[0:rows, 0:w], x, PI, -2 * PI, ALU.is_gt, ALU.mult)
        m.tt(x, x, t1[0:rows, 0:w], ALU.add)
        m.ts(t1[0:rows, 0:w], x, -PI, 2 * PI, ALU.is_lt, ALU.mult)
        m.tt(x, x, t1[0:rows, 0:w], ALU.add)
    m.act(out, x, AF.Sin)


def emit_hyena(cx, io, ctx_out, ystage):
    m = cx.m
    mk = m.mark()
    T = TM
    cw = m.sb("hy_cw", [128, 3, 3, 4], F32)
    cb = m.sb("hy_cb", [128, 3, 4], F32)
    skip = m.sb("hy_skip", [128, 4], F32)
    m.dma(cw[:], io["cw"])
    m.dma(cb[:], io["cb"])
    m.dma(skip[:], io["skip"])
    fw1 = m.sb("hy_w1", [33, 64], F32)
    fw2 = m.sb("hy_w2", [64, 64], F32)
    fw3 = m.sb("hy_w3", [64, 64], F32)
    fb = m.sb("hy_fb", [64, 4], F32)
    fwo = m.sb("hy_wo", [64, 2, 512], F32)
    m.dma(fw1[:], io["fw1"])
    m.dma(fw2[:], io["fw2"])
    m.dma(fw3[:], io["fw3"])
    m.dma(fb[:], io["fb"])
    m.dma(fwo[:], io["fwo"])
    delt = m.sb("hy_delt", [128, 512], F32)
    m.dma(delt[:], io["delt"])
    X0 = [m.sb("hy_X0_%d" % i, [128, T], BF16) for i in range(4)]
    VX = [m.sb("hy_VX_%d" % i, [128, T], BF16) for i in range(4)]
    vxtok = m.sb("hy_vxtok", [128, 18, 512], BF16)
    mB = m.mark()
    ub = [m.sb("hy_u%d" % i, [128, T], BF16) for i in range(2)]
    acc = [m.sb("hy_acc%d" % i, [128, T], F32) for i in range(2)]
    segs = ((0, 256), (256, T))
    for cc in range(4):
        def conv(g, a, u):
            m.dma(u[:], io["hu"][g, cc * 128:(cc + 1) * 128, :])
            m.ts(a[:], u[:], cw[:, 1, g, cc:cc + 1], cb[:, g, cc:cc + 1], ALU.mult, ALU.add)
            for (s0, s1) in segs:
                m.stt(a[:, s0 + 1:s1], u[:, s0:s1 - 1], cw[:, 0, g, cc:cc + 1], a[:, s0 + 1:s1], ALU.mult, ALU.add)
                m.stt(a[:, s0:s1 - 1], u[:, s0 + 1:s1], cw[:, 2, g, cc:cc + 1], a[:, s0:s1 - 1], ALU.mult, ALU.add)
        conv(1, acc[0], ub[0])
        conv(2, acc[1], ub[1])
        m.tt(VX[cc][:], acc[0][:], acc[1][:], ALU.mult, eng="pool")
        conv(0, acc[0], ub[0])
        m.copy(X0[cc][:], acc[0][:], eng="act")
    for t in range(18):
        pb = m.psum[6 + t % 2][:].bitcast(BF16)
        for cc in range(4):
            m.transpose(pb[:, cc * 128:(cc + 1) * 128], VX[cc][:, t * 128:(t + 1) * 128], cx.ident_bf[:])
        m.copy(vxtok[:, t, :], pb[:, 0:512], eng=cx.evac_eng())
    m.release(mB)
    for (name, n, tok0) in (("c", 256, 0), ("l", 2048, 256)):
        if name == "c" and not ctx_out:
            continue
        m.release(mB)
        ntile = n // 128
        nfc = n // 128
        Yc = m.sb("hy_Yc", [128, nfc, 512], BF16)
        Ys = m.sb("hy_Ys", [128, nfc, 512], BF16)
        p1 = m.sb("hy_p1", [128, 512], F32)
        p2 = m.sb("hy_p2", [128, 512], F32)
        mC = m.mark()
        hs = m.sb("hy_hs", [128, ntile, 512], BF16)
        hd = m.sb("hy_hd", [128, ntile, 512], BF16)
        mD = m.mark()
        zT = m.sb("hy_zT", [33, n], F32)
        m.dma(zT[:], io["zT_" + name])
        tcol = m.sb("hy_tcol", [128, ntile], F32)
        m.dma(tcol[:], io["tcol_" + name])
        hA = m.sb("hy_hA", [64, n], F32)
        hB = m.sb("hy_hB", [64, n], F32)
        t1 = m.sb("hy_t1", [64, 512], F32)
        layers = ((fw1, 33, zT, hA, 0), (fw2, 64, hA, hB, 1), (fw3, 64, hB, hA, 2))
        for (wt, kdim, src_, dst, bi) in layers:
            for b0 in range(0, n, 512):
                bw = min(512, n - b0)
                ps = m.psum[(b0 // 512) % 2]
                m.matmul(ps[0:64, 0:bw], wt[0:kdim, :], src_[0:kdim, b0:b0 + bw])
                m.ts(dst[:, b0:b0 + bw], ps[0:64, 0:bw], fb[:, bi:bi + 1], fb[:, 3:4], ALU.add, ALU.mult)
                sin_reduced(cx, dst[:, b0:b0 + bw], dst[:, b0:b0 + bw], 64, bw, t1)
        h3 = hA
        win = m.sb("hy_win", [128, 512], F32)
        hf = m.sb("hy_hf", [128, 512], F32)
        hb = m.sb("hy_hb", [128, 512], F32)
        ntc = m.sb("hy_ntc", [128, ntile], F32)
        m.ts(ntc[:], tcol[:], -1.0, None, ALU.mult)
        for t in range(ntile):
            m.act(win[:], delt[:], AF.Exp, scale=ntc[:, t:t + 1])
            m.matmul(m.psum[2][:, 0:512], h3[0:64, t * 128:(t + 1) * 128], fwo[0:64, 0, :])
            m.matmul(m.psum[3][:, 0:512], h3[0:64, t * 128:(t + 1) * 128], fwo[0:64, 1, :])
            m.tt(hf[:], m.psum[2][:, 0:512], win[:], ALU.mult)
            m.tt(hb[:], m.psum[3][:, 0:512], win[:], ALU.mult)
            if t == 0:
                m.memset(hb[0:1, :], 0.0)
            m.tt(hs[:, t, :], hf[:], hb[:], ALU.add, eng="pool")
            m.tt(hd[:, t, :], hf[:], hb[:], ALU.subtract, eng="pool")
        m.release(mD)
        nfb = max(1, nfc // 4)
        fcb = nfc // nfb
        Cb = [m.sb("hy_Cb%d" % i, [128, ntile, fcb * 128], BF16) for i in range(1)]
        Sb = [m.sb("hy_Sb%d" % i, [128, ntile, fcb * 128], BF16) for i in range(1)]
        Kc = m.sb("hy_Kc", [128, 512], F32)
        Ks = m.sb("hy_Ks", [128, 512], F32)
        dF = io["dftF_" + name]
        tt0 = tok0 // 128
        for fbk in range(nfb):
            C_, S_ = Cb[0], Sb[0]
            m.dma(C_[:], dF[fbk, 0].rearrange("p (k f) -> p k f", f=fcb * 128))
            m.dma(S_[:], dF[fbk, 1].rearrange("p (k f) -> p k f", f=fcb * 128))
            for fi in range(fcb):
                fc = fbk * fcb + fi
                fsl = slice(fi * 128, (fi + 1) * 128)
                for t in range(ntile):
                    m.matmul(m.psum[0][:, 0:512], C_[:, t, fsl], hs[:, t, :], start=(t == 0), stop=(t == ntile - 1))
                for t in range(ntile):
                    m.matmul(m.psum[1][:, 0:512], S_[:, t, fsl], hd[:, t, :], start=(t == 0), stop=(t == ntile - 1))
                m.copy(Kc[:], m.psum[0][:, 0:512], eng="act")
                m.copy(Ks[:], m.psum[1][:, 0:512], eng="act")
                pc, ps_ = m.psum[2 + 2 * (fc % 2)], m.psum[3 + 2 * (fc % 2)]
                for t in range(ntile):
                    m.matmul(pc[:, 0:512], C_[:, t, fsl], vxtok[:, tt0 + t, :], start=(t == 0), stop=(t == ntile - 1))
                for t in range(ntile):
                    m.matmul(ps_[:, 0:512], S_[:, t, fsl], vxtok[:, tt0 + t, :], start=(t == 0), stop=(t == ntile - 1))
                m.tt(p1[:], pc[:, 0:512], Kc[:], ALU.mult)
                m.tt(p2[:], ps_[:, 0:512], Ks[:], ALU.mult)
                m.tt(Yc[:, fc, :], p1[:], p2[:], ALU.subtract, eng="pool")
                m.tt(p1[:], pc[:, 0:512], Ks[:], ALU.mult)
                m.tt(p2[:], ps_[:, 0:512], Kc[:], ALU.mult)
                m.tt(Ys[:, fc, :], p1[:], p2[:], ALU.add, eng="pool")
        m.release(mC)
        dI = io["dftI_" + name]
        tbw = min(512, n)
        Ci = [m.sb("hy_Ci%d" % i, [128, nfc, tbw], BF16) for i in range(2)]
        Si = [m.sb("hy_Si%d" % i, [128, nfc, tbw], BF16) for i in range(2)]
        for tb in range(n // tbw):
            C_, S_ = Ci[tb % 2], Si[tb % 2]
            m.dma(C_[:], dI[tb, 0].rearrange("p (k t) -> p k t", t=tbw))
            m.dma(S_[:], dI[tb, 1].rearrange("p (k t) -> p k t", t=tbw))
            for cc in range(4):
                po = m.psum[cc % 2]
                csl = slice(cc * 128, (cc + 1) * 128)
                for fc in range(nfc):
                    m.matmul(po[:, 0:tbw], Yc[:, fc, csl], C_[:, fc, :], start=(fc == 0), stop=False)
                for fc in range(nfc):
                    m.matmul(po[:, 0:tbw], Ys[:, fc, csl], S_[:, fc, :], start=False, stop=(fc == nfc - 1))
                a0 = tok0 + tb * tbw
                ys = ystage[cc % 2]
                m.stt(p1[:, 0:tbw], VX[cc][:, a0:a0 + tbw], skip[:, cc:cc + 1], po[:, 0:tbw], ALU.mult, ALU.add)
                m.tt(ys[:, 0:tbw], p1[:, 0:tbw], X0[cc][:, a0:a0 + tbw], ALU.mult, eng="pool")
                m.dma(io["myT"][4 + cc, :, a0:a0 + tbw], ys[:, 0:tbw])
    m.release(mk)


def build_Modd_program(odd_idx, ctx_out):
    nc = bass.Bass("TRN2", target_bir_lowering=False)
    cx = Ctx(nc)
    m = cx.m
    dr = lambda n, s, d, k="ExternalInput": m.dram(n, s, d, k)
    cx.consts(dr("ident", [128, 128], F32))
    cx.eps_col = m.sb("eps", [128, 1], F32)
    m.memset(cx.eps_col[:], EPS)
    io = {"odd_idx": odd_idx,
          "hq": dr("hq", [512, TM], BF16), "hz": dr("hz", [2, 512, TM], F32), "hv": dr("hv", [TM, 512], BF16),
          "hg": dr("hg", [512, TM], BF16), "hu": dr("hu", [3, 512, TM], BF16),
          "lbl": dr("lbl", [128, 2, 2, 4], F32), "ng": dr("ng", [128, 4], F32),
          "hmasks": dr("hmasks", [128, 4, 128], F32),
          "cw": dr("cw", [128, 3, 3, 4], F32), "cb": dr("cb", [128, 3, 4], F32), "skip": dr("skip", [128, 4], F32),
          "fw1": dr("fw1", [33, 64], F32), "fw2": dr("fw2", [64, 64], F32), "fw3": dr("fw3", [64, 64], F32),
          "fb": dr("fb", [64, 4], F32), "fwo": dr("fwo", [64, 2, 512], F32), "delt": dr("delt", [128, 512], F32),
          "zT_l": dr("zT_l", [33, 2048], F32), "tcol_l": dr("tcol_l", [128, 16], F32),
          "dftF_l": dr("dftF_l", [2048, 2, 2048], BF16), "dftI_l": dr("dftI_l", [2048, 2, 2048], BF16),
          "myT": dr("myT", [8, 128, TM], BF16, "ExternalOutput")}
    if ctx_out:
        io.update({"zT_c": dr("zT_c", [33, 256], F32), "tcol_c": dr("tcol_c", [128, 2], F32),
                   "dftF_c": dr("dftF_c", [256, 2, 256], BF16), "dftI_c": dr("dftI_c", [256, 2, 256], BF16)})
    ystage = [m.sb("ystage%d" % i, [128, TM], BF16) for i in range(2)]
    emit_hgrn(cx, io, ctx_out, ystage)
    emit_hyena(cx, io, ctx_out, ystage)
    m.emit()
    return nc


_HY_CONST = {}


def hyena_consts(n):
    if n in _HY_CONST:
        return _HY_CONST[n]
    pos = np.arange(n, dtype=np.float32)
    t = (pos / np.float32(max(n - 1, 1))).astype(np.float32)
    bands = np.linspace(1e-4, 15, 16, dtype=np.float32)
    ang = (np.float32(2.0 * math.pi / n) * pos[:, None] * bands[None, :]).astype(np.float32)
    z = np.concatenate([t[:, None], np.cos(ang), -np.sin(ang)], -1).astype(np.float32)
    zT = np.ascontiguousarray(z.T)
    tcol = np.ascontiguousarray(t.reshape(-1, 128).T)
    f = np.arange(n, dtype=np.float64) + 0.5
    M = np.arange(n, dtype=np.float64)[:, None] * (2.0 * math.pi * f[None, :] / (2.0 * n))
    c, s = np.cos(M), np.sin(M)
    dftF = np.stack([c, s], 1).astype(NPBF)
    dftI = np.stack([c.T / n, s.T / n], 1).astype(NPBF)
    nt = n // 128
    nfb = max(1, nt // 4)
    fw = (nt // nfb) * 128
    tbw = min(512, n)
    dftF = np.ascontiguousarray(dftF.reshape(nt, 128, 2, nfb, fw).transpose(3, 2, 1, 0, 4)).reshape(nfb, 2, 128, nt * fw)
    dftI = np.ascontiguousarray(dftI.reshape(nt, 128, 2, n // tbw, tbw).transpose(3, 2, 1, 0, 4)).reshape(n // tbw, 2, 128, nt * tbw)
    _HY_CONST[n] = (zT, tcol, dftF, dftI)
    return _HY_CONST[n]


def hgrn_masks():
    i = np.arange(128)
    s, t = i[:, None], i[None, :]
    same32 = (s // 32) == (t // 32)
    same64 = (s // 64) == (t // 64)
    first = lambda x: (x % 64) < 32
    m1f = same32 & (s <= t)
    m2f = same64 & first(s) & ~first(t)
    m1b = same32 & (s >= t)
    m2b = same64 & ~first(s) & first(t)
    return np.ascontiguousarray(np.stack([m1f, m2f, m1b, m2b], 1)).astype(np.float32)


def hyena_deltas(r):
    mx = math.log(1e-2) / 0.3
    mn = math.log(1e-2) / 1.5
    d = np.abs(np.linspace(mn, mx, 1024, dtype=np.float32))
    return np.ascontiguousarray(np.broadcast_to(d[512 * r:512 * r + 512][None, :], (128, 512))).astype(np.float32)


def assemble_Modd(o0, o1, r, inp, o, ctx_out):
    cs = slice(512 * r, 512 * r + 512)
    hz = np.stack([canon_cols(o0["zT"][d * 1024 + 512 * r:d * 1024 + 512 * r + 512],
                              o1["zT"][d * 1024 + 512 * r:d * 1024 + 512 * r + 512]) for d in range(2)], 0)
    hu = np.stack([canon_cols(o0["uT"][g * 1024 + 512 * r:g * 1024 + 512 * r + 512],
                              o1["uT"][g * 1024 + 512 * r:g * 1024 + 512 * r + 512]) for g in range(3)], 0)
    lbl = np.stack([np.stack([fm_vec(inp["hgrn_lb_logits"][d, oo, cs]) for oo in range(2)], 1) for d in range(2)], 1)
    cwv = inp["hy_conv_w"][o]
    cw = np.stack([np.stack([fm_vec(cwv[tap, g * 1024 + 512 * r:g * 1024 + 512 * r + 512]) for g in range(3)], 1)
                   for tap in range(3)], 1)
    cbv = inp["hy_conv_b"][o]
    cb = np.stack([fm_vec(cbv[g * 1024 + 512 * r:g * 1024 + 512 * r + 512]) for g in range(3)], 1)
    wo = inp["hy_filt_wout"][o]
    zl, tl, fl, il = hyena_consts(2048)
    d = {"ident": np.eye(128, dtype=np.float32),
         "hq": canon_cols(o0["qT"][cs], o1["qT"][cs]), "hz": np.ascontiguousarray(hz),
         "hv": canon_rows(o0["vi"][:, cs], o1["vi"][:, cs]),
         "hg": canon_cols(o0["gT"][cs], o1["gT"][cs]), "hu": np.ascontiguousarray(hu),
         "lbl": np.ascontiguousarray(lbl).astype(np.float32), "ng": fm_vec(inp["hgrn_norm_g"][o][cs]),
         "hmasks": hgrn_masks(),
         "cw": np.ascontiguousarray(cw).astype(np.float32), "cb": np.ascontiguousarray(cb).astype(np.float32),
         "skip": fm_vec(inp["hy_skip"][o][cs]),
         "fw1": np.ascontiguousarray(inp["hy_filt_w1"][o]), "fw2": np.ascontiguousarray(inp["hy_filt_w2"][o]),
         "fw3": np.ascontiguousarray(inp["hy_filt_w3"][o]),
         "fb": np.ascontiguousarray(np.stack([inp["hy_filt_b1"][o], inp["hy_filt_b2"][o], inp["hy_filt_b3"][o],
                                              inp["hy_filt_freq"][o]], 1)).astype(np.float32),
         "fwo": np.ascontiguousarray(np.stack([wo[:, cs], wo[:, 1024 + 512 * r:1024 + 512 * r + 512]], 1)),
         "delt": hyena_deltas(r), "zT_l": zl, "tcol_l": tl, "dftF_l": fl, "dftI_l": il}
    if ctx_out:
        zc, tc, fc_, ic = hyena_consts(256)
        d.update({"zT_c": zc, "tcol_c": tc, "dftF_c": fc_, "dftI_c": ic})
    return d


def build_fused_program(depth=DEPTH):
    nc = bass.Bass("TRN2", target_bir_lowering=False)
    cx = Ctx(nc)
    m = cx.m
    ext = lambda n, s, d: m.dram(n, s, d, "ExternalInput")
    itn = lambda n, s, d: m.dram(n, s, d, "Internal")
    ident_d = ext("ident", [128, 128], F32)
    cx.consts(ident_d)
    cx.eps_col = m.sb("eps", [128, 1], F32)
    m.memset(cx.eps_col[:], EPS)
    h0_d = ext("h0", [2, 128, KC, TL], F32)
    cT_d = ext("cT", [128, KC, 2], F32)
    ada_w_d = ext("ada_w", [DEPTH, 24, 128, KC * 512], F32)
    ada_b_d = ext("ada_b", [DEPTH, 6 * D], F32)
    gmix_d = ext("gmix", [128, DEPTH, KC], F32)
    gmlp_d = ext("gmlp", [128, DEPTH, KC], F32)
    w_out_d = ext("w_out", [DEPTH, 8, 128, 4096], F32)
    w1_d = ext("mlp_w1", [DEPTH, 32, 128, 4096], F32)
    w2_d = ext("mlp_w2", [DEPTH, 32, 128, 4096], F32)
    evw_d = ext("ev_w_in", [2, EV_NEXT // 256, 128, 4096], F32)
    ukv_d = ext("w_ukv", [2, 2, 128, 4096], F32)
    odw_d = ext("od_w_in", [2, 32, 128, 4096], F32)
    cos_d = ext("cosT", [2, 64, TL], F32)
    sin_d = ext("sinT", [2, 64, TL], F32)
    kvg_d = ext("kvg", [2, 128, 4], F32)
    fg_d = ext("fg", [128, KC], F32)
    nab_d = ext("nabias", [2, 8, 128, 2880], F32)
    namask_d = ext("namask", [128, 2880], F32)
    lbl_d = ext("lbl", [2, 128, 2, 2, 4], F32)
    ng_d = ext("ng", [2, 2, 128, 4], F32)
    hmasks_d = ext("hmasks", [128, 4, 128], F32)
    cw_d = ext("cw", [2, 2, 128, 3, 3, 4], F32)
    cb_d = ext("cb", [2, 2, 128, 3, 4], F32)
    skip_d = ext("skip", [2, 2, 128, 4], F32)
    fw1_d = ext("fw1", [2, 33, 64], F32)
    fw2_d = ext("fw2", [2, 64, 64], F32)
    fw3_d = ext("fw3", [2, 64, 64], F32)
    fb_d = ext("fb", [2, 64, 4], F32)
    fwo_d = ext("fwo", [2, 2, 64, 2, 512], F32)
    delt_d = ext("delt", [2, 128, 512], F32)
    hyc = {"zT_l": ext("zT_l", [33, 2048], F32), "tcol_l": ext("tcol_l", [128, 16], F32),
           "dftF_l": ext("dftF_l", [4, 2, 128, 8192], BF16), "dftI_l": ext("dftI_l", [4, 2, 128, 8192], BF16),
           "zT_c": ext("zT_c", [33, 256], F32), "tcol_c": ext("tcol_c", [128, 2], F32),
           "dftF_c": ext("dftF_c", [1, 2, 128, 512], BF16), "dftI_c": ext("dftI_c", [1, 2, 128, 512], BF16)}
    outT_d = m.dram("outT", [2, 128, KC, 1024], F32, "ExternalOutput")
    hT_d = [itn("hT_%d" % v, [128, KC, TL], F32) for v in range(2)]
    RE = [{"qT": itn("re_qT%d" % v, [8, 192, TL], BF16), "kT": itn("re_kT%d" % v, [8, 128, TL], BF16),
           "kpeT": itn("re_kpeT%d" % v, [64, TL], BF16), "v": itn("re_v%d" % v, [TL, 1024], BF16),
           "qnT": itn("re_qnT%d" % v, [1024, TL], BF16), "knT": itn("re_knT%d" % v, [1024, TL], BF16),
           "vn": itn("re_vn%d" % v, [TL, 1024], BF16)} for v in range(2)]
    RO = [{"qT": itn("ro_qT%d" % v, [1024, TL], BF16), "vi": itn("ro_vi%d" % v, [TL, 1024], BF16),
           "zT": itn("ro_zT%d" % v, [2048, TL], F32), "gT": itn("ro_gT%d" % v, [1024, TL], BF16),
           "uT": itn("ro_uT%d" % v, [3072, TL], BF16)} for v in range(2)]
    ME = {"mq": itn("me_mq", [4, 192, TM], BF16), "mk": itn("me_mk", [4, 128, TM], BF16),
          "mkpe": itn("me_mkpe", [64, TM], BF16), "mv": itn("me_mv", [TM, 512], BF16),
          "mqn": itn("me_mqn", [512, TM], BF16), "mkn": itn("me_mkn", [512, TM], BF16),
          "mvn": itn("me_mvn", [TM, 512], BF16)}
    MO = {"hq": itn("mo_hq", [512, TM], BF16), "hz": itn("mo_hz", [2, 512, TM], F32),
          "hv": itn("mo_hv", [TM, 512], BF16), "hg": itn("mo_hg", [512, TM], BF16),
          "hu": itn("mo_hu", [3, 512, TM], BF16)}
    myT_d = [itn("myT_%d" % v, [8, 128, TM], BF16) for v in range(2)]

    def ccols(dst, s0, s1):
        if len(dst.shape) == 3:
            m.dma(dst[:, :, 0:128], s0[:, :, 1024:TL])
            m.dma(dst[:, :, 128:256], s1[:, :, 1024:TL])
            m.dma(dst[:, :, 256:1280], s0[:, :, 0:1024])
            m.dma(dst[:, :, 1280:TM], s1[:, :, 0:1024])
        else:
            m.dma(dst[:, 0:128], s0[:, 1024:TL])
            m.dma(dst[:, 128:256], s1[:, 1024:TL])
            m.dma(dst[:, 256:1280], s0[:, 0:1024])
            m.dma(dst[:, 1280:TM], s1[:, 0:1024])

    def crows(dst, s0, s1):
        m.dma(dst[0:128], s0[1024:TL])
        m.dma(dst[128:256], s1[1024:TL])
        m.dma(dst[256:1280], s0[0:1024])
        m.dma(dst[1280:TM], s1[0:1024])

    modall = m.sb("modall", [128, DEPTH, 128, 2], F32)
    emit_mod(cx, modall[:, :, 0:96, :], cT_d, ada_w_d, ada_b_d, 96, 2)
    emit_gsc(cx, modall, gmix_d, gmlp_d)
    base_mark = m.mark()
    for l in range(depth + 1):
        last = (l == depth)
        for v in range(2):
            m.release(base_mark)
            hT = m.sb("hT", [128, KC, TL], F32)
            m.dma(hT[:], h0_d[v] if l == 0 else hT_d[v])
            actT = m.sb("actT", [128, KC, TL], BF16)
            rstd = m.sb("rstd", [128, TL], F32)
            cx.init_wring(4)
            if l >= 1:
                def load_y(actT_, v=v):
                    for f in range(16):
                        own, loc = (f // 4, f % 4) if f < 8 else ((f - 8) // 4, 4 + (f - 8) % 4)
                        srcT = myT_d[own][loc]
                        m.dma(actT_[:, f, 0:1024], srcT[:, 256 + 1024 * v:256 + 1024 * (v + 1)])
                        m.dma(actT_[:, f, 1024:TL], srcT[:, 128 * v:128 * (v + 1)])
                io = {"load_y": load_y, "w_out": w_out_d[l - 1], "w1": w1_d[l - 1], "w2": w2_d[l - 1]}
                emit_post(cx, l - 1, hT, actT, rstd, modall, io, TL if not last or depth < DEPTH else 1024)
            if not last and l % 2 == 0:
                e = l // 2
                io = dict(RE[v])
                io.update({"w_in": evw_d[e], "w_ukv": ukv_d[e], "cosT": cos_d[v], "sinT": sin_d[v], "kvg": kvg_d[e]})
                emit_pre_even(cx, l, hT, actT, rstd, modall, io)
            elif not last:
                io = dict(RO[v])
                io.update({"w_in": odw_d[l // 2]})
                emit_pre_odd(cx, l, hT, actT, rstd, modall, io)
            if not last:
                m.dma(hT_d[v], hT[:])
            else:
                emit_final(cx, hT, rstd, {"fg": fg_d, "outT": outT_d[v]})
        if last:
            break
        ctx_out = l < DEPTH - 1
        for v in range(2):
            m.release(base_mark)
            hs = slice(4 * v, 4 * v + 4)
            cs = slice(512 * v, 512 * v + 512)
            if l % 2 == 0:
                e = l // 2
                ccols(ME["mq"], RE[0]["qT"][hs], RE[1]["qT"][hs])
                ccols(ME["mk"], RE[0]["kT"][hs], RE[1]["kT"][hs])
                ccols(ME["mkpe"], RE[0]["kpeT"], RE[1]["kpeT"])
                crows(ME["mv"], RE[0]["v"][:, cs], RE[1]["v"][:, cs])
                ccols(ME["mqn"], RE[0]["qnT"][cs], RE[1]["qnT"][cs])
                ccols(ME["mkn"], RE[0]["knT"][cs], RE[1]["knT"][cs])
                crows(ME["mvn"], RE[0]["vn"][:, cs], RE[1]["vn"][:, cs])
                io = dict(ME)
                io.update({"nabias": nab_d[e, hs], "namask": namask_d, "myT": myT_d[v]})
                emit_M_even(cx, io, ctx_out)
            else:
                o = l // 2
                ccols(MO["hq"], RO[0]["qT"][cs], RO[1]["qT"][cs])
                for d in range(2):
                    zs = slice(d * 1024 + 512 * v, d * 1024 + 512 * v + 512)
                    ccols(MO["hz"][d], RO[0]["zT"][zs], RO[1]["zT"][zs])
                crows(MO["hv"], RO[0]["vi"][:, cs], RO[1]["vi"][:, cs])
                ccols(MO["hg"], RO[0]["gT"][cs], RO[1]["gT"][cs])
                for g in range(3):
                    us = slice(g * 1024 + 512 * v, g * 1024 + 512 * v + 512)
                    ccols(MO["hu"][g], RO[0]["uT"][us], RO[1]["uT"][us])
                io = dict(MO)
                io.update(hyc)
                io.update({"odd_idx": o, "lbl": lbl_d[v], "ng": ng_d[o, v], "hmasks": hmasks_d,
                           "cw": cw_d[o, v], "cb": cb_d[o, v], "skip": skip_d[o, v],
                           "fw1": fw1_d[o], "fw2": fw2_d[o], "fw3": fw3_d[o], "fb": fb_d[o],
                           "fwo": fwo_d[o, v], "delt": delt_d[v], "myT": myT_d[v]})
                ystage = [m.sb("ystage%d" % i, [128, TM], BF16) for i in range(2)]
                emit_hgrn(cx, io, ctx_out, ystage)
                emit_hyena(cx, io, ctx_out, ystage)
    m.emit()
    return nc


_FUSED = {}


def host_inputs(inp, b):
    x, ctx, c, c_ctx = inp["x"], inp["ctx"], inp["c"], inp["c_ctx"]
    h0 = np.stack([fm_tokens(np.concatenate([x[b, r * 1024:(r + 1) * 1024], ctx[b, r * 128:(r + 1) * 128]], 0))
                   for r in range(2)], 0)
    d = {"h0": np.ascontiguousarray(h0), "cT": np.ascontiguousarray(np.stack([fm_vec(c[b]), fm_vec(c_ctx)], -1))}
    return d


def slotify(W, nkc, k_blocks=1):
    K, N = W.shape
    ncb = 4096 // nkc
    assert K == k_blocks * nkc * 128 and N % ncb == 0
    a = W.reshape(k_blocks, nkc, 128, N // ncb, ncb).transpose(0, 3, 2, 1, 4)
    return np.ascontiguousarray(a).reshape(k_blocks * (N // ncb), 128, nkc * ncb)


def host_shared(inp):
    adaw = inp["ada_w"].reshape(DEPTH, KC, 128, 24, 512).transpose(0, 3, 2, 1, 4)
    sh = {"ident": np.eye(128, dtype=np.float32),
          "ada_w": np.ascontiguousarray(adaw).reshape(DEPTH, 24, 128, KC * 512), "ada_b": inp["ada_b"],
          "gmix": np.ascontiguousarray(np.stack([fm_vec(inp["norm_mix_g"][l]) for l in range(DEPTH)], 1)),
          "gmlp": np.ascontiguousarray(np.stack([fm_vec(inp["norm_mlp_g"][l]) for l in range(DEPTH)], 1)),
          "w_out": np.stack([slotify(inp["w_out"][l], KC) for l in range(DEPTH)], 0),
          "mlp_w1": np.stack([slotify(inp["mlp_w1"][l], KC) for l in range(DEPTH)], 0),
          "mlp_w2": np.stack([slotify(inp["mlp_w2"][l], 4, 16) for l in range(DEPTH)], 0),
          "ev_w_in": np.stack([slotify(ev_w_in_ext(inp["ev_w_in"][e]), KC) for e in range(2)], 0),
          "w_ukv": np.stack([slotify(ukv_ext(inp["mla_w_ukv"][e]), 4) for e in range(2)], 0),
          "od_w_in": np.stack([slotify(inp["od_w_in"][o], KC) for o in range(2)], 0),
          "kvg": np.stack([fm_vec(inp["mla_kv_norm_g"][e]) for e in range(2)], 0),
          "fg": fm_vec(inp["final_norm_g"])}
    rt = [rope_tables(r) for r in range(2)]
    sh["cosT"] = np.stack([rt[0][0], rt[1][0]], 0)
    sh["sinT"] = np.stack([rt[0][1], rt[1][1]], 0)
    nabs = [na_bias_host(inp["na_rel_bias"][e]) for e in range(2)]
    sh["nabias"] = np.ascontiguousarray(np.stack([nabs[0][0], nabs[1][0]], 0))
    sh["namask"] = nabs[0][1]
    f32 = lambda a: np.ascontiguousarray(a).astype(np.float32)
    lg = inp["hgrn_lb_logits"]
    sh["lbl"] = f32(np.stack([np.stack([np.stack([fm_vec(lg[d, oo, 512 * v:512 * v + 512]) for oo in range(2)], 1)
                                        for d in range(2)], 1) for v in range(2)], 0))
    sh["ng"] = f32(np.stack([np.stack([fm_vec(inp["hgrn_norm_g"][o][512 * v:512 * v + 512]) for v in range(2)], 0)
                             for o in range(2)], 0))
    sh["hmasks"] = hgrn_masks()
    cwl, cbl, skl, fwol = [], [], [], []
    for o in range(2):
        cwv, cbv, wo = inp["hy_conv_w"][o], inp["hy_conv_b"][o], inp["hy_filt_wout"][o]
        cwl.append(np.stack([np.stack([np.stack([fm_vec(cwv[tap, g * 1024 + 512 * v:g * 1024 + 512 * v + 512])
                                                 for g in range(3)], 1) for tap in range(3)], 1) for v in range(2)], 0))
        cbl.append(np.stack([np.stack([fm_vec(cbv[g * 1024 + 512 * v:g * 1024 + 512 * v + 512]) for g in range(3)], 1)
                             for v in range(2)], 0))
        skl.append(np.stack([fm_vec(inp["hy_skip"][o][512 * v:512 * v + 512]) for v in range(2)], 0))
        fwol.append(np.stack([np.stack([wo[:, 512 * v:512 * v + 512], wo[:, 1024 + 512 * v:1024 + 512 * v + 512]], 1)
                              for v in range(2)], 0))
    sh["cw"], sh["cb"], sh["skip"], sh["fwo"] = f32(np.stack(cwl, 0)), f32(np.stack(cbl, 0)), f32(np.stack(skl, 0)), f32(np.stack(fwol, 0))
    sh["fw1"], sh["fw2"], sh["fw3"] = f32(inp["hy_filt_w1"]), f32(inp["hy_filt_w2"]), f32(inp["hy_filt_w3"])
    sh["fb"] = f32(np.stack([inp["hy_filt_b1"], inp["hy_filt_b2"], inp["hy_filt_b3"], inp["hy_filt_freq"]], -1))
    sh["delt"] = np.stack([hyena_deltas(v) for v in range(2)], 0)
    zl, tl, fl, il = hyena_consts(2048)
    zc, tc, fc_, ic = hyena_consts(256)
    sh.update({"zT_l": zl, "tcol_l": tl, "dftF_l": fl, "dftI_l": il, "zT_c": zc, "tcol_c": tc, "dftF_c": fc_, "dftI_c": ic})
    return sh


def kernel(**inp):
    inp = {k: np.asarray(v) for k, v in inp.items()}
    if "nc" not in _FUSED:
        _FUSED["nc"] = build_fused_program()
    sh = host_shared(inp)
    maps = []
    for b in range(4):
        d = dict(sh)
        d.update(host_inputs(inp, b))
        maps.append(d)
    res = run_bass_kernel_spmd(_FUSED["nc"], maps, core_ids=[0, 1, 2, 3])
    out = np.zeros((4, SEQ, D), dtype=np.float32)
    for b in range(4):
        oT = np.asarray(res.results[b]["outT"])
        for r in range(2):
            out[b, r * 1024:(r + 1) * 1024] = oT[r].transpose(2, 1, 0).reshape(1024, D)
    return out
```

```python
import math
import numpy as np
import ml_dtypes
import concourse.bass as bass
import concourse.mybir as mybir
from concourse.bass_utils import run_bass_kernel_spmd

F32 = mybir.dt.float32
BF16 = mybir.dt.bfloat16
AF = mybir.ActivationFunctionType
ALU = mybir.AluOpType
AX = mybir.AxisListType
NPBF = ml_dtypes.bfloat16

D = 2048
SEQ = 2048
CTX = 256
DEPTH = 4
TL = 1152
TM = 2304
KC = 16
HID = 8192
EPS = 1e-6
GRID_W = 64

ENGS = ("pe", "act", "dve", "pool", "sp")
DMA_RING = 12
SB_BASE = 16640
SB_END = 229376


def _dsize(dt):
    return 4 if dt == F32 else 2


class Op:
    __slots__ = ("eng", "fn", "waits", "idx", "is_dma", "dma_sem", "dma_val", "needs_inc")


class MK:
    def __init__(self, nc):
        self.nc = nc
        self.ops = {e: [] for e in ENGS}
        self.acc = {}
        self.waited = {e: {} for e in ENGS}
        self.dma_count = {e: 0 for e in ENGS}
        self.dma_ring_uses = {e: [0] * DMA_RING for e in ENGS}
        self.sb_base = {}
        self.sb_top = SB_BASE
        self.psum = [nc.alloc_psum_tensor("bank%d" % i, [128, 512], F32) for i in range(8)]
        self.n_dram = 0

    def sb(self, name, shape, dtype, at=None):
        per = _dsize(dtype)
        for s in shape[1:]:
            per *= int(s)
        if at is None:
            at = self.sb_top
            self.sb_top = (at + per + 31) // 32 * 32
        assert at % 32 == 0 and at + per <= SB_END, (name, at, per)
        t = self.nc.alloc_sbuf_tensor_at(name, list(shape), dtype, offset=at)
        self.sb_base[t.name] = (at, per)
        return t

    def mark(self):
        return self.sb_top

    def release(self, mark):
        self.sb_top = mark

    def dram(self, name, shape, dtype, kind="Internal"):
        return self.nc.dram_tensor(name, list(shape), dtype, kind=kind).ap()

    def _box(self, ap):
        t = ap.tensor
        space = str(ap.space)
        off = int(ap.offset)
        pairs = [(int(s), int(c)) for (s, c) in ap.ap]
        ds = _dsize(ap.dtype)
        if space == "PSUM":
            return ("P:" + t.name, 0, 128, 0, 1)
        if space == "SB":
            base, per = self.sb_base[t.name]
            P = per // ds
            plo = off // P
            flo = off % P
            pext = 0
            fext = 0
            for (s, c) in pairs:
                if c <= 1:
                    continue
                if s != 0 and s % P == 0:
                    pext += (c - 1) * (s // P)
                else:
                    fext += (c - 1) * abs(s)
            return ("SB", plo, plo + pext + 1, base + flo * ds, base + (flo + fext + 1) * ds)
        ext = 0
        for (s, c) in pairs:
            if c <= 1:
                continue
            ext += (c - 1) * abs(s)
        return ("D:" + t.name, 0, 1, off, off + ext + 1)

    @staticmethod
    def _overlap(a, b):
        return a[1] < b[2] and b[1] < a[2] and a[3] < b[4] and b[3] < a[4]

    @staticmethod
    def _contains(a, b):
        return a[1] <= b[1] and b[2] <= a[2] and a[3] <= b[3] and b[4] <= a[4]

    def _need(self, op, tok):
        e = op.eng
        if tok[0] == "e":
            _, x, idx = tok
            if x == e and e == "pe":
                return
            key = ("e", x)
            val = idx
        else:
            _, q, slot, useno = tok
            key = ("d", q, slot)
            val = useno
        w = self.waited[e]
        if w.get(key, -1) >= val:
            return
        w[key] = val
        op.waits.append((key, val))

    BUCKET = 2048

    def _split(self, b):
        if b[0] != "SB":
            return [b]
        out = []
        lo, hi = b[3], b[4]
        k = lo // self.BUCKET
        while k * self.BUCKET < hi:
            out.append((("SB", k), b[1], b[2], max(lo, k * self.BUCKET), min(hi, (k + 1) * self.BUCKET)))
            k += 1
        return out

    def _track(self, op, reads, writes, tok):
        rb = []
        wb = []
        for a in reads:
            b = self._box(a)
            if b[0].startswith("P:"):
                wb.append(b)
            else:
                rb.extend(self._split(b))
        for a in writes:
            wb.extend(self._split(self._box(a)))
        for b in rb:
            lst = self.acc.get(b[0])
            if lst:
                for rec in lst:
                    if rec[1] == "w" and self._overlap(rec[0], b):
                        self._need(op, rec[2])
        for b in wb:
            lst = self.acc.get(b[0])
            if lst:
                for rec in lst:
                    if self._overlap(rec[0], b):
                        self._need(op, rec[2])
        for b in rb:
            lst = self.acc.setdefault(b[0], [])
            if tok[0] == "e":
                lst[:] = [r for r in lst if not (r[1] == "r" and r[2][0] == "e" and r[2][1] == tok[1]
                                                 and self._contains(b, r[0]))]
            lst.append([b, "r", tok])
        for b in wb:
            lst = self.acc.setdefault(b[0], [])
            lst[:] = [r for r in lst if not self._contains(b, r[0])]
            lst.append([b, "w", tok])

    def op(self, eng, fn, reads, writes):
        o = Op()
        o.eng = eng
        o.fn = fn
        o.waits = []
        o.is_dma = False
        o.needs_inc = False
        o.idx = len(self.ops[eng])
        self._track(o, reads, writes, ("e", eng, o.idx))
        self.ops[eng].append(o)
        return o

    def dma(self, out, in_, q="sp", **kw):
        o = Op()
        o.eng = q
        o.waits = []
        o.is_dma = True
        o.needs_inc = False
        o.idx = len(self.ops[q])
        n = self.dma_count[q]
        self.dma_count[q] = n + 1
        slot = n % DMA_RING
        prev = self.dma_ring_uses[q][slot]
        if prev > 0:
            self._need(o, ("d", q, slot, prev))
        self.dma_ring_uses[q][slot] = prev + 1
        o.dma_sem = (q, slot)
        o.dma_val = prev + 1
        o.fn = lambda eng, out=out, in_=in_, kw=kw: eng.dma_start(out=out, in_=in_, **kw)
        self._track(o, [in_], [out], ("d", q, slot, prev + 1))
        self.ops[q].append(o)
        return o

    def matmul(self, out, lhsT, rhs, start=True, stop=True, **kw):
        return self.op("pe", lambda e: e.matmul(out, lhsT, rhs, start=start, stop=stop, **kw),
                       [lhsT, rhs], [out])

    def transpose(self, out, in_, ident):
        return self.op("pe", lambda e: e.transpose(out, in_, ident), [in_, ident], [out])

    def act(self, out, in_, func, bias=None, scale=None, accum_out=None):
        reads = [in_]
        kw = {}
        if bias is not None:
            kw["bias"] = bias
            if not isinstance(bias, (int, float)):
                reads.append(bias)
        if scale is not None:
            kw["scale"] = scale
            if not isinstance(scale, (int, float)):
                reads.append(scale)
        writes = [out]
        if accum_out is not None:
            kw["accum_out"] = accum_out
            writes.append(accum_out)
        return self.op("act", lambda e: e.activation(out, in_, func, **kw), reads, writes)

    def tt(self, out, in0, in1, op, eng="dve"):
        return self.op(eng, lambda e: e.tensor_tensor(out, in0, in1, op), [in0, in1], [out])

    def ts(self, out, in0, s1, s2, op0, op1=None, eng="dve"):
        reads = [in0]
        if not isinstance(s1, (int, float)):
            reads.append(s1)
        if s2 is not None and not isinstance(s2, (int, float)):
            reads.append(s2)
        if op1 is None:
            return self.op(eng, lambda e: e.tensor_scalar(out, in0, s1, None, op0), reads, [out])
        return self.op(eng, lambda e: e.tensor_scalar(out, in0, s1, s2, op0, op1), reads, [out])

    def stt(self, out, in0, scalar, in1, op0, op1):
        reads = [in0, in1]
        if not isinstance(scalar, (int, float)):
            reads.append(scalar)
        return self.op("dve", lambda e: e.scalar_tensor_tensor(out, in0, scalar, in1, op0, op1),
                       reads, [out])

    def copy(self, out, in_, eng="dve"):
        if eng == "act":
            return self.op("act", lambda e: e.copy(out, in_), [in_], [out])
        return self.op(eng, lambda e: e.tensor_copy(out, in_), [in_], [out])

    def memset(self, ap, val, eng="dve"):
        return self.op(eng, lambda e: e.memset(ap, val), [], [ap])

    def recip(self, out, in_):
        return self.op("dve", lambda e: e.reciprocal(out, in_), [in_], [out])

    def scan(self, out, d0, d1, init, op0, op1):
        reads = [d0, d1]
        if not isinstance(init, (int, float)):
            reads.append(init)
        return self.op("dve", lambda e: e.tensor_tensor_scan(out, d0, d1, init, op0, op1), reads, [out])

    def emit(self):
        nc = self.nc
        for e in ENGS:
            for o in self.ops[e]:
                for (key, val) in o.waits:
                    if key[0] == "e":
                        self.ops[key[1]][val].needs_inc = True
        cnt = {}
        for e in ENGS:
            c = 0
            arr = []
            for o in self.ops[e]:
                if o.needs_inc and not o.is_dma:
                    c += 1
                arr.append(c)
            cnt[e] = arr
        from contextlib import ExitStack
        with ExitStack() as st:
            esem = {e: st.enter_context(nc.semaphore("s_" + e)) for e in ENGS}
            dsem = {}
            for q in ENGS:
                for s in range(min(DMA_RING, self.dma_count[q])):
                    dsem[(q, s)] = st.enter_context(nc.semaphore("d_%s_%d" % (q, s)))
            block = st.enter_context(nc.Block())
            engmap = {"pe": "tensor", "act": "scalar", "dve": "vector", "pool": "gpsimd", "sp": "sync"}

            def make(e):
                def body(eng):
                    for o in self.ops[e]:
                        ws = [((esem[key[1]], cnt[key[1]][val]) if key[0] == "e" else
                               (dsem[(key[1], key[2])], 16 * val)) for (key, val) in o.waits]
                        attach = None
                        if ws and not o.is_dma:
                            attach = ws.pop()
                        for (s_, v_) in ws:
                            eng.wait_ge(s_, v_)
                        ins = o.fn(eng)
                        if attach is not None:
                            ins._wait_ge(attach[0], attach[1])
                        if o.is_dma:
                            ins.then_inc(dsem[o.dma_sem], 16)
                        elif o.needs_inc:
                            ins.then_inc(esem[e], 1)
                    for s in range(min(DMA_RING, self.dma_count[e])):
                        eng.wait_ge(dsem[(e, s)], 16 * self.dma_ring_uses[e][s])
                return body

            for e in ENGS:
                if len(self.ops[e]) == 0:
                    continue
                getattr(block, engmap[e])(make(e))
        return nc


MOD_SH_A, MOD_SC_A, MOD_G_A, MOD_SH_M, MOD_SC_M, MOD_G_M, MOD_GSC_A, MOD_GSC_M = [16 * i for i in range(8)]


class Ctx:
    def __init__(self, nc):
        self.nc = nc
        self.m = MK(nc)
        self.wring = None
        self.wi = 0
        self.ev = 0

    def consts(self, ident_d, need_bf=True):
        m = self.m
        self.ident = m.sb("ident", [128, 128], F32)
        m.dma(self.ident[:], ident_d)
        self.ones_bf = m.sb("ones_bf", [128, 128], BF16)
        m.memset(self.ones_bf[:], 1.0)
        self.ident_bf = m.sb("ident_bf", [128, 128], BF16)
        m.copy(self.ident_bf[:], self.ident[:])

    def init_wring(self, nslot=5):
        m = self.m
        self.wring = [m.sb("wslot%d" % i, [128, 4096], BF16) for i in range(nslot)]
        self.wi = 0

    def wload(self, Ws, si, kcb, ncb, nuse=None):
        assert kcb * ncb <= 4096
        slot = self.wring[self.wi % len(self.wring)]
        self.wi += 1
        view = slot[:, 0:kcb * ncb].rearrange("p (k n) -> p k n", n=ncb)
        src = Ws[si].rearrange("p (k n) -> p k n", n=ncb)
        if nuse is not None and nuse < ncb:
            self.m.dma(view[:, :, 0:nuse], src[:, :, 0:nuse], q="pool")
        else:
            self.m.dma(view, src, q="pool")
        return view

    def evac_eng(self):
        self.ev += 1
        return "act" if self.ev % 2 == 0 else "dve"


def rstd_compute(cx, src, nch, T, nfeat, rstd, sq, ps_bank, tmp):
    m = cx.m
    ps = cx.m.psum[ps_bank]
    for t0 in range(0, T, 384):
        tw = min(384, T - t0)
        for k in range(nch):
            m.act(sq[:, k, 0:tw], src[:, k, t0:t0 + tw], AF.Square)
        for k in range(nch):
            m.matmul(ps[:, 0:tw], cx.ones_bf[:, 0:128], sq[:, k, 0:tw], start=(k == 0), stop=(k == nch - 1))
        m.act(tmp[:, 0:tw], ps[:, 0:tw], AF.Sqrt, bias=cx.eps_col[:, 0:1], scale=1.0 / nfeat)
        m.recip(rstd[:, t0:t0 + tw], tmp[:, 0:tw])


def norm_modulate(cx, hT, actT, rstd, modall, l, gsc_base, sh_base, tmp4):
    m = cx.m
    for k in range(KC):
        for (c0, c1, col) in ((0, 1024, 0), (1024, TL, 1)):
            gs = modall[:, l, gsc_base + k, col:col + 1]
            sh = modall[:, l, sh_base + k, col:col + 1]
            m.stt(tmp4[:, c0:c1], hT[:, k, c0:c1], gs, rstd[:, c0:c1], ALU.mult, ALU.mult)
            m.act(actT[:, k, c0:c1], tmp4[:, c0:c1], AF.Identity, bias=sh, scale=1.0)


def proj_fm(cx, W2d, ncols_total, col0, chunk_cols, inT, nkc, consume, T=TL, bank_sets=((0, 1, 2), (3, 4, 5))):
    m = cx.m
    ncb = 4096 // nkc
    nchunks = ncols_total // chunk_cols
    per_slot = ncb // chunk_cols
    ci = 0
    tbs = [(t0, min(384, T - t0)) for t0 in range(0, T, 384)]
    assert col0 % ncb == 0
    for s0 in range(0, nchunks, per_slot):
        nhere = min(per_slot, nchunks - s0)
        slot = cx.wload(W2d, (col0 + s0 * chunk_cols) // ncb, nkc, ncb, nhere * chunk_cols)
        for j in range(nhere):
            banks = bank_sets[ci % len(bank_sets)]
            for k in range(nkc):
                for ti, (t0, tw) in enumerate(tbs):
                    m.matmul(m.psum[banks[ti]][0:chunk_cols, 0:tw],
                             slot[:, k, j * chunk_cols:(j + 1) * chunk_cols],
                             inT[:, k, t0:t0 + tw], start=(k == 0), stop=(k == nkc - 1))
            for ti, (t0, tw) in enumerate(tbs):
                consume(ci, ti, m.psum[banks[ti]][0:chunk_cols, 0:tw], t0, tw)
            ci += 1


def emit_mod(cx, out, cT_d, ada_w_d, ada_b_d, nch, nr):
    m = cx.m
    mk = m.mark()
    sc = m.sb("mod_sc", [128, KC, nr], F32)
    m.dma(sc[:], cT_d)
    m.act(sc[:], sc[:], AF.Silu)
    adabT = m.sb("mod_adabT", [128, DEPTH, nch], F32)
    btmp = m.sb("mod_btmp", [nch, 128], F32)
    for l in range(DEPTH):
        m.dma(btmp[:], ada_b_d[l].rearrange("(c p) -> c p", p=128))
        m.transpose(m.psum[6][:, 0:nch], btmp[:], cx.ident[0:nch, 0:nch])
        m.copy(adabT[:, l, :], m.psum[6][:, 0:nch])
    wsl = [m.sb("mod_w%d" % i, [128, KC, 512], F32) for i in range(3)]
    wi = 0
    for l in range(DEPTH):
        ps = m.psum[l % 2]
        for nb in range(nch // 4):
            slot = wsl[wi % 3]
            wi += 1
            m.dma(slot[:], ada_w_d[l, nb].rearrange("p (k n) -> p k n", n=512), q="sp")
            for jj in range(4):
                j = nb * 4 + jj
                for k in range(KC):
                    m.matmul(ps[:, nr * j:nr * j + nr], slot[:, k, jj * 128:(jj + 1) * 128], sc[:, k, :],
                             start=(k == 0), stop=(k == KC - 1))
        m.tt(out[:, l, :, :], ps[:, 0:nch * nr].rearrange("p (c t) -> p c t", t=nr),
             adabT[:, l, :].unsqueeze(2).to_broadcast([128, nch, nr]), ALU.add)
    m.release(mk)


def emit_gsc(cx, modall, gmix_d, gmlp_d):
    m = cx.m
    mk = m.mark()
    gmix = m.sb("mod_gmix", [128, DEPTH, KC], F32)
    gmlp = m.sb("mod_gmlp", [128, DEPTH, KC], F32)
    m.dma(gmix[:], gmix_d)
    m.dma(gmlp[:], gmlp_d)
    for l in range(DEPTH):
        m.stt(modall[:, l, MOD_GSC_A:MOD_GSC_A + 16, :], modall[:, l, MOD_SC_A:MOD_SC_A + 16, :], 1.0,
              gmix[:, l, :].unsqueeze(2).to_broadcast([128, KC, 2]), ALU.add, ALU.mult)
        m.stt(modall[:, l, MOD_GSC_M:MOD_GSC_M + 16, :], modall[:, l, MOD_SC_M:MOD_SC_M + 16, :], 1.0,
              gmlp[:, l, :].unsqueeze(2).to_broadcast([128, KC, 2]), ALU.add, ALU.mult)
    m.release(mk)


def build_mod_program():
    nc = bass.Bass("TRN2", target_bir_lowering=False)
    cx = Ctx(nc)
    m = cx.m
    ident_d = m.dram("ident", [128, 128], F32, "ExternalInput")
    cT_d = m.dram("cT", [128, KC, 5], F32, "ExternalInput")
    ada_w_d = m.dram("ada_w", [DEPTH, D, 1536], F32, "ExternalInput")
    ada_b_d = m.dram("ada_b", [DEPTH, 1536], F32, "ExternalInput")
    out_d = m.dram("modpart", [128, DEPTH, 12, 5], F32, "ExternalOutput")
    cx.consts(ident_d)
    outt = m.sb("modpart", [128, DEPTH, 12, 5], F32)
    emit_mod(cx, outt, cT_d, ada_w_d, ada_b_d, 12, 5)
    m.dma(out_d, outt[:])
    m.emit()
    return nc


EV_QNOPE, EV_CKV, EV_QNA, EV_KNA, EV_PE, EV_VNA, EV_NEXT = 0, 1024, 1536, 2560, 3584, 4864, 5888
SEGS = ((0, 1024, 0), (1024, TL, 1))


def proj_tm(cx, W2d, k0, nkc, col0, ncols, inT, consume, T, banks=(6, 7)):
    m = cx.m
    ncb = 4096 // nkc
    bi = 0
    assert col0 % ncb == 0 and k0 == 0
    for c0 in range(0, ncols, ncb):
        cw = min(ncb, ncols - c0)
        slot = cx.wload(W2d, (col0 + c0) // ncb, nkc, ncb, cw)
        for t in range(T // 128):
            for n0 in range(0, cw, 512):
                nw = min(512, cw - n0)
                ps = m.psum[banks[bi % 2]]
                bi += 1
                for k in range(nkc):
                    m.matmul(ps[:, 0:nw], inT[:, k, t * 128:(t + 1) * 128], slot[:, k, n0:n0 + nw],
                             start=(k == 0), stop=(k == nkc - 1))
                consume(t, c0 + n0, nw, ps[:, 0:nw])


def rmw_consumer(cx, hT, modall, lidx, gate_base):
    m = cx.m

    def consume(ci, ti, ps, t0, tw):
        for (c0, c1, col) in SEGS:
            a, b = max(c0, t0), min(c1, t0 + tw)
            if a >= b:
                continue
            m.stt(hT[:, ci, a:b], ps[:, a - t0:b - t0], modall[:, lidx, gate_base + ci, col:col + 1],
                  hT[:, ci, a:b], ALU.mult, ALU.add)
    return consume


def emit_post(cx, lp, hT, actT, rstd, modall, io, T):
    m = cx.m
    if "load_y" in io:
        io["load_y"](actT)
    else:
        m.dma(actT[:], io["yT"])
    proj_fm(cx, io["w_out"], D, 0, 128, actT, KC, rmw_consumer(cx, hT, modall, lp, MOD_G_A), T=T)
    mk = m.mark()
    sq = m.sb("sq", [128, KC, 384], BF16)
    tmp4 = m.sb("tmp4", [128, TL], F32)
    tmp = m.sb("tmp", [128, 384], F32)
    rstd_compute(cx, hT, KC, T, D, rstd, sq, 6, tmp)
    norm_modulate(cx, hT, actT, rstd, modall, lp, MOD_GSC_M, MOD_SH_M, tmp4)
    m.release(mk)
    mk = m.mark()
    hid = [m.sb("hid%d" % i, [128, 4, TL], BF16) for i in range(2)]
    rtmp = [m.sb("rtmp%d" % i, [128, 384], F32) for i in range(2)]
    rmw = rmw_consumer(cx, hT, modall, lp, MOD_G_M)
    cnt = [0]
    tbs = [(t0, min(384, T - t0)) for t0 in range(0, T, 384)]
    for hb in range(HID // 512):
        hbuf = hid[hb % 2]

        def relu2(ci, ti, ps, t0, tw, hbuf=hbuf):
            r = rtmp[cnt[0] % 2]
            cnt[0] += 1
            m.act(r[:, 0:tw], ps, AF.Relu)
            m.tt(hbuf[:, ci, t0:t0 + tw], r[:, 0:tw], r[:, 0:tw], ALU.mult, eng="dve")
        proj_fm(cx, io["w1"], 512, hb * 512, 128, actT, KC, relu2, T=T)
        for half in range(2):
            slot = cx.wload(io["w2"], hb * 2 + half, 4, 1024)
            for j in range(8):
                oc = half * 8 + j
                banks = ((0, 1, 2), (3, 4, 5))[oc % 2]
                for k in range(4):
                    for ti, (t0, tw) in enumerate(tbs):
                        m.matmul(m.psum[banks[ti]][:, 0:tw], slot[:, k, j * 128:(j + 1) * 128],
                                 hbuf[:, k, t0:t0 + tw], start=(k == 0), stop=(k == 3))
                for ti, (t0, tw) in enumerate(tbs):
                    rmw(oc, ti, m.psum[banks[ti]][:, 0:tw], t0, tw)
    m.release(mk)


def stager(cx, stage, dst_of_chunk, kind="copy"):
    m = cx.m

    def consume(ci, ti, ps, t0, tw):
        st = stage[ci % len(stage)]
        rows = ps.shape[0]
        if kind == "silu":
            m.act(st[0:rows, t0:t0 + tw], ps, AF.Silu)
        else:
            if cx.evac_eng() == "act":
                m.act(st[0:rows, t0:t0 + tw], ps, AF.Identity)
            else:
                m.copy(st[0:rows, t0:t0 + tw], ps)
        if t0 + tw >= TL:
            m.dma(dst_of_chunk(ci), st[0:rows, :])
    return consume


def emit_pre_even(cx, l, hT, actT, rstd, modall, io):
    m = cx.m
    e = l // 2
    mk = m.mark()
    sq = m.sb("sq", [128, KC, 384], BF16)
    tmp4 = m.sb("tmp4", [128, TL], F32)
    tmp = m.sb("tmp", [128, 384], F32)
    rstd_compute(cx, hT, KC, TL, D, rstd, sq, 6, tmp)
    norm_modulate(cx, hT, actT, rstd, modall, l, MOD_GSC_A, MOD_SH_A, tmp4)
    m.release(mk)
    mk = m.mark()
    stage = [m.sb("stg%d" % i, [128, TL], BF16) for i in range(4)]
    ckvT = m.sb("ckvT", [128, 4, TL], F32)
    kvnT = m.sb("kvnT", [128, 4, TL], BF16)
    cosT = m.sb("cosT", [64, TL], F32)
    sinT = m.sb("sinT", [64, TL], F32)
    rt = [m.sb("ropet%d" % i, [64, 384], F32) for i in range(2)]
    sq4 = m.sb("sq4", [128, 4, 384], BF16)
    tmp = m.sb("tmpb", [128, 384], F32)
    kvg = m.sb("kvg", [128, 4], F32)
    m.dma(cosT[:], io["cosT"])
    m.dma(sinT[:], io["sinT"])
    m.dma(kvg[:], io["kvg"])
    W = io["w_in"]
    proj_fm(cx, W, 1024, EV_QNOPE, 128, actT, KC, stager(cx, stage, lambda ci: io["qT"][ci, 0:128, :]))

    def ckv_c(ci, ti, ps, t0, tw):
        m.copy(ckvT[:, ci, t0:t0 + tw], ps, eng=cx.evac_eng())
    proj_fm(cx, W, 512, EV_CKV, 128, actT, KC, ckv_c)
    proj_fm(cx, W, 1024, EV_QNA, 128, actT, KC, stager(cx, stage, lambda ci: io["qnT"][ci * 128:(ci + 1) * 128, :]))
    proj_fm(cx, W, 1024, EV_KNA, 128, actT, KC, stager(cx, stage, lambda ci: io["knT"][ci * 128:(ci + 1) * 128, :]))

    keep = {}

    def rope_c(ci, ti, ps, t0, tw):
        if ci % 2 == 0:
            keep[ti] = ps
            return
        p = ci // 2
        st = stage[p % len(stage)]
        m.tt(rt[0][:, 0:tw], keep[ti], cosT[:, t0:t0 + tw], ALU.mult)
        m.tt(rt[1][:, 0:tw], ps, sinT[:, t0:t0 + tw], ALU.mult)
        m.tt(st[0:64, t0:t0 + tw], rt[0][:, 0:tw], rt[1][:, 0:tw], ALU.add)
        if t0 + tw >= TL:
            dst = io["qT"][p, 128:192, :] if p < 8 else io["kpeT"]
            m.dma(dst, st[0:64, :])
    proj_fm(cx, W, 18 * 64, EV_PE, 64, actT, KC, rope_c)

    def vna_c(t, c0, nw, ps):
        st = stage[(t + c0 // 256) % len(stage)]
        m.copy(st[:, 0:nw], ps, eng=cx.evac_eng())
        m.dma(io["vn"][t * 128:(t + 1) * 128, c0:c0 + nw], st[:, 0:nw])
    proj_tm(cx, W, 0, KC, EV_VNA, 1024, actT, vna_c, TL)

    rstd_compute(cx, ckvT, 4, TL, 512, rstd, sq4, 6, tmp)
    for k in range(4):
        m.stt(ckvT[:, k, :], ckvT[:, k, :], kvg[:, k:k + 1], rstd[:, :], ALU.mult, ALU.mult)
        m.copy(kvnT[:, k, :], ckvT[:, k, :], eng="act")
    proj_fm(cx, io["w_ukv"], 1024, 0, 128, kvnT, 4, stager(cx, stage, lambda ci: io["kT"][ci, :, :]))

    def v_c(t, c0, nw, ps):
        st = stage[(t + c0 // 512) % len(stage)]
        m.copy(st[:, 0:nw], ps, eng=cx.evac_eng())
        m.dma(io["v"][t * 128:(t + 1) * 128, c0:c0 + nw], st[:, 0:nw])
    proj_tm(cx, io["w_ukv"], 0, 4, 1024, 1024, kvnT, v_c, TL)
    m.release(mk)


def emit_pre_odd(cx, l, hT, actT, rstd, modall, io):
    m = cx.m
    mk = m.mark()
    sq = m.sb("sq", [128, KC, 384], BF16)
    tmp4 = m.sb("tmp4", [128, TL], F32)
    tmp = m.sb("tmp", [128, 384], F32)
    rstd_compute(cx, hT, KC, TL, D, rstd, sq, 6, tmp)
    norm_modulate(cx, hT, actT, rstd, modall, l, MOD_GSC_A, MOD_SH_A, tmp4)
    m.release(mk)
    mk = m.mark()
    stage = [m.sb("stg%d" % i, [128, TL], BF16) for i in range(4)]
    stage32 = [m.sb("stg32_%d" % i, [128, TL], F32) for i in range(2)]
    W = io["w_in"]
    proj_fm(cx, W, 1024, 0, 128, actT, KC,
            stager(cx, stage, lambda ci: io["qT"][ci * 128:(ci + 1) * 128, :], kind="silu"))

    def vi_c(t, c0, nw, ps):
        st = stage[(t + c0 // 256) % len(stage)]
        m.copy(st[:, 0:nw], ps, eng=cx.evac_eng())
        m.dma(io["vi"][t * 128:(t + 1) * 128, c0:c0 + nw], st[:, 0:nw])
    proj_tm(cx, W, 0, KC, 1024, 1024, actT, vi_c, TL)
    proj_fm(cx, W, 2048, 2048, 128, actT, KC,
            stager(cx, stage32, lambda ci: io["zT"][ci * 128:(ci + 1) * 128, :]))
    proj_fm(cx, W, 1024, 4096, 128, actT, KC,
            stager(cx, stage, lambda ci: io["gT"][ci * 128:(ci + 1) * 128, :], kind="silu"))
    proj_fm(cx, W, 3072, 5120, 128, actT, KC,
            stager(cx, stage, lambda ci: io["uT"][ci * 128:(ci + 1) * 128, :]))
    m.release(mk)


def emit_final(cx, hT, rstd, io):
    m = cx.m
    mk = m.mark()
    sq = m.sb("sq", [128, KC, 384], BF16)
    tmp = m.sb("tmp", [128, 384], F32)
    fg = m.sb("fg", [128, KC], F32)
    stage32 = [m.sb("stg32_%d" % i, [128, 1024], F32) for i in range(2)]
    m.dma(fg[:], io["fg"])
    rstd_compute(cx, hT, KC, 1024, D, rstd, sq, 6, tmp)
    for k in range(KC):
        st = stage32[k % 2]
        m.stt(st[:, :], hT[:, k, 0:1024], fg[:, k:k + 1], rstd[:, 0:1024], ALU.mult, ALU.mult)
        m.dma(io["outT"][:, k, :], st[:, :])
    m.release(mk)


def build_R_program(l):
    nc = bass.Bass("TRN2", target_bir_lowering=False)
    cx = Ctx(nc)
    m = cx.m
    dr = lambda n, s, d, k="ExternalInput": m.dram(n, s, d, k)
    ident_d = dr("ident", [128, 128], F32)
    hT_d = dr("hT", [128, KC, TL], F32)
    mod_d = dr("modraw", [128, DEPTH, 96, 2], F32)
    cx.consts(ident_d)
    cx.eps_col = m.sb("eps", [128, 1], F32)
    m.memset(cx.eps_col[:], EPS)
    modall = m.sb("modall", [128, DEPTH, 128, 2], F32)
    m.dma(modall[:, :, 0:96, :], mod_d)
    emit_gsc(cx, modall, dr("gmix", [128, DEPTH, KC], F32), dr("gmlp", [128, DEPTH, KC], F32))
    hT = m.sb("hT", [128, KC, TL], F32)
    m.dma(hT[:], hT_d)
    actT = m.sb("actT", [128, KC, TL], BF16)
    rstd = m.sb("rstd", [128, TL], F32)
    cx.init_wring(4)
    if l >= 1:
        io = {"yT": dr("yT", [128, KC, TL], BF16), "w_out": dr("w_out", [D, D], F32),
              "w1": dr("w1", [D, HID], F32), "w2": dr("w2", [HID, D], F32)}
        emit_post(cx, l - 1, hT, actT, rstd, modall, io, TL if l <= 3 else 1024)
    if l <= 3 and l % 2 == 0:
        io = {"w_in": dr("w_in", [D, EV_NEXT], F32), "w_ukv": dr("w_ukv", [512, 2048], F32),
              "cosT": dr("cosT", [64, TL], F32), "sinT": dr("sinT", [64, TL], F32),
              "kvg": dr("kvg", [128, 4], F32),
              "qT": dr("qT", [8, 192, TL], BF16, "ExternalOutput"),
              "kT": dr("kT", [8, 128, TL], BF16, "ExternalOutput"),
              "kpeT": dr("kpeT", [64, TL], BF16, "ExternalOutput"),
              "v": dr("v", [TL, 1024], BF16, "ExternalOutput"),
              "qnT": dr("qnT", [1024, TL], BF16, "ExternalOutput"),
              "knT": dr("knT", [1024, TL], BF16, "ExternalOutput"),
              "vn": dr("vn", [TL, 1024], BF16, "ExternalOutput")}
        emit_pre_even(cx, l, hT, actT, rstd, modall, io)
    elif l <= 3:
        io = {"w_in": dr("w_in", [D, 8192], F32),
              "qT": dr("qT", [1024, TL], BF16, "ExternalOutput"),
              "vi": dr("vi", [TL, 1024], BF16, "ExternalOutput"),
              "zT": dr("zT", [2048, TL], F32, "ExternalOutput"),
              "gT": dr("gT", [1024, TL], BF16, "ExternalOutput"),
              "uT": dr("uT", [3072, TL], BF16, "ExternalOutput")}
        emit_pre_odd(cx, l, hT, actT, rstd, modall, io)
    if l <= 3:
        hout = dr("hT_out", [128, KC, TL], F32, "ExternalOutput")
        m.dma(hout, hT[:])
    else:
        io = {"fg": dr("fg", [128, KC], F32), "outT": dr("outT", [128, KC, 1024], F32, "ExternalOutput")}
        emit_final(cx, hT, rstd, io)
    m.emit()
    return nc


def fm_vec(v):
    return np.ascontiguousarray(np.asarray(v).reshape(-1, 128).T)


def fm_tokens(tok):
    T = tok.shape[0]
    return np.ascontiguousarray(tok.reshape(T, KC, 128).transpose(2, 1, 0))


def rope_perm_idx():
    idx = np.zeros(64, dtype=np.int64)
    for j in range(64):
        jj = j % 32
        idx[j] = j + 16 if jj < 16 else j - 16
    return idx


def ev_w_in_ext(w):
    perm = rope_perm_idx()
    cols = []
    for h in range(8):
        cols.append(np.arange(h * 192, h * 192 + 128))
    cols.append(np.arange(1536, 2048))
    cols.append(np.arange(2112, 3136))
    cols.append(np.arange(3136, 4160))
    for h in range(8):
        base = h * 192 + 128
        cols.append(base + np.arange(64))
        cols.append(base + perm)
    cols.append(2048 + np.arange(64))
    cols.append(2048 + perm)
    cols.append(np.zeros(128, dtype=np.int64))
    cols.append(np.arange(4160, 5184))
    idx = np.concatenate(cols)
    assert idx.shape[0] == EV_NEXT
    return np.ascontiguousarray(w[:, idx])


def ukv_ext(w):
    kc = np.concatenate([np.arange(h * 256, h * 256 + 128) for h in range(8)])
    vc = np.concatenate([np.arange(h * 256 + 128, h * 256 + 256) for h in range(8)])
    return np.ascontiguousarray(w[:, np.concatenate([kc, vc])])


def rope_tables(rank):
    pos = np.arange(rank * 1024, rank * 1024 + 1024)
    rows = (pos // GRID_W).astype(np.float32)
    cols = (pos % GRID_W).astype(np.float32)
    inv_freq = (np.float32(10000.0) ** (-np.arange(0, 32, 2, dtype=np.float32) / np.float32(32))).astype(np.float32)
    cosT = np.ones((64, TL), dtype=np.float32)
    sinT = np.zeros((64, TL), dtype=np.float32)
    for j in range(64):
        p = rows if j < 32 else cols
        jj = j % 32
        ang = (p * inv_freq[jj % 16]).astype(np.float32)
        cosT[j, :1024] = np.cos(ang)
        s = np.sin(ang)
        sinT[j, :1024] = -s if jj < 16 else s
    return cosT, sinT


NA_CLS = {(0, 0): 0, (1, 0): 1, (2, 0): 2, (3, 0): 3, (4, 0): 4, (5, 1): 5, (5, 0): 6, (6, 0): 7, (7, 0): 8}


def na_row_info(r):
    rs = min(max(r - 4, 0), 24)
    base = 2 * (rs // 2)
    cls = NA_CLS[(r - base, rs - base)]
    ntiles = 5 if rs - base == 1 else 4
    return base // 2, cls, ntiles


def na_tables():
    ridx = np.zeros((9, 5, 128, 64), dtype=np.int64)
    cidx = np.zeros((9, 5, 128, 64), dtype=np.int64)
    mask = np.zeros((9, 5, 128, 64), dtype=np.float32)
    p = np.arange(128)
    qc = np.arange(64)
    kcol = (p % 64)[:, None]
    col_start = np.clip(qc - 8, 0, 48)[None, :]
    col_ok = (kcol >= col_start) & (kcol < col_start + 16)
    coff = np.clip(kcol - qc[None, :], -15, 15) + 15
    for (dr, off), c in NA_CLS.items():
        for j in range(5):
            krel = 2 * j + p // 64
            inband = (krel >= off) & (krel < off + 8)
            roff = np.clip(krel - dr + 7, 0, 14)
            ridx[c, j] = roff[:, None]
            cidx[c, j] = coff
            ok = inband[:, None] & col_ok
            mask[c, j] = np.where(ok, 0.0, -30000.0)
    return ridx, cidx, mask


def dense_attn(cx, parts, vtile, key_tiles, q0, qn, scale, yT, pbuf, rsb):
    m = cx.m
    nk = len(key_tiles)
    for qb in range(q0, q0 + qn, 512):
        qw = min(512, q0 + qn - qb)
        O = m.psum[4]
        Sm = m.psum[5]

        def s_mm(i):
            sb = m.psum[i % 4]
            kt = key_tiles[i]
            for pi, (qp, kp) in enumerate(parts):
                m.matmul(sb[:, 0:qw], kp[:, kt * 128:(kt + 1) * 128], qp[:, qb:qb + qw],
                         start=(pi == 0), stop=(pi == len(parts) - 1))
        s_mm(0)
        for i in range(nk):
            if i + 1 < nk:
                s_mm(i + 1)
            P = pbuf[i % len(pbuf)]
            m.act(P[:, 0:qw], m.psum[i % 4][:, 0:qw], AF.Exp, scale=scale)
            m.matmul(O[:, 0:qw], vtile(key_tiles[i]), P[:, 0:qw], start=(i == 0), stop=(i == nk - 1))
            m.matmul(Sm[:, 0:qw], cx.ones_bf[:, 0:128], P[:, 0:qw], start=(i == 0), stop=(i == nk - 1))
        m.recip(rsb[:, 0:qw], Sm[:, 0:qw])
        m.tt(yT[:, qb:qb + qw], O[:, 0:qw], rsb[:, 0:qw], ALU.mult)


def emit_M_even(cx, io, ctx_out):
    m = cx.m
    mk = m.mark()
    LT = list(range(2, 18))
    CTt = [0, 1]
    allk = CTt + LT
    pbuf = [m.sb("pbuf%d" % i, [128, 512], BF16) for i in range(3)]
    rsb = m.sb("rsb", [128, 512], F32)
    ystage = [m.sb("ystage%d" % i, [128, TM], BF16) for i in range(2)]
    vsb = m.sb("vsb", [128, 18, 512], BF16)
    m.dma(vsb[:], io["mv"].rearrange("(t p) c -> p t c", p=128))
    kpe = m.sb("kpe", [64, TM], BF16)
    m.dma(kpe[:], io["mkpe"])
    qn_ = [m.sb("qnope%d" % i, [128, TM], BF16) for i in range(2)]
    qp_ = [m.sb("qpe%d" % i, [64, TM], BF16) for i in range(2)]
    kn_ = [m.sb("knope%d" % i, [128, TM], BF16) for i in range(2)]
    mla_scale = 192.0 ** -0.5
    c0 = 0 if ctx_out else 256
    for h in range(4):
        qn, qp, kn = qn_[h % 2], qp_[h % 2], kn_[h % 2]
        m.dma(qn[:], io["mq"][h, 0:128, :])
        m.dma(qp[:], io["mq"][h, 128:192, :])
        m.dma(kn[:], io["mk"][h])
        ys = ystage[h % 2]
        parts = [(qn, kn), (qp, kpe)]
        vt = lambda kt, h=h: vsb[:, kt, h * 128:(h + 1) * 128]
        if ctx_out:
            dense_attn(cx, parts, vt, CTt, 0, 256, mla_scale, ys, pbuf, rsb)
        dense_attn(cx, parts, vt, allk, 256, 2048, mla_scale, ys, pbuf, rsb)
        m.dma(io["myT"][h, :, c0:TM], ys[:, c0:TM])
    m.dma(vsb[:], io["mvn"].rearrange("(t p) c -> p t c", p=128))
    mask = m.sb("namask", [128, 9 * 5 * 64], F32)
    m.dma(mask[:], io["namask"])
    bias_ = [m.sb("nabias%d" % i, [128, 9 * 5 * 64], F32) for i in range(2)]
    lg_ = [m.sb("nalg%d" % i, [128, 320], F32) for i in range(2)]
    P_ = [m.sb("naP%d" % i, [128, 448], BF16) for i in range(2)]
    nrs_ = [m.sb("nars%d" % i, [128, 64], F32) for i in range(2)]
    na_scale = 128.0 ** -0.5
    it = 0
    for h in range(4):
        qn, kn = qn_[h % 2], kn_[h % 2]
        m.dma(qn[:], io["mqn"][h * 128:(h + 1) * 128, :])
        m.dma(kn[:], io["mkn"][h * 128:(h + 1) * 128, :])
        bias = bias_[h % 2]
        m.dma(bias[:], io["nabias"][h])
        m.stt(bias[:], bias[:], 1.0, mask[:], ALU.mult, ALU.add)
        bv = bias[:].rearrange("p (c x) -> p c x", c=9)
        ys = ystage[h % 2]
        vt = lambda kt, h=h: vsb[:, kt, h * 128:(h + 1) * 128]
        if ctx_out:
            dense_attn(cx, [(qn, kn)], vt, CTt, 0, 256, na_scale, ys, pbuf, rsb)
        for r in range(32):
            j0, cls, nt = na_row_info(r)
            q0 = 256 + 64 * r
            S = m.psum[it % 2]
            O = m.psum[2 + it % 2]
            lg, P, nrs = lg_[it % 2], P_[it % 2], nrs_[it % 2]
            it += 1
            tiles = [2 + j0 + j for j in range(nt)]
            for j, kt in enumerate(tiles):
                m.matmul(S[:, 64 * j:64 * j + 64], kn[:, kt * 128:(kt + 1) * 128], qn[:, q0:q0 + 64])
            for j, kt in enumerate(CTt):
                m.matmul(S[:, 320 + 64 * j:384 + 64 * j], kn[:, kt * 128:(kt + 1) * 128], qn[:, q0:q0 + 64])
            m.stt(lg[:, 0:64 * nt], S[:, 0:64 * nt], na_scale, bv[:, cls, 0:64 * nt], ALU.mult, ALU.add)
            m.act(P[:, 0:64 * nt], lg[:, 0:64 * nt], AF.Exp)
            m.act(P[:, 320:448], S[:, 320:448], AF.Exp, scale=na_scale)
            srcs = [(kt, P[:, 64 * j:64 * j + 64]) for j, kt in enumerate(tiles)]
            srcs += [(kt, P[:, 320 + 64 * j:384 + 64 * j]) for j, kt in enumerate(CTt)]
            for i, (kt, pp) in enumerate(srcs):
                m.matmul(O[:, 0:64], vt(kt), pp, start=(i == 0), stop=(i == len(srcs) - 1), skip_group_check=True)
                m.matmul(O[:, 64:128], cx.ones_bf[:, 0:128], pp, start=False, stop=(i == len(srcs) - 1),
                         skip_group_check=True)
            m.recip(nrs[:, :], O[:, 64:128])
            m.tt(ys[:, q0:q0 + 64], O[:, 0:64], nrs[:, :], ALU.mult)
        m.dma(io["myT"][4 + h, :, c0:TM], ys[:, c0:TM])
    m.release(mk)


def build_Meven_program(ctx_out):
    nc = bass.Bass("TRN2", target_bir_lowering=False)
    cx = Ctx(nc)
    m = cx.m
    dr = lambda n, s, d, k="ExternalInput": m.dram(n, s, d, k)
    cx.consts(dr("ident", [128, 128], F32))
    io = {"mq": dr("mq", [4, 192, TM], BF16), "mk": dr("mk", [4, 128, TM], BF16),
          "mkpe": dr("mkpe", [64, TM], BF16), "mv": dr("mv", [TM, 512], BF16),
          "mqn": dr("mqn", [512, TM], BF16), "mkn": dr("mkn", [512, TM], BF16),
          "mvn": dr("mvn", [TM, 512], BF16),
          "nabias": dr("nabias", [4, 128, 2880], F32), "namask": dr("namask", [128, 2880], F32),
          "myT": dr("myT", [8, 128, TM], BF16, "ExternalOutput")}
    emit_M_even(cx, io, ctx_out)
    m.emit()
    return nc


def canon_cols(a0, a1):
    return np.ascontiguousarray(np.concatenate([a0[..., 1024:], a1[..., 1024:], a0[..., :1024], a1[..., :1024]], -1))


def canon_rows(a0, a1):
    return np.ascontiguousarray(np.concatenate([a0[1024:], a1[1024:], a0[:1024], a1[:1024]], 0))


_NA_TAB = None


def na_bias_host(rel_bias_e):
    global _NA_TAB
    if _NA_TAB is None:
        _NA_TAB = na_tables()
    ridx, cidx, mask = _NA_TAB
    g = rel_bias_e[:, ridx, cidx]
    g = np.ascontiguousarray(g.transpose(0, 3, 1, 2, 4).reshape(8, 128, 2880)).astype(np.float32)
    mk = np.ascontiguousarray(mask.transpose(2, 0, 1, 3).reshape(128, 2880))
    return g, mk


def assemble_Meven(o0, o1, r, nab, namask):
    hs = slice(4 * r, 4 * r + 4)
    cs = slice(512 * r, 512 * r + 512)
    return {"ident": np.eye(128, dtype=np.float32),
            "mq": canon_cols(o0["qT"][hs], o1["qT"][hs]),
            "mk": canon_cols(o0["kT"][hs], o1["kT"][hs]),
            "mkpe": canon_cols(o0["kpeT"], o1["kpeT"]),
            "mv": canon_rows(o0["v"][:, cs], o1["v"][:, cs]),
            "mqn": canon_cols(o0["qnT"][cs], o1["qnT"][cs]),
            "mkn": canon_cols(o0["knT"][cs], o1["knT"][cs]),
            "mvn": canon_rows(o0["vn"][:, cs], o1["vn"][:, cs]),
            "nabias": np.ascontiguousarray(nab[hs]), "namask": namask}


def assemble_y(m0, m1, r):
    full = np.zeros((16, 128, TM), dtype=m0["myT"].dtype)
    for rr, mm in ((0, m0), (1, m1)):
        full[4 * rr:4 * rr + 4] = mm["myT"][0:4]
        full[8 + 4 * rr:8 + 4 * rr + 4] = mm["myT"][4:8]
    loc = np.concatenate([full[:, :, 256 + 1024 * r:256 + 1024 * (r + 1)], full[:, :, 128 * r:128 * (r + 1)]], -1)
    return np.ascontiguousarray(loc.transpose(1, 0, 2))


NCH = 36


def hgrn_sigma(d, c):
    if d == 0:
        return c
    return 3 - c if c < 4 else 35 - (c - 4)


def emit_hgrn(cx, io, ctx_out, ystage):
    m = cx.m
    mk = m.mark()
    T = TM
    BIG = 2.0e17
    vtok = m.sb("hg_vtok", [128, 18, 512], BF16)
    m.dma(vtok[:], io["hv"].rearrange("(t p) c -> p t c", p=128))
    lbl = m.sb("hg_lbl", [128, 2, 2, 4], F32)
    m.dma(lbl[:], io["lbl"])
    lb = m.sb("hg_lb", [128, 2, 4], F32)
    oml = m.sb("hg_oml", [128, 2, 4], F32)
    if io["odd_idx"] == 0:
        m.memset(lb[:], 0.0)
    else:
        m.tt(lb[:], lbl[:, :, 1, :], lbl[:, :, 0, :], ALU.subtract)
        m.act(lb[:], lb[:], AF.Sigmoid)
    m.ts(oml[:], lb[:], -1.0, 1.0, ALU.mult, ALU.add)
    ng = m.sb("hg_ng", [128, 4], F32)
    m.dma(ng[:], io["ng"])
    masks = m.sb("hg_masks", [128, 4, 128], F32)
    m.dma(masks[:], io["hmasks"])
    ones_f = m.sb("hg_ones", [128, T], BF16)
    m.memset(ones_f[:], 1.0)
    qb = m.sb("hg_q", [128, T], BF16)
    gb = m.sb("hg_g", [128, T], BF16)
    Qi = [m.sb("hg_Qi%d" % d, [128, T], BF16) for d in range(2)]
    Ki = [m.sb("hg_Ki%d" % d, [128, T], BF16) for d in range(2)]
    Qo = [m.sb("hg_Qo%d" % d, [128, T], BF16) for d in range(2)]
    Ko = [m.sb("hg_Ko%d" % d, [128, T], BF16) for d in range(2)]
    Qs = [m.sb("hg_Qs%d" % d, [128, T], BF16) for d in range(2)]
    Sbf = [m.sb("hg_Sbf%d" % d, [128, NCH, 128], BF16) for d in range(2)]
    KlT = m.sb("hg_KlT", [128, T], BF16)
    Kltok = m.sb("hg_Kltok", [128, 18, 128], BF16)
    z = m.sb("hg_z", [128, T], F32)
    lf = m.sb("hg_lf", [128, T], F32)
    kk = m.sb("hg_k", [128, T], F32)
    ee = m.sb("hg_e", [128, T], F32)
    gaddr = m.mark()
    G = m.sb("hg_G", [128, T], F32)
    Gx = m.sb("hg_Gx", [128, T], F32)
    Dfull = m.sb("hg_Dfull", [128, 128 * NCH], F32, at=gaddr)
    Sst = m.sb("hg_Sst", [128, 128 * NCH], F32)
    Dsc = m.sb("hg_Dsc", [128, NCH], F32)
    Dtmp = m.sb("hg_Dtmp", [128, NCH], F32)
    Am = [m.sb("hg_Am%d" % i, [128, 128], BF16) for i in range(4)]
    T1 = [m.sb("hg_T1_%d" % i, [128, 128], F32) for i in range(2)]
    T2 = [m.sb("hg_T2_%d" % i, [128, 128], F32) for i in range(2)]
    oT = lf
    rstd = ee
    sq1 = m.sb("hg_sq", [128, 1, 384], BF16)
    tmp = m.sb("hg_tmp", [128, 384], F32)

    v64 = lambda t_: t_[:].rearrange("p (c j) -> p c j", j=64)
    v32 = lambda t_: t_[:].rearrange("p (c j) -> p c j", j=32)
    bc64 = lambda t_, j: v64(t_)[:, :, j:j + 1].to_broadcast([128, NCH, 64])
    bc32 = lambda t_, j: v32(t_)[:, :, j:j + 1].to_broadcast([128, 2 * NCH, 32])
    c0 = 0 if ctx_out else 256
    ai = 0
    for h in range(4):
        m.dma(qb[:], io["hq"][h * 128:(h + 1) * 128, :])
        m.dma(gb[:], io["hg"][h * 128:(h + 1) * 128, :])
        for d in range(2):
            m.dma(z[:], io["hz"][d, h * 128:(h + 1) * 128, :])
            m.act(z[:], z[:], AF.Sigmoid)
            m.ts(z[:], z[:], oml[:, d, h:h + 1], lb[:, d, h:h + 1], ALU.mult, ALU.add)
            m.ts(z[:], z[:], 1e-30, None, ALU.max, eng="pool")
            m.act(lf[:], z[:], AF.Ln)
            m.ts(kk[:], z[:], -1.0, 1.0, ALU.mult, ALU.add, eng="pool")
            m.scan(G[:], ones_f[:], lf[:], 0.0, ALU.mult, ALU.add)
            m.tt(Gx[:], G[:], lf[:], ALU.subtract, eng="pool")
            A = G if d == 0 else Gx
            sgn = 1.0 if d == 0 else -1.0
            m.tt(v32(z), v32(A), bc32(A, 15 if d == 0 else 16), ALU.subtract, eng="pool")
            m.act(ee[:], z[:], AF.Exp, scale=sgn)
            m.stt(Qi[d][:], ee[:], BIG, qb[:], ALU.min, ALU.mult)
            m.act(ee[:], z[:], AF.Exp, scale=-sgn)
            m.stt(Ki[d][:], ee[:], BIG, kk[:], ALU.min, ALU.mult)
            m.tt(v64(z), v64(A), bc64(A, 31 if d == 0 else 32), ALU.subtract, eng="pool")
            m.act(ee[:], z[:], AF.Exp, scale=sgn)
            m.stt(Qo[d][:], ee[:], 1.0, qb[:], ALU.min, ALU.mult)
            m.act(ee[:], z[:], AF.Exp, scale=-sgn)
            m.stt(Ko[d][:], ee[:], 1.0, kk[:], ALU.min, ALU.mult)
            if d == 0:
                m.tt(v64(z), v64(G), bc64(Gx, 0), ALU.subtract, eng="pool")
            else:
                m.tt(v64(z), v64(Gx), bc64(G, 63), ALU.subtract, eng="pool")
            m.act(ee[:], z[:], AF.Exp, scale=sgn)
            m.stt(Qs[d][:], ee[:], 1.0, qb[:], ALU.min, ALU.mult)
            if d == 0:
                m.tt(v64(z), v64(G), bc64(G, 63), ALU.subtract, eng="pool")
            else:
                m.tt(v64(z), v64(Gx), bc64(Gx, 0), ALU.subtract, eng="pool")
            m.act(ee[:], z[:], AF.Exp, scale=-sgn)
            m.stt(KlT[:], ee[:], 1.0, kk[:], ALU.min, ALU.mult)
            m.tt(Dtmp[:], v64(G)[:, :, 63], v64(Gx)[:, :, 0], ALU.subtract)
            m.act(Dtmp[:], Dtmp[:], AF.Exp)
            if d == 0:
                m.copy(Dsc[:, 1:NCH], Dtmp[:, 1:NCH], eng="pool")
                m.memset(Dsc[:, 0:1], 0.0, eng="pool")
            else:
                for c in range(NCH):
                    s = hgrn_sigma(1, c)
                    if s == 0:
                        m.memset(Dsc[:, 0:1], 0.0, eng="pool")
                    else:
                        m.copy(Dsc[:, s:s + 1], Dtmp[:, c:c + 1], eng="pool")
            m.copy(Dfull[:].rearrange("p (e s) -> p e s", s=NCH),
                   Dsc[:].unsqueeze(1).to_broadcast([128, 128, NCH]), eng="pool")
            for t4 in range(0, 18, 4):
                nt = min(4, 18 - t4)
                pb = m.psum[6 + (t4 // 4) % 2][:].bitcast(BF16)
                for i in range(nt):
                    m.transpose(pb[:, i * 128:(i + 1) * 128], KlT[:, (t4 + i) * 128:(t4 + i + 1) * 128], cx.ident_bf[:])
                m.copy(Kltok[:, t4:t4 + nt, :], pb[:, 0:nt * 128].rearrange("p (t d) -> p t d", d=128),
                       eng=cx.evac_eng())
            S3 = Sst[:].rearrange("p (e s) -> p e s", s=NCH)
            for c in range(NCH):
                t, j = c // 2, c % 2
                ps = m.psum[c % 4]
                m.matmul(ps[:, 0:128], Kltok[64 * j:64 * j + 64, t, :], vtok[64 * j:64 * j + 64, t, h * 128:(h + 1) * 128])
                s = hgrn_sigma(d, c)
                m.copy(S3[:, :, s], ps[:, 0:128], eng=cx.evac_eng())
            m.scan(Sst[:], Dfull[:], Sst[:], 0.0, ALU.mult, ALU.add)
            m.copy(Sbf[d][:], Sst[:].rearrange("p (e s) -> p s e", s=NCH), eng="pool")
        for t in range(c0 // 128, 18):
            psO = m.psum[4 + t % 2]
            ams = []
            for d in range(2):
                tsl = slice(t * 128, (t + 1) * 128)
                psA1 = m.psum[(2 * d) % 4]
                psA2 = m.psum[(2 * d + 1) % 4]
                m.matmul(psA1[:, 0:128], Ki[d][:, tsl], Qi[d][:, tsl])
                m.matmul(psA2[:, 0:128], Ko[d][:, tsl], Qo[d][:, tsl])
                am = Am[ai % 4]
                ai += 1
                m.tt(T1[d][:], psA1[:, 0:128], masks[:, 2 * d, :], ALU.mult)
                m.tt(T2[d][:], psA2[:, 0:128], masks[:, 2 * d + 1, :], ALU.mult)
                m.tt(am[:], T1[d][:], T2[d][:], ALU.add, eng="pool")
                ams.append(am)
            mms = []
            for d in range(2):
                mms.append((psO[:, 0:128], vtok[:, t, h * 128:(h + 1) * 128], ams[d][:]))
                for j in range(2):
                    c = 2 * t + j
                    s = hgrn_sigma(d, c)
                    if s >= 1:
                        mms.append((psO[:, 64 * j:64 * j + 64], Sbf[d][:, s - 1, :], Qs[d][:, c * 64:(c + 1) * 64]))
            for i, (o_, l_, r_) in enumerate(mms):
                m.matmul(o_, l_, r_, start=(i == 0), stop=(i == len(mms) - 1), skip_group_check=True)
            m.copy(oT[:, t * 128:(t + 1) * 128], psO[:, 0:128], eng=cx.evac_eng())
        rstd_compute(cx, oT[:].rearrange("p (o t) -> p o t", o=1)[:, :, c0:T], 1, T - c0, 128, rstd, sq1, 7, tmp)
        ys = ystage[h % 2]
        m.stt(oT[:, c0:T], oT[:, c0:T], ng[:, h:h + 1], rstd[:, 0:T - c0], ALU.mult, ALU.mult)
        m.tt(ys[:, c0:T], oT[:, c0:T], gb[:, c0:T], ALU.mult, eng="pool")
        m.dma(io["myT"][h, :, c0:T], ys[:, c0:T])
    m.release(mk)


def sin_reduced(cx, out, x, rows, w, t1):
    m = cx.m
    PI = math.pi
    for _ in range(2):
        m.ts(t1[0:rows, 0:w], x, PI, -2 * PI, ALU.is_gt, ALU.mult)
        m.tt(x, x, t1[0:rows, 0:w], ALU.add)
        m.ts(t1[0:rows, 0:w], x, -PI, 2 * PI, ALU.is_lt, ALU.mult)
        m.tt(x, x, t1[0:rows, 0:w], ALU.add)
    m.act(out, x, AF.Sin)


def emit_hyena(cx, io, ctx_out, ystage):
    m = cx.m
    mk = m.mark()
    T = TM
    cw = m.sb("hy_cw", [128, 3, 3, 4], F32)
    cb = m.sb("hy_cb", [128, 3, 4], F32)
    skip = m.sb("hy_skip", [128, 4], F32)
    m.dma(cw[:], io["cw"])
    m.dma(cb[:], io["cb"])
    m.dma(skip[:], io["skip"])
    fw1 = m.sb("hy_w1", [33, 64], F32)
    fw2 = m.sb("hy_w2", [64, 64], F32)
    fw3 = m.sb("hy_w3", [64, 64], F32)
    fb = m.sb("hy_fb", [64, 4], F32)
    fwo = m.sb("hy_wo", [64, 2, 512], F32)
    m.dma(fw1[:], io["fw1"])
    m.dma(fw2[:], io["fw2"])
    m.dma(fw3[:], io["fw3"])
    m.dma(fb[:], io["fb"])
    m.dma(fwo[:], io["fwo"])
    delt = m.sb("hy_delt", [128, 512], F32)
    m.dma(delt[:], io["delt"])
    X0 = [m.sb("hy_X0_%d" % i, [128, T], BF16) for i in range(4)]
    VX = [m.sb("hy_VX_%d" % i, [128, T], BF16) for i in range(4)]
    vxtok = m.sb("hy_vxtok", [128, 18, 512], BF16)
    mB = m.mark()
    ub = [m.sb("hy_u%d" % i, [128, T], BF16) for i in range(2)]
    acc = [m.sb("hy_acc%d" % i, [128, T], F32) for i in range(2)]
    segs = ((0, 256), (256, T))
    for cc in range(4):
        def conv(g, a, u):
            m.dma(u[:], io["hu"][g, cc * 128:(cc + 1) * 128, :])
            m.ts(a[:], u[:], cw[:, 1, g, cc:cc + 1], cb[:, g, cc:cc + 1], ALU.mult, ALU.add)
            for (s0, s1) in segs:
                m.stt(a[:, s0 + 1:s1], u[:, s0:s1 - 1], cw[:, 0, g, cc:cc + 1], a[:, s0 + 1:s1], ALU.mult, ALU.add)
                m.stt(a[:, s0:s1 - 1], u[:, s0 + 1:s1], cw[:, 2, g, cc:cc + 1], a[:, s0:s1 - 1], ALU.mult, ALU.add)
        conv(1, acc[0], ub[0])
        conv(2, acc[1], ub[1])
        m.tt(VX[cc][:], acc[0][:], acc[1][:], ALU.mult, eng="pool")
        conv(0, acc[0], ub[0])
        m.copy(X0[cc][:], acc[0][:], eng="act")
    for t in range(18):
        pb = m.psum[6 + t % 2][:].bitcast(BF16)
        for cc in range(4):
            m.transpose(pb[:, cc * 128:(cc + 1) * 128], VX[cc][:, t * 128:(t + 1) * 128], cx.ident_bf[:])
        m.copy(vxtok[:, t, :], pb[:, 0:512], eng=cx.evac_eng())
    m.release(mB)
    for (name, n, tok0) in (("c", 256, 0), ("l", 2048, 256)):
        if name == "c" and not ctx_out:
            continue
        m.release(mB)
        ntile = n // 128
        nfc = n // 128
        Yc = m.sb("hy_Yc", [128, nfc, 512], BF16)
        Ys = m.sb("hy_Ys", [128, nfc, 512], BF16)
        p1 = m.sb("hy_p1", [128, 512], F32)
        p2 = m.sb("hy_p2", [128, 512], F32)
        mC = m.mark()
        hs = m.sb("hy_hs", [128, ntile, 512], BF16)
        hd = m.sb("hy_hd", [128, ntile, 512], BF16)
        mD = m.mark()
        zT = m.sb("hy_zT", [33, n], F32)
        m.dma(zT[:], io["zT_" + name])
        tcol = m.sb("hy_tcol", [128, ntile], F32)
        m.dma(tcol[:], io["tcol_" + name])
        hA = m.sb("hy_hA", [64, n], F32)
        hB = m.sb("hy_hB", [64, n], F32)
        t1 = m.sb("hy_t1", [64, 512], F32)
        layers = ((fw1, 33, zT, hA, 0), (fw2, 64, hA, hB, 1), (fw3, 64, hB, hA, 2))
        for (wt, kdim, src_, dst, bi) in layers:
            for b0 in range(0, n, 512):
                bw = min(512, n - b0)
                ps = m.psum[(b0 // 512) % 2]
                m.matmul(ps[0:64, 0:bw], wt[0:kdim, :], src_[0:kdim, b0:b0 + bw])
                m.ts(dst[:, b0:b0 + bw], ps[0:64, 0:bw], fb[:, bi:bi + 1], fb[:, 3:4], ALU.add, ALU.mult)
                sin_reduced(cx, dst[:, b0:b0 + bw], dst[:, b0:b0 + bw], 64, bw, t1)
        h3 = hA
        win = m.sb("hy_win", [128, 512], F32)
        hf = m.sb("hy_hf", [128, 512], F32)
        hb = m.sb("hy_hb", [128, 512], F32)
        ntc = m.sb("hy_ntc", [128, ntile], F32)
        m.ts(ntc[:], tcol[:], -1.0, None, ALU.mult)
        for t in range(ntile):
            m.act(win[:], delt[:], AF.Exp, scale=ntc[:, t:t + 1])
            m.matmul(m.psum[2][:, 0:512], h3[0:64, t * 128:(t + 1) * 128], fwo[0:64, 0, :])
            m.matmul(m.psum[3][:, 0:512], h3[0:64, t * 128:(t + 1) * 128], fwo[0:64, 1, :])
            m.tt(hf[:], m.psum[2][:, 0:512], win[:], ALU.mult)
            m.tt(hb[:], m.psum[3][:, 0:512], win[:], ALU.mult)
            if t == 0:
                m.memset(hb[0:1, :], 0.0)
            m.tt(hs[:, t, :], hf[:], hb[:], ALU.add, eng="pool")
            m.tt(hd[:, t, :], hf[:], hb[:], ALU.subtract, eng="pool")
        m.release(mD)
        nfb = max(1, nfc // 4)
        fcb = nfc // nfb
        Cb = [m.sb("hy_Cb%d" % i, [128, ntile, fcb * 128], BF16) for i in range(1)]
        Sb = [m.sb("hy_Sb%d" % i, [128, ntile, fcb * 128], BF16) for i in range(1)]
        Kc = m.sb("hy_Kc", [128, 512], F32)
        Ks = m.sb("hy_Ks", [128, 512], F32)
        dF = io["dftF_" + name]
        tt0 = tok0 // 128
        for fbk in range(nfb):
            C_, S_ = Cb[0], Sb[0]
            m.dma(C_[:], dF[fbk, 0].rearrange("p (k f) -> p k f", f=fcb * 128))
            m.dma(S_[:], dF[fbk, 1].rearrange("p (k f) -> p k f", f=fcb * 128))
            for fi in range(fcb):
                fc = fbk * fcb + fi
                fsl = slice(fi * 128, (fi + 1) * 128)
                for t in range(ntile):
                    m.matmul(m.psum[0][:, 0:512], C_[:, t, fsl], hs[:, t, :], start=(t == 0), stop=(t == ntile - 1))
                for t in range(ntile):
                    m.matmul(m.psum[1][:, 0:512], S_[:, t, fsl], hd[:, t, :], start=(t == 0), stop=(t == ntile - 1))
                m.copy(Kc[:], m.psum[0][:, 0:512], eng="act")
                m.copy(Ks[:], m.psum[1][:, 0:512], eng="act")
                pc, ps_ = m.psum[2 + 2 * (fc % 2)], m.psum[3 + 2 * (fc % 2)]
                for t in range(ntile):
                    m.matmul(pc[:, 0:512], C_[:, t, fsl], vxtok[:, tt0 + t, :], start=(t == 0), stop=(t == ntile - 1))
                for t in range(ntile):
                    m.matmul(ps_[:, 0:512], S_[:, t, fsl], vxtok[:, tt0 + t, :], start=(t == 0), stop=(t == ntile - 1))
                m.tt(p1[:], pc[:, 0:512], Kc[:], ALU.mult)
                m.tt(p2[:], ps_[:, 0:512], Ks[:], ALU.mult)
                m.tt(Yc[:, fc, :], p1[:], p2[:], ALU.subtract, eng="pool")
                m.tt(p1[:], pc[:, 0:512], Ks[:], ALU.mult)
                m.tt(p2[:], ps_[:, 0:512], Kc[:], ALU.mult)
                m.tt(Ys[:, fc, :], p1[:], p2[:], ALU.add, eng="pool")
        m.release(mC)
        dI = io["dftI_" + name]
        tbw = min(512, n)
        Ci = [m.sb("hy_Ci%d" % i, [128, nfc, tbw], BF16) for i in range(2)]
        Si = [m.sb("hy_Si%d" % i, [128, nfc, tbw], BF16) for i in range(2)]
        for tb in range(n // tbw):
            C_, S_ = Ci[tb % 2], Si[tb % 2]
            m.dma(C_[:], dI[tb, 0].rearrange("p (k t) -> p k t", t=tbw))
            m.dma(S_[:], dI[tb, 1].rearrange("p (k t) -> p k t", t=tbw))
            for cc in range(4):
                po = m.psum[cc % 2]
                csl = slice(cc * 128, (cc + 1) * 128)
                for fc in range(nfc):
                    m.matmul(po[:, 0:tbw], Yc[:, fc, csl], C_[:, fc, :], start=(fc == 0), stop=False)
                for fc in range(nfc):
                    m.matmul(po[:, 0:tbw], Ys[:, fc, csl], S_[:, fc, :], start=False, stop=(fc == nfc - 1))
                a0 = tok0 + tb * tbw
                ys = ystage[cc % 2]
                m.stt(p1[:, 0:tbw], VX[cc][:, a0:a0 + tbw], skip[:, cc:cc + 1], po[:, 0:tbw], ALU.mult, ALU.add)
                m.tt(ys[:, 0:tbw], p1[:, 0:tbw], X0[cc][:, a0:a0 + tbw], ALU.mult, eng="pool")
                m.dma(io["myT"][4 + cc, :, a0:a0 + tbw], ys[:, 0:tbw])
    m.release(mk)


def build_Modd_program(odd_idx, ctx_out):
    nc = bass.Bass("TRN2", target_bir_lowering=False)
    cx = Ctx(nc)
    m = cx.m
    dr = lambda n, s, d, k="ExternalInput": m.dram(n, s, d, k)
    cx.consts(dr("ident", [128, 128], F32))
    cx.eps_col = m.sb("eps", [128, 1], F32)
    m.memset(cx.eps_col[:], EPS)
    io = {"odd_idx": odd_idx,
          "hq": dr("hq", [512, TM], BF16), "hz": dr("hz", [2, 512, TM], F32), "hv": dr("hv", [TM, 512], BF16),
          "hg": dr("hg", [512, TM], BF16), "hu": dr("hu", [3, 512, TM], BF16),
          "lbl": dr("lbl", [128, 2, 2, 4], F32), "ng": dr("ng", [128, 4], F32),
          "hmasks": dr("hmasks", [128, 4, 128], F32),
          "cw": dr("cw", [128, 3, 3, 4], F32), "cb": dr("cb", [128, 3, 4], F32), "skip": dr("skip", [128, 4], F32),
          "fw1": dr("fw1", [33, 64], F32), "fw2": dr("fw2", [64, 64], F32), "fw3": dr("fw3", [64, 64], F32),
          "fb": dr("fb", [64, 4], F32), "fwo": dr("fwo", [64, 2, 512], F32), "delt": dr("delt", [128, 512], F32),
          "zT_l": dr("zT_l", [33, 2048], F32), "tcol_l": dr("tcol_l", [128, 16], F32),
          "dftF_l": dr("dftF_l", [2048, 2, 2048], BF16), "dftI_l": dr("dftI_l", [2048, 2, 2048], BF16),
          "myT": dr("myT", [8, 128, TM], BF16, "ExternalOutput")}
    if ctx_out:
        io.update({"zT_c": dr("zT_c", [33, 256], F32), "tcol_c": dr("tcol_c", [128, 2], F32),
                   "dftF_c": dr("dftF_c", [256, 2, 256], BF16), "dftI_c": dr("dftI_c", [256, 2, 256], BF16)})
    ystage = [m.sb("ystage%d" % i, [128, TM], BF16) for i in range(2)]
    emit_hgrn(cx, io, ctx_out, ystage)
    emit_hyena(cx, io, ctx_out, ystage)
    m.emit()
    return nc


_HY_CONST = {}


def hyena_consts(n):
    if n in _HY_CONST:
        return _HY_CONST[n]
    pos = np.arange(n, dtype=np.float32)
    t = (pos / np.float32(max(n - 1, 1))).astype(np.float32)
    bands = np.linspace(1e-4, 15, 16, dtype=np.float32)
    ang = (np.float32(2.0 * math.pi / n) * pos[:, None] * bands[None, :]).astype(np.float32)
    z = np.concatenate([t[:, None], np.cos(ang), -np.sin(ang)], -1).astype(np.float32)
    zT = np.ascontiguousarray(z.T)
    tcol = np.ascontiguousarray(t.reshape(-1, 128).T)
    f = np.arange(n, dtype=np.float64) + 0.5
    M = np.arange(n, dtype=np.float64)[:, None] * (2.0 * math.pi * f[None, :] / (2.0 * n))
    c, s = np.cos(M), np.sin(M)
    dftF = np.stack([c, s], 1).astype(NPBF)
    dftI = np.stack([c.T / n, s.T / n], 1).astype(NPBF)
    nt = n // 128
    nfb = max(1, nt // 4)
    fw = (nt // nfb) * 128
    tbw = min(512, n)
    dftF = np.ascontiguousarray(dftF.reshape(nt, 128, 2, nfb, fw).transpose(3, 2, 1, 0, 4)).reshape(nfb, 2, 128, nt * fw)
    dftI = np.ascontiguousarray(dftI.reshape(nt, 128, 2, n // tbw, tbw).transpose(3, 2, 1, 0, 4)).reshape(n // tbw, 2, 128, nt * tbw)
    _HY_CONST[n] = (zT, tcol, dftF, dftI)
    return _HY_CONST[n]


def hgrn_masks():
    i = np.arange(128)
    s, t = i[:, None], i[None, :]
    same32 = (s // 32) == (t // 32)
    same64 = (s // 64) == (t // 64)
    first = lambda x: (x % 64) < 32
    m1f = same32 & (s <= t)
    m2f = same64 & first(s) & ~first(t)
    m1b = same32 & (s >= t)
    m2b = same64 & ~first(s) & first(t)
    return np.ascontiguousarray(np.stack([m1f, m2f, m1b, m2b], 1)).astype(np.float32)


def hyena_deltas(r):
    mx = math.log(1e-2) / 0.3
    mn = math.log(1e-2) / 1.5
    d = np.abs(np.linspace(mn, mx, 1024, dtype=np.float32))
    return np.ascontiguousarray(np.broadcast_to(d[512 * r:512 * r + 512][None, :], (128, 512))).astype(np.float32)


def assemble_Modd(o0, o1, r, inp, o, ctx_out):
    cs = slice(512 * r, 512 * r + 512)
    hz = np.stack([canon_cols(o0["zT"][d * 1024 + 512 * r:d * 1024 + 512 * r + 512],
                              o1["zT"][d * 1024 + 512 * r:d * 1024 + 512 * r + 512]) for d in range(2)], 0)
    hu = np.stack([canon_cols(o0["uT"][g * 1024 + 512 * r:g * 1024 + 512 * r + 512],
                              o1["uT"][g * 1024 + 512 * r:g * 1024 + 512 * r + 512]) for g in range(3)], 0)
    lbl = np.stack([np.stack([fm_vec(inp["hgrn_lb_logits"][d, oo, cs]) for oo in range(2)], 1) for d in range(2)], 1)
    cwv = inp["hy_conv_w"][o]
    cw = np.stack([np.stack([fm_vec(cwv[tap, g * 1024 + 512 * r:g * 1024 + 512 * r + 512]) for g in range(3)], 1)
                   for tap in range(3)], 1)
    cbv = inp["hy_conv_b"][o]
    cb = np.stack([fm_vec(cbv[g * 1024 + 512 * r:g * 1024 + 512 * r + 512]) for g in range(3)], 1)
    wo = inp["hy_filt_wout"][o]
    zl, tl, fl, il = hyena_consts(2048)
    d = {"ident": np.eye(128, dtype=np.float32),
         "hq": canon_cols(o0["qT"][cs], o1["qT"][cs]), "hz": np.ascontiguousarray(hz),
         "hv": canon_rows(o0["vi"][:, cs], o1["vi"][:, cs]),
         "hg": canon_cols(o0["gT"][cs], o1["gT"][cs]), "hu": np.ascontiguousarray(hu),
         "lbl": np.ascontiguousarray(lbl).astype(np.float32), "ng": fm_vec(inp["hgrn_norm_g"][o][cs]),
         "hmasks": hgrn_masks(),
         "cw": np.ascontiguousarray(cw).astype(np.float32), "cb": np.ascontiguousarray(cb).astype(np.float32),
         "skip": fm_vec(inp["hy_skip"][o][cs]),
         "fw1": np.ascontiguousarray(inp["hy_filt_w1"][o]), "fw2": np.ascontiguousarray(inp["hy_filt_w2"][o]),
         "fw3": np.ascontiguousarray(inp["hy_filt_w3"][o]),
         "fb": np.ascontiguousarray(np.stack([inp["hy_filt_b1"][o], inp["hy_filt_b2"][o], inp["hy_filt_b3"][o],
                                              inp["hy_filt_freq"][o]], 1)).astype(np.float32),
         "fwo": np.ascontiguousarray(np.stack([wo[:, cs], wo[:, 1024 + 512 * r:1024 + 512 * r + 512]], 1)),
         "delt": hyena_deltas(r), "zT_l": zl, "tcol_l": tl, "dftF_l": fl, "dftI_l": il}
    if ctx_out:
        zc, tc, fc_, ic = hyena_consts(256)
        d.update({"zT_c": zc, "tcol_c": tc, "dftF_c": fc_, "dftI_c": ic})
    return d


def build_fused_program(depth=DEPTH):
    nc = bass.Bass("TRN2", target_bir_lowering=False)
    cx = Ctx(nc)
    m = cx.m
    ext = lambda n, s, d: m.dram(n, s, d, "ExternalInput")
    itn = lambda n, s, d: m.dram(n, s, d, "Internal")
    ident_d = ext("ident", [128, 128], F32)
    cx.consts(ident_d)
    cx.eps_col = m.sb("eps", [128, 1], F32)
    m.memset(cx.eps_col[:], EPS)
    h0_d = ext("h0", [2, 128, KC, TL], F32)
    cT_d = ext("cT", [128, KC, 2], F32)
    ada_w_d = ext("ada_w", [DEPTH, 24, 128, KC * 512], F32)
    ada_b_d = ext("ada_b", [DEPTH, 6 * D], F32)
    gmix_d = ext("gmix", [128, DEPTH, KC], F32)
    gmlp_d = ext("gmlp", [128, DEPTH, KC], F32)
    w_out_d = ext("w_out", [DEPTH, 8, 128, 4096], F32)
    w1_d = ext("mlp_w1", [DEPTH, 32, 128, 4096], F32)
    w2_d = ext("mlp_w2", [DEPTH, 32, 128, 4096], F32)
    evw_d = ext("ev_w_in", [2, EV_NEXT // 256, 128, 4096], F32)
    ukv_d = ext("w_ukv", [2, 2, 128, 4096], F32)
    odw_d = ext("od_w_in", [2, 32, 128, 4096], F32)
    cos_d = ext("cosT", [2, 64, TL], F32)
    sin_d = ext("sinT", [2, 64, TL], F32)
    kvg_d = ext("kvg", [2, 128, 4], F32)
    fg_d = ext("fg", [128, KC], F32)
    nab_d = ext("nabias", [2, 8, 128, 2880], F32)
    namask_d = ext("namask", [128, 2880], F32)
    lbl_d = ext("lbl", [2, 128, 2, 2, 4], F32)
    ng_d = ext("ng", [2, 2, 128, 4], F32)
    hmasks_d = ext("hmasks", [128, 4, 128], F32)
    cw_d = ext("cw", [2, 2, 128, 3, 3, 4], F32)
    cb_d = ext("cb", [2, 2, 128, 3, 4], F32)
    skip_d = ext("skip", [2, 2, 128, 4], F32)
    fw1_d = ext("fw1", [2, 33, 64], F32)
    fw2_d = ext("fw2", [2, 64, 64], F32)
    fw3_d = ext("fw3", [2, 64, 64], F32)
    fb_d = ext("fb", [2, 64, 4], F32)
    fwo_d = ext("fwo", [2, 2, 64, 2, 512], F32)
    delt_d = ext("delt", [2, 128, 512], F32)
    hyc = {"zT_l": ext("zT_l", [33, 2048], F32), "tcol_l": ext("tcol_l", [128, 16], F32),
           "dftF_l": ext("dftF_l", [4, 2, 128, 8192], BF16), "dftI_l": ext("dftI_l", [4, 2, 128, 8192], BF16),
           "zT_c": ext("zT_c", [33, 256], F32), "tcol_c": ext("tcol_c", [128, 2], F32),
           "dftF_c": ext("dftF_c", [1, 2, 128, 512], BF16), "dftI_c": ext("dftI_c", [1, 2, 128, 512], BF16)}
    outT_d = m.dram("outT", [2, 128, KC, 1024], F32, "ExternalOutput")
    hT_d = [itn("hT_%d" % v, [128, KC, TL], F32) for v in range(2)]
    RE = [{"qT": itn("re_qT%d" % v, [8, 192, TL], BF16), "kT": itn("re_kT%d" % v, [8, 128, TL], BF16),
           "kpeT": itn("re_kpeT%d" % v, [64, TL], BF16), "v": itn("re_v%d" % v, [TL, 1024], BF16),
           "qnT": itn("re_qnT%d" % v, [1024, TL], BF16), "knT": itn("re_knT%d" % v, [1024, TL], BF16),
           "vn": itn("re_vn%d" % v, [TL, 1024], BF16)} for v in range(2)]
    RO = [{"qT": itn("ro_qT%d" % v, [1024, TL], BF16), "vi": itn("ro_vi%d" % v, [TL, 1024], BF16),
           "zT": itn("ro_zT%d" % v, [2048, TL], F32), "gT": itn("ro_gT%d" % v, [1024, TL], BF16),
           "uT": itn("ro_uT%d" % v, [3072, TL], BF16)} for v in range(2)]
    ME = {"mq": itn("me_mq", [4, 192, TM], BF16), "mk": itn("me_mk", [4, 128, TM], BF16),
          "mkpe": itn("me_mkpe", [64, TM], BF16), "mv": itn("me_mv", [TM, 512], BF16),
          "mqn": itn("me_mqn", [512, TM], BF16), "mkn": itn("me_mkn", [512, TM], BF16),
          "mvn": itn("me_mvn", [TM, 512], BF16)}
    MO = {"hq": itn("mo_hq", [512, TM], BF16), "hz": itn("mo_hz", [2, 512, TM], F32),
          "hv": itn("mo_hv", [TM, 512], BF16), "hg": itn("mo_hg", [512, TM], BF16),
          "hu": itn("mo_hu", [3, 512, TM], BF16)}
    myT_d = [itn("myT_%d" % v, [8, 128, TM], BF16) for v in range(2)]

    def ccols(dst, s0, s1):
        if len(dst.shape) == 3:
            m.dma(dst[:, :, 0:128], s0[:, :, 1024:TL])
            m.dma(dst[:, :, 128:256], s1[:, :, 1024:TL])
            m.dma(dst[:, :, 256:1280], s0[:, :, 0:1024])
            m.dma(dst[:, :, 1280:TM], s1[:, :, 0:1024])
        else:
            m.dma(dst[:, 0:128], s0[:, 1024:TL])
            m.dma(dst[:, 128:256], s1[:, 1024:TL])
            m.dma(dst[:, 256:1280], s0[:, 0:1024])
            m.dma(dst[:, 1280:TM], s1[:, 0:1024])

    def crows(dst, s0, s1):
        m.dma(dst[0:128], s0[1024:TL])
        m.dma(dst[128:256], s1[1024:TL])
        m.dma(dst[256:1280], s0[0:1024])
        m.dma(dst[1280:TM], s1[0:1024])

    modall = m.sb("modall", [128, DEPTH, 128, 2], F32)
    emit_mod(cx, modall[:, :, 0:96, :], cT_d, ada_w_d, ada_b_d, 96, 2)
    emit_gsc(cx, modall, gmix_d, gmlp_d)
    base_mark = m.mark()
    for l in range(depth + 1):
        last = (l == depth)
        for v in range(2):
            m.release(base_mark)
            hT = m.sb("hT", [128, KC, TL], F32)
            m.dma(hT[:], h0_d[v] if l == 0 else hT_d[v])
            actT = m.sb("actT", [128, KC, TL], BF16)
            rstd = m.sb("rstd", [128, TL], F32)
            cx.init_wring(4)
            if l >= 1:
                def load_y(actT_, v=v):
                    for f in range(16):
                        own, loc = (f // 4, f % 4) if f < 8 else ((f - 8) // 4, 4 + (f - 8) % 4)
                        srcT = myT_d[own][loc]
                        m.dma(actT_[:, f, 0:1024], srcT[:, 256 + 1024 * v:256 + 1024 * (v + 1)])
                        m.dma(actT_[:, f, 1024:TL], srcT[:, 128 * v:128 * (v + 1)])
                io = {"load_y": load_y, "w_out": w_out_d[l - 1], "w1": w1_d[l - 1], "w2": w2_d[l - 1]}
                emit_post(cx, l - 1, hT, actT, rstd, modall, io, TL if not last or depth < DEPTH else 1024)
            if not last and l % 2 == 0:
                e = l // 2
                io = dict(RE[v])
                io.update({"w_in": evw_d[e], "w_ukv": ukv_d[e], "cosT": cos_d[v], "sinT": sin_d[v], "kvg": kvg_d[e]})
                emit_pre_even(cx, l, hT, actT, rstd, modall, io)
            elif not last:
                io = dict(RO[v])
                io.update({"w_in": odw_d[l // 2]})
                emit_pre_odd(cx, l, hT, actT, rstd, modall, io)
            if not last:
                m.dma(hT_d[v], hT[:])
            else:
                emit_final(cx, hT, rstd, {"fg": fg_d, "outT": outT_d[v]})
        if last:
            break
        ctx_out = l < DEPTH - 1
        for v in range(2):
            m.release(base_mark)
            hs = slice(4 * v, 4 * v + 4)
            cs = slice(512 * v, 512 * v + 512)
            if l % 2 == 0:
                e = l // 2
                ccols(ME["mq"], RE[0]["qT"][hs], RE[1]["qT"][hs])
                ccols(ME["mk"], RE[0]["kT"][hs], RE[1]["kT"][hs])
                ccols(ME["mkpe"], RE[0]["kpeT"], RE[1]["kpeT"])
                crows(ME["mv"], RE[0]["v"][:, cs], RE[1]["v"][:, cs])
                ccols(ME["mqn"], RE[0]["qnT"][cs], RE[1]["qnT"][cs])
                ccols(ME["mkn"], RE[0]["knT"][cs], RE[1]["knT"][cs])
                crows(ME["mvn"], RE[0]["vn"][:, cs], RE[1]["vn"][:, cs])
                io = dict(ME)
                io.update({"nabias": nab_d[e, hs], "namask": namask_d, "myT": myT_d[v]})
                emit_M_even(cx, io, ctx_out)
            else:
                o = l // 2
                ccols(MO["hq"], RO[0]["qT"][cs], RO[1]["qT"][cs])
                for d in range(2):
                    zs = slice(d * 1024 + 512 * v, d * 1024 + 512 * v + 512)
                    ccols(MO["hz"][d], RO[0]["zT"][zs], RO[1]["zT"][zs])
                crows(MO["hv"], RO[0]["vi"][:, cs], RO[1]["vi"][:, cs])
                ccols(MO["hg"], RO[0]["gT"][cs], RO[1]["gT"][cs])
                for g in range(3):
                    us = slice(g * 1024 + 512 * v, g * 1024 + 512 * v + 512)
                    ccols(MO["hu"][g], RO[0]["uT"][us], RO[1]["uT"][us])
                io = dict(MO)
                io.update(hyc)
                io.update({"odd_idx": o, "lbl": lbl_d[v], "ng": ng_d[o, v], "hmasks": hmasks_d,
                           "cw": cw_d[o, v], "cb": cb_d[o, v], "skip": skip_d[o, v],
                           "fw1": fw1_d[o], "fw2": fw2_d[o], "fw3": fw3_d[o], "fb": fb_d[o],
                           "fwo": fwo_d[o, v], "delt": delt_d[v], "myT": myT_d[v]})
                ystage = [m.sb("ystage%d" % i, [128, TM], BF16) for i in range(2)]
                emit_hgrn(cx, io, ctx_out, ystage)
                emit_hyena(cx, io, ctx_out, ystage)
    m.emit()
    return nc


_FUSED = {}


def host_inputs(inp, b):
    x, ctx, c, c_ctx = inp["x"], inp["ctx"], inp["c"], inp["c_ctx"]
    h0 = np.stack([fm_tokens(np.concatenate([x[b, r * 1024:(r + 1) * 1024], ctx[b, r * 128:(r + 1) * 128]], 0))
                   for r in range(2)], 0)
    d = {"h0": np.ascontiguousarray(h0), "cT": np.ascontiguousarray(np.stack([fm_vec(c[b]), fm_vec(c_ctx)], -1))}
    return d


def slotify(W, nkc, k_blocks=1):
    K, N = W.shape
    ncb = 4096 // nkc
    assert K == k_blocks * nkc * 128 and N % ncb == 0
    a = W.reshape(k_blocks, nkc, 128, N // ncb, ncb).transpose(0, 3, 2, 1, 4)
    return np.ascontiguousarray(a).reshape(k_blocks * (N // ncb), 128, nkc * ncb)


def host_shared(inp):
    adaw = inp["ada_w"].reshape(DEPTH, KC, 128, 24, 512).transpose(0, 3, 2, 1, 4)
    sh = {"ident": np.eye(128, dtype=np.float32),
          "ada_w": np.ascontiguousarray(adaw).reshape(DEPTH, 24, 128, KC * 512), "ada_b": inp["ada_b"],
          "gmix": np.ascontiguousarray(np.stack([fm_vec(inp["norm_mix_g"][l]) for l in range(DEPTH)], 1)),
          "gmlp": np.ascontiguousarray(np.stack([fm_vec(inp["norm_mlp_g"][l]) for l in range(DEPTH)], 1)),
          "w_out": np.stack([slotify(inp["w_out"][l], KC) for l in range(DEPTH)], 0),
          "mlp_w1": np.stack([slotify(inp["mlp_w1"][l], KC) for l in range(DEPTH)], 0),
          "mlp_w2": np.stack([slotify(inp["mlp_w2"][l], 4, 16) for l in range(DEPTH)], 0),
          "ev_w_in": np.stack([slotify(ev_w_in_ext(inp["ev_w_in"][e]), KC) for e in range(2)], 0),
          "w_ukv": np.stack([slotify(ukv_ext(inp["mla_w_ukv"][e]), 4) for e in range(2)], 0),
          "od_w_in": np.stack([slotify(inp["od_w_in"][o], KC) for o in range(2)], 0),
          "kvg": np.stack([fm_vec(inp["mla_kv_norm_g"][e]) for e in range(2)], 0),
          "fg": fm_vec(inp["final_norm_g"])}
    rt = [rope_tables(r) for r in range(2)]
    sh["cosT"] = np.stack([rt[0][0], rt[1][0]], 0)
    sh["sinT"] = np.stack([rt[0][1], rt[1][1]], 0)
    nabs = [na_bias_host(inp["na_rel_bias"][e]) for e in range(2)]
    sh["nabias"] = np.ascontiguousarray(np.stack([nabs[0][0], nabs[1][0]], 0))
    sh["namask"] = nabs[0][1]
    f32 = lambda a: np.ascontiguousarray(a).astype(np.float32)
    lg = inp["hgrn_lb_logits"]
    sh["lbl"] = f32(np.stack([np.stack([np.stack([fm_vec(lg[d, oo, 512 * v:512 * v + 512]) for oo in range(2)], 1)
                                        for d in range(2)], 1) for v in range(2)], 0))
    sh["ng"] = f32(np.stack([np.stack([fm_vec(inp["hgrn_norm_g"][o][512 * v:512 * v + 512]) for v in range(2)], 0)
                             for o in range(2)], 0))
    sh["hmasks"] = hgrn_masks()
    cwl, cbl, skl, fwol = [], [], [], []
    for o in range(2):
        cwv, cbv, wo = inp["hy_conv_w"][o], inp["hy_conv_b"][o], inp["hy_filt_wout"][o]
        cwl.append(np.stack([np.stack([np.stack([fm_vec(cwv[tap, g * 1024 + 512 * v:g * 1024 + 512 * v + 512])
                                                 for g in range(3)], 1) for tap in range(3)], 1) for v in range(2)], 0))
        cbl.append(np.stack([np.stack([fm_vec(cbv[g * 1024 + 512 * v:g * 1024 + 512 * v + 512]) for g in range(3)], 1)
                             for v in range(2)], 0))
        skl.append(np.stack([fm_vec(inp["hy_skip"][o][512 * v:512 * v + 512]) for v in range(2)], 0))
        fwol.append(np.stack([np.stack([wo[:, 512 * v:512 * v + 512], wo[:, 1024 + 512 * v:1024 + 512 * v + 512]], 1)
                              for v in range(2)], 0))
    sh["cw"], sh["cb"], sh["skip"], sh["fwo"] = f32(np.stack(cwl, 0)), f32(np.stack(cbl, 0)), f32(np.stack(skl, 0)), f32(np.stack(fwol, 0))
    sh["fw1"], sh["fw2"], sh["fw3"] = f32(inp["hy_filt_w1"]), f32(inp["hy_filt_w2"]), f32(inp["hy_filt_w3"])
    sh["fb"] = f32(np.stack([inp["hy_filt_b1"], inp["hy_filt_b2"], inp["hy_filt_b3"], inp["hy_filt_freq"]], -1))
    sh["delt"] = np.stack([hyena_deltas(v) for v in range(2)], 0)
    zl, tl, fl, il = hyena_consts(2048)
    zc, tc, fc_, ic = hyena_consts(256)
    sh.update({"zT_l": zl, "tcol_l": tl, "dftF_l": fl, "dftI_l": il, "zT_c": zc, "tcol_c": tc, "dftF_c": fc_, "dftI_c": ic})
    return sh


def kernel(**inp):
    inp = {k: np.asarray(v) for k, v in inp.items()}
    if "nc" not in _FUSED:
        _FUSED["nc"] = build_fused_program()
    sh = host_shared(inp)
    per_b = [host_inputs(inp, b) for b in range(4)]
    maps = []
    for core in range(8):
        d = dict(sh)
        d.update(per_b[core % 4])
        maps.append(d)
    res = run_bass_kernel_spmd(_FUSED["nc"], maps, core_ids=list(range(8)))
    out = np.zeros((4, SEQ, D), dtype=np.float32)
    for b in range(4):
        oT = np.asarray(res.results[b]["outT"])
        for r in range(2):
            out[b, r * 1024:(r + 1) * 1024] = oT[r].transpose(2, 1, 0).reshape(1024, D)
    return out
```

```python
import math
import numpy as np
import ml_dtypes
import concourse.bass as bass
import concourse.mybir as mybir
from concourse.bass_utils import run_bass_kernel_spmd

F32 = mybir.dt.float32
BF16 = mybir.dt.bfloat16
AF = mybir.ActivationFunctionType
ALU = mybir.AluOpType
AX = mybir.AxisListType
NPBF = ml_dtypes.bfloat16

D = 2048
SEQ = 2048
CTX = 256
DEPTH = 4
TL = 1152
TM = 2304
KC = 16
HID = 8192
EPS = 1e-6
GRID_W = 64

ENGS = ("pe", "act", "dve", "pool", "sp")
DMA_RING = 12
SB_BASE = 16640
SB_END = 229376


def _dsize(dt):
    return 4 if dt == F32 else 2


class Op:
    __slots__ = ("eng", "fn", "waits", "idx", "is_dma", "dma_sem", "dma_val", "needs_inc")


class MK:
    def __init__(self, nc):
        self.nc = nc
        self.ops = {e: [] for e in ENGS}
        self.acc = {}
        self.waited = {e: {} for e in ENGS}
        self.dma_count = {e: 0 for e in ENGS}
        self.dma_ring_uses = {e: [0] * DMA_RING for e in ENGS}
        self.sb_base = {}
        self.sb_top = SB_BASE
        self.psum = [nc.alloc_psum_tensor("bank%d" % i, [128, 512], F32) for i in range(8)]
        self.n_dram = 0

    def sb(self, name, shape, dtype, at=None):
        per = _dsize(dtype)
        for s in shape[1:]:
            per *= int(s)
        if at is None:
            at = self.sb_top
            self.sb_top = (at + per + 31) // 32 * 32
        assert at % 32 == 0 and at + per <= SB_END, (name, at, per)
        t = self.nc.alloc_sbuf_tensor_at(name, list(shape), dtype, offset=at)
        self.sb_base[t.name] = (at, per)
        return t

    def mark(self):
        return self.sb_top

    def release(self, mark):
        self.sb_top = mark

    def dram(self, name, shape, dtype, kind="Internal"):
        return self.nc.dram_tensor(name, list(shape), dtype, kind=kind).ap()

    def _box(self, ap):
        t = ap.tensor
        space = str(ap.space)
        off = int(ap.offset)
        pairs = [(int(s), int(c)) for (s, c) in ap.ap]
        ds = _dsize(ap.dtype)
        if space == "PSUM":
            return ("P:" + t.name, 0, 128, 0, 1)
        if space == "SB":
            base, per = self.sb_base[t.name]
            P = per // ds
            plo = off // P
            flo = off % P
            pext = 0
            fext = 0
            for (s, c) in pairs:
                if c <= 1:
                    continue
                if s != 0 and s % P == 0:
                    pext += (c - 1) * (s // P)
                else:
                    fext += (c - 1) * abs(s)
            return ("SB", plo, plo + pext + 1, base + flo * ds, base + (flo + fext + 1) * ds)
        ext = 0
        for (s, c) in pairs:
            if c <= 1:
                continue
            ext += (c - 1) * abs(s)
        return ("D:" + t.name, 0, 1, off, off + ext + 1)

    @staticmethod
    def _overlap(a, b):
        return a[1] < b[2] and b[1] < a[2] and a[3] < b[4] and b[3] < a[4]

    @staticmethod
    def _contains(a, b):
        return a[1] <= b[1] and b[2] <= a[2] and a[3] <= b[3] and b[4] <= a[4]

    def _need(self, op, tok):
        e = op.eng
        if tok[0] == "e":
            _, x, idx = tok
            if x == e and e == "pe":
                return
            key = ("e", x)
            val = idx
        else:
            _, q, slot, useno = tok
            key = ("d", q, slot)
            val = useno
        w = self.waited[e]
        if w.get(key, -1) >= val:
            return
        w[key] = val
        op.waits.append((key, val))

    BUCKET = 2048

    def _split(self, b):
        if b[0] != "SB":
            return [b]
        out = []
        lo, hi = b[3], b[4]
        k = lo // self.BUCKET
        while k * self.BUCKET < hi:
            out.append((("SB", k), b[1], b[2], max(lo, k * self.BUCKET), min(hi, (k + 1) * self.BUCKET)))
            k += 1
        return out

    def _track(self, op, reads, writes, tok):
        rb = []
        wb = []
        for a in reads:
            b = self._box(a)
            if b[0].startswith("P:"):
                wb.append(b)
            else:
                rb.extend(self._split(b))
        for a in writes:
            wb.extend(self._split(self._box(a)))
        for b in rb:
            lst = self.acc.get(b[0])
            if lst:
                for rec in lst:
                    if rec[1] == "w" and self._overlap(rec[0], b):
                        self._need(op, rec[2])
        for b in wb:
            lst = self.acc.get(b[0])
            if lst:
                for rec in lst:
                    if self._overlap(rec[0], b):
                        self._need(op, rec[2])
        for b in rb:
            lst = self.acc.setdefault(b[0], [])
            if tok[0] == "e":
                lst[:] = [r for r in lst if not (r[1] == "r" and r[2][0] == "e" and r[2][1] == tok[1]
                                                 and self._contains(b, r[0]))]
            lst.append([b, "r", tok])
        for b in wb:
            lst = self.acc.setdefault(b[0], [])
            lst[:] = [r for r in lst if not self._contains(b, r[0])]
            lst.append([b, "w", tok])

    def op(self, eng, fn, reads, writes):
        o = Op()
        o.eng = eng
        o.fn = fn
        o.waits = []
        o.is_dma = False
        o.needs_inc = False
        o.idx = len(self.ops[eng])
        self._track(o, reads, writes, ("e", eng, o.idx))
        self.ops[eng].append(o)
        return o

    def dma(self, out, in_, q="sp", **kw):
        o = Op()
        o.eng = q
        o.waits = []
        o.is_dma = True
        o.needs_inc = False
        o.idx = len(self.ops[q])
        n = self.dma_count[q]
        self.dma_count[q] = n + 1
        slot = n % DMA_RING
        prev = self.dma_ring_uses[q][slot]
        if prev > 0:
            self._need(o, ("d", q, slot, prev))
        self.dma_ring_uses[q][slot] = prev + 1
        o.dma_sem = (q, slot)
        o.dma_val = prev + 1
        o.fn = lambda eng, out=out, in_=in_, kw=kw: eng.dma_start(out=out, in_=in_, **kw)
        self._track(o, [in_], [out], ("d", q, slot, prev + 1))
        self.ops[q].append(o)
        return o

    def matmul(self, out, lhsT, rhs, start=True, stop=True, **kw):
        return self.op("pe", lambda e: e.matmul(out, lhsT, rhs, start=start, stop=stop, **kw),
                       [lhsT, rhs], [out])

    def transpose(self, out, in_, ident):
        return self.op("pe", lambda e: e.transpose(out, in_, ident), [in_, ident], [out])

    def act(self, out, in_, func, bias=None, scale=None, accum_out=None):
        reads = [in_]
        kw = {}
        if bias is not None:
            kw["bias"] = bias
            if not isinstance(bias, (int, float)):
                reads.append(bias)
        if scale is not None:
            kw["scale"] = scale
            if not isinstance(scale, (int, float)):
                reads.append(scale)
        writes = [out]
        if accum_out is not None:
            kw["accum_out"] = accum_out
            writes.append(accum_out)
        return self.op("act", lambda e: e.activation(out, in_, func, **kw), reads, writes)

    def tt(self, out, in0, in1, op, eng="dve"):
        return self.op(eng, lambda e: e.tensor_tensor(out, in0, in1, op), [in0, in1], [out])

    def ts(self, out, in0, s1, s2, op0, op1=None, eng="dve"):
        reads = [in0]
        if not isinstance(s1, (int, float)):
            reads.append(s1)
        if s2 is not None and not isinstance(s2, (int, float)):
            reads.append(s2)
        if op1 is None:
            return self.op(eng, lambda e: e.tensor_scalar(out, in0, s1, None, op0), reads, [out])
        return self.op(eng, lambda e: e.tensor_scalar(out, in0, s1, s2, op0, op1), reads, [out])

    def stt(self, out, in0, scalar, in1, op0, op1):
        reads = [in0, in1]
        if not isinstance(scalar, (int, float)):
            reads.append(scalar)
        return self.op("dve", lambda e: e.scalar_tensor_tensor(out, in0, scalar, in1, op0, op1),
                       reads, [out])

    def copy(self, out, in_, eng="dve"):
        if eng == "act":
            return self.op("act", lambda e: e.copy(out, in_), [in_], [out])
        return self.op(eng, lambda e: e.tensor_copy(out, in_), [in_], [out])

    def memset(self, ap, val, eng="dve"):
        return self.op(eng, lambda e: e.memset(ap, val), [], [ap])

    def recip(self, out, in_):
        return self.op("dve", lambda e: e.reciprocal(out, in_), [in_], [out])

    def scan(self, out, d0, d1, init, op0, op1):
        reads = [d0, d1]
        if not isinstance(init, (int, float)):
            reads.append(init)
        return self.op("dve", lambda e: e.tensor_tensor_scan(out, d0, d1, init, op0, op1), reads, [out])

    def emit(self):
        nc = self.nc
        for e in ENGS:
            for o in self.ops[e]:
                for (key, val) in o.waits:
                    if key[0] == "e":
                        self.ops[key[1]][val].needs_inc = True
        cnt = {}
        for e in ENGS:
            c = 0
            arr = []
            for o in self.ops[e]:
                if o.needs_inc and not o.is_dma:
                    c += 1
                arr.append(c)
            cnt[e] = arr
        from contextlib import ExitStack
        with ExitStack() as st:
            esem = {e: st.enter_context(nc.semaphore("s_" + e)) for e in ENGS}
            dsem = {}
            for q in ENGS:
                for s in range(min(DMA_RING, self.dma_count[q])):
                    dsem[(q, s)] = st.enter_context(nc.semaphore("d_%s_%d" % (q, s)))
            block = st.enter_context(nc.Block())
            engmap = {"pe": "tensor", "act": "scalar", "dve": "vector", "pool": "gpsimd", "sp": "sync"}

            def make(e):
                def body(eng):
                    for o in self.ops[e]:
                        ws = [((esem[key[1]], cnt[key[1]][val]) if key[0] == "e" else
                               (dsem[(key[1], key[2])], 16 * val)) for (key, val) in o.waits]
                        attach = None
                        if ws and not o.is_dma:
                            attach = ws.pop()
                        for (s_, v_) in ws:
                            eng.wait_ge(s_, v_)
                        ins = o.fn(eng)
                        if attach is not None:
                            ins._wait_ge(attach[0], attach[1])
                        if o.is_dma:
                            ins.then_inc(dsem[o.dma_sem], 16)
                        elif o.needs_inc:
                            ins.then_inc(esem[e], 1)
                    for s in range(min(DMA_RING, self.dma_count[e])):
                        eng.wait_ge(dsem[(e, s)], 16 * self.dma_ring_uses[e][s])
                return body

            for e in ENGS:
                if len(self.ops[e]) == 0:
                    continue
                getattr(block, engmap[e])(make(e))
        return nc


MOD_SH_A, MOD_SC_A, MOD_G_A, MOD_SH_M, MOD_SC_M, MOD_G_M, MOD_GSC_A, MOD_GSC_M = [16 * i for i in range(8)]


class Ctx:
    def __init__(self, nc):
        self.nc = nc
        self.m = MK(nc)
        self.wring = None
        self.wi = 0
        self.ev = 0

    def consts(self, ident_d, need_bf=True):
        m = self.m
        self.ident = m.sb("ident", [128, 128], F32)
        m.dma(self.ident[:], ident_d)
        self.ones_bf = m.sb("ones_bf", [128, 128], BF16)
        m.memset(self.ones_bf[:], 1.0)
        self.ident_bf = m.sb("ident_bf", [128, 128], BF16)
        m.copy(self.ident_bf[:], self.ident[:])

    def init_wring(self, nslot=5):
        m = self.m
        self.wring = [m.sb("wslot%d" % i, [128, 4096], BF16) for i in range(nslot)]
        self.wi = 0

    def wload(self, Ws, si, kcb, ncb, nuse=None):
        assert kcb * ncb <= 4096
        slot = self.wring[self.wi % len(self.wring)]
        self.wi += 1
        view = slot[:, 0:kcb * ncb].rearrange("p (k n) -> p k n", n=ncb)
        src = Ws[si].rearrange("p (k n) -> p k n", n=ncb)
        if nuse is not None and nuse < ncb:
            self.m.dma(view[:, :, 0:nuse], src[:, :, 0:nuse], q="pool")
        else:
            self.m.dma(view, src, q="pool")
        return view

    def evac_eng(self):
        self.ev += 1
        return "act" if self.ev % 2 == 0 else "dve"


def rstd_compute(cx, src, nch, T, nfeat, rstd, sq, ps_bank, tmp):
    m = cx.m
    ps = cx.m.psum[ps_bank]
    for t0 in range(0, T, 384):
        tw = min(384, T - t0)
        for k in range(nch):
            m.act(sq[:, k, 0:tw], src[:, k, t0:t0 + tw], AF.Square)
        for k in range(nch):
            m.matmul(ps[:, 0:tw], cx.ones_bf[:, 0:128], sq[:, k, 0:tw], start=(k == 0), stop=(k == nch - 1))
        m.act(tmp[:, 0:tw], ps[:, 0:tw], AF.Sqrt, bias=cx.eps_col[:, 0:1], scale=1.0 / nfeat)
        m.recip(rstd[:, t0:t0 + tw], tmp[:, 0:tw])


def norm_modulate(cx, hT, actT, rstd, modall, l, gsc_base, sh_base, tmp4):
    m = cx.m
    for k in range(KC):
        for (c0, c1, col) in ((0, 1024, 0), (1024, TL, 1)):
            gs = modall[:, l, gsc_base + k, col:col + 1]
            sh = modall[:, l, sh_base + k, col:col + 1]
            m.stt(tmp4[:, c0:c1], hT[:, k, c0:c1], gs, rstd[:, c0:c1], ALU.mult, ALU.mult)
            m.act(actT[:, k, c0:c1], tmp4[:, c0:c1], AF.Identity, bias=sh, scale=1.0)


def proj_fm(cx, W2d, ncols_total, col0, chunk_cols, inT, nkc, consume, T=TL, bank_sets=((0, 1, 2), (3, 4, 5))):
    m = cx.m
    ncb = 4096 // nkc
    nchunks = ncols_total // chunk_cols
    per_slot = ncb // chunk_cols
    ci = 0
    tbs = [(t0, min(384, T - t0)) for t0 in range(0, T, 384)]
    assert col0 % ncb == 0
    for s0 in range(0, nchunks, per_slot):
        nhere = min(per_slot, nchunks - s0)
        slot = cx.wload(W2d, (col0 + s0 * chunk_cols) // ncb, nkc, ncb, nhere * chunk_cols)
        for j in range(nhere):
            banks = bank_sets[ci % len(bank_sets)]
            for k in range(nkc):
                for ti, (t0, tw) in enumerate(tbs):
                    m.matmul(m.psum[banks[ti]][0:chunk_cols, 0:tw],
                             slot[:, k, j * chunk_cols:(j + 1) * chunk_cols],
                             inT[:, k, t0:t0 + tw], start=(k == 0), stop=(k == nkc - 1))
            for ti, (t0, tw) in enumerate(tbs):
                consume(ci, ti, m.psum[banks[ti]][0:chunk_cols, 0:tw], t0, tw)
            ci += 1


def emit_mod(cx, out, cT_d, ada_w_d, ada_b_d, nch, nr):
    m = cx.m
    mk = m.mark()
    sc = m.sb("mod_sc", [128, KC, nr], F32)
    scb = m.sb("mod_scb", [128, KC, nr], BF16)
    m.dma(sc[:], cT_d)
    m.act(sc[:], sc[:], AF.Silu)
    m.copy(scb[:], sc[:])
    adabT = m.sb("mod_adabT", [128, DEPTH, nch], F32)
    btmp = m.sb("mod_btmp", [nch, 128], F32)
    for l in range(DEPTH):
        m.dma(btmp[:], ada_b_d[l].rearrange("(c p) -> c p", p=128))
        m.transpose(m.psum[6][:, 0:nch], btmp[:], cx.ident[0:nch, 0:nch])
        m.copy(adabT[:, l, :], m.psum[6][:, 0:nch])
    wsl = [m.sb("mod_w%d" % i, [128, KC, 512], BF16) for i in range(4)]
    wi = 0
    for l in range(DEPTH):
        ps = m.psum[l % 2]
        for nb in range(nch // 4):
            slot = wsl[wi % 4]
            wi += 1
            m.dma(slot[:], ada_w_d[l, nb].rearrange("p (k n) -> p k n", n=512), q="pool")
            for jj in range(4):
                j = nb * 4 + jj
                for k in range(KC):
                    m.matmul(ps[:, nr * j:nr * j + nr], slot[:, k, jj * 128:(jj + 1) * 128], scb[:, k, :],
                             start=(k == 0), stop=(k == KC - 1))
        m.tt(out[:, l, :, :], ps[:, 0:nch * nr].rearrange("p (c t) -> p c t", t=nr),
             adabT[:, l, :].unsqueeze(2).to_broadcast([128, nch, nr]), ALU.add)
    m.release(mk)


def emit_gsc(cx, modall, gmix_d, gmlp_d):
    m = cx.m
    mk = m.mark()
    gmix = m.sb("mod_gmix", [128, DEPTH, KC], F32)
    gmlp = m.sb("mod_gmlp", [128, DEPTH, KC], F32)
    m.dma(gmix[:], gmix_d)
    m.dma(gmlp[:], gmlp_d)
    for l in range(DEPTH):
        m.stt(modall[:, l, MOD_GSC_A:MOD_GSC_A + 16, :], modall[:, l, MOD_SC_A:MOD_SC_A + 16, :], 1.0,
              gmix[:, l, :].unsqueeze(2).to_broadcast([128, KC, 2]), ALU.add, ALU.mult)
        m.stt(modall[:, l, MOD_GSC_M:MOD_GSC_M + 16, :], modall[:, l, MOD_SC_M:MOD_SC_M + 16, :], 1.0,
              gmlp[:, l, :].unsqueeze(2).to_broadcast([128, KC, 2]), ALU.add, ALU.mult)
    m.release(mk)


def build_mod_program():
    nc = bass.Bass("TRN2", target_bir_lowering=False)
    cx = Ctx(nc)
    m = cx.m
    ident_d = m.dram("ident", [128, 128], F32, "ExternalInput")
    cT_d = m.dram("cT", [128, KC, 5], F32, "ExternalInput")
    ada_w_d = m.dram("ada_w", [DEPTH, D, 1536], F32, "ExternalInput")
    ada_b_d = m.dram("ada_b", [DEPTH, 1536], F32, "ExternalInput")
    out_d = m.dram("modpart", [128, DEPTH, 12, 5], F32, "ExternalOutput")
    cx.consts(ident_d)
    outt = m.sb("modpart", [128, DEPTH, 12, 5], F32)
    emit_mod(cx, outt, cT_d, ada_w_d, ada_b_d, 12, 5)
    m.dma(out_d, outt[:])
    m.emit()
    return nc


EV_QNOPE, EV_CKV, EV_QNA, EV_KNA, EV_PE, EV_VNA, EV_NEXT = 0, 1024, 1536, 2560, 3584, 4864, 5888
SEGS = ((0, 1024, 0), (1024, TL, 1))


def proj_tm(cx, W2d, k0, nkc, col0, ncols, inT, consume, T, banks=(6, 7)):
    m = cx.m
    ncb = 4096 // nkc
    bi = 0
    assert col0 % ncb == 0 and k0 == 0
    for c0 in range(0, ncols, ncb):
        cw = min(ncb, ncols - c0)
        slot = cx.wload(W2d, (col0 + c0) // ncb, nkc, ncb, cw)
        for t in range(T // 128):
            for n0 in range(0, cw, 512):
                nw = min(512, cw - n0)
                ps = m.psum[banks[bi % 2]]
                bi += 1
                for k in range(nkc):
                    m.matmul(ps[:, 0:nw], inT[:, k, t * 128:(t + 1) * 128], slot[:, k, n0:n0 + nw],
                             start=(k == 0), stop=(k == nkc - 1))
                consume(t, c0 + n0, nw, ps[:, 0:nw])


def rmw_consumer(cx, hT, modall, lidx, gate_base):
    m = cx.m

    def consume(ci, ti, ps, t0, tw):
        for (c0, c1, col) in SEGS:
            a, b = max(c0, t0), min(c1, t0 + tw)
            if a >= b:
                continue
            m.stt(hT[:, ci, a:b], ps[:, a - t0:b - t0], modall[:, lidx, gate_base + ci, col:col + 1],
                  hT[:, ci, a:b], ALU.mult, ALU.add)
    return consume


def emit_post(cx, lp, hT, actT, rstd, modall, io, T):
    m = cx.m
    if "load_y" in io:
        io["load_y"](actT)
    else:
        m.dma(actT[:], io["yT"])
    proj_fm(cx, io["w_out"], D, 0, 128, actT, KC, rmw_consumer(cx, hT, modall, lp, MOD_G_A), T=T)
    mk = m.mark()
    sq = m.sb("sq", [128, KC, 384], BF16)
    tmp4 = m.sb("tmp4", [128, TL], F32)
    tmp = m.sb("tmp", [128, 384], F32)
    rstd_compute(cx, hT, KC, T, D, rstd, sq, 6, tmp)
    norm_modulate(cx, hT, actT, rstd, modall, lp, MOD_GSC_M, MOD_SH_M, tmp4)
    m.release(mk)
    mk = m.mark()
    hid = [m.sb("hid%d" % i, [128, 4, TL], BF16) for i in range(2)]
    rtmp = [m.sb("rtmp%d" % i, [128, 384], F32) for i in range(2)]
    rmw = rmw_consumer(cx, hT, modall, lp, MOD_G_M)
    cnt = [0]
    tbs = [(t0, min(384, T - t0)) for t0 in range(0, T, 384)]
    for hb in range(HID // 512):
        hbuf = hid[hb % 2]

        def relu2(ci, ti, ps, t0, tw, hbuf=hbuf):
            r = rtmp[cnt[0] % 2]
            cnt[0] += 1
            m.act(r[:, 0:tw], ps, AF.Relu)
            m.tt(hbuf[:, ci, t0:t0 + tw], r[:, 0:tw], r[:, 0:tw], ALU.mult, eng="dve")
        proj_fm(cx, io["w1"], 512, hb * 512, 128, actT, KC, relu2, T=T)
        for half in range(2):
            slot = cx.wload(io["w2"], hb * 2 + half, 4, 1024)
            for j in range(8):
                oc = half * 8 + j
                banks = ((0, 1, 2), (3, 4, 5))[oc % 2]
                for k in range(4):
                    for ti, (t0, tw) in enumerate(tbs):
                        m.matmul(m.psum[banks[ti]][:, 0:tw], slot[:, k, j * 128:(j + 1) * 128],
                                 hbuf[:, k, t0:t0 + tw], start=(k == 0), stop=(k == 3))
                for ti, (t0, tw) in enumerate(tbs):
                    rmw(oc, ti, m.psum[banks[ti]][:, 0:tw], t0, tw)
    m.release(mk)


def stager(cx, stage, dst_of_chunk, kind="copy"):
    m = cx.m

    def consume(ci, ti, ps, t0, tw):
        st = stage[ci % len(stage)]
        rows = ps.shape[0]
        if kind == "silu":
            m.act(st[0:rows, t0:t0 + tw], ps, AF.Silu)
        else:
            if cx.evac_eng() == "act":
                m.act(st[0:rows, t0:t0 + tw], ps, AF.Identity)
            else:
                m.copy(st[0:rows, t0:t0 + tw], ps)
        if t0 + tw >= TL:
            m.dma(dst_of_chunk(ci), st[0:rows, :])
    return consume


def emit_pre_even(cx, l, hT, actT, rstd, modall, io):
    m = cx.m
    e = l // 2
    mk = m.mark()
    sq = m.sb("sq", [128, KC, 384], BF16)
    tmp4 = m.sb("tmp4", [128, TL], F32)
    tmp = m.sb("tmp", [128, 384], F32)
    rstd_compute(cx, hT, KC, TL, D, rstd, sq, 6, tmp)
    norm_modulate(cx, hT, actT, rstd, modall, l, MOD_GSC_A, MOD_SH_A, tmp4)
    m.release(mk)
    mk = m.mark()
    stage = [m.sb("stg%d" % i, [128, TL], BF16) for i in range(4)]
    ckvT = m.sb("ckvT", [128, 4, TL], F32)
    kvnT = m.sb("kvnT", [128, 4, TL], BF16)
    cosT = m.sb("cosT", [64, TL], F32)
    sinT = m.sb("sinT", [64, TL], F32)
    rt = [m.sb("ropet%d" % i, [64, 384], F32) for i in range(2)]
    sq4 = m.sb("sq4", [128, 4, 384], BF16)
    tmp = m.sb("tmpb", [128, 384], F32)
    kvg = m.sb("kvg", [128, 4], F32)
    m.dma(cosT[:], io["cosT"])
    m.dma(sinT[:], io["sinT"])
    m.dma(kvg[:], io["kvg"])
    W = io["w_in"]
    proj_fm(cx, W, 1024, EV_QNOPE, 128, actT, KC, stager(cx, stage, lambda ci: io["qT"][ci, 0:128, :]))

    def ckv_c(ci, ti, ps, t0, tw):
        m.copy(ckvT[:, ci, t0:t0 + tw], ps, eng=cx.evac_eng())
    proj_fm(cx, W, 512, EV_CKV, 128, actT, KC, ckv_c)
    proj_fm(cx, W, 1024, EV_QNA, 128, actT, KC, stager(cx, stage, lambda ci: io["qnT"][ci * 128:(ci + 1) * 128, :]))
    proj_fm(cx, W, 1024, EV_KNA, 128, actT, KC, stager(cx, stage, lambda ci: io["knT"][ci * 128:(ci + 1) * 128, :]))

    keep = {}

    def rope_c(ci, ti, ps, t0, tw):
        if ci % 2 == 0:
            keep[ti] = ps
            return
        p = ci // 2
        st = stage[p % len(stage)]
        m.tt(rt[0][:, 0:tw], keep[ti], cosT[:, t0:t0 + tw], ALU.mult)
        m.tt(rt[1][:, 0:tw], ps, sinT[:, t0:t0 + tw], ALU.mult)
        m.tt(st[0:64, t0:t0 + tw], rt[0][:, 0:tw], rt[1][:, 0:tw], ALU.add)
        if t0 + tw >= TL:
            dst = io["qT"][p, 128:192, :] if p < 8 else io["kpeT"]
            m.dma(dst, st[0:64, :])
    proj_fm(cx, W, 18 * 64, EV_PE, 64, actT, KC, rope_c)

    def vna_c(t, c0, nw, ps):
        st = stage[(t + c0 // 256) % len(stage)]
        m.copy(st[:, 0:nw], ps, eng=cx.evac_eng())
        m.dma(io["vn"][t * 128:(t + 1) * 128, c0:c0 + nw], st[:, 0:nw])
    proj_tm(cx, W, 0, KC, EV_VNA, 1024, actT, vna_c, TL)

    rstd_compute(cx, ckvT, 4, TL, 512, rstd, sq4, 6, tmp)
    for k in range(4):
        m.stt(ckvT[:, k, :], ckvT[:, k, :], kvg[:, k:k + 1], rstd[:, :], ALU.mult, ALU.mult)
        m.copy(kvnT[:, k, :], ckvT[:, k, :], eng="act")
    proj_fm(cx, io["w_ukv"], 1024, 0, 128, kvnT, 4, stager(cx, stage, lambda ci: io["kT"][ci, :, :]))

    def v_c(t, c0, nw, ps):
        st = stage[(t + c0 // 512) % len(stage)]
        m.copy(st[:, 0:nw], ps, eng=cx.evac_eng())
        m.dma(io["v"][t * 128:(t + 1) * 128, c0:c0 + nw], st[:, 0:nw])
    proj_tm(cx, io["w_ukv"], 0, 4, 1024, 1024, kvnT, v_c, TL)
    m.release(mk)


def emit_pre_odd(cx, l, hT, actT, rstd, modall, io):
    m = cx.m
    mk = m.mark()
    sq = m.sb("sq", [128, KC, 384], BF16)
    tmp4 = m.sb("tmp4", [128, TL], F32)
    tmp = m.sb("tmp", [128, 384], F32)
    rstd_compute(cx, hT, KC, TL, D, rstd, sq, 6, tmp)
    norm_modulate(cx, hT, actT, rstd, modall, l, MOD_GSC_A, MOD_SH_A, tmp4)
    m.release(mk)
    mk = m.mark()
    stage = [m.sb("stg%d" % i, [128, TL], BF16) for i in range(4)]
    stage32 = [m.sb("stg32_%d" % i, [128, TL], F32) for i in range(2)]
    W = io["w_in"]
    proj_fm(cx, W, 1024, 0, 128, actT, KC,
            stager(cx, stage, lambda ci: io["qT"][ci * 128:(ci + 1) * 128, :], kind="silu"))

    def vi_c(t, c0, nw, ps):
        st = stage[(t + c0 // 256) % len(stage)]
        m.copy(st[:, 0:nw], ps, eng=cx.evac_eng())
        m.dma(io["vi"][t * 128:(t + 1) * 128, c0:c0 + nw], st[:, 0:nw])
    proj_tm(cx, W, 0, KC, 1024, 1024, actT, vi_c, TL)
    proj_fm(cx, W, 2048, 2048, 128, actT, KC,
            stager(cx, stage32, lambda ci: io["zT"][ci * 128:(ci + 1) * 128, :]))
    proj_fm(cx, W, 1024, 4096, 128, actT, KC,
            stager(cx, stage, lambda ci: io["gT"][ci * 128:(ci + 1) * 128, :], kind="silu"))
    proj_fm(cx, W, 3072, 5120, 128, actT, KC,
            stager(cx, stage, lambda ci: io["uT"][ci * 128:(ci + 1) * 128, :]))
    m.release(mk)


def emit_final(cx, hT, rstd, io):
    m = cx.m
    mk = m.mark()
    sq = m.sb("sq", [128, KC, 384], BF16)
    tmp = m.sb("tmp", [128, 384], F32)
    fg = m.sb("fg", [128, KC], F32)
    stage32 = [m.sb("stg32_%d" % i, [128, 1024], F32) for i in range(2)]
    m.dma(fg[:], io["fg"])
    rstd_compute(cx, hT, KC, 1024, D, rstd, sq, 6, tmp)
    for k in range(KC):
        st = stage32[k % 2]
        m.stt(st[:, :], hT[:, k, 0:1024], fg[:, k:k + 1], rstd[:, 0:1024], ALU.mult, ALU.mult)
        m.dma(io["outT"][:, k, :], st[:, :])
    m.release(mk)


def build_R_program(l):
    nc = bass.Bass("TRN2", target_bir_lowering=False)
    cx = Ctx(nc)
    m = cx.m
    dr = lambda n, s, d, k="ExternalInput": m.dram(n, s, d, k)
    ident_d = dr("ident", [128, 128], F32)
    hT_d = dr("hT", [128, KC, TL], F32)
    mod_d = dr("modraw", [128, DEPTH, 96, 2], F32)
    cx.consts(ident_d)
    cx.eps_col = m.sb("eps", [128, 1], F32)
    m.memset(cx.eps_col[:], EPS)
    modall = m.sb("modall", [128, DEPTH, 128, 2], F32)
    m.dma(modall[:, :, 0:96, :], mod_d)
    emit_gsc(cx, modall, dr("gmix", [128, DEPTH, KC], F32), dr("gmlp", [128, DEPTH, KC], F32))
    hT = m.sb("hT", [128, KC, TL], F32)
    m.dma(hT[:], hT_d)
    actT = m.sb("actT", [128, KC, TL], BF16)
    rstd = m.sb("rstd", [128, TL], F32)
    cx.init_wring(4)
    if l >= 1:
        io = {"yT": dr("yT", [128, KC, TL], BF16), "w_out": dr("w_out", [D, D], F32),
              "w1": dr("w1", [D, HID], F32), "w2": dr("w2", [HID, D], F32)}
        emit_post(cx, l - 1, hT, actT, rstd, modall, io, TL if l <= 3 else 1024)
    if l <= 3 and l % 2 == 0:
        io = {"w_in": dr("w_in", [D, EV_NEXT], F32), "w_ukv": dr("w_ukv", [512, 2048], F32),
              "cosT": dr("cosT", [64, TL], F32), "sinT": dr("sinT", [64, TL], F32),
              "kvg": dr("kvg", [128, 4], F32),
              "qT": dr("qT", [8, 192, TL], BF16, "ExternalOutput"),
              "kT": dr("kT", [8, 128, TL], BF16, "ExternalOutput"),
              "kpeT": dr("kpeT", [64, TL], BF16, "ExternalOutput"),
              "v": dr("v", [TL, 1024], BF16, "ExternalOutput"),
              "qnT": dr("qnT", [1024, TL], BF16, "ExternalOutput"),
              "knT": dr("knT", [1024, TL], BF16, "ExternalOutput"),
              "vn": dr("vn", [TL, 1024], BF16, "ExternalOutput")}
        emit_pre_even(cx, l, hT, actT, rstd, modall, io)
    elif l <= 3:
        io = {"w_in": dr("w_in", [D, 8192], F32),
              "qT": dr("qT", [1024, TL], BF16, "ExternalOutput"),
              "vi": dr("vi", [TL, 1024], BF16, "ExternalOutput"),
              "zT": dr("zT", [2048, TL], F32, "ExternalOutput"),
              "gT": dr("gT", [1024, TL], BF16, "ExternalOutput"),
              "uT": dr("uT", [3072, TL], BF16, "ExternalOutput")}
        emit_pre_odd(cx, l, hT, actT, rstd, modall, io)
    if l <= 3:
        hout = dr("hT_out", [128, KC, TL], F32, "ExternalOutput")
        m.dma(hout, hT[:])
    else:
        io = {"fg": dr("fg", [128, KC], F32), "outT": dr("outT", [128, KC, 1024], F32, "ExternalOutput")}
        emit_final(cx, hT, rstd, io)
    m.emit()
    return nc


def fm_vec(v):
    return np.ascontiguousarray(np.asarray(v).reshape(-1, 128).T)


def fm_tokens(tok):
    T = tok.shape[0]
    return np.ascontiguousarray(tok.reshape(T, KC, 128).transpose(2, 1, 0))


def rope_perm_idx():
    idx = np.zeros(64, dtype=np.int64)
    for j in range(64):
        jj = j % 32
        idx[j] = j + 16 if jj < 16 else j - 16
    return idx


def ev_w_in_ext(w):
    perm = rope_perm_idx()
    cols = []
    for h in range(8):
        cols.append(np.arange(h * 192, h * 192 + 128))
    cols.append(np.arange(1536, 2048))
    cols.append(np.arange(2112, 3136))
    cols.append(np.arange(3136, 4160))
    for h in range(8):
        base = h * 192 + 128
        cols.append(base + np.arange(64))
        cols.append(base + perm)
    cols.append(2048 + np.arange(64))
    cols.append(2048 + perm)
    cols.append(np.zeros(128, dtype=np.int64))
    cols.append(np.arange(4160, 5184))
    idx = np.concatenate(cols)
    assert idx.shape[0] == EV_NEXT
    return np.ascontiguousarray(w[:, idx])


def ukv_ext(w):
    kc = np.concatenate([np.arange(h * 256, h * 256 + 128) for h in range(8)])
    vc = np.concatenate([np.arange(h * 256 + 128, h * 256 + 256) for h in range(8)])
    return np.ascontiguousarray(w[:, np.concatenate([kc, vc])])


def rope_tables(rank):
    pos = np.arange(rank * 1024, rank * 1024 + 1024)
    rows = (pos // GRID_W).astype(np.float32)
    cols = (pos % GRID_W).astype(np.float32)
    inv_freq = (np.float32(10000.0) ** (-np.arange(0, 32, 2, dtype=np.float32) / np.float32(32))).astype(np.float32)
    cosT = np.ones((64, TL), dtype=np.float32)
    sinT = np.zeros((64, TL), dtype=np.float32)
    for j in range(64):
        p = rows if j < 32 else cols
        jj = j % 32
        ang = (p * inv_freq[jj % 16]).astype(np.float32)
        cosT[j, :1024] = np.cos(ang)
        s = np.sin(ang)
        sinT[j, :1024] = -s if jj < 16 else s
    return cosT, sinT


NA_CLS = {(0, 0): 0, (1, 0): 1, (2, 0): 2, (3, 0): 3, (4, 0): 4, (5, 1): 5, (5, 0): 6, (6, 0): 7, (7, 0): 8}


def na_row_info(r):
    rs = min(max(r - 4, 0), 24)
    base = 2 * (rs // 2)
    cls = NA_CLS[(r - base, rs - base)]
    ntiles = 5 if rs - base == 1 else 4
    return base // 2, cls, ntiles


def na_tables():
    ridx = np.zeros((9, 5, 128, 64), dtype=np.int64)
    cidx = np.zeros((9, 5, 128, 64), dtype=np.int64)
    mask = np.zeros((9, 5, 128, 64), dtype=np.float32)
    p = np.arange(128)
    qc = np.arange(64)
    kcol = (p % 64)[:, None]
    col_start = np.clip(qc - 8, 0, 48)[None, :]
    col_ok = (kcol >= col_start) & (kcol < col_start + 16)
    coff = np.clip(kcol - qc[None, :], -15, 15) + 15
    for (dr, off), c in NA_CLS.items():
        for j in range(5):
            krel = 2 * j + p // 64
            inband = (krel >= off) & (krel < off + 8)
            roff = np.clip(krel - dr + 7, 0, 14)
            ridx[c, j] = roff[:, None]
            cidx[c, j] = coff
            ok = inband[:, None] & col_ok
            mask[c, j] = np.where(ok, 0.0, -30000.0)
    return ridx, cidx, mask


def dense_attn(cx, parts, vtile, key_tiles, q0, qn, scale, yT, pbuf, rsb):
    m = cx.m
    nk = len(key_tiles)
    for qb in range(q0, q0 + qn, 512):
        qw = min(512, q0 + qn - qb)
        O = m.psum[4]
        Sm = m.psum[5]

        def s_mm(i):
            sb = m.psum[i % 4]
            kt = key_tiles[i]
            for pi, (qp, kp) in enumerate(parts):
                m.matmul(sb[:, 0:qw], kp[:, kt * 128:(kt + 1) * 128], qp[:, qb:qb + qw],
                         start=(pi == 0), stop=(pi == len(parts) - 1))
        s_mm(0)
        for i in range(nk):
            if i + 1 < nk:
                s_mm(i + 1)
            P = pbuf[i % len(pbuf)]
            m.act(P[:, 0:qw], m.psum[i % 4][:, 0:qw], AF.Exp, scale=scale)
            m.matmul(O[:, 0:qw], vtile(key_tiles[i]), P[:, 0:qw], start=(i == 0), stop=(i == nk - 1))
            m.matmul(Sm[:, 0:qw], cx.ones_bf[:, 0:128], P[:, 0:qw], start=(i == 0), stop=(i == nk - 1))
        m.recip(rsb[:, 0:qw], Sm[:, 0:qw])
        m.tt(yT[:, qb:qb + qw], O[:, 0:qw], rsb[:, 0:qw], ALU.mult)


def emit_M_even(cx, io, ctx_out):
    m = cx.m
    mk = m.mark()
    LT = list(range(2, 18))
    CTt = [0, 1]
    allk = CTt + LT
    pbuf = [m.sb("pbuf%d" % i, [128, 512], BF16) for i in range(3)]
    rsb = m.sb("rsb", [128, 512], F32)
    ystage = [m.sb("ystage%d" % i, [128, TM], BF16) for i in range(2)]
    vsb = m.sb("vsb", [128, 18, 512], BF16)
    m.dma(vsb[:], io["mv"].rearrange("(t p) c -> p t c", p=128))
    kpe = m.sb("kpe", [64, TM], BF16)
    m.dma(kpe[:], io["mkpe"])
    qn_ = [m.sb("qnope%d" % i, [128, TM], BF16) for i in range(2)]
    qp_ = [m.sb("qpe%d" % i, [64, TM], BF16) for i in range(2)]
    kn_ = [m.sb("knope%d" % i, [128, TM], BF16) for i in range(2)]
    mla_scale = 192.0 ** -0.5
    c0 = 0 if ctx_out else 256
    for h in range(4):
        qn, qp, kn = qn_[h % 2], qp_[h % 2], kn_[h % 2]
        m.dma(qn[:], io["mq"][h, 0:128, :])
        m.dma(qp[:], io["mq"][h, 128:192, :])
        m.dma(kn[:], io["mk"][h])
        ys = ystage[h % 2]
        parts = [(qn, kn), (qp, kpe)]
        vt = lambda kt, h=h: vsb[:, kt, h * 128:(h + 1) * 128]
        if ctx_out:
            dense_attn(cx, parts, vt, CTt, 0, 256, mla_scale, ys, pbuf, rsb)
        dense_attn(cx, parts, vt, allk, 256, 2048, mla_scale, ys, pbuf, rsb)
        m.dma(io["myT"][h, :, c0:TM], ys[:, c0:TM])
    m.dma(vsb[:], io["mvn"].rearrange("(t p) c -> p t c", p=128))
    mask = m.sb("namask", [128, 9 * 5 * 64], F32)
    m.dma(mask[:], io["namask"])
    bias_ = [m.sb("nabias%d" % i, [128, 9 * 5 * 64], F32) for i in range(2)]
    lg_ = [m.sb("nalg%d" % i, [128, 320], F32) for i in range(4)]
    P_ = [m.sb("naP%d" % i, [128, 448], BF16) for i in range(4)]
    nrs_ = [m.sb("nars%d" % i, [128, 64], F32) for i in range(4)]
    na_scale = 128.0 ** -0.5
    it = 0
    for h in range(4):
        qn, kn = qn_[h % 2], kn_[h % 2]
        m.dma(qn[:], io["mqn"][h * 128:(h + 1) * 128, :])
        m.dma(kn[:], io["mkn"][h * 128:(h + 1) * 128, :])
        bias = bias_[h % 2]
        m.dma(bias[:], io["nabias"][h])
        m.stt(bias[:], bias[:], 1.0, mask[:], ALU.mult, ALU.add)
        bv = bias[:].rearrange("p (c x) -> p c x", c=9)
        ys = ystage[h % 2]
        vt = lambda kt, h=h: vsb[:, kt, h * 128:(h + 1) * 128]
        if ctx_out:
            dense_attn(cx, [(qn, kn)], vt, CTt, 0, 256, na_scale, ys, pbuf, rsb)
        units = []
        for r in range(32):
            j0, cls, nt = na_row_info(r)
            q0 = 256 + 64 * r
            S = m.psum[it % 4]
            O = m.psum[4 + it % 4]
            lg, P, nrs = lg_[it % 4], P_[it % 4], nrs_[it % 4]
            it += 1
            tiles = [2 + j0 + j for j in range(nt)]

            def stage_a(q0=q0, S=S, lg=lg, P=P, tiles=tiles, cls=cls, nt=nt):
                for j, kt in enumerate(tiles):
                    m.matmul(S[:, 64 * j:64 * j + 64], kn[:, kt * 128:(kt + 1) * 128], qn[:, q0:q0 + 64])
                for j, kt in enumerate(CTt):
                    m.matmul(S[:, 320 + 64 * j:384 + 64 * j], kn[:, kt * 128:(kt + 1) * 128], qn[:, q0:q0 + 64])
                m.stt(lg[:, 0:64 * nt], S[:, 0:64 * nt], na_scale, bv[:, cls, 0:64 * nt], ALU.mult, ALU.add)
                m.act(P[:, 0:64 * nt], lg[:, 0:64 * nt], AF.Exp)
                m.act(P[:, 320:448], S[:, 320:448], AF.Exp, scale=na_scale)

            def stage_b(q0=q0, O=O, P=P, nrs=nrs, tiles=tiles):
                srcs = [(kt, P[:, 64 * j:64 * j + 64]) for j, kt in enumerate(tiles)]
                srcs += [(kt, P[:, 320 + 64 * j:384 + 64 * j]) for j, kt in enumerate(CTt)]
                for i, (kt, pp) in enumerate(srcs):
                    m.matmul(O[:, 0:64], vt(kt), pp, start=(i == 0), stop=(i == len(srcs) - 1), skip_group_check=True)
                    m.matmul(O[:, 64:128], cx.ones_bf[:, 0:128], pp, start=False, stop=(i == len(srcs) - 1),
                             skip_group_check=True)
                m.recip(nrs[:, :], O[:, 64:128])
                m.tt(ys[:, q0:q0 + 64], O[:, 0:64], nrs[:, :], ALU.mult)
            units.append((stage_a, stage_b))
        LOOK = 2
        for u in range(len(units) + LOOK):
            if u < len(units):
                units[u][0]()
            if u >= LOOK:
                units[u - LOOK][1]()
        m.dma(io["myT"][4 + h, :, c0:TM], ys[:, c0:TM])
    m.release(mk)


def build_Meven_program(ctx_out):
    nc = bass.Bass("TRN2", target_bir_lowering=False)
    cx = Ctx(nc)
    m = cx.m
    dr = lambda n, s, d, k="ExternalInput": m.dram(n, s, d, k)
    cx.consts(dr("ident", [128, 128], F32))
    io = {"mq": dr("mq", [4, 192, TM], BF16), "mk": dr("mk", [4, 128, TM], BF16),
          "mkpe": dr("mkpe", [64, TM], BF16), "mv": dr("mv", [TM, 512], BF16),
          "mqn": dr("mqn", [512, TM], BF16), "mkn": dr("mkn", [512, TM], BF16),
          "mvn": dr("mvn", [TM, 512], BF16),
          "nabias": dr("nabias", [4, 128, 2880], F32), "namask": dr("namask", [128, 2880], F32),
          "myT": dr("myT", [8, 128, TM], BF16, "ExternalOutput")}
    emit_M_even(cx, io, ctx_out)
    m.emit()
    return nc


def canon_cols(a0, a1):
    return np.ascontiguousarray(np.concatenate([a0[..., 1024:], a1[..., 1024:], a0[..., :1024], a1[..., :1024]], -1))


def canon_rows(a0, a1):
    return np.ascontiguousarray(np.concatenate([a0[1024:], a1[1024:], a0[:1024], a1[:1024]], 0))


_NA_TAB = None


def na_bias_host(rel_bias_e):
    global _NA_TAB
    if _NA_TAB is None:
        _NA_TAB = na_tables()
    ridx, cidx, mask = _NA_TAB
    g = rel_bias_e[:, ridx, cidx]
    g = np.ascontiguousarray(g.transpose(0, 3, 1, 2, 4).reshape(8, 128, 2880)).astype(np.float32)
    mk = np.ascontiguousarray(mask.transpose(2, 0, 1, 3).reshape(128, 2880))
    return g, mk


def assemble_Meven(o0, o1, r, nab, namask):
    hs = slice(4 * r, 4 * r + 4)
    cs = slice(512 * r, 512 * r + 512)
    return {"ident": np.eye(128, dtype=np.float32),
            "mq": canon_cols(o0["qT"][hs], o1["qT"][hs]),
            "mk": canon_cols(o0["kT"][hs], o1["kT"][hs]),
            "mkpe": canon_cols(o0["kpeT"], o1["kpeT"]),
            "mv": canon_rows(o0["v"][:, cs], o1["v"][:, cs]),
            "mqn": canon_cols(o0["qnT"][cs], o1["qnT"][cs]),
            "mkn": canon_cols(o0["knT"][cs], o1["knT"][cs]),
            "mvn": canon_rows(o0["vn"][:, cs], o1["vn"][:, cs]),
            "nabias": np.ascontiguousarray(nab[hs]), "namask": namask}


def assemble_y(m0, m1, r):
    full = np.zeros((16, 128, TM), dtype=m0["myT"].dtype)
    for rr, mm in ((0, m0), (1, m1)):
        full[4 * rr:4 * rr + 4] = mm["myT"][0:4]
        full[8 + 4 * rr:8 + 4 * rr + 4] = mm["myT"][4:8]
    loc = np.concatenate([full[:, :, 256 + 1024 * r:256 + 1024 * (r + 1)], full[:, :, 128 * r:128 * (r + 1)]], -1)
    return np.ascontiguousarray(loc.transpose(1, 0, 2))


NCH = 36


def hgrn_sigma(d, c):
    if d == 0:
        return c
    return 3 - c if c < 4 else 35 - (c - 4)


def emit_hgrn(cx, io, ctx_out, ystage):
    m = cx.m
    mk = m.mark()
    T = TM
    BIG = 2.0e17
    vtok = m.sb("hg_vtok", [128, 18, 512], BF16)
    m.dma(vtok[:], io["hv"].rearrange("(t p) c -> p t c", p=128))
    lbl = m.sb("hg_lbl", [128, 2, 2, 4], F32)
    m.dma(lbl[:], io["lbl"])
    lb = m.sb("hg_lb", [128, 2, 4], F32)
    oml = m.sb("hg_oml", [128, 2, 4], F32)
    if io["odd_idx"] == 0:
        m.memset(lb[:], 0.0)
    else:
        m.tt(lb[:], lbl[:, :, 1, :], lbl[:, :, 0, :], ALU.subtract)
        m.act(lb[:], lb[:], AF.Sigmoid)
    m.ts(oml[:], lb[:], -1.0, 1.0, ALU.mult, ALU.add)
    ng = m.sb("hg_ng", [128, 4], F32)
    m.dma(ng[:], io["ng"])
    masks = m.sb("hg_masks", [128, 4, 128], F32)
    m.dma(masks[:], io["hmasks"])
    ones_f = m.sb("hg_ones", [128, T], BF16)
    m.memset(ones_f[:], 1.0)
    qb = m.sb("hg_q", [128, T], BF16)
    gb = m.sb("hg_g", [128, T], BF16)
    Qi = [m.sb("hg_Qi%d" % d, [128, T], BF16) for d in range(2)]
    Ki = [m.sb("hg_Ki%d" % d, [128, T], BF16) for d in range(2)]
    Qo = [m.sb("hg_Qo%d" % d, [128, T], BF16) for d in range(2)]
    Ko = [m.sb("hg_Ko%d" % d, [128, T], BF16) for d in range(2)]
    Qs = [m.sb("hg_Qs%d" % d, [128, T], BF16) for d in range(2)]
    Sbf = [m.sb("hg_Sbf%d" % d, [128, NCH, 128], BF16) for d in range(2)]
    KlT = m.sb("hg_KlT", [128, T], BF16)
    Kltok = m.sb("hg_Kltok", [128, 18, 128], BF16)
    z = m.sb("hg_z", [128, T], F32)
    lf = m.sb("hg_lf", [128, T], F32)
    kk = m.sb("hg_k", [128, T], F32)
    ee = m.sb("hg_e", [128, T], F32)
    gaddr = m.mark()
    G = m.sb("hg_G", [128, T], F32)
    Gx = m.sb("hg_Gx", [128, T], F32)
    Dfull = m.sb("hg_Dfull", [128, 128 * NCH], F32, at=gaddr)
    Sst = m.sb("hg_Sst", [128, 128 * NCH], F32)
    Dsc = m.sb("hg_Dsc", [128, NCH], F32)
    Dtmp = m.sb("hg_Dtmp", [128, NCH], F32)
    Am = [m.sb("hg_Am%d" % i, [128, 128], BF16) for i in range(4)]
    T12 = [m.sb("hg_T12_%d" % i, [128, 256], F32) for i in range(4)]
    oT = lf
    rstd = ee
    sq1 = m.sb("hg_sq", [128, 1, 384], BF16)
    tmp = m.sb("hg_tmp", [128, 384], F32)

    v64 = lambda t_: t_[:].rearrange("p (c j) -> p c j", j=64)
    v32 = lambda t_: t_[:].rearrange("p (c j) -> p c j", j=32)
    bc64 = lambda t_, j: v64(t_)[:, :, j:j + 1].to_broadcast([128, NCH, 64])
    bc32 = lambda t_, j: v32(t_)[:, :, j:j + 1].to_broadcast([128, 2 * NCH, 32])
    c0 = 0 if ctx_out else 256
    ai = 0
    for h in range(4):
        m.dma(qb[:], io["hq"][h * 128:(h + 1) * 128, :])
        m.dma(gb[:], io["hg"][h * 128:(h + 1) * 128, :])
        for d in range(2):
            A = G if d == 0 else Gx
            sgn = 1.0 if d == 0 else -1.0
            HALVES = ((0, T // 2), (T // 2, T))
            for (ca, cb_) in HALVES:
                cs = slice(ca, cb_)
                nch_h = (cb_ - ca) // 64
                w64 = lambda t_: t_[:, cs].rearrange("p (c j) -> p c j", j=64)
                w32 = lambda t_: t_[:, cs].rearrange("p (c j) -> p c j", j=32)
                b64 = lambda t_, j: w64(t_)[:, :, j:j + 1].to_broadcast([128, nch_h, 64])
                b32 = lambda t_, j: w32(t_)[:, :, j:j + 1].to_broadcast([128, 2 * nch_h, 32])
                m.dma(z[:, cs], io["hz"][d, h * 128:(h + 1) * 128, cs])
                m.act(z[:, cs], z[:, cs], AF.Sigmoid)
                m.ts(z[:, cs], z[:, cs], oml[:, d, h:h + 1], lb[:, d, h:h + 1], ALU.mult, ALU.add)
                m.ts(z[:, cs], z[:, cs], 1e-30, None, ALU.max, eng="pool")
                m.act(lf[:, cs], z[:, cs], AF.Ln)
                m.ts(kk[:, cs], z[:, cs], -1.0, 1.0, ALU.mult, ALU.add, eng="pool")
                m.scan(G[:, cs], ones_f[:, cs], lf[:, cs], 0.0 if ca == 0 else G[:, ca - 1:ca], ALU.mult, ALU.add)
                m.tt(Gx[:, cs], G[:, cs], lf[:, cs], ALU.subtract, eng="pool")
                m.tt(w32(z), w32(A), b32(A, 15 if d == 0 else 16), ALU.subtract, eng="pool")
                m.act(ee[:, cs], z[:, cs], AF.Exp, scale=sgn)
                m.stt(Qi[d][:, cs], ee[:, cs], BIG, qb[:, cs], ALU.min, ALU.mult)
                m.act(ee[:, cs], z[:, cs], AF.Exp, scale=-sgn)
                m.stt(Ki[d][:, cs], ee[:, cs], BIG, kk[:, cs], ALU.min, ALU.mult)
                m.tt(w64(z), w64(A), b64(A, 31 if d == 0 else 32), ALU.subtract, eng="pool")
                m.act(ee[:, cs], z[:, cs], AF.Exp, scale=sgn)
                m.stt(Qo[d][:, cs], ee[:, cs], 1.0, qb[:, cs], ALU.min, ALU.mult)
                m.act(ee[:, cs], z[:, cs], AF.Exp, scale=-sgn)
                m.stt(Ko[d][:, cs], ee[:, cs], 1.0, kk[:, cs], ALU.min, ALU.mult)
                if d == 0:
                    m.tt(w64(z), w64(G), b64(Gx, 0), ALU.subtract, eng="pool")
                else:
                    m.tt(w64(z), w64(Gx), b64(G, 63), ALU.subtract, eng="pool")
                m.act(ee[:, cs], z[:, cs], AF.Exp, scale=sgn)
                m.stt(Qs[d][:, cs], ee[:, cs], 1.0, qb[:, cs], ALU.min, ALU.mult)
                if d == 0:
                    m.tt(w64(z), w64(G), b64(G, 63), ALU.subtract, eng="pool")
                else:
                    m.tt(w64(z), w64(Gx), b64(Gx, 0), ALU.subtract, eng="pool")
                m.act(ee[:, cs], z[:, cs], AF.Exp, scale=-sgn)
                m.stt(KlT[:, cs], ee[:, cs], 1.0, kk[:, cs], ALU.min, ALU.mult)
            m.tt(Dtmp[:], v64(G)[:, :, 63], v64(Gx)[:, :, 0], ALU.subtract)
            m.act(Dtmp[:], Dtmp[:], AF.Exp)
            if d == 0:
                m.copy(Dsc[:, 1:NCH], Dtmp[:, 1:NCH], eng="pool")
                m.memset(Dsc[:, 0:1], 0.0, eng="pool")
            else:
                for c in range(NCH):
                    s = hgrn_sigma(1, c)
                    if s == 0:
                        m.memset(Dsc[:, 0:1], 0.0, eng="pool")
                    else:
                        m.copy(Dsc[:, s:s + 1], Dtmp[:, c:c + 1], eng="pool")
            m.copy(Dfull[:].rearrange("p (e s) -> p e s", s=NCH),
                   Dsc[:].unsqueeze(1).to_broadcast([128, 128, NCH]), eng="pool")
            for t4 in range(0, 18, 4):
                nt = min(4, 18 - t4)
                pb = m.psum[6 + (t4 // 4) % 2][:].bitcast(BF16)
                for i in range(nt):
                    m.transpose(pb[:, i * 128:(i + 1) * 128], KlT[:, (t4 + i) * 128:(t4 + i + 1) * 128], cx.ident_bf[:])
                m.copy(Kltok[:, t4:t4 + nt, :], pb[:, 0:nt * 128].rearrange("p (t d) -> p t d", d=128),
                       eng=cx.evac_eng())
            S3 = Sst[:].rearrange("p (e s) -> p e s", s=NCH)
            for c in range(NCH):
                t, j = c // 2, c % 2
                ps = m.psum[c % 4]
                m.matmul(ps[:, 0:128], Kltok[64 * j:64 * j + 64, t, :], vtok[64 * j:64 * j + 64, t, h * 128:(h + 1) * 128])
                s = hgrn_sigma(d, c)
                m.copy(S3[:, :, s], ps[:, 0:128], eng=cx.evac_eng())
            m.scan(Sst[:], Dfull[:], Sst[:], 0.0, ALU.mult, ALU.add)
            m.copy(Sbf[d][:], Sst[:].rearrange("p (e s) -> p s e", s=NCH), eng="pool")
        mflat = masks[:].rearrange("p a t -> p (a t)")
        tiles_ = list(range(c0 // 128, 18))
        stA, stB = {}, {}
        for t in tiles_:
            def stage_a(t=t):
                ams = []
                tsl = slice(t * 128, (t + 1) * 128)
                for d in range(2):
                    psA = m.psum[2 * (t % 2) + d]
                    m.matmul(psA[:, 0:128], Ki[d][:, tsl], Qi[d][:, tsl])
                    m.matmul(psA[:, 128:256], Ko[d][:, tsl], Qo[d][:, tsl])
                    am = Am[2 * (t % 2) + d]
                    t12 = T12[2 * (t % 2) + d]
                    m.tt(t12[:], psA[:, 0:256], mflat[:, 256 * d:256 * d + 256], ALU.mult)
                    m.tt(am[:], t12[:, 0:128], t12[:, 128:256], ALU.add, eng="pool")
                    ams.append(am)
                stA[t] = ams

            def stage_b(t=t):
                ams = stA[t]
                psO = m.psum[4 + t % 2]
                mms = []
                for d in range(2):
                    mms.append((psO[:, 0:128], vtok[:, t, h * 128:(h + 1) * 128], ams[d][:]))
                    for j in range(2):
                        c = 2 * t + j
                        s = hgrn_sigma(d, c)
                        if s >= 1:
                            mms.append((psO[:, 64 * j:64 * j + 64], Sbf[d][:, s - 1, :], Qs[d][:, c * 64:(c + 1) * 64]))
                for i, (o_, l_, r_) in enumerate(mms):
                    m.matmul(o_, l_, r_, start=(i == 0), stop=(i == len(mms) - 1), skip_group_check=True)
                m.copy(oT[:, t * 128:(t + 1) * 128], psO[:, 0:128], eng=cx.evac_eng())
            stB[t] = stage_b
            stA[("f", t)] = stage_a
        for i in range(len(tiles_) + 1):
            if i < len(tiles_):
                stA[("f", tiles_[i])]()
            if i >= 1:
                stB[tiles_[i - 1]]()
        rstd_compute(cx, oT[:].rearrange("p (o t) -> p o t", o=1)[:, :, c0:T], 1, T - c0, 128, rstd, sq1, 7, tmp)
        ys = ystage[h % 2]
        m.stt(oT[:, c0:T], oT[:, c0:T], ng[:, h:h + 1], rstd[:, 0:T - c0], ALU.mult, ALU.mult)
        m.tt(ys[:, c0:T], oT[:, c0:T], gb[:, c0:T], ALU.mult, eng="pool")
        m.dma(io["myT"][h, :, c0:T], ys[:, c0:T])
    m.release(mk)


def sin_reduced(cx, out, x, rows, w, t1):
    m = cx.m
    PI = math.pi
    for _ in range(2):
        m.ts(t1[0:rows, 0:w], x, PI, -2 * PI, ALU.is_gt, ALU.mult)
        m.tt(x, x, t1[0:rows, 0:w], ALU.add)
        m.ts(t1[0:rows, 0:w], x, -PI, 2 * PI, ALU.is_lt, ALU.mult)
        m.tt(x, x, t1[0:rows, 0:w], ALU.add)
    m.act(out, x, AF.Sin)


def emit_hyena(cx, io, ctx_out, ystage):
    m = cx.m
    mk = m.mark()
    T = TM
    cw = m.sb("hy_cw", [128, 3, 3, 4], F32)
    cb = m.sb("hy_cb", [128, 3, 4], F32)
    skip = m.sb("hy_skip", [128, 4], F32)
    m.dma(cw[:], io["cw"])
    m.dma(cb[:], io["cb"])
    m.dma(skip[:], io["skip"])
    fw1 = m.sb("hy_w1", [33, 64], F32)
    fw2 = m.sb("hy_w2", [64, 64], F32)
    fw3 = m.sb("hy_w3", [64, 64], F32)
    fb = m.sb("hy_fb", [64, 4], F32)
    fwo = m.sb("hy_wo", [64, 2, 512], F32)
    m.dma(fw1[:], io["fw1"])
    m.dma(fw2[:], io["fw2"])
    m.dma(fw3[:], io["fw3"])
    m.dma(fb[:], io["fb"])
    m.dma(fwo[:], io["fwo"])
    delt = m.sb("hy_delt", [128, 512], F32)
    m.dma(delt[:], io["delt"])
    X0 = [m.sb("hy_X0_%d" % i, [128, T], BF16) for i in range(4)]
    VX = [m.sb("hy_VX_%d" % i, [128, T], BF16) for i in range(4)]
    vxtok = m.sb("hy_vxtok", [128, 18, 512], BF16)
    mB = m.mark()
    ub = [m.sb("hy_u%d" % i, [128, T], BF16) for i in range(2)]
    acc = [m.sb("hy_acc%d" % i, [128, T], F32) for i in range(2)]
    segs = ((0, 256), (256, T))
    for cc in range(4):
        def conv(g, a, u):
            m.dma(u[:], io["hu"][g, cc * 128:(cc + 1) * 128, :])
            m.ts(a[:], u[:], cw[:, 1, g, cc:cc + 1], cb[:, g, cc:cc + 1], ALU.mult, ALU.add)
            for (s0, s1) in segs:
                m.stt(a[:, s0 + 1:s1], u[:, s0:s1 - 1], cw[:, 0, g, cc:cc + 1], a[:, s0 + 1:s1], ALU.mult, ALU.add)
                m.stt(a[:, s0:s1 - 1], u[:, s0 + 1:s1], cw[:, 2, g, cc:cc + 1], a[:, s0:s1 - 1], ALU.mult, ALU.add)
        conv(1, acc[0], ub[0])
        conv(2, acc[1], ub[1])
        m.tt(VX[cc][:], acc[0][:], acc[1][:], ALU.mult, eng="pool")
        conv(0, acc[0], ub[0])
        m.copy(X0[cc][:], acc[0][:], eng="act")
    for t in range(18):
        pb = m.psum[6 + t % 2][:].bitcast(BF16)
        for cc in range(4):
            m.transpose(pb[:, cc * 128:(cc + 1) * 128], VX[cc][:, t * 128:(t + 1) * 128], cx.ident_bf[:])
        m.copy(vxtok[:, t, :], pb[:, 0:512], eng=cx.evac_eng())
    m.release(mB)
    for (name, n, tok0) in (("c", 256, 0), ("l", 2048, 256)):
        if name == "c" and not ctx_out:
            continue
        m.release(mB)
        ntile = n // 128
        nfc = n // 128
        Yc = m.sb("hy_Yc", [128, nfc, 512], BF16)
        Ys = m.sb("hy_Ys", [128, nfc, 512], BF16)
        p1 = m.sb("hy_p1", [128, 512], F32)
        p2 = m.sb("hy_p2", [128, 512], F32)
        mC = m.mark()
        hs = m.sb("hy_hs", [128, ntile, 512], BF16)
        hd = m.sb("hy_hd", [128, ntile, 512], BF16)
        mD = m.mark()
        zT = m.sb("hy_zT", [33, n], F32)
        m.dma(zT[:], io["zT_" + name])
        tcol = m.sb("hy_tcol", [128, ntile], F32)
        m.dma(tcol[:], io["tcol_" + name])
        hA = m.sb("hy_hA", [64, n], F32)
        hB = m.sb("hy_hB", [64, n], F32)
        t1 = m.sb("hy_t1", [64, 512], F32)
        layers = ((fw1, 33, zT, hA, 0), (fw2, 64, hA, hB, 1), (fw3, 64, hB, hA, 2))
        for (wt, kdim, src_, dst, bi) in layers:
            for b0 in range(0, n, 512):
                bw = min(512, n - b0)
                ps = m.psum[(b0 // 512) % 2]
                m.matmul(ps[0:64, 0:bw], wt[0:kdim, :], src_[0:kdim, b0:b0 + bw])
                m.ts(dst[:, b0:b0 + bw], ps[0:64, 0:bw], fb[:, bi:bi + 1], fb[:, 3:4], ALU.add, ALU.mult)
                sin_reduced(cx, dst[:, b0:b0 + bw], dst[:, b0:b0 + bw], 64, bw, t1)
        h3 = hA
        win = m.sb("hy_win", [128, 512], F32)
        hf = m.sb("hy_hf", [128, 512], F32)
        hb = m.sb("hy_hb", [128, 512], F32)
        ntc = m.sb("hy_ntc", [128, ntile], F32)
        m.ts(ntc[:], tcol[:], -1.0, None, ALU.mult)
        for t in range(ntile):
            m.act(win[:], delt[:], AF.Exp, scale=ntc[:, t:t + 1])
            m.matmul(m.psum[2][:, 0:512], h3[0:64, t * 128:(t + 1) * 128], fwo[0:64, 0, :])
            m.matmul(m.psum[3][:, 0:512], h3[0:64, t * 128:(t + 1) * 128], fwo[0:64, 1, :])
            m.tt(hf[:], m.psum[2][:, 0:512], win[:], ALU.mult)
            m.tt(hb[:], m.psum[3][:, 0:512], win[:], ALU.mult)
            if t == 0:
                m.memset(hb[0:1, :], 0.0)
            m.tt(hs[:, t, :], hf[:], hb[:], ALU.add, eng="pool")
            m.tt(hd[:, t, :], hf[:], hb[:], ALU.subtract, eng="pool")
        m.release(mD)
        nfb = max(1, nfc // 4)
        fcb = nfc // nfb
        Cb = [m.sb("hy_Cb%d" % i, [128, ntile, fcb * 128], BF16) for i in range(1)]
        Sb = [m.sb("hy_Sb%d" % i, [128, ntile, fcb * 128], BF16) for i in range(1)]
        Kc = m.sb("hy_Kc", [128, 512], F32)
        Ks = m.sb("hy_Ks", [128, 512], F32)
        dF = io["dftF_" + name]
        tt0 = tok0 // 128
        for fbk in range(nfb):
            C_, S_ = Cb[0], Sb[0]
            m.dma(C_[:], dF[fbk, 0].rearrange("p (k f) -> p k f", f=fcb * 128))
            m.dma(S_[:], dF[fbk, 1].rearrange("p (k f) -> p k f", f=fcb * 128))
            for fi in range(fcb):
                fc = fbk * fcb + fi
                fsl = slice(fi * 128, (fi + 1) * 128)
                for t in range(ntile):
                    m.matmul(m.psum[0][:, 0:512], C_[:, t, fsl], hs[:, t, :], start=(t == 0), stop=(t == ntile - 1))
                for t in range(ntile):
                    m.matmul(m.psum[1][:, 0:512], S_[:, t, fsl], hd[:, t, :], start=(t == 0), stop=(t == ntile - 1))
                m.copy(Kc[:], m.psum[0][:, 0:512], eng="act")
                m.copy(Ks[:], m.psum[1][:, 0:512], eng="act")
                pc, ps_ = m.psum[2 + 2 * (fc % 2)], m.psum[3 + 2 * (fc % 2)]
                for t in range(ntile):
                    m.matmul(pc[:, 0:512], C_[:, t, fsl], vxtok[:, tt0 + t, :], start=(t == 0), stop=(t == ntile - 1))
                for t in range(ntile):
                    m.matmul(ps_[:, 0:512], S_[:, t, fsl], vxtok[:, tt0 + t, :], start=(t == 0), stop=(t == ntile - 1))
                m.tt(p1[:], pc[:, 0:512], Kc[:], ALU.mult)
                m.tt(p2[:], ps_[:, 0:512], Ks[:], ALU.mult)
                m.tt(Yc[:, fc, :], p1[:], p2[:], ALU.subtract, eng="pool")
                m.tt(p1[:], pc[:, 0:512], Ks[:], ALU.mult)
                m.tt(p2[:], ps_[:, 0:512], Kc[:], ALU.mult)
                m.tt(Ys[:, fc, :], p1[:], p2[:], ALU.add, eng="pool")
        m.release(mC)
        dI = io["dftI_" + name]
        tbw = min(512, n)
        Ci = [m.sb("hy_Ci%d" % i, [128, nfc, tbw], BF16) for i in range(2)]
        Si = [m.sb("hy_Si%d" % i, [128, nfc, tbw], BF16) for i in range(2)]
        for tb in range(n // tbw):
            C_, S_ = Ci[tb % 2], Si[tb % 2]
            m.dma(C_[:], dI[tb, 0].rearrange("p (k t) -> p k t", t=tbw))
            m.dma(S_[:], dI[tb, 1].rearrange("p (k t) -> p k t", t=tbw))
            for cc in range(4):
                po = m.psum[cc % 2]
                csl = slice(cc * 128, (cc + 1) * 128)
                for fc in range(nfc):
                    m.matmul(po[:, 0:tbw], Yc[:, fc, csl], C_[:, fc, :], start=(fc == 0), stop=False)
                for fc in range(nfc):
                    m.matmul(po[:, 0:tbw], Ys[:, fc, csl], S_[:, fc, :], start=False, stop=(fc == nfc - 1))
                a0 = tok0 + tb * tbw
                ys = ystage[cc % 2]
                m.stt(p1[:, 0:tbw], VX[cc][:, a0:a0 + tbw], skip[:, cc:cc + 1], po[:, 0:tbw], ALU.mult, ALU.add)
                m.tt(ys[:, 0:tbw], p1[:, 0:tbw], X0[cc][:, a0:a0 + tbw], ALU.mult, eng="pool")
                m.dma(io["myT"][4 + cc, :, a0:a0 + tbw], ys[:, 0:tbw])
    m.release(mk)


def build_Modd_program(odd_idx, ctx_out):
    nc = bass.Bass("TRN2", target_bir_lowering=False)
    cx = Ctx(nc)
    m = cx.m
    dr = lambda n, s, d, k="ExternalInput": m.dram(n, s, d, k)
    cx.consts(dr("ident", [128, 128], F32))
    cx.eps_col = m.sb("eps", [128, 1], F32)
    m.memset(cx.eps_col[:], EPS)
    io = {"odd_idx": odd_idx,
          "hq": dr("hq", [512, TM], BF16), "hz": dr("hz", [2, 512, TM], F32), "hv": dr("hv", [TM, 512], BF16),
          "hg": dr("hg", [512, TM], BF16), "hu": dr("hu", [3, 512, TM], BF16),
          "lbl": dr("lbl", [128, 2, 2, 4], F32), "ng": dr("ng", [128, 4], F32),
          "hmasks": dr("hmasks", [128, 4, 128], F32),
          "cw": dr("cw", [128, 3, 3, 4], F32), "cb": dr("cb", [128, 3, 4], F32), "skip": dr("skip", [128, 4], F32),
          "fw1": dr("fw1", [33, 64], F32), "fw2": dr("fw2", [64, 64], F32), "fw3": dr("fw3", [64, 64], F32),
          "fb": dr("fb", [64, 4], F32), "fwo": dr("fwo", [64, 2, 512], F32), "delt": dr("delt", [128, 512], F32),
          "zT_l": dr("zT_l", [33, 2048], F32), "tcol_l": dr("tcol_l", [128, 16], F32),
          "dftF_l": dr("dftF_l", [2048, 2, 2048], BF16), "dftI_l": dr("dftI_l", [2048, 2, 2048], BF16),
          "myT": dr("myT", [8, 128, TM], BF16, "ExternalOutput")}
    if ctx_out:
        io.update({"zT_c": dr("zT_c", [33, 256], F32), "tcol_c": dr("tcol_c", [128, 2], F32),
                   "dftF_c": dr("dftF_c", [256, 2, 256], BF16), "dftI_c": dr("dftI_c", [256, 2, 256], BF16)})
    ystage = [m.sb("ystage%d" % i, [128, TM], BF16) for i in range(2)]
    emit_hgrn(cx, io, ctx_out, ystage)
    emit_hyena(cx, io, ctx_out, ystage)
    m.emit()
    return nc


_HY_CONST = {}


def hyena_consts(n):
    if n in _HY_CONST:
        return _HY_CONST[n]
    pos = np.arange(n, dtype=np.float32)
    t = (pos / np.float32(max(n - 1, 1))).astype(np.float32)
    bands = np.linspace(1e-4, 15, 16, dtype=np.float32)
    ang = (np.float32(2.0 * math.pi / n) * pos[:, None] * bands[None, :]).astype(np.float32)
    z = np.concatenate([t[:, None], np.cos(ang), -np.sin(ang)], -1).astype(np.float32)
    zT = np.ascontiguousarray(z.T)
    tcol = np.ascontiguousarray(t.reshape(-1, 128).T)
    f = np.arange(n, dtype=np.float64) + 0.5
    M = np.arange(n, dtype=np.float64)[:, None] * (2.0 * math.pi * f[None, :] / (2.0 * n))
    c, s = np.cos(M), np.sin(M)
    dftF = np.stack([c, s], 1).astype(NPBF)
    dftI = np.stack([c.T / n, s.T / n], 1).astype(NPBF)
    nt = n // 128
    nfb = max(1, nt // 4)
    fw = (nt // nfb) * 128
    tbw = min(512, n)
    dftF = np.ascontiguousarray(dftF.reshape(nt, 128, 2, nfb, fw).transpose(3, 2, 1, 0, 4)).reshape(nfb, 2, 128, nt * fw)
    dftI = np.ascontiguousarray(dftI.reshape(nt, 128, 2, n // tbw, tbw).transpose(3, 2, 1, 0, 4)).reshape(n // tbw, 2, 128, nt * tbw)
    _HY_CONST[n] = (zT, tcol, dftF, dftI)
    return _HY_CONST[n]


def hgrn_masks():
    i = np.arange(128)
    s, t = i[:, None], i[None, :]
    same32 = (s // 32) == (t // 32)
    same64 = (s // 64) == (t // 64)
    first = lambda x: (x % 64) < 32
    m1f = same32 & (s <= t)
    m2f = same64 & first(s) & ~first(t)
    m1b = same32 & (s >= t)
    m2b = same64 & ~first(s) & first(t)
    return np.ascontiguousarray(np.stack([m1f, m2f, m1b, m2b], 1)).astype(np.float32)


def hyena_deltas(r):
    mx = math.log(1e-2) / 0.3
    mn = math.log(1e-2) / 1.5
    d = np.abs(np.linspace(mn, mx, 1024, dtype=np.float32))
    return np.ascontiguousarray(np.broadcast_to(d[512 * r:512 * r + 512][None, :], (128, 512))).astype(np.float32)


def assemble_Modd(o0, o1, r, inp, o, ctx_out):
    cs = slice(512 * r, 512 * r + 512)
    hz = np.stack([canon_cols(o0["zT"][d * 1024 + 512 * r:d * 1024 + 512 * r + 512],
                              o1["zT"][d * 1024 + 512 * r:d * 1024 + 512 * r + 512]) for d in range(2)], 0)
    hu = np.stack([canon_cols(o0["uT"][g * 1024 + 512 * r:g * 1024 + 512 * r + 512],
                              o1["uT"][g * 1024 + 512 * r:g * 1024 + 512 * r + 512]) for g in range(3)], 0)
    lbl = np.stack([np.stack([fm_vec(inp["hgrn_lb_logits"][d, oo, cs]) for oo in range(2)], 1) for d in range(2)], 1)
    cwv = inp["hy_conv_w"][o]
    cw = np.stack([np.stack([fm_vec(cwv[tap, g * 1024 + 512 * r:g * 1024 + 512 * r + 512]) for g in range(3)], 1)
                   for tap in range(3)], 1)
    cbv = inp["hy_conv_b"][o]
    cb = np.stack([fm_vec(cbv[g * 1024 + 512 * r:g * 1024 + 512 * r + 512]) for g in range(3)], 1)
    wo = inp["hy_filt_wout"][o]
    zl, tl, fl, il = hyena_consts(2048)
    d = {"ident": np.eye(128, dtype=np.float32),
         "hq": canon_cols(o0["qT"][cs], o1["qT"][cs]), "hz": np.ascontiguousarray(hz),
         "hv": canon_rows(o0["vi"][:, cs], o1["vi"][:, cs]),
         "hg": canon_cols(o0["gT"][cs], o1["gT"][cs]), "hu": np.ascontiguousarray(hu),
         "lbl": np.ascontiguousarray(lbl).astype(np.float32), "ng": fm_vec(inp["hgrn_norm_g"][o][cs]),
         "hmasks": hgrn_masks(),
         "cw": np.ascontiguousarray(cw).astype(np.float32), "cb": np.ascontiguousarray(cb).astype(np.float32),
         "skip": fm_vec(inp["hy_skip"][o][cs]),
         "fw1": np.ascontiguousarray(inp["hy_filt_w1"][o]), "fw2": np.ascontiguousarray(inp["hy_filt_w2"][o]),
         "fw3": np.ascontiguousarray(inp["hy_filt_w3"][o]),
         "fb": np.ascontiguousarray(np.stack([inp["hy_filt_b1"][o], inp["hy_filt_b2"][o], inp["hy_filt_b3"][o],
                                              inp["hy_filt_freq"][o]], 1)).astype(np.float32),
         "fwo": np.ascontiguousarray(np.stack([wo[:, cs], wo[:, 1024 + 512 * r:1024 + 512 * r + 512]], 1)),
         "delt": hyena_deltas(r), "zT_l": zl, "tcol_l": tl, "dftF_l": fl, "dftI_l": il}
    if ctx_out:
        zc, tc, fc_, ic = hyena_consts(256)
        d.update({"zT_c": zc, "tcol_c": tc, "dftF_c": fc_, "dftI_c": ic})
    return d


def build_fused_program(depth=DEPTH):
    nc = bass.Bass("TRN2", target_bir_lowering=False)
    cx = Ctx(nc)
    m = cx.m
    ext = lambda n, s, d: m.dram(n, s, d, "ExternalInput")
    itn = lambda n, s, d: m.dram(n, s, d, "Internal")
    ident_d = ext("ident", [128, 128], F32)
    cx.consts(ident_d)
    cx.eps_col = m.sb("eps", [128, 1], F32)
    m.memset(cx.eps_col[:], EPS)
    h0_d = ext("h0", [2, 128, KC, TL], F32)
    cT_d = ext("cT", [128, KC, 2], F32)
    ada_w_d = ext("ada_w", [DEPTH, 24, 128, KC * 512], F32)
    ada_b_d = ext("ada_b", [DEPTH, 6 * D], F32)
    gmix_d = ext("gmix", [128, DEPTH, KC], F32)
    gmlp_d = ext("gmlp", [128, DEPTH, KC], F32)
    w_out_d = ext("w_out", [DEPTH, 8, 128, 4096], F32)
    w1_d = ext("mlp_w1", [DEPTH, 32, 128, 4096], F32)
    w2_d = ext("mlp_w2", [DEPTH, 32, 128, 4096], F32)
    evw_d = ext("ev_w_in", [2, EV_NEXT // 256, 128, 4096], F32)
    ukv_d = ext("w_ukv", [2, 2, 128, 4096], F32)
    odw_d = ext("od_w_in", [2, 32, 128, 4096], F32)
    cos_d = ext("cosT", [2, 64, TL], F32)
    sin_d = ext("sinT", [2, 64, TL], F32)
    kvg_d = ext("kvg", [2, 128, 4], F32)
    fg_d = ext("fg", [128, KC], F32)
    nab_d = ext("nabias", [2, 8, 128, 2880], F32)
    namask_d = ext("namask", [128, 2880], F32)
    lbl_d = ext("lbl", [2, 128, 2, 2, 4], F32)
    ng_d = ext("ng", [2, 2, 128, 4], F32)
    hmasks_d = ext("hmasks", [128, 4, 128], F32)
    cw_d = ext("cw", [2, 2, 128, 3, 3, 4], F32)
    cb_d = ext("cb", [2, 2, 128, 3, 4], F32)
    skip_d = ext("skip", [2, 2, 128, 4], F32)
    fw1_d = ext("fw1", [2, 33, 64], F32)
    fw2_d = ext("fw2", [2, 64, 64], F32)
    fw3_d = ext("fw3", [2, 64, 64], F32)
    fb_d = ext("fb", [2, 64, 4], F32)
    fwo_d = ext("fwo", [2, 2, 64, 2, 512], F32)
    delt_d = ext("delt", [2, 128, 512], F32)
    hyc = {"zT_l": ext("zT_l", [33, 2048], F32), "tcol_l": ext("tcol_l", [128, 16], F32),
           "dftF_l": ext("dftF_l", [4, 2, 128, 8192], BF16), "dftI_l": ext("dftI_l", [4, 2, 128, 8192], BF16),
           "zT_c": ext("zT_c", [33, 256], F32), "tcol_c": ext("tcol_c", [128, 2], F32),
           "dftF_c": ext("dftF_c", [1, 2, 128, 512], BF16), "dftI_c": ext("dftI_c", [1, 2, 128, 512], BF16)}
    outT_d = m.dram("outT", [2, 128, KC, 1024], F32, "ExternalOutput")
    hT_d = [itn("hT_%d" % v, [128, KC, TL], F32) for v in range(2)]
    RE = [{"qT": itn("re_qT%d" % v, [8, 192, TL], BF16), "kT": itn("re_kT%d" % v, [8, 128, TL], BF16),
           "kpeT": itn("re_kpeT%d" % v, [64, TL], BF16), "v": itn("re_v%d" % v, [TL, 1024], BF16),
           "qnT": itn("re_qnT%d" % v, [1024, TL], BF16), "knT": itn("re_knT%d" % v, [1024, TL], BF16),
           "vn": itn("re_vn%d" % v, [TL, 1024], BF16)} for v in range(2)]
    RO = [{"qT": itn("ro_qT%d" % v, [1024, TL], BF16), "vi": itn("ro_vi%d" % v, [TL, 1024], BF16),
           "zT": itn("ro_zT%d" % v, [2048, TL], F32), "gT": itn("ro_gT%d" % v, [1024, TL], BF16),
           "uT": itn("ro_uT%d" % v, [3072, TL], BF16)} for v in range(2)]
    ME = {"mq": itn("me_mq", [4, 192, TM], BF16), "mk": itn("me_mk", [4, 128, TM], BF16),
          "mkpe": itn("me_mkpe", [64, TM], BF16), "mv": itn("me_mv", [TM, 512], BF16),
          "mqn": itn("me_mqn", [512, TM], BF16), "mkn": itn("me_mkn", [512, TM], BF16),
          "mvn": itn("me_mvn", [TM, 512], BF16)}
    MO = {"hq": itn("mo_hq", [512, TM], BF16), "hz": itn("mo_hz", [2, 512, TM], F32),
          "hv": itn("mo_hv", [TM, 512], BF16), "hg": itn("mo_hg", [512, TM], BF16),
          "hu": itn("mo_hu", [3, 512, TM], BF16)}
    myT_d = [itn("myT_%d" % v, [8, 128, TM], BF16) for v in range(2)]

    def ccols(dst, s0, s1):
        if len(dst.shape) == 3:
            m.dma(dst[:, :, 0:128], s0[:, :, 1024:TL])
            m.dma(dst[:, :, 128:256], s1[:, :, 1024:TL])
            m.dma(dst[:, :, 256:1280], s0[:, :, 0:1024])
            m.dma(dst[:, :, 1280:TM], s1[:, :, 0:1024])
        else:
            m.dma(dst[:, 0:128], s0[:, 1024:TL])
            m.dma(dst[:, 128:256], s1[:, 1024:TL])
            m.dma(dst[:, 256:1280], s0[:, 0:1024])
            m.dma(dst[:, 1280:TM], s1[:, 0:1024])

    def crows(dst, s0, s1):
        m.dma(dst[0:128], s0[1024:TL])
        m.dma(dst[128:256], s1[1024:TL])
        m.dma(dst[256:1280], s0[0:1024])
        m.dma(dst[1280:TM], s1[0:1024])

    modall = m.sb("modall", [128, DEPTH, 128, 2], F32)
    emit_mod(cx, modall[:, :, 0:96, :], cT_d, ada_w_d, ada_b_d, 96, 2)
    emit_gsc(cx, modall, gmix_d, gmlp_d)
    base_mark = m.mark()
    for l in range(depth + 1):
        last = (l == depth)
        for v in range(2):
            m.release(base_mark)
            hT = m.sb("hT", [128, KC, TL], F32)
            m.dma(hT[:], h0_d[v] if l == 0 else hT_d[v])
            actT = m.sb("actT", [128, KC, TL], BF16)
            rstd = m.sb("rstd", [128, TL], F32)
            cx.init_wring(4)
            if l >= 1:
                def load_y(actT_, v=v):
                    for f in range(16):
                        own, loc = (f // 4, f % 4) if f < 8 else ((f - 8) // 4, 4 + (f - 8) % 4)
                        srcT = myT_d[own][loc]
                        m.dma(actT_[:, f, 0:1024], srcT[:, 256 + 1024 * v:256 + 1024 * (v + 1)])
                        m.dma(actT_[:, f, 1024:TL], srcT[:, 128 * v:128 * (v + 1)])
                io = {"load_y": load_y, "w_out": w_out_d[l - 1], "w1": w1_d[l - 1], "w2": w2_d[l - 1]}
                emit_post(cx, l - 1, hT, actT, rstd, modall, io, TL if not last or depth < DEPTH else 1024)
            if not last and l % 2 == 0:
                e = l // 2
                io = dict(RE[v])
                io.update({"w_in": evw_d[e], "w_ukv": ukv_d[e], "cosT": cos_d[v], "sinT": sin_d[v], "kvg": kvg_d[e]})
                emit_pre_even(cx, l, hT, actT, rstd, modall, io)
            elif not last:
                io = dict(RO[v])
                io.update({"w_in": odw_d[l // 2]})
                emit_pre_odd(cx, l, hT, actT, rstd, modall, io)
            if not last:
                m.dma(hT_d[v], hT[:])
            else:
                emit_final(cx, hT, rstd, {"fg": fg_d, "outT": outT_d[v]})
        if last:
            break
        ctx_out = l < DEPTH - 1
        for v in range(2):
            m.release(base_mark)
            hs = slice(4 * v, 4 * v + 4)
            cs = slice(512 * v, 512 * v + 512)
            if l % 2 == 0:
                e = l // 2
                ccols(ME["mq"], RE[0]["qT"][hs], RE[1]["qT"][hs])
                ccols(ME["mk"], RE[0]["kT"][hs], RE[1]["kT"][hs])
                ccols(ME["mkpe"], RE[0]["kpeT"], RE[1]["kpeT"])
                crows(ME["mv"], RE[0]["v"][:, cs], RE[1]["v"][:, cs])
                ccols(ME["mqn"], RE[0]["qnT"][cs], RE[1]["qnT"][cs])
                ccols(ME["mkn"], RE[0]["knT"][cs], RE[1]["knT"][cs])
                crows(ME["mvn"], RE[0]["vn"][:, cs], RE[1]["vn"][:, cs])
                io = dict(ME)
                io.update({"nabias": nab_d[e, hs], "namask": namask_d, "myT": myT_d[v]})
                emit_M_even(cx, io, ctx_out)
            else:
                o = l // 2
                ccols(MO["hq"], RO[0]["qT"][cs], RO[1]["qT"][cs])
                for d in range(2):
                    zs = slice(d * 1024 + 512 * v, d * 1024 + 512 * v + 512)
                    ccols(MO["hz"][d], RO[0]["zT"][zs], RO[1]["zT"][zs])
                crows(MO["hv"], RO[0]["vi"][:, cs], RO[1]["vi"][:, cs])
                ccols(MO["hg"], RO[0]["gT"][cs], RO[1]["gT"][cs])
                for g in range(3):
                    us = slice(g * 1024 + 512 * v, g * 1024 + 512 * v + 512)
                    ccols(MO["hu"][g], RO[0]["uT"][us], RO[1]["uT"][us])
                io = dict(MO)
                io.update(hyc)
                io.update({"odd_idx": o, "lbl": lbl_d[v], "ng": ng_d[o, v], "hmasks": hmasks_d,
                           "cw": cw_d[o, v], "cb": cb_d[o, v], "skip": skip_d[o, v],
                           "fw1": fw1_d[o], "fw2": fw2_d[o], "fw3": fw3_d[o], "fb": fb_d[o],
                           "fwo": fwo_d[o, v], "delt": delt_d[v], "myT": myT_d[v]})
                ystage = [m.sb("ystage%d" % i, [128, TM], BF16) for i in range(2)]
                emit_hgrn(cx, io, ctx_out, ystage)
                emit_hyena(cx, io, ctx_out, ystage)
    m.emit()
    return nc


_FUSED = {}


def host_inputs(inp, b):
    x, ctx, c, c_ctx = inp["x"], inp["ctx"], inp["c"], inp["c_ctx"]
    h0 = np.stack([fm_tokens(np.concatenate([x[b, r * 1024:(r + 1) * 1024], ctx[b, r * 128:(r + 1) * 128]], 0))
                   for r in range(2)], 0)
    d = {"h0": np.ascontiguousarray(h0), "cT": np.ascontiguousarray(np.stack([fm_vec(c[b]), fm_vec(c_ctx)], -1))}
    return d


def slotify(W, nkc, k_blocks=1):
    K, N = W.shape
    ncb = 4096 // nkc
    assert K == k_blocks * nkc * 128 and N % ncb == 0
    a = W.reshape(k_blocks, nkc, 128, N // ncb, ncb).transpose(0, 3, 2, 1, 4)
    return np.ascontiguousarray(a).reshape(k_blocks * (N // ncb), 128, nkc * ncb)


def host_shared(inp):
    adaw = inp["ada_w"].reshape(DEPTH, KC, 128, 24, 512).transpose(0, 3, 2, 1, 4)
    sh = {"ident": np.eye(128, dtype=np.float32),
          "ada_w": np.ascontiguousarray(adaw).reshape(DEPTH, 24, 128, KC * 512), "ada_b": inp["ada_b"],
          "gmix": np.ascontiguousarray(np.stack([fm_vec(inp["norm_mix_g"][l]) for l in range(DEPTH)], 1)),
          "gmlp": np.ascontiguousarray(np.stack([fm_vec(inp["norm_mlp_g"][l]) for l in range(DEPTH)], 1)),
          "w_out": np.stack([slotify(inp["w_out"][l], KC) for l in range(DEPTH)], 0),
          "mlp_w1": np.stack([slotify(inp["mlp_w1"][l], KC) for l in range(DEPTH)], 0),
          "mlp_w2": np.stack([slotify(inp["mlp_w2"][l], 4, 16) for l in range(DEPTH)], 0),
          "ev_w_in": np.stack([slotify(ev_w_in_ext(inp["ev_w_in"][e]), KC) for e in range(2)], 0),
          "w_ukv": np.stack([slotify(ukv_ext(inp["mla_w_ukv"][e]), 4) for e in range(2)], 0),
          "od_w_in": np.stack([slotify(inp["od_w_in"][o], KC) for o in range(2)], 0),
          "kvg": np.stack([fm_vec(inp["mla_kv_norm_g"][e]) for e in range(2)], 0),
          "fg": fm_vec(inp["final_norm_g"])}
    rt = [rope_tables(r) for r in range(2)]
    sh["cosT"] = np.stack([rt[0][0], rt[1][0]], 0)
    sh["sinT"] = np.stack([rt[0][1], rt[1][1]], 0)
    nabs = [na_bias_host(inp["na_rel_bias"][e]) for e in range(2)]
    sh["nabias"] = np.ascontiguousarray(np.stack([nabs[0][0], nabs[1][0]], 0))
    sh["namask"] = nabs[0][1]
    f32 = lambda a: np.ascontiguousarray(a).astype(np.float32)
    lg = inp["hgrn_lb_logits"]
    sh["lbl"] = f32(np.stack([np.stack([np.stack([fm_vec(lg[d, oo, 512 * v:512 * v + 512]) for oo in range(2)], 1)
                                        for d in range(2)], 1) for v in range(2)], 0))
    sh["ng"] = f32(np.stack([np.stack([fm_vec(inp["hgrn_norm_g"][o][512 * v:512 * v + 512]) for v in range(2)], 0)
                             for o in range(2)], 0))
    sh["hmasks"] = hgrn_masks()
    cwl, cbl, skl, fwol = [], [], [], []
    for o in range(2):
        cwv, cbv, wo = inp["hy_conv_w"][o], inp["hy_conv_b"][o], inp["hy_filt_wout"][o]
        cwl.append(np.stack([np.stack([np.stack([fm_vec(cwv[tap, g * 1024 + 512 * v:g * 1024 + 512 * v + 512])
                                                 for g in range(3)], 1) for tap in range(3)], 1) for v in range(2)], 0))
        cbl.append(np.stack([np.stack([fm_vec(cbv[g * 1024 + 512 * v:g * 1024 + 512 * v + 512]) for g in range(3)], 1)
                             for v in range(2)], 0))
        skl.append(np.stack([fm_vec(inp["hy_skip"][o][512 * v:512 * v + 512]) for v in range(2)], 0))
        fwol.append(np.stack([np.stack([wo[:, 512 * v:512 * v + 512], wo[:, 1024 + 512 * v:1024 + 512 * v + 512]], 1)
                              for v in range(2)], 0))
    sh["cw"], sh["cb"], sh["skip"], sh["fwo"] = f32(np.stack(cwl, 0)), f32(np.stack(cbl, 0)), f32(np.stack(skl, 0)), f32(np.stack(fwol, 0))
    sh["fw1"], sh["fw2"], sh["fw3"] = f32(inp["hy_filt_w1"]), f32(inp["hy_filt_w2"]), f32(inp["hy_filt_w3"])
    sh["fb"] = f32(np.stack([inp["hy_filt_b1"], inp["hy_filt_b2"], inp["hy_filt_b3"], inp["hy_filt_freq"]], -1))
    sh["delt"] = np.stack([hyena_deltas(v) for v in range(2)], 0)
    zl, tl, fl, il = hyena_consts(2048)
    zc, tc, fc_, ic = hyena_consts(256)
    sh.update({"zT_l": zl, "tcol_l": tl, "dftF_l": fl, "dftI_l": il, "zT_c": zc, "tcol_c": tc, "dftF_c": fc_, "dftI_c": ic})
    return sh


def kernel(**inp):
    inp = {k: np.asarray(v) for k, v in inp.items()}
    if "nc" not in _FUSED:
        _FUSED["nc"] = build_fused_program()
    sh = host_shared(inp)
    per_b = [host_inputs(inp, b) for b in range(4)]
    maps = []
    for core in range(8):
        d = dict(sh)
        d.update(per_b[core % 4])
        maps.append(d)
    res = run_bass_kernel_spmd(_FUSED["nc"], maps, core_ids=list(range(8)))
    out = np.zeros((4, SEQ, D), dtype=np.float32)
    for b in range(4):
        oT = np.asarray(res.results[b]["outT"])
        for r in range(2):
            out[b, r * 1024:(r + 1) * 1024] = oT[r].transpose(2, 1, 0).reshape(1024, D)
    return out
```

```python
import math
import numpy as np
import ml_dtypes
import concourse.bass as bass
import concourse.mybir as mybir
from concourse.bass_utils import run_bass_kernel_spmd

F32 = mybir.dt.float32
BF16 = mybir.dt.bfloat16
AF = mybir.ActivationFunctionType
ALU = mybir.AluOpType
AX = mybir.AxisListType
NPBF = ml_dtypes.bfloat16

D = 2048
SEQ = 2048
CTX = 256
DEPTH = 4
TL = 1152
TM = 2304
KC = 16
HID = 8192
EPS = 1e-6
GRID_W = 64

ENGS = ("pe", "act", "dve", "pool", "sp")
DMA_RING = 12
SB_BASE = 16640
SB_END = 229376


def _dsize(dt):
    return 4 if dt == F32 else 2


class Op:
    __slots__ = ("eng", "fn", "waits", "idx", "is_dma", "dma_sem", "dma_val", "needs_inc")


class MK:
    def __init__(self, nc):
        self.nc = nc
        self.ops = {e: [] for e in ENGS}
        self.acc = {}
        self.waited = {e: {} for e in ENGS}
        self.dma_count = {e: 0 for e in ENGS}
        self.dma_ring_uses = {e: [0] * DMA_RING for e in ENGS}
        self.sb_base = {}
        self.sb_top = SB_BASE
        self.psum = [nc.alloc_psum_tensor("bank%d" % i, [128, 512], F32) for i in range(8)]
        self.n_dram = 0

    def sb(self, name, shape, dtype, at=None):
        per = _dsize(dtype)
        for s in shape[1:]:
            per *= int(s)
        if at is None:
            at = self.sb_top
            self.sb_top = (at + per + 31) // 32 * 32
        assert at % 32 == 0 and at + per <= SB_END, (name, at, per)
        t = self.nc.alloc_sbuf_tensor_at(name, list(shape), dtype, offset=at)
        self.sb_base[t.name] = (at, per)
        return t

    def mark(self):
        return self.sb_top

    def release(self, mark):
        self.sb_top = mark

    def dram(self, name, shape, dtype, kind="Internal"):
        return self.nc.dram_tensor(name, list(shape), dtype, kind=kind).ap()

    def _box(self, ap):
        t = ap.tensor
        space = str(ap.space)
        off = int(ap.offset)
        pairs = [(int(s), int(c)) for (s, c) in ap.ap]
        ds = _dsize(ap.dtype)
        if space == "PSUM":
            return ("P:" + t.name, 0, 128, 0, 1)
        if space == "SB":
            base, per = self.sb_base[t.name]
            P = per // ds
            plo = off // P
            flo = off % P
            pext = 0
            fext = 0
            for (s, c) in pairs:
                if c <= 1:
                    continue
                if s != 0 and s % P == 0:
                    pext += (c - 1) * (s // P)
                else:
                    fext += (c - 1) * abs(s)
            return ("SB", plo, plo + pext + 1, base + flo * ds, base + (flo + fext + 1) * ds)
        ext = 0
        for (s, c) in pairs:
            if c <= 1:
                continue
            ext += (c - 1) * abs(s)
        return ("D:" + t.name, 0, 1, off, off + ext + 1)

    @staticmethod
    def _overlap(a, b):
        return a[1] < b[2] and b[1] < a[2] and a[3] < b[4] and b[3] < a[4]

    @staticmethod
    def _contains(a, b):
        return a[1] <= b[1] and b[2] <= a[2] and a[3] <= b[3] and b[4] <= a[4]

    def _need(self, op, tok):
        e = op.eng
        if tok[0] == "e":
            _, x, idx = tok
            if x == e and e == "pe":
                return
            key = ("e", x)
            val = idx
        else:
            _, q, slot, useno = tok
            key = ("d", q, slot)
            val = useno
        w = self.waited[e]
        if w.get(key, -1) >= val:
            return
        w[key] = val
        op.waits.append((key, val))

    BUCKET = 2048

    def _split(self, b):
        if b[0] != "SB":
            return [b]
        out = []
        lo, hi = b[3], b[4]
        k = lo // self.BUCKET
        while k * self.BUCKET < hi:
            out.append((("SB", k), b[1], b[2], max(lo, k * self.BUCKET), min(hi, (k + 1) * self.BUCKET)))
            k += 1
        return out

    def _track(self, op, reads, writes, tok):
        rb = []
        wb = []
        for a in reads:
            b = self._box(a)
            if b[0].startswith("P:"):
                wb.append(b)
            else:
                rb.extend(self._split(b))
        for a in writes:
            wb.extend(self._split(self._box(a)))
        for b in rb:
            lst = self.acc.get(b[0])
            if lst:
                for rec in lst:
                    if rec[1] == "w" and self._overlap(rec[0], b):
                        self._need(op, rec[2])
        for b in wb:
            lst = self.acc.get(b[0])
            if lst:
                for rec in lst:
                    if self._overlap(rec[0], b):
                        self._need(op, rec[2])
        for b in rb:
            lst = self.acc.setdefault(b[0], [])
            if tok[0] == "e":
                lst[:] = [r for r in lst if not (r[1] == "r" and r[2][0] == "e" and r[2][1] == tok[1]
                                                 and self._contains(b, r[0]))]
            lst.append([b, "r", tok])
        for b in wb:
            lst = self.acc.setdefault(b[0], [])
            lst[:] = [r for r in lst if not self._contains(b, r[0])]
            lst.append([b, "w", tok])

    def op(self, eng, fn, reads, writes):
        o = Op()
        o.eng = eng
        o.fn = fn
        o.waits = []
        o.is_dma = False
        o.needs_inc = False
        o.idx = len(self.ops[eng])
        self._track(o, reads, writes, ("e", eng, o.idx))
        self.ops[eng].append(o)
        return o

    def dma(self, out, in_, q="sp", **kw):
        o = Op()
        o.eng = q
        o.waits = []
        o.is_dma = True
        o.needs_inc = False
        o.idx = len(self.ops[q])
        n = self.dma_count[q]
        self.dma_count[q] = n + 1
        slot = n % DMA_RING
        prev = self.dma_ring_uses[q][slot]
        if prev > 0:
            self._need(o, ("d", q, slot, prev))
        self.dma_ring_uses[q][slot] = prev + 1
        o.dma_sem = (q, slot)
        o.dma_val = prev + 1
        o.fn = lambda eng, out=out, in_=in_, kw=kw: eng.dma_start(out=out, in_=in_, **kw)
        self._track(o, [in_], [out], ("d", q, slot, prev + 1))
        self.ops[q].append(o)
        return o

    def matmul(self, out, lhsT, rhs, start=True, stop=True, **kw):
        return self.op("pe", lambda e: e.matmul(out, lhsT, rhs, start=start, stop=stop, **kw),
                       [lhsT, rhs], [out])

    def transpose(self, out, in_, ident):
        return self.op("pe", lambda e: e.transpose(out, in_, ident), [in_, ident], [out])

    def act(self, out, in_, func, bias=None, scale=None, accum_out=None):
        reads = [in_]
        kw = {}
        if bias is not None:
            kw["bias"] = bias
            if not isinstance(bias, (int, float)):
                reads.append(bias)
        if scale is not None:
            kw["scale"] = scale
            if not isinstance(scale, (int, float)):
                reads.append(scale)
        writes = [out]
        if accum_out is not None:
            kw["accum_out"] = accum_out
            writes.append(accum_out)
        return self.op("act", lambda e: e.activation(out, in_, func, **kw), reads, writes)

    def tt(self, out, in0, in1, op, eng="dve"):
        return self.op(eng, lambda e: e.tensor_tensor(out, in0, in1, op), [in0, in1], [out])

    def ts(self, out, in0, s1, s2, op0, op1=None, eng="dve"):
        reads = [in0]
        if not isinstance(s1, (int, float)):
            reads.append(s1)
        if s2 is not None and not isinstance(s2, (int, float)):
            reads.append(s2)
        if op1 is None:
            return self.op(eng, lambda e: e.tensor_scalar(out, in0, s1, None, op0), reads, [out])
        return self.op(eng, lambda e: e.tensor_scalar(out, in0, s1, s2, op0, op1), reads, [out])

    def stt(self, out, in0, scalar, in1, op0, op1):
        reads = [in0, in1]
        if not isinstance(scalar, (int, float)):
            reads.append(scalar)
        return self.op("dve", lambda e: e.scalar_tensor_tensor(out, in0, scalar, in1, op0, op1),
                       reads, [out])

    def copy(self, out, in_, eng="dve"):
        if eng == "act":
            return self.op("act", lambda e: e.copy(out, in_), [in_], [out])
        return self.op(eng, lambda e: e.tensor_copy(out, in_), [in_], [out])

    def memset(self, ap, val, eng="dve"):
        return self.op(eng, lambda e: e.memset(ap, val), [], [ap])

    def recip(self, out, in_):
        return self.op("dve", lambda e: e.reciprocal(out, in_), [in_], [out])

    def scan(self, out, d0, d1, init, op0, op1):
        reads = [d0, d1]
        if not isinstance(init, (int, float)):
            reads.append(init)
        return self.op("dve", lambda e: e.tensor_tensor_scan(out, d0, d1, init, op0, op1), reads, [out])

    def emit(self):
        nc = self.nc
        for e in ENGS:
            for o in self.ops[e]:
                for (key, val) in o.waits:
                    if key[0] == "e":
                        self.ops[key[1]][val].needs_inc = True
        cnt = {}
        for e in ENGS:
            c = 0
            arr = []
            for o in self.ops[e]:
                if o.needs_inc and not o.is_dma:
                    c += 1
                arr.append(c)
            cnt[e] = arr
        from contextlib import ExitStack
        with ExitStack() as st:
            esem = {e: st.enter_context(nc.semaphore("s_" + e)) for e in ENGS}
            dsem = {}
            for q in ENGS:
                for s in range(min(DMA_RING, self.dma_count[q])):
                    dsem[(q, s)] = st.enter_context(nc.semaphore("d_%s_%d" % (q, s)))
            block = st.enter_context(nc.Block())
            engmap = {"pe": "tensor", "act": "scalar", "dve": "vector", "pool": "gpsimd", "sp": "sync"}

            def make(e):
                def body(eng):
                    for o in self.ops[e]:
                        ws = [((esem[key[1]], cnt[key[1]][val]) if key[0] == "e" else
                               (dsem[(key[1], key[2])], 16 * val)) for (key, val) in o.waits]
                        attach = None
                        if ws and not o.is_dma:
                            attach = ws.pop()
                        for (s_, v_) in ws:
                            eng.wait_ge(s_, v_)
                        ins = o.fn(eng)
                        if attach is not None:
                            ins._wait_ge(attach[0], attach[1])
                        if o.is_dma:
                            ins.then_inc(dsem[o.dma_sem], 16)
                        elif o.needs_inc:
                            ins.then_inc(esem[e], 1)
                    for s in range(min(DMA_RING, self.dma_count[e])):
                        eng.wait_ge(dsem[(e, s)], 16 * self.dma_ring_uses[e][s])
                return body

            for e in ENGS:
                if len(self.ops[e]) == 0:
                    continue
                getattr(block, engmap[e])(make(e))
        return nc


MOD_SH_A, MOD_SC_A, MOD_G_A, MOD_SH_M, MOD_SC_M, MOD_G_M, MOD_GSC_A, MOD_GSC_M = [16 * i for i in range(8)]


class Ctx:
    def __init__(self, nc):
        self.nc = nc
        self.m = MK(nc)
        self.wring = None
        self.wi = 0
        self.ev = 0

    def consts(self, ident_d, need_bf=True):
        m = self.m
        self.ident = m.sb("ident", [128, 128], F32)
        m.dma(self.ident[:], ident_d)
        self.ones_bf = m.sb("ones_bf", [128, 128], BF16)
        m.memset(self.ones_bf[:], 1.0)
        self.ident_bf = m.sb("ident_bf", [128, 128], BF16)
        m.copy(self.ident_bf[:], self.ident[:])

    def init_wring(self, nslot=5):
        m = self.m
        self.wring = [m.sb("wslot%d" % i, [128, 4096], BF16) for i in range(nslot)]
        self.wi = 0

    def wload(self, Ws, si, kcb, ncb, nuse=None):
        assert kcb * ncb <= 4096
        slot = self.wring[self.wi % len(self.wring)]
        self.wi += 1
        view = slot[:, 0:kcb * ncb].rearrange("p (k n) -> p k n", n=ncb)
        src = Ws[si].rearrange("p (k n) -> p k n", n=ncb)
        if nuse is not None and nuse < ncb:
            self.m.dma(view[:, :, 0:nuse], src[:, :, 0:nuse], q="pool")
        else:
            self.m.dma(view, src, q="pool")
        return view

    def evac_eng(self):
        self.ev += 1
        return "act" if self.ev % 2 == 0 else "dve"


def rstd_compute(cx, src, nch, T, nfeat, rstd, sq, ps_bank, tmp):
    m = cx.m
    ps = cx.m.psum[ps_bank]
    for t0 in range(0, T, 384):
        tw = min(384, T - t0)
        for k in range(nch):
            m.act(sq[:, k, 0:tw], src[:, k, t0:t0 + tw], AF.Square)
        for k in range(nch):
            m.matmul(ps[:, 0:tw], cx.ones_bf[:, 0:128], sq[:, k, 0:tw], start=(k == 0), stop=(k == nch - 1))
        m.act(tmp[:, 0:tw], ps[:, 0:tw], AF.Sqrt, bias=cx.eps_col[:, 0:1], scale=1.0 / nfeat)
        m.recip(rstd[:, t0:t0 + tw], tmp[:, 0:tw])


def norm_modulate(cx, hT, actT, rstd, modall, l, gsc_base, sh_base, tmp4):
    m = cx.m
    for k in range(KC):
        for (c0, c1, col) in ((0, 1024, 0), (1024, TL, 1)):
            gs = modall[:, l, gsc_base + k, col:col + 1]
            sh = modall[:, l, sh_base + k, col:col + 1]
            m.stt(tmp4[:, c0:c1], hT[:, k, c0:c1], gs, rstd[:, c0:c1], ALU.mult, ALU.mult)
            m.act(actT[:, k, c0:c1], tmp4[:, c0:c1], AF.Identity, bias=sh, scale=1.0)


def proj_fm(cx, W2d, ncols_total, col0, chunk_cols, inT, nkc, consume, T=TL, bank_sets=((0, 1, 2), (3, 4, 5))):
    m = cx.m
    ncb = 4096 // nkc
    nchunks = ncols_total // chunk_cols
    per_slot = ncb // chunk_cols
    ci = 0
    tbs = [(t0, min(384, T - t0)) for t0 in range(0, T, 384)]
    assert col0 % ncb == 0
    for s0 in range(0, nchunks, per_slot):
        nhere = min(per_slot, nchunks - s0)
        slot = cx.wload(W2d, (col0 + s0 * chunk_cols) // ncb, nkc, ncb, nhere * chunk_cols)
        for j in range(nhere):
            banks = bank_sets[ci % len(bank_sets)]
            for k in range(nkc):
                for ti, (t0, tw) in enumerate(tbs):
                    m.matmul(m.psum[banks[ti]][0:chunk_cols, 0:tw],
                             slot[:, k, j * chunk_cols:(j + 1) * chunk_cols],
                             inT[:, k, t0:t0 + tw], start=(k == 0), stop=(k == nkc - 1))
            for ti, (t0, tw) in enumerate(tbs):
                consume(ci, ti, m.psum[banks[ti]][0:chunk_cols, 0:tw], t0, tw)
            ci += 1


def emit_mod(cx, out, cT_d, ada_w_d, ada_b_d, nch, nr):
    m = cx.m
    mk = m.mark()
    sc = m.sb("mod_sc", [128, KC, nr], F32)
    scb = m.sb("mod_scb", [128, KC, nr], BF16)
    m.dma(sc[:], cT_d)
    m.act(sc[:], sc[:], AF.Silu)
    m.copy(scb[:], sc[:])
    adabT = m.sb("mod_adabT", [128, DEPTH, nch], F32)
    btmp = m.sb("mod_btmp", [nch, 128], F32)
    for l in range(DEPTH):
        m.dma(btmp[:], ada_b_d[l].rearrange("(c p) -> c p", p=128))
        m.transpose(m.psum[6][:, 0:nch], btmp[:], cx.ident[0:nch, 0:nch])
        m.copy(adabT[:, l, :], m.psum[6][:, 0:nch])
    wsl = [m.sb("mod_w%d" % i, [128, KC, 512], BF16) for i in range(4)]
    wi = 0
    for l in range(DEPTH):
        ps = m.psum[l % 2]
        for nb in range(nch // 4):
            slot = wsl[wi % 4]
            wi += 1
            m.dma(slot[:], ada_w_d[l, nb].rearrange("p (k n) -> p k n", n=512), q="pool")
            for jj in range(4):
                j = nb * 4 + jj
                for k in range(KC):
                    m.matmul(ps[:, nr * j:nr * j + nr], slot[:, k, jj * 128:(jj + 1) * 128], scb[:, k, :],
                             start=(k == 0), stop=(k == KC - 1))
        m.tt(out[:, l, :, :], ps[:, 0:nch * nr].rearrange("p (c t) -> p c t", t=nr),
             adabT[:, l, :].unsqueeze(2).to_broadcast([128, nch, nr]), ALU.add)
    m.release(mk)


def emit_gsc(cx, modall, gmix_d, gmlp_d):
    m = cx.m
    mk = m.mark()
    gmix = m.sb("mod_gmix", [128, DEPTH, KC], F32)
    gmlp = m.sb("mod_gmlp", [128, DEPTH, KC], F32)
    m.dma(gmix[:], gmix_d)
    m.dma(gmlp[:], gmlp_d)
    for l in range(DEPTH):
        m.stt(modall[:, l, MOD_GSC_A:MOD_GSC_A + 16, :], modall[:, l, MOD_SC_A:MOD_SC_A + 16, :], 1.0,
              gmix[:, l, :].unsqueeze(2).to_broadcast([128, KC, 2]), ALU.add, ALU.mult)
        m.stt(modall[:, l, MOD_GSC_M:MOD_GSC_M + 16, :], modall[:, l, MOD_SC_M:MOD_SC_M + 16, :], 1.0,
              gmlp[:, l, :].unsqueeze(2).to_broadcast([128, KC, 2]), ALU.add, ALU.mult)
    m.release(mk)


def build_mod_program():
    nc = bass.Bass("TRN2", target_bir_lowering=False)
    cx = Ctx(nc)
    m = cx.m
    ident_d = m.dram("ident", [128, 128], F32, "ExternalInput")
    cT_d = m.dram("cT", [128, KC, 5], F32, "ExternalInput")
    ada_w_d = m.dram("ada_w", [DEPTH, D, 1536], F32, "ExternalInput")
    ada_b_d = m.dram("ada_b", [DEPTH, 1536], F32, "ExternalInput")
    out_d = m.dram("modpart", [128, DEPTH, 12, 5], F32, "ExternalOutput")
    cx.consts(ident_d)
    outt = m.sb("modpart", [128, DEPTH, 12, 5], F32)
    emit_mod(cx, outt, cT_d, ada_w_d, ada_b_d, 12, 5)
    m.dma(out_d, outt[:])
    m.emit()
    return nc


EV_QNOPE, EV_CKV, EV_QNA, EV_KNA, EV_PE, EV_VNA, EV_NEXT = 0, 1024, 1536, 2560, 3584, 4864, 5888
SEGS = ((0, 1024, 0), (1024, TL, 1))


def proj_tm(cx, W2d, k0, nkc, col0, ncols, inT, consume, T, banks=(6, 7)):
    m = cx.m
    ncb = 4096 // nkc
    bi = 0
    assert col0 % ncb == 0 and k0 == 0
    for c0 in range(0, ncols, ncb):
        cw = min(ncb, ncols - c0)
        slot = cx.wload(W2d, (col0 + c0) // ncb, nkc, ncb, cw)
        for t in range(T // 128):
            for n0 in range(0, cw, 512):
                nw = min(512, cw - n0)
                ps = m.psum[banks[bi % 2]]
                bi += 1
                for k in range(nkc):
                    m.matmul(ps[:, 0:nw], inT[:, k, t * 128:(t + 1) * 128], slot[:, k, n0:n0 + nw],
                             start=(k == 0), stop=(k == nkc - 1))
                consume(t, c0 + n0, nw, ps[:, 0:nw])


def rmw_consumer(cx, hT, modall, lidx, gate_base):
    m = cx.m

    def consume(ci, ti, ps, t0, tw):
        for (c0, c1, col) in SEGS:
            a, b = max(c0, t0), min(c1, t0 + tw)
            if a >= b:
                continue
            m.stt(hT[:, ci, a:b], ps[:, a - t0:b - t0], modall[:, lidx, gate_base + ci, col:col + 1],
                  hT[:, ci, a:b], ALU.mult, ALU.add)
    return consume


def emit_post(cx, lp, hT, actT, rstd, modall, io, T):
    m = cx.m
    if "load_y" in io:
        io["load_y"](actT)
    else:
        m.dma(actT[:], io["yT"])
    proj_fm(cx, io["w_out"], D, 0, 128, actT, KC, rmw_consumer(cx, hT, modall, lp, MOD_G_A), T=T)
    mk = m.mark()
    sq = m.sb("sq", [128, KC, 384], BF16)
    tmp4 = m.sb("tmp4", [128, TL], F32)
    tmp = m.sb("tmp", [128, 384], F32)
    rstd_compute(cx, hT, KC, T, D, rstd, sq, 6, tmp)
    norm_modulate(cx, hT, actT, rstd, modall, lp, MOD_GSC_M, MOD_SH_M, tmp4)
    m.release(mk)
    mk = m.mark()
    hid = [m.sb("hid%d" % i, [128, 4, TL], BF16) for i in range(2)]
    rtmp = [m.sb("rtmp%d" % i, [128, 384], F32) for i in range(2)]
    rmw = rmw_consumer(cx, hT, modall, lp, MOD_G_M)
    cnt = [0]
    tbs = [(t0, min(384, T - t0)) for t0 in range(0, T, 384)]
    for hb in range(HID // 512):
        hbuf = hid[hb % 2]

        def relu2(ci, ti, ps, t0, tw, hbuf=hbuf):
            r = rtmp[cnt[0] % 2]
            cnt[0] += 1
            m.act(r[:, 0:tw], ps, AF.Relu)
            m.tt(hbuf[:, ci, t0:t0 + tw], r[:, 0:tw], r[:, 0:tw], ALU.mult, eng="dve")
        proj_fm(cx, io["w1"], 512, hb * 512, 128, actT, KC, relu2, T=T)
        for half in range(2):
            slot = cx.wload(io["w2"], hb * 2 + half, 4, 1024)
            for j in range(8):
                oc = half * 8 + j
                banks = ((0, 1, 2), (3, 4, 5))[oc % 2]
                for k in range(4):
                    for ti, (t0, tw) in enumerate(tbs):
                        m.matmul(m.psum[banks[ti]][:, 0:tw], slot[:, k, j * 128:(j + 1) * 128],
                                 hbuf[:, k, t0:t0 + tw], start=(k == 0), stop=(k == 3))
                for ti, (t0, tw) in enumerate(tbs):
                    rmw(oc, ti, m.psum[banks[ti]][:, 0:tw], t0, tw)
    m.release(mk)


def stager(cx, stage, dst_of_chunk, kind="copy"):
    m = cx.m

    def consume(ci, ti, ps, t0, tw):
        st = stage[ci % len(stage)]
        rows = ps.shape[0]
        if kind == "silu":
            m.act(st[0:rows, t0:t0 + tw], ps, AF.Silu)
        else:
            if cx.evac_eng() == "act":
                m.act(st[0:rows, t0:t0 + tw], ps, AF.Identity)
            else:
                m.copy(st[0:rows, t0:t0 + tw], ps)
        if t0 + tw >= TL:
            m.dma(dst_of_chunk(ci), st[0:rows, :])
    return consume


def emit_pre_even(cx, l, hT, actT, rstd, modall, io):
    m = cx.m
    e = l // 2
    mk = m.mark()
    sq = m.sb("sq", [128, KC, 384], BF16)
    tmp4 = m.sb("tmp4", [128, TL], F32)
    tmp = m.sb("tmp", [128, 384], F32)
    rstd_compute(cx, hT, KC, TL, D, rstd, sq, 6, tmp)
    norm_modulate(cx, hT, actT, rstd, modall, l, MOD_GSC_A, MOD_SH_A, tmp4)
    m.release(mk)
    mk = m.mark()
    stage = [m.sb("stg%d" % i, [128, TL], BF16) for i in range(4)]
    ckvT = m.sb("ckvT", [128, 4, TL], F32)
    kvnT = m.sb("kvnT", [128, 4, TL], BF16)
    cosT = m.sb("cosT", [64, TL], F32)
    sinT = m.sb("sinT", [64, TL], F32)
    rt = [m.sb("ropet%d" % i, [64, 384], F32) for i in range(2)]
    sq4 = m.sb("sq4", [128, 4, 384], BF16)
    tmp = m.sb("tmpb", [128, 384], F32)
    kvg = m.sb("kvg", [128, 4], F32)
    m.dma(cosT[:], io["cosT"])
    m.dma(sinT[:], io["sinT"])
    m.dma(kvg[:], io["kvg"])
    W = io["w_in"]
    proj_fm(cx, W, 1024, EV_QNOPE, 128, actT, KC, stager(cx, stage, lambda ci: io["qT"][ci, 0:128, :]))

    def ckv_c(ci, ti, ps, t0, tw):
        m.copy(ckvT[:, ci, t0:t0 + tw], ps, eng=cx.evac_eng())
    proj_fm(cx, W, 512, EV_CKV, 128, actT, KC, ckv_c)
    proj_fm(cx, W, 1024, EV_QNA, 128, actT, KC, stager(cx, stage, lambda ci: io["qnT"][ci * 128:(ci + 1) * 128, :]))
    proj_fm(cx, W, 1024, EV_KNA, 128, actT, KC, stager(cx, stage, lambda ci: io["knT"][ci * 128:(ci + 1) * 128, :]))

    keep = {}

    def rope_c(ci, ti, ps, t0, tw):
        if ci % 2 == 0:
            keep[ti] = ps
            return
        p = ci // 2
        st = stage[p % len(stage)]
        m.tt(rt[0][:, 0:tw], keep[ti], cosT[:, t0:t0 + tw], ALU.mult)
        m.tt(rt[1][:, 0:tw], ps, sinT[:, t0:t0 + tw], ALU.mult)
        m.tt(st[0:64, t0:t0 + tw], rt[0][:, 0:tw], rt[1][:, 0:tw], ALU.add)
        if t0 + tw >= TL:
            dst = io["qT"][p, 128:192, :] if p < 8 else io["kpeT"]
            m.dma(dst, st[0:64, :])
    proj_fm(cx, W, 18 * 64, EV_PE, 64, actT, KC, rope_c)

    def vna_c(t, c0, nw, ps):
        st = stage[(t + c0 // 256) % len(stage)]
        m.copy(st[:, 0:nw], ps, eng=cx.evac_eng())
        m.dma(io["vn"][t * 128:(t + 1) * 128, c0:c0 + nw], st[:, 0:nw])
    proj_tm(cx, W, 0, KC, EV_VNA, 1024, actT, vna_c, TL)

    rstd_compute(cx, ckvT, 4, TL, 512, rstd, sq4, 6, tmp)
    for k in range(4):
        m.stt(ckvT[:, k, :], ckvT[:, k, :], kvg[:, k:k + 1], rstd[:, :], ALU.mult, ALU.mult)
        m.copy(kvnT[:, k, :], ckvT[:, k, :], eng="act")
    proj_fm(cx, io["w_ukv"], 1024, 0, 128, kvnT, 4, stager(cx, stage, lambda ci: io["kT"][ci, :, :]))

    def v_c(t, c0, nw, ps):
        st = stage[(t + c0 // 512) % len(stage)]
        m.copy(st[:, 0:nw], ps, eng=cx.evac_eng())
        m.dma(io["v"][t * 128:(t + 1) * 128, c0:c0 + nw], st[:, 0:nw])
    proj_tm(cx, io["w_ukv"], 0, 4, 1024, 1024, kvnT, v_c, TL)
    m.release(mk)


def emit_pre_odd(cx, l, hT, actT, rstd, modall, io):
    m = cx.m
    mk = m.mark()
    sq = m.sb("sq", [128, KC, 384], BF16)
    tmp4 = m.sb("tmp4", [128, TL], F32)
    tmp = m.sb("tmp", [128, 384], F32)
    rstd_compute(cx, hT, KC, TL, D, rstd, sq, 6, tmp)
    norm_modulate(cx, hT, actT, rstd, modall, l, MOD_GSC_A, MOD_SH_A, tmp4)
    m.release(mk)
    mk = m.mark()
    stage = [m.sb("stg%d" % i, [128, TL], BF16) for i in range(4)]
    stage32 = [m.sb("stg32_%d" % i, [128, TL], F32) for i in range(2)]
    W = io["w_in"]
    proj_fm(cx, W, 1024, 0, 128, actT, KC,
            stager(cx, stage, lambda ci: io["qT"][ci * 128:(ci + 1) * 128, :], kind="silu"))

    def vi_c(t, c0, nw, ps):
        st = stage[(t + c0 // 256) % len(stage)]
        m.copy(st[:, 0:nw], ps, eng=cx.evac_eng())
        m.dma(io["vi"][t * 128:(t + 1) * 128, c0:c0 + nw], st[:, 0:nw])
    proj_tm(cx, W, 0, KC, 1024, 1024, actT, vi_c, TL)
    proj_fm(cx, W, 2048, 2048, 128, actT, KC,
            stager(cx, stage32, lambda ci: io["zT"][ci * 128:(ci + 1) * 128, :]))
    proj_fm(cx, W, 1024, 4096, 128, actT, KC,
            stager(cx, stage, lambda ci: io["gT"][ci * 128:(ci + 1) * 128, :], kind="silu"))
    proj_fm(cx, W, 3072, 5120, 128, actT, KC,
            stager(cx, stage, lambda ci: io["uT"][ci * 128:(ci + 1) * 128, :]))
    m.release(mk)


def emit_final(cx, hT, rstd, io):
    m = cx.m
    mk = m.mark()
    sq = m.sb("sq", [128, KC, 384], BF16)
    tmp = m.sb("tmp", [128, 384], F32)
    fg = m.sb("fg", [128, KC], F32)
    stage32 = [m.sb("stg32_%d" % i, [128, 1024], F32) for i in range(2)]
    m.dma(fg[:], io["fg"])
    rstd_compute(cx, hT, KC, 1024, D, rstd, sq, 6, tmp)
    for k in range(KC):
        st = stage32[k % 2]
        m.stt(st[:, :], hT[:, k, 0:1024], fg[:, k:k + 1], rstd[:, 0:1024], ALU.mult, ALU.mult)
        m.dma(io["outT"][:, k, :], st[:, :])
    m.release(mk)


def build_R_program(l):
    nc = bass.Bass("TRN2", target_bir_lowering=False)
    cx = Ctx(nc)
    m = cx.m
    dr = lambda n, s, d, k="ExternalInput": m.dram(n, s, d, k)
    ident_d = dr("ident", [128, 128], F32)
    hT_d = dr("hT", [128, KC, TL], F32)
    mod_d = dr("modraw", [128, DEPTH, 96, 2], F32)
    cx.consts(ident_d)
    cx.eps_col = m.sb("eps", [128, 1], F32)
    m.memset(cx.eps_col[:], EPS)
    modall = m.sb("modall", [128, DEPTH, 128, 2], F32)
    m.dma(modall[:, :, 0:96, :], mod_d)
    emit_gsc(cx, modall, dr("gmix", [128, DEPTH, KC], F32), dr("gmlp", [128, DEPTH, KC], F32))
    hT = m.sb("hT", [128, KC, TL], F32)
    m.dma(hT[:], hT_d)
    actT = m.sb("actT", [128, KC, TL], BF16)
    rstd = m.sb("rstd", [128, TL], F32)
    cx.init_wring(4)
    if l >= 1:
        io = {"yT": dr("yT", [128, KC, TL], BF16), "w_out": dr("w_out", [D, D], F32),
              "w1": dr("w1", [D, HID], F32), "w2": dr("w2", [HID, D], F32)}
        emit_post(cx, l - 1, hT, actT, rstd, modall, io, TL if l <= 3 else 1024)
    if l <= 3 and l % 2 == 0:
        io = {"w_in": dr("w_in", [D, EV_NEXT], F32), "w_ukv": dr("w_ukv", [512, 2048], F32),
              "cosT": dr("cosT", [64, TL], F32), "sinT": dr("sinT", [64, TL], F32),
              "kvg": dr("kvg", [128, 4], F32),
              "qT": dr("qT", [8, 192, TL], BF16, "ExternalOutput"),
              "kT": dr("kT", [8, 128, TL], BF16, "ExternalOutput"),
              "kpeT": dr("kpeT", [64, TL], BF16, "ExternalOutput"),
              "v": dr("v", [TL, 1024], BF16, "ExternalOutput"),
              "qnT": dr("qnT", [1024, TL], BF16, "ExternalOutput"),
              "knT": dr("knT", [1024, TL], BF16, "ExternalOutput"),
              "vn": dr("vn", [TL, 1024], BF16, "ExternalOutput")}
        emit_pre_even(cx, l, hT, actT, rstd, modall, io)
    elif l <= 3:
        io = {"w_in": dr("w_in", [D, 8192], F32),
              "qT": dr("qT", [1024, TL], BF16, "ExternalOutput"),
              "vi": dr("vi", [TL, 1024], BF16, "ExternalOutput"),
              "zT": dr("zT", [2048, TL], F32, "ExternalOutput"),
              "gT": dr("gT", [1024, TL], BF16, "ExternalOutput"),
              "uT": dr("uT", [3072, TL], BF16, "ExternalOutput")}
        emit_pre_odd(cx, l, hT, actT, rstd, modall, io)
    if l <= 3:
        hout = dr("hT_out", [128, KC, TL], F32, "ExternalOutput")
        m.dma(hout, hT[:])
    else:
        io = {"fg": dr("fg", [128, KC], F32), "outT": dr("outT", [128, KC, 1024], F32, "ExternalOutput")}
        emit_final(cx, hT, rstd, io)
    m.emit()
    return nc


def fm_vec(v):
    return np.ascontiguousarray(np.asarray(v).reshape(-1, 128).T)


def fm_tokens(tok):
    T = tok.shape[0]
    return np.ascontiguousarray(tok.reshape(T, KC, 128).transpose(2, 1, 0))


def rope_perm_idx():
    idx = np.zeros(64, dtype=np.int64)
    for j in range(64):
        jj = j % 32
        idx[j] = j + 16 if jj < 16 else j - 16
    return idx


def ev_w_in_ext(w):
    perm = rope_perm_idx()
    cols = []
    for h in range(8):
        cols.append(np.arange(h * 192, h * 192 + 128))
    cols.append(np.arange(1536, 2048))
    cols.append(np.arange(2112, 3136))
    cols.append(np.arange(3136, 4160))
    for h in range(8):
        base = h * 192 + 128
        cols.append(base + np.arange(64))
        cols.append(base + perm)
    cols.append(2048 + np.arange(64))
    cols.append(2048 + perm)
    cols.append(np.zeros(128, dtype=np.int64))
    cols.append(np.arange(4160, 5184))
    idx = np.concatenate(cols)
    assert idx.shape[0] == EV_NEXT
    return np.ascontiguousarray(w[:, idx])


def ukv_ext(w):
    kc = np.concatenate([np.arange(h * 256, h * 256 + 128) for h in range(8)])
    vc = np.concatenate([np.arange(h * 256 + 128, h * 256 + 256) for h in range(8)])
    return np.ascontiguousarray(w[:, np.concatenate([kc, vc])])


def rope_tables(rank):
    pos = np.arange(rank * 1024, rank * 1024 + 1024)
    rows = (pos // GRID_W).astype(np.float32)
    cols = (pos % GRID_W).astype(np.float32)
    inv_freq = (np.float32(10000.0) ** (-np.arange(0, 32, 2, dtype=np.float32) / np.float32(32))).astype(np.float32)
    cosT = np.ones((64, TL), dtype=np.float32)
    sinT = np.zeros((64, TL), dtype=np.float32)
    for j in range(64):
        p = rows if j < 32 else cols
        jj = j % 32
        ang = (p * inv_freq[jj % 16]).astype(np.float32)
        cosT[j, :1024] = np.cos(ang)
        s = np.sin(ang)
        sinT[j, :1024] = -s if jj < 16 else s
    return cosT, sinT


NA_CLS = {(0, 0): 0, (1, 0): 1, (2, 0): 2, (3, 0): 3, (4, 0): 4, (5, 1): 5, (5, 0): 6, (6, 0): 7, (7, 0): 8}


def na_row_info(r):
    rs = min(max(r - 4, 0), 24)
    base = 2 * (rs // 2)
    cls = NA_CLS[(r - base, rs - base)]
    ntiles = 5 if rs - base == 1 else 4
    return base // 2, cls, ntiles


def na_tables():
    ridx = np.zeros((9, 5, 128, 64), dtype=np.int64)
    cidx = np.zeros((9, 5, 128, 64), dtype=np.int64)
    mask = np.zeros((9, 5, 128, 64), dtype=np.float32)
    p = np.arange(128)
    qc = np.arange(64)
    kcol = (p % 64)[:, None]
    col_start = np.clip(qc - 8, 0, 48)[None, :]
    col_ok = (kcol >= col_start) & (kcol < col_start + 16)
    coff = np.clip(kcol - qc[None, :], -15, 15) + 15
    for (dr, off), c in NA_CLS.items():
        for j in range(5):
            krel = 2 * j + p // 64
            inband = (krel >= off) & (krel < off + 8)
            roff = np.clip(krel - dr + 7, 0, 14)
            ridx[c, j] = roff[:, None]
            cidx[c, j] = coff
            ok = inband[:, None] & col_ok
            mask[c, j] = np.where(ok, 0.0, -30000.0)
    return ridx, cidx, mask


def dense_attn(cx, parts, vtile, key_tiles, q0, qn, scale, yT, pbuf, rsb):
    m = cx.m
    nk = len(key_tiles)
    for qb in range(q0, q0 + qn, 512):
        qw = min(512, q0 + qn - qb)
        O = m.psum[4]
        Sm = m.psum[5]

        def s_mm(i):
            sb = m.psum[i % 4]
            kt = key_tiles[i]
            for pi, (qp, kp) in enumerate(parts):
                m.matmul(sb[:, 0:qw], kp[:, kt * 128:(kt + 1) * 128], qp[:, qb:qb + qw],
                         start=(pi == 0), stop=(pi == len(parts) - 1))
        s_mm(0)
        for i in range(nk):
            if i + 1 < nk:
                s_mm(i + 1)
            P = pbuf[i % len(pbuf)]
            m.act(P[:, 0:qw], m.psum[i % 4][:, 0:qw], AF.Exp, scale=scale)
            m.matmul(O[:, 0:qw], vtile(key_tiles[i]), P[:, 0:qw], start=(i == 0), stop=(i == nk - 1))
            m.matmul(Sm[:, 0:qw], cx.ones_bf[:, 0:128], P[:, 0:qw], start=(i == 0), stop=(i == nk - 1))
        m.recip(rsb[:, 0:qw], Sm[:, 0:qw])
        m.tt(yT[:, qb:qb + qw], O[:, 0:qw], rsb[:, 0:qw], ALU.mult)


def emit_M_even(cx, io, ctx_out):
    m = cx.m
    mk = m.mark()
    LT = list(range(2, 18))
    CTt = [0, 1]
    allk = CTt + LT
    pbuf = [m.sb("pbuf%d" % i, [128, 512], BF16) for i in range(3)]
    rsb = m.sb("rsb", [128, 512], F32)
    ystage = [m.sb("ystage%d" % i, [128, TM], BF16) for i in range(2)]
    vsb = m.sb("vsb", [128, 18, 512], BF16)
    m.dma(vsb[:], io["mv"].rearrange("(t p) c -> p t c", p=128))
    kpe = m.sb("kpe", [64, TM], BF16)
    m.dma(kpe[:], io["mkpe"])
    qn_ = [m.sb("qnope%d" % i, [128, TM], BF16) for i in range(2)]
    qp_ = [m.sb("qpe%d" % i, [64, TM], BF16) for i in range(2)]
    kn_ = [m.sb("knope%d" % i, [128, TM], BF16) for i in range(2)]
    mla_scale = 192.0 ** -0.5
    c0 = 0 if ctx_out else 256
    for h in range(4):
        qn, qp, kn = qn_[h % 2], qp_[h % 2], kn_[h % 2]
        m.dma(qn[:], io["mq"][h, 0:128, :])
        m.dma(qp[:], io["mq"][h, 128:192, :])
        m.dma(kn[:], io["mk"][h])
        ys = ystage[h % 2]
        parts = [(qn, kn), (qp, kpe)]
        vt = lambda kt, h=h: vsb[:, kt, h * 128:(h + 1) * 128]
        if ctx_out:
            dense_attn(cx, parts, vt, CTt, 0, 256, mla_scale, ys, pbuf, rsb)
        dense_attn(cx, parts, vt, allk, 256, 2048, mla_scale, ys, pbuf, rsb)
        m.dma(io["myT"][h, :, c0:TM], ys[:, c0:TM])
    m.dma(vsb[:], io["mvn"].rearrange("(t p) c -> p t c", p=128))
    mask = m.sb("namask", [128, 9 * 5 * 64], F32)
    m.dma(mask[:], io["namask"])
    bias_ = [m.sb("nabias%d" % i, [128, 9 * 5 * 64], F32) for i in range(2)]
    lg_ = [m.sb("nalg%d" % i, [128, 320], F32) for i in range(4)]
    P_ = [m.sb("naP%d" % i, [128, 448], BF16) for i in range(4)]
    nrs_ = [m.sb("nars%d" % i, [128, 64], F32) for i in range(4)]
    na_scale = 128.0 ** -0.5
    it = 0
    for h in range(4):
        qn, kn = qn_[h % 2], kn_[h % 2]
        m.dma(qn[:], io["mqn"][h * 128:(h + 1) * 128, :])
        m.dma(kn[:], io["mkn"][h * 128:(h + 1) * 128, :])
        bias = bias_[h % 2]
        m.dma(bias[:], io["nabias"][h])
        m.stt(bias[:], bias[:], 1.0, mask[:], ALU.mult, ALU.add)
        bv = bias[:].rearrange("p (c x) -> p c x", c=9)
        ys = ystage[h % 2]
        vt = lambda kt, h=h: vsb[:, kt, h * 128:(h + 1) * 128]
        if ctx_out:
            dense_attn(cx, [(qn, kn)], vt, CTt, 0, 256, na_scale, ys, pbuf, rsb)
        units = []
        for r in range(32):
            j0, cls, nt = na_row_info(r)
            q0 = 256 + 64 * r
            S = m.psum[it % 4]
            O = m.psum[4 + it % 4]
            lg, P, nrs = lg_[it % 4], P_[it % 4], nrs_[it % 4]
            it += 1
            tiles = [2 + j0 + j for j in range(nt)]

            def stage_a(q0=q0, S=S, lg=lg, P=P, tiles=tiles, cls=cls, nt=nt):
                for j, kt in enumerate(tiles):
                    m.matmul(S[:, 64 * j:64 * j + 64], kn[:, kt * 128:(kt + 1) * 128], qn[:, q0:q0 + 64])
                for j, kt in enumerate(CTt):
                    m.matmul(S[:, 320 + 64 * j:384 + 64 * j], kn[:, kt * 128:(kt + 1) * 128], qn[:, q0:q0 + 64])
                m.stt(lg[:, 0:64 * nt], S[:, 0:64 * nt], na_scale, bv[:, cls, 0:64 * nt], ALU.mult, ALU.add)
                m.act(P[:, 0:64 * nt], lg[:, 0:64 * nt], AF.Exp)
                m.act(P[:, 320:448], S[:, 320:448], AF.Exp, scale=na_scale)

            def stage_b(q0=q0, O=O, P=P, nrs=nrs, tiles=tiles):
                srcs = [(kt, P[:, 64 * j:64 * j + 64]) for j, kt in enumerate(tiles)]
                srcs += [(kt, P[:, 320 + 64 * j:384 + 64 * j]) for j, kt in enumerate(CTt)]
                for i, (kt, pp) in enumerate(srcs):
                    m.matmul(O[:, 0:64], vt(kt), pp, start=(i == 0), stop=(i == len(srcs) - 1), skip_group_check=True)
                    m.matmul(O[:, 64:128], cx.ones_bf[:, 0:128], pp, start=False, stop=(i == len(srcs) - 1),
                             skip_group_check=True)
                m.recip(nrs[:, :], O[:, 64:128])
                m.tt(ys[:, q0:q0 + 64], O[:, 0:64], nrs[:, :], ALU.mult)
            units.append((stage_a, stage_b))
        LOOK = 2
        for u in range(len(units) + LOOK):
            if u < len(units):
                units[u][0]()
            if u >= LOOK:
                units[u - LOOK][1]()
        m.dma(io["myT"][4 + h, :, c0:TM], ys[:, c0:TM])
    m.release(mk)


def build_Meven_program(ctx_out):
    nc = bass.Bass("TRN2", target_bir_lowering=False)
    cx = Ctx(nc)
    m = cx.m
    dr = lambda n, s, d, k="ExternalInput": m.dram(n, s, d, k)
    cx.consts(dr("ident", [128, 128], F32))
    io = {"mq": dr("mq", [4, 192, TM], BF16), "mk": dr("mk", [4, 128, TM], BF16),
          "mkpe": dr("mkpe", [64, TM], BF16), "mv": dr("mv", [TM, 512], BF16),
          "mqn": dr("mqn", [512, TM], BF16), "mkn": dr("mkn", [512, TM], BF16),
          "mvn": dr("mvn", [TM, 512], BF16),
          "nabias": dr("nabias", [4, 128, 2880], F32), "namask": dr("namask", [128, 2880], F32),
          "myT": dr("myT", [8, 128, TM], BF16, "ExternalOutput")}
    emit_M_even(cx, io, ctx_out)
    m.emit()
    return nc


def canon_cols(a0, a1):
    return np.ascontiguousarray(np.concatenate([a0[..., 1024:], a1[..., 1024:], a0[..., :1024], a1[..., :1024]], -1))


def canon_rows(a0, a1):
    return np.ascontiguousarray(np.concatenate([a0[1024:], a1[1024:], a0[:1024], a1[:1024]], 0))


_NA_TAB = None


def na_bias_host(rel_bias_e):
    global _NA_TAB
    if _NA_TAB is None:
        _NA_TAB = na_tables()
    ridx, cidx, mask = _NA_TAB
    g = rel_bias_e[:, ridx, cidx]
    g = np.ascontiguousarray(g.transpose(0, 3, 1, 2, 4).reshape(8, 128, 2880)).astype(np.float32)
    mk = np.ascontiguousarray(mask.transpose(2, 0, 1, 3).reshape(128, 2880))
    return g, mk


def assemble_Meven(o0, o1, r, nab, namask):
    hs = slice(4 * r, 4 * r + 4)
    cs = slice(512 * r, 512 * r + 512)
    return {"ident": np.eye(128, dtype=np.float32),
            "mq": canon_cols(o0["qT"][hs], o1["qT"][hs]),
            "mk": canon_cols(o0["kT"][hs], o1["kT"][hs]),
            "mkpe": canon_cols(o0["kpeT"], o1["kpeT"]),
            "mv": canon_rows(o0["v"][:, cs], o1["v"][:, cs]),
            "mqn": canon_cols(o0["qnT"][cs], o1["qnT"][cs]),
            "mkn": canon_cols(o0["knT"][cs], o1["knT"][cs]),
            "mvn": canon_rows(o0["vn"][:, cs], o1["vn"][:, cs]),
            "nabias": np.ascontiguousarray(nab[hs]), "namask": namask}


def assemble_y(m0, m1, r):
    full = np.zeros((16, 128, TM), dtype=m0["myT"].dtype)
    for rr, mm in ((0, m0), (1, m1)):
        full[4 * rr:4 * rr + 4] = mm["myT"][0:4]
        full[8 + 4 * rr:8 + 4 * rr + 4] = mm["myT"][4:8]
    loc = np.concatenate([full[:, :, 256 + 1024 * r:256 + 1024 * (r + 1)], full[:, :, 128 * r:128 * (r + 1)]], -1)
    return np.ascontiguousarray(loc.transpose(1, 0, 2))


NCH = 36


def hgrn_sigma(d, c):
    if d == 0:
        return c
    return 3 - c if c < 4 else 35 - (c - 4)


def emit_hgrn(cx, io, ctx_out, ystage):
    m = cx.m
    mk = m.mark()
    T = TM
    BIG = 2.0e17
    vtok = m.sb("hg_vtok", [128, 18, 512], BF16)
    m.dma(vtok[:], io["hv"].rearrange("(t p) c -> p t c", p=128))
    lbl = m.sb("hg_lbl", [128, 2, 2, 4], F32)
    m.dma(lbl[:], io["lbl"])
    lb = m.sb("hg_lb", [128, 2, 4], F32)
    oml = m.sb("hg_oml", [128, 2, 4], F32)
    if io["odd_idx"] == 0:
        m.memset(lb[:], 0.0)
    else:
        m.tt(lb[:], lbl[:, :, 1, :], lbl[:, :, 0, :], ALU.subtract)
        m.act(lb[:], lb[:], AF.Sigmoid)
    m.ts(oml[:], lb[:], -1.0, 1.0, ALU.mult, ALU.add)
    ng = m.sb("hg_ng", [128, 4], F32)
    m.dma(ng[:], io["ng"])
    masks = m.sb("hg_masks", [128, 4, 128], F32)
    m.dma(masks[:], io["hmasks"])
    ones_f = m.sb("hg_ones", [128, T], BF16)
    m.memset(ones_f[:], 1.0)
    qb = m.sb("hg_q", [128, T], BF16)
    gb = m.sb("hg_g", [128, T], BF16)
    Qi = [m.sb("hg_Qi%d" % d, [128, T], BF16) for d in range(2)]
    Ki = [m.sb("hg_Ki%d" % d, [128, T], BF16) for d in range(2)]
    Qo = [m.sb("hg_Qo%d" % d, [128, T], BF16) for d in range(2)]
    Ko = [m.sb("hg_Ko%d" % d, [128, T], BF16) for d in range(2)]
    Qs = [m.sb("hg_Qs%d" % d, [128, T], BF16) for d in range(2)]
    Sbf = [m.sb("hg_Sbf%d" % d, [128, NCH, 128], BF16) for d in range(2)]
    KlT = m.sb("hg_KlT", [128, T], BF16)
    Kltok = m.sb("hg_Kltok", [128, 18, 128], BF16)
    z = m.sb("hg_z", [128, T], F32)
    lf = m.sb("hg_lf", [128, T], F32)
    kk = m.sb("hg_k", [128, T], F32)
    ee = m.sb("hg_e", [128, T], F32)
    gaddr = m.mark()
    G = m.sb("hg_G", [128, T], F32)
    Gx = m.sb("hg_Gx", [128, T], F32)
    Dfull = m.sb("hg_Dfull", [128, 128 * NCH], F32, at=gaddr)
    Sst = m.sb("hg_Sst", [128, 128 * NCH], F32)
    Dsc = m.sb("hg_Dsc", [128, NCH], F32)
    Dtmp = m.sb("hg_Dtmp", [128, NCH], F32)
    Am = [m.sb("hg_Am%d" % i, [128, 128], BF16) for i in range(4)]
    T12 = [m.sb("hg_T12_%d" % i, [128, 256], F32) for i in range(4)]
    oT = lf
    rstd = ee
    sq1 = m.sb("hg_sq", [128, 1, 384], BF16)
    tmp = m.sb("hg_tmp", [128, 384], F32)

    v64 = lambda t_: t_[:].rearrange("p (c j) -> p c j", j=64)
    v32 = lambda t_: t_[:].rearrange("p (c j) -> p c j", j=32)
    bc64 = lambda t_, j: v64(t_)[:, :, j:j + 1].to_broadcast([128, NCH, 64])
    bc32 = lambda t_, j: v32(t_)[:, :, j:j + 1].to_broadcast([128, 2 * NCH, 32])
    c0 = 0 if ctx_out else 256
    ai = 0
    for h in range(4):
        m.dma(qb[:], io["hq"][h * 128:(h + 1) * 128, :])
        m.dma(gb[:], io["hg"][h * 128:(h + 1) * 128, :])
        for d in range(2):
            A = G if d == 0 else Gx
            sgn = 1.0 if d == 0 else -1.0
            HALVES = ((0, T // 2), (T // 2, T))
            for (ca, cb_) in HALVES:
                cs = slice(ca, cb_)
                nch_h = (cb_ - ca) // 64
                w64 = lambda t_: t_[:, cs].rearrange("p (c j) -> p c j", j=64)
                w32 = lambda t_: t_[:, cs].rearrange("p (c j) -> p c j", j=32)
                b64 = lambda t_, j: w64(t_)[:, :, j:j + 1].to_broadcast([128, nch_h, 64])
                b32 = lambda t_, j: w32(t_)[:, :, j:j + 1].to_broadcast([128, 2 * nch_h, 32])
                m.dma(z[:, cs], io["hz"][d, h * 128:(h + 1) * 128, cs])
                m.act(z[:, cs], z[:, cs], AF.Sigmoid)
                m.ts(z[:, cs], z[:, cs], oml[:, d, h:h + 1], lb[:, d, h:h + 1], ALU.mult, ALU.add)
                m.ts(z[:, cs], z[:, cs], 1e-30, None, ALU.max, eng="pool")
                m.act(lf[:, cs], z[:, cs], AF.Ln)
                m.ts(kk[:, cs], z[:, cs], -1.0, 1.0, ALU.mult, ALU.add, eng="pool")
                m.scan(G[:, cs], ones_f[:, cs], lf[:, cs], 0.0 if ca == 0 else G[:, ca - 1:ca], ALU.mult, ALU.add)
                m.tt(Gx[:, cs], G[:, cs], lf[:, cs], ALU.subtract, eng="pool")
                m.tt(w32(z), w32(A), b32(A, 15 if d == 0 else 16), ALU.subtract, eng="pool")
                m.act(ee[:, cs], z[:, cs], AF.Exp, scale=sgn)
                m.stt(Qi[d][:, cs], ee[:, cs], BIG, qb[:, cs], ALU.min, ALU.mult)
                m.act(ee[:, cs], z[:, cs], AF.Exp, scale=-sgn)
                m.stt(Ki[d][:, cs], ee[:, cs], BIG, kk[:, cs], ALU.min, ALU.mult)
                m.tt(w64(z), w64(A), b64(A, 31 if d == 0 else 32), ALU.subtract, eng="pool")
                m.act(ee[:, cs], z[:, cs], AF.Exp, scale=sgn)
                m.stt(Qo[d][:, cs], ee[:, cs], 1.0, qb[:, cs], ALU.min, ALU.mult)
                m.act(ee[:, cs], z[:, cs], AF.Exp, scale=-sgn)
                m.stt(Ko[d][:, cs], ee[:, cs], 1.0, kk[:, cs], ALU.min, ALU.mult)
                if d == 0:
                    m.tt(w64(z), w64(G), b64(Gx, 0), ALU.subtract, eng="pool")
                else:
                    m.tt(w64(z), w64(Gx), b64(G, 63), ALU.subtract, eng="pool")
                m.act(ee[:, cs], z[:, cs], AF.Exp, scale=sgn)
                m.stt(Qs[d][:, cs], ee[:, cs], 1.0, qb[:, cs], ALU.min, ALU.mult)
                if d == 0:
                    m.tt(w64(z), w64(G), b64(G, 63), ALU.subtract, eng="pool")
                else:
                    m.tt(w64(z), w64(Gx), b64(Gx, 0), ALU.subtract, eng="pool")
                m.act(ee[:, cs], z[:, cs], AF.Exp, scale=-sgn)
                m.stt(KlT[:, cs], ee[:, cs], 1.0, kk[:, cs], ALU.min, ALU.mult)
            m.tt(Dtmp[:], v64(G)[:, :, 63], v64(Gx)[:, :, 0], ALU.subtract)
            m.act(Dtmp[:], Dtmp[:], AF.Exp)
            if d == 0:
                m.copy(Dsc[:, 1:NCH], Dtmp[:, 1:NCH], eng="pool")
                m.memset(Dsc[:, 0:1], 0.0, eng="pool")
            else:
                for c in range(NCH):
                    s = hgrn_sigma(1, c)
                    if s == 0:
                        m.memset(Dsc[:, 0:1], 0.0, eng="pool")
                    else:
                        m.copy(Dsc[:, s:s + 1], Dtmp[:, c:c + 1], eng="pool")
            m.copy(Dfull[:].rearrange("p (e s) -> p e s", s=NCH),
                   Dsc[:].unsqueeze(1).to_broadcast([128, 128, NCH]), eng="pool")
            for t4 in range(0, 18, 4):
                nt = min(4, 18 - t4)
                pb = m.psum[6 + (t4 // 4) % 2][:].bitcast(BF16)
                for i in range(nt):
                    m.transpose(pb[:, i * 128:(i + 1) * 128], KlT[:, (t4 + i) * 128:(t4 + i + 1) * 128], cx.ident_bf[:])
                m.copy(Kltok[:, t4:t4 + nt, :], pb[:, 0:nt * 128].rearrange("p (t d) -> p t d", d=128),
                       eng=cx.evac_eng())
            S3 = Sst[:].rearrange("p (e s) -> p e s", s=NCH)
            for c in range(NCH):
                t, j = c // 2, c % 2
                ps = m.psum[c % 4]
                m.matmul(ps[:, 0:128], Kltok[64 * j:64 * j + 64, t, :], vtok[64 * j:64 * j + 64, t, h * 128:(h + 1) * 128])
                s = hgrn_sigma(d, c)
                m.copy(S3[:, :, s], ps[:, 0:128], eng=cx.evac_eng())
            m.scan(Sst[:], Dfull[:], Sst[:], 0.0, ALU.mult, ALU.add)
            m.copy(Sbf[d][:], Sst[:].rearrange("p (e s) -> p s e", s=NCH), eng="pool")
        mflat = masks[:].rearrange("p a t -> p (a t)")
        tiles_ = list(range(c0 // 128, 18))
        stA, stB = {}, {}
        for t in tiles_:
            def stage_a(t=t):
                ams = []
                tsl = slice(t * 128, (t + 1) * 128)
                for d in range(2):
                    psA = m.psum[2 * (t % 2) + d]
                    m.matmul(psA[:, 0:128], Ki[d][:, tsl], Qi[d][:, tsl])
                    m.matmul(psA[:, 128:256], Ko[d][:, tsl], Qo[d][:, tsl])
                    am = Am[2 * (t % 2) + d]
                    t12 = T12[2 * (t % 2) + d]
                    m.tt(t12[:], psA[:, 0:256], mflat[:, 256 * d:256 * d + 256], ALU.mult)
                    m.tt(am[:], t12[:, 0:128], t12[:, 128:256], ALU.add, eng="pool")
                    ams.append(am)
                stA[t] = ams

            def stage_b(t=t):
                ams = stA[t]
                psO = m.psum[4 + t % 2]
                mms = []
                for d in range(2):
                    mms.append((psO[:, 0:128], vtok[:, t, h * 128:(h + 1) * 128], ams[d][:]))
                    for j in range(2):
                        c = 2 * t + j
                        s = hgrn_sigma(d, c)
                        if s >= 1:
                            mms.append((psO[:, 64 * j:64 * j + 64], Sbf[d][:, s - 1, :], Qs[d][:, c * 64:(c + 1) * 64]))
                for i, (o_, l_, r_) in enumerate(mms):
                    m.matmul(o_, l_, r_, start=(i == 0), stop=(i == len(mms) - 1), skip_group_check=True)
                m.copy(oT[:, t * 128:(t + 1) * 128], psO[:, 0:128], eng=cx.evac_eng())
            stB[t] = stage_b
            stA[("f", t)] = stage_a
        for i in range(len(tiles_) + 1):
            if i < len(tiles_):
                stA[("f", tiles_[i])]()
            if i >= 1:
                stB[tiles_[i - 1]]()
        rstd_compute(cx, oT[:].rearrange("p (o t) -> p o t", o=1)[:, :, c0:T], 1, T - c0, 128, rstd, sq1, 7, tmp)
        ys = ystage[h % 2]
        m.stt(oT[:, c0:T], oT[:, c0:T], ng[:, h:h + 1], rstd[:, 0:T - c0], ALU.mult, ALU.mult)
        m.tt(ys[:, c0:T], oT[:, c0:T], gb[:, c0:T], ALU.mult, eng="pool")
        m.dma(io["myT"][h, :, c0:T], ys[:, c0:T])
    m.release(mk)


def sin_reduced(cx, out, x, rows, w, t1):
    m = cx.m
    PI = math.pi
    for _ in range(2):
        m.ts(t1[0:rows, 0:w], x, PI, -2 * PI, ALU.is_gt, ALU.mult)
        m.tt(x, x, t1[0:rows, 0:w], ALU.add)
        m.ts(t1[0:rows, 0:w], x, -PI, 2 * PI, ALU.is_lt, ALU.mult)
        m.tt(x, x, t1[0:rows, 0:w], ALU.add)
    m.act(out, x, AF.Sin)


def emit_hyena(cx, io, ctx_out, ystage):
    m = cx.m
    mk = m.mark()
    T = TM
    cw = m.sb("hy_cw", [128, 3, 3, 4], F32)
    cb = m.sb("hy_cb", [128, 3, 4], F32)
    skip = m.sb("hy_skip", [128, 4], F32)
    m.dma(cw[:], io["cw"])
    m.dma(cb[:], io["cb"])
    m.dma(skip[:], io["skip"])
    fw1 = m.sb("hy_w1", [33, 64], F32)
    fw2 = m.sb("hy_w2", [64, 64], F32)
    fw3 = m.sb("hy_w3", [64, 64], F32)
    fb = m.sb("hy_fb", [64, 4], F32)
    fwo = m.sb("hy_wo", [64, 2, 512], F32)
    m.dma(fw1[:], io["fw1"])
    m.dma(fw2[:], io["fw2"])
    m.dma(fw3[:], io["fw3"])
    m.dma(fb[:], io["fb"])
    m.dma(fwo[:], io["fwo"])
    delt = m.sb("hy_delt", [128, 512], F32)
    m.dma(delt[:], io["delt"])
    X0 = [m.sb("hy_X0_%d" % i, [128, T], BF16) for i in range(4)]
    VX = [m.sb("hy_VX_%d" % i, [128, T], BF16) for i in range(4)]
    vxtok = m.sb("hy_vxtok", [128, 18, 512], BF16)
    mB = m.mark()
    ub = [m.sb("hy_u%d" % i, [128, T], BF16) for i in range(2)]
    acc = [m.sb("hy_acc%d" % i, [128, T], F32) for i in range(2)]
    segs = ((0, 256), (256, T))
    for cc in range(4):
        def conv(g, a, u):
            m.dma(u[:], io["hu"][g, cc * 128:(cc + 1) * 128, :])
            m.ts(a[:], u[:], cw[:, 1, g, cc:cc + 1], cb[:, g, cc:cc + 1], ALU.mult, ALU.add)
            for (s0, s1) in segs:
                m.stt(a[:, s0 + 1:s1], u[:, s0:s1 - 1], cw[:, 0, g, cc:cc + 1], a[:, s0 + 1:s1], ALU.mult, ALU.add)
                m.stt(a[:, s0:s1 - 1], u[:, s0 + 1:s1], cw[:, 2, g, cc:cc + 1], a[:, s0:s1 - 1], ALU.mult, ALU.add)
        conv(1, acc[0], ub[0])
        conv(2, acc[1], ub[1])
        m.tt(VX[cc][:], acc[0][:], acc[1][:], ALU.mult, eng="pool")
        conv(0, acc[0], ub[0])
        m.copy(X0[cc][:], acc[0][:], eng="act")
    for t in range(18):
        pb = m.psum[6 + t % 2][:].bitcast(BF16)
        for cc in range(4):
            m.transpose(pb[:, cc * 128:(cc + 1) * 128], VX[cc][:, t * 128:(t + 1) * 128], cx.ident_bf[:])
        m.copy(vxtok[:, t, :], pb[:, 0:512], eng=cx.evac_eng())
    m.release(mB)
    for (name, n, tok0) in (("c", 256, 0), ("l", 2048, 256)):
        if name == "c" and not ctx_out:
            continue
        m.release(mB)
        ntile = n // 128
        nfc = n // 128
        Yc = m.sb("hy_Yc", [128, nfc, 512], BF16)
        Ys = m.sb("hy_Ys", [128, nfc, 512], BF16)
        p1 = m.sb("hy_p1", [128, 512], F32)
        p2 = m.sb("hy_p2", [128, 512], F32)
        mC = m.mark()
        hs = m.sb("hy_hs", [128, ntile, 512], BF16)
        hd = m.sb("hy_hd", [128, ntile, 512], BF16)
        mD = m.mark()
        zT = m.sb("hy_zT", [33, n], F32)
        m.dma(zT[:], io["zT_" + name])
        tcol = m.sb("hy_tcol", [128, ntile], F32)
        m.dma(tcol[:], io["tcol_" + name])
        hA = m.sb("hy_hA", [64, n], F32)
        hB = m.sb("hy_hB", [64, n], F32)
        t1 = m.sb("hy_t1", [64, 512], F32)
        layers = ((fw1, 33, zT, hA, 0), (fw2, 64, hA, hB, 1), (fw3, 64, hB, hA, 2))
        for (wt, kdim, src_, dst, bi) in layers:
            for b0 in range(0, n, 512):
                bw = min(512, n - b0)
                ps = m.psum[(b0 // 512) % 2]
                m.matmul(ps[0:64, 0:bw], wt[0:kdim, :], src_[0:kdim, b0:b0 + bw])
                m.ts(dst[:, b0:b0 + bw], ps[0:64, 0:bw], fb[:, bi:bi + 1], fb[:, 3:4], ALU.add, ALU.mult)
                sin_reduced(cx, dst[:, b0:b0 + bw], dst[:, b0:b0 + bw], 64, bw, t1)
        h3 = hA
        win = m.sb("hy_win", [128, 512], F32)
        hf = m.sb("hy_hf", [128, 512], F32)
        hb = m.sb("hy_hb", [128, 512], F32)
        ntc = m.sb("hy_ntc", [128, ntile], F32)
        m.ts(ntc[:], tcol[:], -1.0, None, ALU.mult)
        for t in range(ntile):
            m.act(win[:], delt[:], AF.Exp, scale=ntc[:, t:t + 1])
            m.matmul(m.psum[2][:, 0:512], h3[0:64, t * 128:(t + 1) * 128], fwo[0:64, 0, :])
            m.matmul(m.psum[3][:, 0:512], h3[0:64, t * 128:(t + 1) * 128], fwo[0:64, 1, :])
            m.tt(hf[:], m.psum[2][:, 0:512], win[:], ALU.mult)
            m.tt(hb[:], m.psum[3][:, 0:512], win[:], ALU.mult)
            if t == 0:
                m.memset(hb[0:1, :], 0.0)
            m.tt(hs[:, t, :], hf[:], hb[:], ALU.add, eng="pool")
            m.tt(hd[:, t, :], hf[:], hb[:], ALU.subtract, eng="pool")
        m.release(mD)
        nfb = max(1, nfc // 4)
        fcb = nfc // nfb
        Cb = [m.sb("hy_Cb%d" % i, [128, ntile, fcb * 128], BF16) for i in range(1)]
        Sb = [m.sb("hy_Sb%d" % i, [128, ntile, fcb * 128], BF16) for i in range(1)]
        Kc = m.sb("hy_Kc", [128, 512], F32)
        Ks = m.sb("hy_Ks", [128, 512], F32)
        dF = io["dftF_" + name]
        tt0 = tok0 // 128
        for fbk in range(nfb):
            C_, S_ = Cb[0], Sb[0]
            m.dma(C_[:], dF[fbk, 0].rearrange("p (k f) -> p k f", f=fcb * 128))
            m.dma(S_[:], dF[fbk, 1].rearrange("p (k f) -> p k f", f=fcb * 128))
            for fi in range(fcb):
                fc = fbk * fcb + fi
                fsl = slice(fi * 128, (fi + 1) * 128)
                for t in range(ntile):
                    m.matmul(m.psum[0][:, 0:512], C_[:, t, fsl], hs[:, t, :], start=(t == 0), stop=(t == ntile - 1))
                for t in range(ntile):
                    m.matmul(m.psum[1][:, 0:512], S_[:, t, fsl], hd[:, t, :], start=(t == 0), stop=(t == ntile - 1))
                m.copy(Kc[:], m.psum[0][:, 0:512], eng="act")
                m.copy(Ks[:], m.psum[1][:, 0:512], eng="act")
                pc, ps_ = m.psum[2 + 2 * (fc % 2)], m.psum[3 + 2 * (fc % 2)]
                for t in range(ntile):
                    m.matmul(pc[:, 0:512], C_[:, t, fsl], vxtok[:, tt0 + t, :], start=(t == 0), stop=(t == ntile - 1))
                for t in range(ntile):
                    m.matmul(ps_[:, 0:512], S_[:, t, fsl], vxtok[:, tt0 + t, :], start=(t == 0), stop=(t == ntile - 1))
                m.tt(p1[:], pc[:, 0:512], Kc[:], ALU.mult)
                m.tt(p2[:], ps_[:, 0:512], Ks[:], ALU.mult)
                m.tt(Yc[:, fc, :], p1[:], p2[:], ALU.subtract, eng="pool")
                m.tt(p1[:], pc[:, 0:512], Ks[:], ALU.mult)
                m.tt(p2[:], ps_[:, 0:512], Kc[:], ALU.mult)
                m.tt(Ys[:, fc, :], p1[:], p2[:], ALU.add, eng="pool")
        m.release(mC)
        dI = io["dftI_" + name]
        tbw = min(512, n)
        Ci = [m.sb("hy_Ci%d" % i, [128, nfc, tbw], BF16) for i in range(2)]
        Si = [m.sb("hy_Si%d" % i, [128, nfc, tbw], BF16) for i in range(2)]
        for tb in range(n // tbw):
            C_, S_ = Ci[tb % 2], Si[tb % 2]
            m.dma(C_[:], dI[tb, 0].rearrange("p (k t) -> p k t", t=tbw))
            m.dma(S_[:], dI[tb, 1].rearrange("p (k t) -> p k t", t=tbw))
            for cc in range(4):
                po = m.psum[cc % 2]
                csl = slice(cc * 128, (cc + 1) * 128)
                for fc in range(nfc):
                    m.matmul(po[:, 0:tbw], Yc[:, fc, csl], C_[:, fc, :], start=(fc == 0), stop=False)
                for fc in range(nfc):
                    m.matmul(po[:, 0:tbw], Ys[:, fc, csl], S_[:, fc, :], start=False, stop=(fc == nfc - 1))
                a0 = tok0 + tb * tbw
                ys = ystage[cc % 2]
                m.stt(p1[:, 0:tbw], VX[cc][:, a0:a0 + tbw], skip[:, cc:cc + 1], po[:, 0:tbw], ALU.mult, ALU.add)
                m.tt(ys[:, 0:tbw], p1[:, 0:tbw], X0[cc][:, a0:a0 + tbw], ALU.mult, eng="pool")
                m.dma(io["myT"][4 + cc, :, a0:a0 + tbw], ys[:, 0:tbw])
    m.release(mk)


def build_Modd_program(odd_idx, ctx_out):
    nc = bass.Bass("TRN2", target_bir_lowering=False)
    cx = Ctx(nc)
    m = cx.m
    dr = lambda n, s, d, k="ExternalInput": m.dram(n, s, d, k)
    cx.consts(dr("ident", [128, 128], F32))
    cx.eps_col = m.sb("eps", [128, 1], F32)
    m.memset(cx.eps_col[:], EPS)
    io = {"odd_idx": odd_idx,
          "hq": dr("hq", [512, TM], BF16), "hz": dr("hz", [2, 512, TM], F32), "hv": dr("hv", [TM, 512], BF16),
          "hg": dr("hg", [512, TM], BF16), "hu": dr("hu", [3, 512, TM], BF16),
          "lbl": dr("lbl", [128, 2, 2, 4], F32), "ng": dr("ng", [128, 4], F32),
          "hmasks": dr("hmasks", [128, 4, 128], F32),
          "cw": dr("cw", [128, 3, 3, 4], F32), "cb": dr("cb", [128, 3, 4], F32), "skip": dr("skip", [128, 4], F32),
          "fw1": dr("fw1", [33, 64], F32), "fw2": dr("fw2", [64, 64], F32), "fw3": dr("fw3", [64, 64], F32),
          "fb": dr("fb", [64, 4], F32), "fwo": dr("fwo", [64, 2, 512], F32), "delt": dr("delt", [128, 512], F32),
          "zT_l": dr("zT_l", [33, 2048], F32), "tcol_l": dr("tcol_l", [128, 16], F32),
          "dftF_l": dr("dftF_l", [2048, 2, 2048], BF16), "dftI_l": dr("dftI_l", [2048, 2, 2048], BF16),
          "myT": dr("myT", [8, 128, TM], BF16, "ExternalOutput")}
    if ctx_out:
        io.update({"zT_c": dr("zT_c", [33, 256], F32), "tcol_c": dr("tcol_c", [128, 2], F32),
                   "dftF_c": dr("dftF_c", [256, 2, 256], BF16), "dftI_c": dr("dftI_c", [256, 2, 256], BF16)})
    ystage = [m.sb("ystage%d" % i, [128, TM], BF16) for i in range(2)]
    emit_hgrn(cx, io, ctx_out, ystage)
    emit_hyena(cx, io, ctx_out, ystage)
    m.emit()
    return nc


_HY_CONST = {}


def hyena_consts(n):
    if n in _HY_CONST:
        return _HY_CONST[n]
    pos = np.arange(n, dtype=np.float32)
    t = (pos / np.float32(max(n - 1, 1))).astype(np.float32)
    bands = np.linspace(1e-4, 15, 16, dtype=np.float32)
    ang = (np.float32(2.0 * math.pi / n) * pos[:, None] * bands[None, :]).astype(np.float32)
    z = np.concatenate([t[:, None], np.cos(ang), -np.sin(ang)], -1).astype(np.float32)
    zT = np.ascontiguousarray(z.T)
    tcol = np.ascontiguousarray(t.reshape(-1, 128).T)
    f = np.arange(n, dtype=np.float64) + 0.5
    M = np.arange(n, dtype=np.float64)[:, None] * (2.0 * math.pi * f[None, :] / (2.0 * n))
    c, s = np.cos(M), np.sin(M)
    dftF = np.stack([c, s], 1).astype(NPBF)
    dftI = np.stack([c.T / n, s.T / n], 1).astype(NPBF)
    nt = n // 128
    nfb = max(1, nt // 4)
    fw = (nt // nfb) * 128
    tbw = min(512, n)
    dftF = np.ascontiguousarray(dftF.reshape(nt, 128, 2, nfb, fw).transpose(3, 2, 1, 0, 4)).reshape(nfb, 2, 128, nt * fw)
    dftI = np.ascontiguousarray(dftI.reshape(nt, 128, 2, n // tbw, tbw).transpose(3, 2, 1, 0, 4)).reshape(n // tbw, 2, 128, nt * tbw)
    _HY_CONST[n] = (zT, tcol, dftF, dftI)
    return _HY_CONST[n]


def hgrn_masks():
    i = np.arange(128)
    s, t = i[:, None], i[None, :]
    same32 = (s // 32) == (t // 32)
    same64 = (s // 64) == (t // 64)
    first = lambda x: (x % 64) < 32
    m1f = same32 & (s <= t)
    m2f = same64 & first(s) & ~first(t)
    m1b = same32 & (s >= t)
    m2b = same64 & ~first(s) & first(t)
    return np.ascontiguousarray(np.stack([m1f, m2f, m1b, m2b], 1)).astype(np.float32)


def hyena_deltas(r):
    mx = math.log(1e-2) / 0.3
    mn = math.log(1e-2) / 1.5
    d = np.abs(np.linspace(mn, mx, 1024, dtype=np.float32))
    return np.ascontiguousarray(np.broadcast_to(d[512 * r:512 * r + 512][None, :], (128, 512))).astype(np.float32)


def assemble_Modd(o0, o1, r, inp, o, ctx_out):
    cs = slice(512 * r, 512 * r + 512)
    hz = np.stack([canon_cols(o0["zT"][d * 1024 + 512 * r:d * 1024 + 512 * r + 512],
                              o1["zT"][d * 1024 + 512 * r:d * 1024 + 512 * r + 512]) for d in range(2)], 0)
    hu = np.stack([canon_cols(o0["uT"][g * 1024 + 512 * r:g * 1024 + 512 * r + 512],
                              o1["uT"][g * 1024 + 512 * r:g * 1024 + 512 * r + 512]) for g in range(3)], 0)
    lbl = np.stack([np.stack([fm_vec(inp["hgrn_lb_logits"][d, oo, cs]) for oo in range(2)], 1) for d in range(2)], 1)
    cwv = inp["hy_conv_w"][o]
    cw = np.stack([np.stack([fm_vec(cwv[tap, g * 1024 + 512 * r:g * 1024 + 512 * r + 512]) for g in range(3)], 1)
                   for tap in range(3)], 1)
    cbv = inp["hy_conv_b"][o]
    cb = np.stack([fm_vec(cbv[g * 1024 + 512 * r:g * 1024 + 512 * r + 512]) for g in range(3)], 1)
    wo = inp["hy_filt_wout"][o]
    zl, tl, fl, il = hyena_consts(2048)
    d = {"ident": np.eye(128, dtype=np.float32),
         "hq": canon_cols(o0["qT"][cs], o1["qT"][cs]), "hz": np.ascontiguousarray(hz),
         "hv": canon_rows(o0["vi"][:, cs], o1["vi"][:, cs]),
         "hg": canon_cols(o0["gT"][cs], o1["gT"][cs]), "hu": np.ascontiguousarray(hu),
         "lbl": np.ascontiguousarray(lbl).astype(np.float32), "ng": fm_vec(inp["hgrn_norm_g"][o][cs]),
         "hmasks": hgrn_masks(),
         "cw": np.ascontiguousarray(cw).astype(np.float32), "cb": np.ascontiguousarray(cb).astype(np.float32),
         "skip": fm_vec(inp["hy_skip"][o][cs]),
         "fw1": np.ascontiguousarray(inp["hy_filt_w1"][o]), "fw2": np.ascontiguousarray(inp["hy_filt_w2"][o]),
         "fw3": np.ascontiguousarray(inp["hy_filt_w3"][o]),
         "fb": np.ascontiguousarray(np.stack([inp["hy_filt_b1"][o], inp["hy_filt_b2"][o], inp["hy_filt_b3"][o],
                                              inp["hy_filt_freq"][o]], 1)).astype(np.float32),
         "fwo": np.ascontiguousarray(np.stack([wo[:, cs], wo[:, 1024 + 512 * r:1024 + 512 * r + 512]], 1)),
         "delt": hyena_deltas(r), "zT_l": zl, "tcol_l": tl, "dftF_l": fl, "dftI_l": il}
    if ctx_out:
        zc, tc, fc_, ic = hyena_consts(256)
        d.update({"zT_c": zc, "tcol_c": tc, "dftF_c": fc_, "dftI_c": ic})
    return d


def build_fused_program(depth=DEPTH):
    nc = bass.Bass("TRN2", target_bir_lowering=False)
    cx = Ctx(nc)
    m = cx.m
    ext = lambda n, s, d: m.dram(n, s, d, "ExternalInput")
    itn = lambda n, s, d: m.dram(n, s, d, "Internal")
    ident_d = ext("ident", [128, 128], F32)
    cx.consts(ident_d)
    cx.eps_col = m.sb("eps", [128, 1], F32)
    m.memset(cx.eps_col[:], EPS)
    h0_d = ext("h0", [2, 128, KC, TL], F32)
    cT_d = ext("cT", [128, KC, 2], F32)
    ada_w_d = ext("ada_w", [DEPTH, 24, 128, KC * 512], F32)
    ada_b_d = ext("ada_b", [DEPTH, 6 * D], F32)
    gmix_d = ext("gmix", [128, DEPTH, KC], F32)
    gmlp_d = ext("gmlp", [128, DEPTH, KC], F32)
    w_out_d = ext("w_out", [DEPTH, 8, 128, 4096], F32)
    w1_d = ext("mlp_w1", [DEPTH, 32, 128, 4096], F32)
    w2_d = ext("mlp_w2", [DEPTH, 32, 128, 4096], F32)
    evw_d = ext("ev_w_in", [2, EV_NEXT // 256, 128, 4096], F32)
    ukv_d = ext("w_ukv", [2, 2, 128, 4096], F32)
    odw_d = ext("od_w_in", [2, 32, 128, 4096], F32)
    cos_d = ext("cosT", [2, 64, TL], F32)
    sin_d = ext("sinT", [2, 64, TL], F32)
    kvg_d = ext("kvg", [2, 128, 4], F32)
    fg_d = ext("fg", [128, KC], F32)
    nab_d = ext("nabias", [2, 8, 128, 2880], F32)
    namask_d = ext("namask", [128, 2880], F32)
    lbl_d = ext("lbl", [2, 128, 2, 2, 4], F32)
    ng_d = ext("ng", [2, 2, 128, 4], F32)
    hmasks_d = ext("hmasks", [128, 4, 128], F32)
    cw_d = ext("cw", [2, 2, 128, 3, 3, 4], F32)
    cb_d = ext("cb", [2, 2, 128, 3, 4], F32)
    skip_d = ext("skip", [2, 2, 128, 4], F32)
    fw1_d = ext("fw1", [2, 33, 64], F32)
    fw2_d = ext("fw2", [2, 64, 64], F32)
    fw3_d = ext("fw3", [2, 64, 64], F32)
    fb_d = ext("fb", [2, 64, 4], F32)
    fwo_d = ext("fwo", [2, 2, 64, 2, 512], F32)
    delt_d = ext("delt", [2, 128, 512], F32)
    hyc = {"zT_l": ext("zT_l", [33, 2048], F32), "tcol_l": ext("tcol_l", [128, 16], F32),
           "dftF_l": ext("dftF_l", [4, 2, 128, 8192], BF16), "dftI_l": ext("dftI_l", [4, 2, 128, 8192], BF16),
           "zT_c": ext("zT_c", [33, 256], F32), "tcol_c": ext("tcol_c", [128, 2], F32),
           "dftF_c": ext("dftF_c", [1, 2, 128, 512], BF16), "dftI_c": ext("dftI_c", [1, 2, 128, 512], BF16)}
    outT_d = m.dram("outT", [2, 128, KC, 1024], F32, "ExternalOutput")
    hT_d = [itn("hT_%d" % v, [128, KC, TL], F32) for v in range(2)]
    RE = [{"qT": itn("re_qT%d" % v, [8, 192, TL], BF16), "kT": itn("re_kT%d" % v, [8, 128, TL], BF16),
           "kpeT": itn("re_kpeT%d" % v, [64, TL], BF16), "v": itn("re_v%d" % v, [TL, 1024], BF16),
           "qnT": itn("re_qnT%d" % v, [1024, TL], BF16), "knT": itn("re_knT%d" % v, [1024, TL], BF16),
           "vn": itn("re_vn%d" % v, [TL, 1024], BF16)} for v in range(2)]
    RO = [{"qT": itn("ro_qT%d" % v, [1024, TL], BF16), "vi": itn("ro_vi%d" % v, [TL, 1024], BF16),
           "zT": itn("ro_zT%d" % v, [2048, TL], F32), "gT": itn("ro_gT%d" % v, [1024, TL], BF16),
           "uT": itn("ro_uT%d" % v, [3072, TL], BF16)} for v in range(2)]
    ME = [{"mq": itn("me_mq%d" % v, [4, 192, TM], BF16), "mk": itn("me_mk%d" % v, [4, 128, TM], BF16),
           "mkpe": itn("me_mkpe%d" % v, [64, TM], BF16), "mv": itn("me_mv%d" % v, [TM, 512], BF16),
           "mqn": itn("me_mqn%d" % v, [512, TM], BF16), "mkn": itn("me_mkn%d" % v, [512, TM], BF16),
           "mvn": itn("me_mvn%d" % v, [TM, 512], BF16)} for v in range(2)]
    MO = [{"hq": itn("mo_hq%d" % v, [512, TM], BF16), "hz": itn("mo_hz%d" % v, [2, 512, TM], F32),
           "hv": itn("mo_hv%d" % v, [TM, 512], BF16), "hg": itn("mo_hg%d" % v, [512, TM], BF16),
           "hu": itn("mo_hu%d" % v, [3, 512, TM], BF16)} for v in range(2)]
    myT_d = [itn("myT_%d" % v, [8, 128, TM], BF16) for v in range(2)]

    def pcols(dst, s, sv):
        if len(dst.shape) == 3:
            m.dma(dst[:, :, 128 * sv:128 * sv + 128], s[:, :, 1024:TL])
            m.dma(dst[:, :, 256 + 1024 * sv:1280 + 1024 * sv], s[:, :, 0:1024])
        else:
            m.dma(dst[:, 128 * sv:128 * sv + 128], s[:, 1024:TL])
            m.dma(dst[:, 256 + 1024 * sv:1280 + 1024 * sv], s[:, 0:1024])

    def prows(dst, s, sv):
        m.dma(dst[128 * sv:128 * sv + 128], s[1024:TL])
        m.dma(dst[256 + 1024 * sv:1280 + 1024 * sv], s[0:1024])

    def assemble_from(l, sv):
        for tv in range(2):
            hs = slice(4 * tv, 4 * tv + 4)
            cs = slice(512 * tv, 512 * tv + 512)
            if l % 2 == 0:
                pcols(ME[tv]["mq"], RE[sv]["qT"][hs], sv)
                pcols(ME[tv]["mk"], RE[sv]["kT"][hs], sv)
                pcols(ME[tv]["mkpe"], RE[sv]["kpeT"], sv)
                prows(ME[tv]["mv"], RE[sv]["v"][:, cs], sv)
                pcols(ME[tv]["mqn"], RE[sv]["qnT"][cs], sv)
                pcols(ME[tv]["mkn"], RE[sv]["knT"][cs], sv)
                prows(ME[tv]["mvn"], RE[sv]["vn"][:, cs], sv)
            else:
                pcols(MO[tv]["hq"], RO[sv]["qT"][cs], sv)
                for d in range(2):
                    zs = slice(d * 1024 + 512 * tv, d * 1024 + 512 * tv + 512)
                    pcols(MO[tv]["hz"][d], RO[sv]["zT"][zs], sv)
                prows(MO[tv]["hv"], RO[sv]["vi"][:, cs], sv)
                pcols(MO[tv]["hg"], RO[sv]["gT"][cs], sv)
                for g in range(3):
                    us = slice(g * 1024 + 512 * tv, g * 1024 + 512 * tv + 512)
                    pcols(MO[tv]["hu"][g], RO[sv]["uT"][us], sv)

    modall = m.sb("modall", [128, DEPTH, 128, 2], F32)
    emit_mod(cx, modall[:, :, 0:96, :], cT_d, ada_w_d, ada_b_d, 96, 2)
    emit_gsc(cx, modall, gmix_d, gmlp_d)
    base_mark = m.mark()
    for l in range(depth + 1):
        last = (l == depth)
        for v in range(2):
            m.release(base_mark)
            hT = m.sb("hT", [128, KC, TL], F32)
            m.dma(hT[:], h0_d[v] if l == 0 else hT_d[v])
            actT = m.sb("actT", [128, KC, TL], BF16)
            rstd = m.sb("rstd", [128, TL], F32)
            cx.init_wring(4)
            if l >= 1:
                def load_y(actT_, v=v):
                    for f in range(16):
                        own, loc = (f // 4, f % 4) if f < 8 else ((f - 8) // 4, 4 + (f - 8) % 4)
                        srcT = myT_d[own][loc]
                        m.dma(actT_[:, f, 0:1024], srcT[:, 256 + 1024 * v:256 + 1024 * (v + 1)])
                        m.dma(actT_[:, f, 1024:TL], srcT[:, 128 * v:128 * (v + 1)])
                io = {"load_y": load_y, "w_out": w_out_d[l - 1], "w1": w1_d[l - 1], "w2": w2_d[l - 1]}
                emit_post(cx, l - 1, hT, actT, rstd, modall, io, TL if not last or depth < DEPTH else 1024)
            if not last and l % 2 == 0:
                e = l // 2
                io = dict(RE[v])
                io.update({"w_in": evw_d[e], "w_ukv": ukv_d[e], "cosT": cos_d[v], "sinT": sin_d[v], "kvg": kvg_d[e]})
                emit_pre_even(cx, l, hT, actT, rstd, modall, io)
            elif not last:
                io = dict(RO[v])
                io.update({"w_in": odw_d[l // 2]})
                emit_pre_odd(cx, l, hT, actT, rstd, modall, io)
            if not last:
                m.dma(hT_d[v], hT[:])
                assemble_from(l, v)
            else:
                emit_final(cx, hT, rstd, {"fg": fg_d, "outT": outT_d[v]})
        if last:
            break
        ctx_out = l < DEPTH - 1
        for v in range(2):
            m.release(base_mark)
            hs = slice(4 * v, 4 * v + 4)
            cs = slice(512 * v, 512 * v + 512)
            if l % 2 == 0:
                e = l // 2
                io = dict(ME[v])
                io.update({"nabias": nab_d[e, hs], "namask": namask_d, "myT": myT_d[v]})
                emit_M_even(cx, io, ctx_out)
            else:
                o = l // 2
                io = dict(MO[v])
                io.update(hyc)
                io.update({"odd_idx": o, "lbl": lbl_d[v], "ng": ng_d[o, v], "hmasks": hmasks_d,
                           "cw": cw_d[o, v], "cb": cb_d[o, v], "skip": skip_d[o, v],
                           "fw1": fw1_d[o], "fw2": fw2_d[o], "fw3": fw3_d[o], "fb": fb_d[o],
                           "fwo": fwo_d[o, v], "delt": delt_d[v], "myT": myT_d[v]})
                ystage = [m.sb("ystage%d" % i, [128, TM], BF16) for i in range(2)]
                emit_hgrn(cx, io, ctx_out, ystage)
                emit_hyena(cx, io, ctx_out, ystage)
    m.emit()
    return nc


_FUSED = {}


def host_inputs(inp, b):
    x, ctx, c, c_ctx = inp["x"], inp["ctx"], inp["c"], inp["c_ctx"]
    h0 = np.stack([fm_tokens(np.concatenate([x[b, r * 1024:(r + 1) * 1024], ctx[b, r * 128:(r + 1) * 128]], 0))
                   for r in range(2)], 0)
    d = {"h0": np.ascontiguousarray(h0), "cT": np.ascontiguousarray(np.stack([fm_vec(c[b]), fm_vec(c_ctx)], -1))}
    return d


def slotify(W, nkc, k_blocks=1):
    K, N = W.shape
    ncb = 4096 // nkc
    assert K == k_blocks * nkc * 128 and N % ncb == 0
    a = W.reshape(k_blocks, nkc, 128, N // ncb, ncb).transpose(0, 3, 2, 1, 4)
    return np.ascontiguousarray(a).reshape(k_blocks * (N // ncb), 128, nkc * ncb)


def host_shared(inp):
    adaw = inp["ada_w"].reshape(DEPTH, KC, 128, 24, 512).transpose(0, 3, 2, 1, 4)
    sh = {"ident": np.eye(128, dtype=np.float32),
          "ada_w": np.ascontiguousarray(adaw).reshape(DEPTH, 24, 128, KC * 512), "ada_b": inp["ada_b"],
          "gmix": np.ascontiguousarray(np.stack([fm_vec(inp["norm_mix_g"][l]) for l in range(DEPTH)], 1)),
          "gmlp": np.ascontiguousarray(np.stack([fm_vec(inp["norm_mlp_g"][l]) for l in range(DEPTH)], 1)),
          "w_out": np.stack([slotify(inp["w_out"][l], KC) for l in range(DEPTH)], 0),
          "mlp_w1": np.stack([slotify(inp["mlp_w1"][l], KC) for l in range(DEPTH)], 0),
          "mlp_w2": np.stack([slotify(inp["mlp_w2"][l], 4, 16) for l in range(DEPTH)], 0),
          "ev_w_in": np.stack([slotify(ev_w_in_ext(inp["ev_w_in"][e]), KC) for e in range(2)], 0),
          "w_ukv": np.stack([slotify(ukv_ext(inp["mla_w_ukv"][e]), 4) for e in range(2)], 0),
          "od_w_in": np.stack([slotify(inp["od_w_in"][o], KC) for o in range(2)], 0),
          "kvg": np.stack([fm_vec(inp["mla_kv_norm_g"][e]) for e in range(2)], 0),
          "fg": fm_vec(inp["final_norm_g"])}
    rt = [rope_tables(r) for r in range(2)]
    sh["cosT"] = np.stack([rt[0][0], rt[1][0]], 0)
    sh["sinT"] = np.stack([rt[0][1], rt[1][1]], 0)
    nabs = [na_bias_host(inp["na_rel_bias"][e]) for e in range(2)]
    sh["nabias"] = np.ascontiguousarray(np.stack([nabs[0][0], nabs[1][0]], 0))
    sh["namask"] = nabs[0][1]
    f32 = lambda a: np.ascontiguousarray(a).astype(np.float32)
    lg = inp["hgrn_lb_logits"]
    sh["lbl"] = f32(np.stack([np.stack([np.stack([fm_vec(lg[d, oo, 512 * v:512 * v + 512]) for oo in range(2)], 1)
                                        for d in range(2)], 1) for v in range(2)], 0))
    sh["ng"] = f32(np.stack([np.stack([fm_vec(inp["hgrn_norm_g"][o][512 * v:512 * v + 512]) for v in range(2)], 0)
                             for o in range(2)], 0))
    sh["hmasks"] = hgrn_masks()
    cwl, cbl, skl, fwol = [], [], [], []
    for o in range(2):
        cwv, cbv, wo = inp["hy_conv_w"][o], inp["hy_conv_b"][o], inp["hy_filt_wout"][o]
        cwl.append(np.stack([np.stack([np.stack([fm_vec(cwv[tap, g * 1024 + 512 * v:g * 1024 + 512 * v + 512])
                                                 for g in range(3)], 1) for tap in range(3)], 1) for v in range(2)], 0))
        cbl.append(np.stack([np.stack([fm_vec(cbv[g * 1024 + 512 * v:g * 1024 + 512 * v + 512]) for g in range(3)], 1)
                             for v in range(2)], 0))
        skl.append(np.stack([fm_vec(inp["hy_skip"][o][512 * v:512 * v + 512]) for v in range(2)], 0))
        fwol.append(np.stack([np.stack([wo[:, 512 * v:512 * v + 512], wo[:, 1024 + 512 * v:1024 + 512 * v + 512]], 1)
                              for v in range(2)], 0))
    sh["cw"], sh["cb"], sh["skip"], sh["fwo"] = f32(np.stack(cwl, 0)), f32(np.stack(cbl, 0)), f32(np.stack(skl, 0)), f32(np.stack(fwol, 0))
    sh["fw1"], sh["fw2"], sh["fw3"] = f32(inp["hy_filt_w1"]), f32(inp["hy_filt_w2"]), f32(inp["hy_filt_w3"])
    sh["fb"] = f32(np.stack([inp["hy_filt_b1"], inp["hy_filt_b2"], inp["hy_filt_b3"], inp["hy_filt_freq"]], -1))
    sh["delt"] = np.stack([hyena_deltas(v) for v in range(2)], 0)
    zl, tl, fl, il = hyena_consts(2048)
    zc, tc, fc_, ic = hyena_consts(256)
    sh.update({"zT_l": zl, "tcol_l": tl, "dftF_l": fl, "dftI_l": il, "zT_c": zc, "tcol_c": tc, "dftF_c": fc_, "dftI_c": ic})
    return sh


def kernel(**inp):
    inp = {k: np.asarray(v) for k, v in inp.items()}
    if "nc" not in _FUSED:
        _FUSED["nc"] = build_fused_program()
    sh = host_shared(inp)
    per_b = [host_inputs(inp, b) for b in range(4)]
    maps = []
    for core in range(8):
        d = dict(sh)
        d.update(per_b[core % 4])
        maps.append(d)
    res = run_bass_kernel_spmd(_FUSED["nc"], maps, core_ids=list(range(8)))
    out = np.zeros((4, SEQ, D), dtype=np.float32)
    for b in range(4):
        oT = np.asarray(res.results[b]["outT"])
        for r in range(2):
            out[b, r * 1024:(r + 1) * 1024] = oT[r].transpose(2, 1, 0).reshape(1024, D)
    return out
```

```python
import math
import numpy as np
import ml_dtypes
import concourse.bass as bass
import concourse.mybir as mybir
from concourse.bass_utils import run_bass_kernel_spmd

F32 = mybir.dt.float32
BF16 = mybir.dt.bfloat16
AF = mybir.ActivationFunctionType
ALU = mybir.AluOpType
AX = mybir.AxisListType
NPBF = ml_dtypes.bfloat16

D = 2048
SEQ = 2048
CTX = 256
DEPTH = 4
TL = 1152
TM = 2304
KC = 16
HID = 8192
EPS = 1e-6
GRID_W = 64

ENGS = ("pe", "act", "dve", "pool", "sp")
DMA_RING = 12
SB_BASE = 16640
SB_END = 229376


def _dsize(dt):
    return 4 if dt == F32 else 2


class Op:
    __slots__ = ("eng", "fn", "waits", "idx", "is_dma", "dma_sem", "dma_val", "needs_inc")


class MK:
    def __init__(self, nc):
        self.nc = nc
        self.ops = {e: [] for e in ENGS}
        self.acc = {}
        self.waited = {e: {} for e in ENGS}
        self.dma_count = {e: 0 for e in ENGS}
        self.dma_ring_uses = {e: [0] * DMA_RING for e in ENGS}
        self.sb_base = {}
        self.sb_top = SB_BASE
        self.psum = [nc.alloc_psum_tensor("bank%d" % i, [128, 512], F32) for i in range(8)]
        self.n_dram = 0

    def sb(self, name, shape, dtype, at=None):
        per = _dsize(dtype)
        for s in shape[1:]:
            per *= int(s)
        if at is None:
            at = self.sb_top
            self.sb_top = (at + per + 31) // 32 * 32
        assert at % 32 == 0 and at + per <= SB_END, (name, at, per)
        t = self.nc.alloc_sbuf_tensor_at(name, list(shape), dtype, offset=at)
        self.sb_base[t.name] = (at, per)
        return t

    def mark(self):
        return self.sb_top

    def release(self, mark):
        self.sb_top = mark

    def dram(self, name, shape, dtype, kind="Internal"):
        return self.nc.dram_tensor(name, list(shape), dtype, kind=kind).ap()

    def _box(self, ap):
        t = ap.tensor
        space = str(ap.space)
        off = int(ap.offset)
        pairs = [(int(s), int(c)) for (s, c) in ap.ap]
        ds = _dsize(ap.dtype)
        if space == "PSUM":
            return ("P:" + t.name, 0, 128, 0, 1)
        if space == "SB":
            base, per = self.sb_base[t.name]
            P = per // ds
            plo = off // P
            flo = off % P
            pext = 0
            fext = 0
            for (s, c) in pairs:
                if c <= 1:
                    continue
                if s != 0 and s % P == 0:
                    pext += (c - 1) * (s // P)
                else:
                    fext += (c - 1) * abs(s)
            return ("SB", plo, plo + pext + 1, base + flo * ds, base + (flo + fext + 1) * ds)
        ext = 0
        for (s, c) in pairs:
            if c <= 1:
                continue
            ext += (c - 1) * abs(s)
        return ("D:" + t.name, 0, 1, off, off + ext + 1)

    @staticmethod
    def _overlap(a, b):
        return a[1] < b[2] and b[1] < a[2] and a[3] < b[4] and b[3] < a[4]

    @staticmethod
    def _contains(a, b):
        return a[1] <= b[1] and b[2] <= a[2] and a[3] <= b[3] and b[4] <= a[4]

    def _need(self, op, tok):
        e = op.eng
        if tok[0] == "e":
            _, x, idx = tok
            if x == e and e == "pe":
                return
            key = ("e", x)
            val = idx
        else:
            _, q, slot, useno = tok
            key = ("d", q, slot)
            val = useno
        w = self.waited[e]
        if w.get(key, -1) >= val:
            return
        w[key] = val
        op.waits.append((key, val))

    BUCKET = 2048

    def _split(self, b):
        if b[0] != "SB":
            return [b]
        out = []
        lo, hi = b[3], b[4]
        k = lo // self.BUCKET
        while k * self.BUCKET < hi:
            out.append((("SB", k), b[1], b[2], max(lo, k * self.BUCKET), min(hi, (k + 1) * self.BUCKET)))
            k += 1
        return out

    def _track(self, op, reads, writes, tok):
        rb = []
        wb = []
        for a in reads:
            b = self._box(a)
            if b[0].startswith("P:"):
                wb.append(b)
            else:
                rb.extend(self._split(b))
        for a in writes:
            wb.extend(self._split(self._box(a)))
        for b in rb:
            lst = self.acc.get(b[0])
            if lst:
                for rec in lst:
                    if rec[1] == "w" and self._overlap(rec[0], b):
                        self._need(op, rec[2])
        for b in wb:
            lst = self.acc.get(b[0])
            if lst:
                for rec in lst:
                    if self._overlap(rec[0], b):
                        self._need(op, rec[2])
        for b in rb:
            lst = self.acc.setdefault(b[0], [])
            if tok[0] == "e":
                lst[:] = [r for r in lst if not (r[1] == "r" and r[2][0] == "e" and r[2][1] == tok[1]
                                                 and self._contains(b, r[0]))]
            lst.append([b, "r", tok])
        for b in wb:
            lst = self.acc.setdefault(b[0], [])
            lst[:] = [r for r in lst if not self._contains(b, r[0])]
            lst.append([b, "w", tok])

    def op(self, eng, fn, reads, writes):
        o = Op()
        o.eng = eng
        o.fn = fn
        o.waits = []
        o.is_dma = False
        o.needs_inc = False
        o.idx = len(self.ops[eng])
        self._track(o, reads, writes, ("e", eng, o.idx))
        self.ops[eng].append(o)
        return o

    def dma(self, out, in_, q="sp", **kw):
        o = Op()
        o.eng = q
        o.waits = []
        o.is_dma = True
        o.needs_inc = False
        o.idx = len(self.ops[q])
        n = self.dma_count[q]
        self.dma_count[q] = n + 1
        slot = n % DMA_RING
        prev = self.dma_ring_uses[q][slot]
        if prev > 0:
            self._need(o, ("d", q, slot, prev))
        self.dma_ring_uses[q][slot] = prev + 1
        o.dma_sem = (q, slot)
        o.dma_val = prev + 1
        o.fn = lambda eng, out=out, in_=in_, kw=kw: eng.dma_start(out=out, in_=in_, **kw)
        self._track(o, [in_], [out], ("d", q, slot, prev + 1))
        self.ops[q].append(o)
        return o

    def matmul(self, out, lhsT, rhs, start=True, stop=True, **kw):
        return self.op("pe", lambda e: e.matmul(out, lhsT, rhs, start=start, stop=stop, **kw),
                       [lhsT, rhs], [out])

    def transpose(self, out, in_, ident):
        return self.op("pe", lambda e: e.transpose(out, in_, ident), [in_, ident], [out])

    def act(self, out, in_, func, bias=None, scale=None, accum_out=None):
        reads = [in_]
        kw = {}
        if bias is not None:
            kw["bias"] = bias
            if not isinstance(bias, (int, float)):
                reads.append(bias)
        if scale is not None:
            kw["scale"] = scale
            if not isinstance(scale, (int, float)):
                reads.append(scale)
        writes = [out]
        if accum_out is not None:
            kw["accum_out"] = accum_out
            writes.append(accum_out)
        return self.op("act", lambda e: e.activation(out, in_, func, **kw), reads, writes)

    def tt(self, out, in0, in1, op, eng="dve"):
        return self.op(eng, lambda e: e.tensor_tensor(out, in0, in1, op), [in0, in1], [out])

    def ts(self, out, in0, s1, s2, op0, op1=None, eng="dve"):
        reads = [in0]
        if not isinstance(s1, (int, float)):
            reads.append(s1)
        if s2 is not None and not isinstance(s2, (int, float)):
            reads.append(s2)
        if op1 is None:
            return self.op(eng, lambda e: e.tensor_scalar(out, in0, s1, None, op0), reads, [out])
        return self.op(eng, lambda e: e.tensor_scalar(out, in0, s1, s2, op0, op1), reads, [out])

    def stt(self, out, in0, scalar, in1, op0, op1):
        reads = [in0, in1]
        if not isinstance(scalar, (int, float)):
            reads.append(scalar)
        return self.op("dve", lambda e: e.scalar_tensor_tensor(out, in0, scalar, in1, op0, op1),
                       reads, [out])

    def copy(self, out, in_, eng="dve"):
        if eng == "act":
            return self.op("act", lambda e: e.copy(out, in_), [in_], [out])
        return self.op(eng, lambda e: e.tensor_copy(out, in_), [in_], [out])

    def memset(self, ap, val, eng="dve"):
        return self.op(eng, lambda e: e.memset(ap, val), [], [ap])

    def recip(self, out, in_):
        return self.op("dve", lambda e: e.reciprocal(out, in_), [in_], [out])

    def scan(self, out, d0, d1, init, op0, op1):
        reads = [d0, d1]
        if not isinstance(init, (int, float)):
            reads.append(init)
        return self.op("dve", lambda e: e.tensor_tensor_scan(out, d0, d1, init, op0, op1), reads, [out])

    def emit(self):
        nc = self.nc
        for e in ENGS:
            for o in self.ops[e]:
                for (key, val) in o.waits:
                    if key[0] == "e":
                        self.ops[key[1]][val].needs_inc = True
        cnt = {}
        for e in ENGS:
            c = 0
            arr = []
            for o in self.ops[e]:
                if o.needs_inc and not o.is_dma:
                    c += 1
                arr.append(c)
            cnt[e] = arr
        from contextlib import ExitStack
        with ExitStack() as st:
            esem = {e: st.enter_context(nc.semaphore("s_" + e)) for e in ENGS}
            dsem = {}
            for q in ENGS:
                for s in range(min(DMA_RING, self.dma_count[q])):
                    dsem[(q, s)] = st.enter_context(nc.semaphore("d_%s_%d" % (q, s)))
            block = st.enter_context(nc.Block())
            engmap = {"pe": "tensor", "act": "scalar", "dve": "vector", "pool": "gpsimd", "sp": "sync"}

            def make(e):
                def body(eng):
                    for o in self.ops[e]:
                        ws = [((esem[key[1]], cnt[key[1]][val]) if key[0] == "e" else
                               (dsem[(key[1], key[2])], 16 * val)) for (key, val) in o.waits]
                        attach = None
                        if ws and not o.is_dma:
                            attach = ws.pop()
                        for (s_, v_) in ws:
                            eng.wait_ge(s_, v_)
                        ins = o.fn(eng)
                        if attach is not None:
                            ins._wait_ge(attach[0], attach[1])
                        if o.is_dma:
                            ins.then_inc(dsem[o.dma_sem], 16)
                        elif o.needs_inc:
                            ins.then_inc(esem[e], 1)
                    for s in range(min(DMA_RING, self.dma_count[e])):
                        eng.wait_ge(dsem[(e, s)], 16 * self.dma_ring_uses[e][s])
                return body

            for e in ENGS:
                if len(self.ops[e]) == 0:
                    continue
                getattr(block, engmap[e])(make(e))
        return nc


MOD_SH_A, MOD_SC_A, MOD_G_A, MOD_SH_M, MOD_SC_M, MOD_G_M, MOD_GSC_A, MOD_GSC_M = [16 * i for i in range(8)]


class Ctx:
    def __init__(self, nc):
        self.nc = nc
        self.m = MK(nc)
        self.wring = None
        self.wi = 0
        self.ev = 0

    def consts(self, ident_d, need_bf=True):
        m = self.m
        self.ident = m.sb("ident", [128, 128], F32)
        m.dma(self.ident[:], ident_d)
        self.ones_bf = m.sb("ones_bf", [128, 128], BF16)
        m.memset(self.ones_bf[:], 1.0)
        self.ident_bf = m.sb("ident_bf", [128, 128], BF16)
        m.copy(self.ident_bf[:], self.ident[:])

    def init_wring(self, nslot=5):
        m = self.m
        self.wring = [m.sb("wslot%d" % i, [128, 4096], BF16) for i in range(nslot)]
        self.wi = 0

    def wload(self, Ws, si, kcb, ncb, nuse=None):
        assert kcb * ncb <= 4096
        slot = self.wring[self.wi % len(self.wring)]
        self.wi += 1
        view = slot[:, 0:kcb * ncb].rearrange("p (k n) -> p k n", n=ncb)
        src = Ws[si].rearrange("p (k n) -> p k n", n=ncb)
        if nuse is not None and nuse < ncb:
            self.m.dma(view[:, :, 0:nuse], src[:, :, 0:nuse], q="pool")
        else:
            self.m.dma(view, src, q="pool")
        return view

    def evac_eng(self):
        self.ev += 1
        return "act" if self.ev % 2 == 0 else "dve"


def rstd_compute(cx, src, nch, T, nfeat, rstd, sq, ps_bank, tmp):
    m = cx.m
    ps = cx.m.psum[ps_bank]
    for t0 in range(0, T, 384):
        tw = min(384, T - t0)
        for k in range(nch):
            m.act(sq[:, k, 0:tw], src[:, k, t0:t0 + tw], AF.Square)
        for k in range(nch):
            m.matmul(ps[:, 0:tw], cx.ones_bf[:, 0:128], sq[:, k, 0:tw], start=(k == 0), stop=(k == nch - 1))
        m.act(tmp[:, 0:tw], ps[:, 0:tw], AF.Sqrt, bias=cx.eps_col[:, 0:1], scale=1.0 / nfeat)
        m.recip(rstd[:, t0:t0 + tw], tmp[:, 0:tw])


def norm_modulate(cx, hT, actT, rstd, modall, l, gsc_base, sh_base, tmp4):
    m = cx.m
    for k in range(KC):
        for (c0, c1, col) in ((0, 1024, 0), (1024, TL, 1)):
            gs = modall[:, l, gsc_base + k, col:col + 1]
            sh = modall[:, l, sh_base + k, col:col + 1]
            m.stt(tmp4[:, c0:c1], hT[:, k, c0:c1], gs, rstd[:, c0:c1], ALU.mult, ALU.mult)
            m.act(actT[:, k, c0:c1], tmp4[:, c0:c1], AF.Identity, bias=sh, scale=1.0)


def proj_fm(cx, W2d, ncols_total, col0, chunk_cols, inT, nkc, consume, T=TL, bank_sets=((0, 1, 2), (3, 4, 5))):
    m = cx.m
    ncb = 4096 // nkc
    nchunks = ncols_total // chunk_cols
    per_slot = ncb // chunk_cols
    ci = 0
    tbs = [(t0, min(384, T - t0)) for t0 in range(0, T, 384)]
    assert col0 % ncb == 0
    for s0 in range(0, nchunks, per_slot):
        nhere = min(per_slot, nchunks - s0)
        slot = cx.wload(W2d, (col0 + s0 * chunk_cols) // ncb, nkc, ncb, nhere * chunk_cols)
        for j in range(nhere):
            banks = bank_sets[ci % len(bank_sets)]
            for k in range(nkc):
                for ti, (t0, tw) in enumerate(tbs):
                    m.matmul(m.psum[banks[ti]][0:chunk_cols, 0:tw],
                             slot[:, k, j * chunk_cols:(j + 1) * chunk_cols],
                             inT[:, k, t0:t0 + tw], start=(k == 0), stop=(k == nkc - 1))
            for ti, (t0, tw) in enumerate(tbs):
                consume(ci, ti, m.psum[banks[ti]][0:chunk_cols, 0:tw], t0, tw)
            ci += 1


def emit_mod(cx, out, cT_d, ada_w_d, ada_b_d, nch, nr):
    m = cx.m
    mk = m.mark()
    sc = m.sb("mod_sc", [128, KC, nr], F32)
    scb = m.sb("mod_scb", [128, KC, nr], BF16)
    m.dma(sc[:], cT_d)
    m.act(sc[:], sc[:], AF.Silu)
    m.copy(scb[:], sc[:])
    adabT = m.sb("mod_adabT", [128, DEPTH, nch], F32)
    btmp = m.sb("mod_btmp", [nch, 128], F32)
    for l in range(DEPTH):
        m.dma(btmp[:], ada_b_d[l].rearrange("(c p) -> c p", p=128))
        m.transpose(m.psum[6][:, 0:nch], btmp[:], cx.ident[0:nch, 0:nch])
        m.copy(adabT[:, l, :], m.psum[6][:, 0:nch])
    wsl = [m.sb("mod_w%d" % i, [128, KC, 512], BF16) for i in range(4)]
    wi = 0
    for l in range(DEPTH):
        ps = m.psum[l % 2]
        for nb in range(nch // 4):
            slot = wsl[wi % 4]
            wi += 1
            m.dma(slot[:], ada_w_d[l, nb].rearrange("p (k n) -> p k n", n=512), q="pool")
            for jj in range(4):
                j = nb * 4 + jj
                for k in range(KC):
                    m.matmul(ps[:, nr * j:nr * j + nr], slot[:, k, jj * 128:(jj + 1) * 128], scb[:, k, :],
                             start=(k == 0), stop=(k == KC - 1))
        m.tt(out[:, l, :, :], ps[:, 0:nch * nr].rearrange("p (c t) -> p c t", t=nr),
             adabT[:, l, :].unsqueeze(2).to_broadcast([128, nch, nr]), ALU.add)
    m.release(mk)


def emit_gsc(cx, modall, gmix_d, gmlp_d):
    m = cx.m
    mk = m.mark()
    gmix = m.sb("mod_gmix", [128, DEPTH, KC], F32)
    gmlp = m.sb("mod_gmlp", [128, DEPTH, KC], F32)
    m.dma(gmix[:], gmix_d)
    m.dma(gmlp[:], gmlp_d)
    for l in range(DEPTH):
        m.stt(modall[:, l, MOD_GSC_A:MOD_GSC_A + 16, :], modall[:, l, MOD_SC_A:MOD_SC_A + 16, :], 1.0,
              gmix[:, l, :].unsqueeze(2).to_broadcast([128, KC, 2]), ALU.add, ALU.mult)
        m.stt(modall[:, l, MOD_GSC_M:MOD_GSC_M + 16, :], modall[:, l, MOD_SC_M:MOD_SC_M + 16, :], 1.0,
              gmlp[:, l, :].unsqueeze(2).to_broadcast([128, KC, 2]), ALU.add, ALU.mult)
    m.release(mk)


def build_mod_program():
    nc = bass.Bass("TRN2", target_bir_lowering=False)
    cx = Ctx(nc)
    m = cx.m
    ident_d = m.dram("ident", [128, 128], F32, "ExternalInput")
    cT_d = m.dram("cT", [128, KC, 5], F32, "ExternalInput")
    ada_w_d = m.dram("ada_w", [DEPTH, D, 1536], F32, "ExternalInput")
    ada_b_d = m.dram("ada_b", [DEPTH, 1536], F32, "ExternalInput")
    out_d = m.dram("modpart", [128, DEPTH, 12, 5], F32, "ExternalOutput")
    cx.consts(ident_d)
    outt = m.sb("modpart", [128, DEPTH, 12, 5], F32)
    emit_mod(cx, outt, cT_d, ada_w_d, ada_b_d, 12, 5)
    m.dma(out_d, outt[:])
    m.emit()
    return nc


EV_QNOPE, EV_CKV, EV_QNA, EV_KNA, EV_PE, EV_VNA, EV_NEXT = 0, 1024, 1536, 2560, 3584, 4864, 5888
SEGS = ((0, 1024, 0), (1024, TL, 1))


def proj_tm(cx, W2d, k0, nkc, col0, ncols, inT, consume, T, banks=(6, 7)):
    m = cx.m
    ncb = 4096 // nkc
    bi = 0
    assert col0 % ncb == 0 and k0 == 0
    for c0 in range(0, ncols, ncb):
        cw = min(ncb, ncols - c0)
        slot = cx.wload(W2d, (col0 + c0) // ncb, nkc, ncb, cw)
        for t in range(T // 128):
            for n0 in range(0, cw, 512):
                nw = min(512, cw - n0)
                ps = m.psum[banks[bi % 2]]
                bi += 1
                for k in range(nkc):
                    m.matmul(ps[:, 0:nw], inT[:, k, t * 128:(t + 1) * 128], slot[:, k, n0:n0 + nw],
                             start=(k == 0), stop=(k == nkc - 1))
                consume(t, c0 + n0, nw, ps[:, 0:nw])


def rmw_consumer(cx, hT, modall, lidx, gate_base):
    m = cx.m

    def consume(ci, ti, ps, t0, tw):
        for (c0, c1, col) in SEGS:
            a, b = max(c0, t0), min(c1, t0 + tw)
            if a >= b:
                continue
            m.stt(hT[:, ci, a:b], ps[:, a - t0:b - t0], modall[:, lidx, gate_base + ci, col:col + 1],
                  hT[:, ci, a:b], ALU.mult, ALU.add)
    return consume


def emit_post(cx, lp, hT, actT, rstd, modall, io, T):
    m = cx.m
    if "load_y" in io:
        io["load_y"](actT)
    else:
        m.dma(actT[:], io["yT"])
    proj_fm(cx, io["w_out"], D, 0, 128, actT, KC, rmw_consumer(cx, hT, modall, lp, MOD_G_A), T=T)
    mk = m.mark()
    sq = m.sb("sq", [128, KC, 384], BF16)
    tmp4 = m.sb("tmp4", [128, TL], F32)
    tmp = m.sb("tmp", [128, 384], F32)
    rstd_compute(cx, hT, KC, T, D, rstd, sq, 6, tmp)
    norm_modulate(cx, hT, actT, rstd, modall, lp, MOD_GSC_M, MOD_SH_M, tmp4)
    m.release(mk)
    mk = m.mark()
    hid = [m.sb("hid%d" % i, [128, 4, TL], BF16) for i in range(2)]
    rtmp = [m.sb("rtmp%d" % i, [128, 384], F32) for i in range(2)]
    rmw = rmw_consumer(cx, hT, modall, lp, MOD_G_M)
    cnt = [0]
    tbs = [(t0, min(384, T - t0)) for t0 in range(0, T, 384)]
    for hb in range(HID // 512):
        hbuf = hid[hb % 2]

        def relu2(ci, ti, ps, t0, tw, hbuf=hbuf):
            r = rtmp[cnt[0] % 2]
            cnt[0] += 1
            m.act(r[:, 0:tw], ps, AF.Relu)
            m.tt(hbuf[:, ci, t0:t0 + tw], r[:, 0:tw], r[:, 0:tw], ALU.mult, eng="dve")
        proj_fm(cx, io["w1"], 512, hb * 512, 128, actT, KC, relu2, T=T)
        for half in range(2):
            slot = cx.wload(io["w2"], hb * 2 + half, 4, 1024)
            for j in range(8):
                oc = half * 8 + j
                banks = ((0, 1, 2), (3, 4, 5))[oc % 2]
                for k in range(4):
                    for ti, (t0, tw) in enumerate(tbs):
                        m.matmul(m.psum[banks[ti]][:, 0:tw], slot[:, k, j * 128:(j + 1) * 128],
                                 hbuf[:, k, t0:t0 + tw], start=(k == 0), stop=(k == 3))
                for ti, (t0, tw) in enumerate(tbs):
                    rmw(oc, ti, m.psum[banks[ti]][:, 0:tw], t0, tw)
    m.release(mk)


def stager(cx, stage, dst_of_chunk, kind="copy"):
    m = cx.m

    def consume(ci, ti, ps, t0, tw):
        st = stage[ci % len(stage)]
        rows = ps.shape[0]
        if kind == "silu":
            m.act(st[0:rows, t0:t0 + tw], ps, AF.Silu)
        else:
            if cx.evac_eng() == "act":
                m.act(st[0:rows, t0:t0 + tw], ps, AF.Identity)
            else:
                m.copy(st[0:rows, t0:t0 + tw], ps)
        if t0 + tw >= TL:
            m.dma(dst_of_chunk(ci), st[0:rows, :])
    return consume


def emit_pre_even(cx, l, hT, actT, rstd, modall, io):
    m = cx.m
    e = l // 2
    mk = m.mark()
    sq = m.sb("sq", [128, KC, 384], BF16)
    tmp4 = m.sb("tmp4", [128, TL], F32)
    tmp = m.sb("tmp", [128, 384], F32)
    rstd_compute(cx, hT, KC, TL, D, rstd, sq, 6, tmp)
    norm_modulate(cx, hT, actT, rstd, modall, l, MOD_GSC_A, MOD_SH_A, tmp4)
    m.release(mk)
    mk = m.mark()
    stage = [m.sb("stg%d" % i, [128, TL], BF16) for i in range(4)]
    ckvT = m.sb("ckvT", [128, 4, TL], F32)
    kvnT = m.sb("kvnT", [128, 4, TL], BF16)
    cosT = m.sb("cosT", [64, TL], F32)
    sinT = m.sb("sinT", [64, TL], F32)
    rt = [m.sb("ropet%d" % i, [64, 384], F32) for i in range(2)]
    sq4 = m.sb("sq4", [128, 4, 384], BF16)
    tmp = m.sb("tmpb", [128, 384], F32)
    kvg = m.sb("kvg", [128, 4], F32)
    m.dma(cosT[:], io["cosT"])
    m.dma(sinT[:], io["sinT"])
    m.dma(kvg[:], io["kvg"])
    W = io["w_in"]
    proj_fm(cx, W, 1024, EV_QNOPE, 128, actT, KC, stager(cx, stage, lambda ci: io["qT"][ci, 0:128, :]))

    def ckv_c(ci, ti, ps, t0, tw):
        m.copy(ckvT[:, ci, t0:t0 + tw], ps, eng=cx.evac_eng())
    proj_fm(cx, W, 512, EV_CKV, 128, actT, KC, ckv_c)
    proj_fm(cx, W, 1024, EV_QNA, 128, actT, KC, stager(cx, stage, lambda ci: io["qnT"][ci * 128:(ci + 1) * 128, :]))
    proj_fm(cx, W, 1024, EV_KNA, 128, actT, KC, stager(cx, stage, lambda ci: io["knT"][ci * 128:(ci + 1) * 128, :]))

    keep = {}

    def rope_c(ci, ti, ps, t0, tw):
        if ci % 2 == 0:
            keep[ti] = ps
            return
        p = ci // 2
        st = stage[p % len(stage)]
        m.tt(rt[0][:, 0:tw], keep[ti], cosT[:, t0:t0 + tw], ALU.mult)
        m.tt(rt[1][:, 0:tw], ps, sinT[:, t0:t0 + tw], ALU.mult)
        m.tt(st[0:64, t0:t0 + tw], rt[0][:, 0:tw], rt[1][:, 0:tw], ALU.add)
        if t0 + tw >= TL:
            dst = io["qT"][p, 128:192, :] if p < 8 else io["kpeT"]
            m.dma(dst, st[0:64, :])
    proj_fm(cx, W, 18 * 64, EV_PE, 64, actT, KC, rope_c)

    def vna_c(t, c0, nw, ps):
        st = stage[(t + c0 // 256) % len(stage)]
        m.copy(st[:, 0:nw], ps, eng=cx.evac_eng())
        m.dma(io["vn"][t * 128:(t + 1) * 128, c0:c0 + nw], st[:, 0:nw])
    proj_tm(cx, W, 0, KC, EV_VNA, 1024, actT, vna_c, TL)

    rstd_compute(cx, ckvT, 4, TL, 512, rstd, sq4, 6, tmp)
    for k in range(4):
        m.stt(ckvT[:, k, :], ckvT[:, k, :], kvg[:, k:k + 1], rstd[:, :], ALU.mult, ALU.mult)
        m.copy(kvnT[:, k, :], ckvT[:, k, :], eng="act")
    proj_fm(cx, io["w_ukv"], 1024, 0, 128, kvnT, 4, stager(cx, stage, lambda ci: io["kT"][ci, :, :]))

    def v_c(t, c0, nw, ps):
        st = stage[(t + c0 // 512) % len(stage)]
        m.copy(st[:, 0:nw], ps, eng=cx.evac_eng())
        m.dma(io["v"][t * 128:(t + 1) * 128, c0:c0 + nw], st[:, 0:nw])
    proj_tm(cx, io["w_ukv"], 0, 4, 1024, 1024, kvnT, v_c, TL)
    m.release(mk)


def emit_pre_odd(cx, l, hT, actT, rstd, modall, io):
    m = cx.m
    mk = m.mark()
    sq = m.sb("sq", [128, KC, 384], BF16)
    tmp4 = m.sb("tmp4", [128, TL], F32)
    tmp = m.sb("tmp", [128, 384], F32)
    rstd_compute(cx, hT, KC, TL, D, rstd, sq, 6, tmp)
    norm_modulate(cx, hT, actT, rstd, modall, l, MOD_GSC_A, MOD_SH_A, tmp4)
    m.release(mk)
    mk = m.mark()
    stage = [m.sb("stg%d" % i, [128, TL], BF16) for i in range(4)]
    stage32 = [m.sb("stg32_%d" % i, [128, TL], F32) for i in range(2)]
    W = io["w_in"]
    proj_fm(cx, W, 1024, 0, 128, actT, KC,
            stager(cx, stage, lambda ci: io["qT"][ci * 128:(ci + 1) * 128, :], kind="silu"))

    def vi_c(t, c0, nw, ps):
        st = stage[(t + c0 // 256) % len(stage)]
        m.copy(st[:, 0:nw], ps, eng=cx.evac_eng())
        m.dma(io["vi"][t * 128:(t + 1) * 128, c0:c0 + nw], st[:, 0:nw])
    proj_tm(cx, W, 0, KC, 1024, 1024, actT, vi_c, TL)
    proj_fm(cx, W, 2048, 2048, 128, actT, KC,
            stager(cx, stage32, lambda ci: io["zT"][ci * 128:(ci + 1) * 128, :]))
    proj_fm(cx, W, 1024, 4096, 128, actT, KC,
            stager(cx, stage, lambda ci: io["gT"][ci * 128:(ci + 1) * 128, :], kind="silu"))
    proj_fm(cx, W, 3072, 5120, 128, actT, KC,
            stager(cx, stage, lambda ci: io["uT"][ci * 128:(ci + 1) * 128, :]))
    m.release(mk)


def emit_final(cx, hT, rstd, io):
    m = cx.m
    mk = m.mark()
    sq = m.sb("sq", [128, KC, 384], BF16)
    tmp = m.sb("tmp", [128, 384], F32)
    fg = m.sb("fg", [128, KC], F32)
    stage32 = [m.sb("stg32_%d" % i, [128, 1024], F32) for i in range(2)]
    m.dma(fg[:], io["fg"])
    rstd_compute(cx, hT, KC, 1024, D, rstd, sq, 6, tmp)
    for k in range(KC):
        st = stage32[k % 2]
        m.stt(st[:, :], hT[:, k, 0:1024], fg[:, k:k + 1], rstd[:, 0:1024], ALU.mult, ALU.mult)
        m.dma(io["outT"][:, k, :], st[:, :])
    m.release(mk)


def build_R_program(l):
    nc = bass.Bass("TRN2", target_bir_lowering=False)
    cx = Ctx(nc)
    m = cx.m
    dr = lambda n, s, d, k="ExternalInput": m.dram(n, s, d, k)
    ident_d = dr("ident", [128, 128], F32)
    hT_d = dr("hT", [128, KC, TL], F32)
    mod_d = dr("modraw", [128, DEPTH, 96, 2], F32)
    cx.consts(ident_d)
    cx.eps_col = m.sb("eps", [128, 1], F32)
    m.memset(cx.eps_col[:], EPS)
    modall = m.sb("modall", [128, DEPTH, 128, 2], F32)
    m.dma(modall[:, :, 0:96, :], mod_d)
    emit_gsc(cx, modall, dr("gmix", [128, DEPTH, KC], F32), dr("gmlp", [128, DEPTH, KC], F32))
    hT = m.sb("hT", [128, KC, TL], F32)
    m.dma(hT[:], hT_d)
    actT = m.sb("actT", [128, KC, TL], BF16)
    rstd = m.sb("rstd", [128, TL], F32)
    cx.init_wring(4)
    if l >= 1:
        io = {"yT": dr("yT", [128, KC, TL], BF16), "w_out": dr("w_out", [D, D], F32),
              "w1": dr("w1", [D, HID], F32), "w2": dr("w2", [HID, D], F32)}
        emit_post(cx, l - 1, hT, actT, rstd, modall, io, TL if l <= 3 else 1024)
    if l <= 3 and l % 2 == 0:
        io = {"w_in": dr("w_in", [D, EV_NEXT], F32), "w_ukv": dr("w_ukv", [512, 2048], F32),
              "cosT": dr("cosT", [64, TL], F32), "sinT": dr("sinT", [64, TL], F32),
              "kvg": dr("kvg", [128, 4], F32),
              "qT": dr("qT", [8, 192, TL], BF16, "ExternalOutput"),
              "kT": dr("kT", [8, 128, TL], BF16, "ExternalOutput"),
              "kpeT": dr("kpeT", [64, TL], BF16, "ExternalOutput"),
              "v": dr("v", [TL, 1024], BF16, "ExternalOutput"),
              "qnT": dr("qnT", [1024, TL], BF16, "ExternalOutput"),
              "knT": dr("knT", [1024, TL], BF16, "ExternalOutput"),
              "vn": dr("vn", [TL, 1024], BF16, "ExternalOutput")}
        emit_pre_even(cx, l, hT, actT, rstd, modall, io)
    elif l <= 3:
        io = {"w_in": dr("w_in", [D, 8192], F32),
              "qT": dr("qT", [1024, TL], BF16, "ExternalOutput"),
              "vi": dr("vi", [TL, 1024], BF16, "ExternalOutput"),
              "zT": dr("zT", [2048, TL], F32, "ExternalOutput"),
              "gT": dr("gT", [1024, TL], BF16, "ExternalOutput"),
              "uT": dr("uT", [3072, TL], BF16, "ExternalOutput")}
        emit_pre_odd(cx, l, hT, actT, rstd, modall, io)
    if l <= 3:
        hout = dr("hT_out", [128, KC, TL], F32, "ExternalOutput")
        m.dma(hout, hT[:])
    else:
        io = {"fg": dr("fg", [128, KC], F32), "outT": dr("outT", [128, KC, 1024], F32, "ExternalOutput")}
        emit_final(cx, hT, rstd, io)
    m.emit()
    return nc


def fm_vec(v):
    return np.ascontiguousarray(np.asarray(v).reshape(-1, 128).T)


def fm_tokens(tok):
    T = tok.shape[0]
    return np.ascontiguousarray(tok.reshape(T, KC, 128).transpose(2, 1, 0))


def rope_perm_idx():
    idx = np.zeros(64, dtype=np.int64)
    for j in range(64):
        jj = j % 32
        idx[j] = j + 16 if jj < 16 else j - 16
    return idx


def ev_w_in_ext(w):
    perm = rope_perm_idx()
    cols = []
    for h in range(8):
        cols.append(np.arange(h * 192, h * 192 + 128))
    cols.append(np.arange(1536, 2048))
    cols.append(np.arange(2112, 3136))
    cols.append(np.arange(3136, 4160))
    for h in range(8):
        base = h * 192 + 128
        cols.append(base + np.arange(64))
        cols.append(base + perm)
    cols.append(2048 + np.arange(64))
    cols.append(2048 + perm)
    cols.append(np.zeros(128, dtype=np.int64))
    cols.append(np.arange(4160, 5184))
    idx = np.concatenate(cols)
    assert idx.shape[0] == EV_NEXT
    return np.ascontiguousarray(w[:, idx])


def ukv_ext(w):
    kc = np.concatenate([np.arange(h * 256, h * 256 + 128) for h in range(8)])
    vc = np.concatenate([np.arange(h * 256 + 128, h * 256 + 256) for h in range(8)])
    return np.ascontiguousarray(w[:, np.concatenate([kc, vc])])


def rope_tables(rank):
    pos = np.arange(rank * 1024, rank * 1024 + 1024)
    rows = (pos // GRID_W).astype(np.float32)
    cols = (pos % GRID_W).astype(np.float32)
    inv_freq = (np.float32(10000.0) ** (-np.arange(0, 32, 2, dtype=np.float32) / np.float32(32))).astype(np.float32)
    cosT = np.ones((64, TL), dtype=np.float32)
    sinT = np.zeros((64, TL), dtype=np.float32)
    for j in range(64):
        p = rows if j < 32 else cols
        jj = j % 32
        ang = (p * inv_freq[jj % 16]).astype(np.float32)
        cosT[j, :1024] = np.cos(ang)
        s = np.sin(ang)
        sinT[j, :1024] = -s if jj < 16 else s
    return cosT, sinT


NA_CLS = {(0, 0): 0, (1, 0): 1, (2, 0): 2, (3, 0): 3, (4, 0): 4, (5, 1): 5, (5, 0): 6, (6, 0): 7, (7, 0): 8}


def na_row_info(r):
    rs = min(max(r - 4, 0), 24)
    base = 2 * (rs // 2)
    cls = NA_CLS[(r - base, rs - base)]
    ntiles = 5 if rs - base == 1 else 4
    return base // 2, cls, ntiles


def na_tables():
    ridx = np.zeros((9, 5, 128, 64), dtype=np.int64)
    cidx = np.zeros((9, 5, 128, 64), dtype=np.int64)
    mask = np.zeros((9, 5, 128, 64), dtype=np.float32)
    p = np.arange(128)
    qc = np.arange(64)
    kcol = (p % 64)[:, None]
    col_start = np.clip(qc - 8, 0, 48)[None, :]
    col_ok = (kcol >= col_start) & (kcol < col_start + 16)
    coff = np.clip(kcol - qc[None, :], -15, 15) + 15
    for (dr, off), c in NA_CLS.items():
        for j in range(5):
            krel = 2 * j + p // 64
            inband = (krel >= off) & (krel < off + 8)
            roff = np.clip(krel - dr + 7, 0, 14)
            ridx[c, j] = roff[:, None]
            cidx[c, j] = coff
            ok = inband[:, None] & col_ok
            mask[c, j] = np.where(ok, 0.0, -30000.0)
    return ridx, cidx, mask


def dense_attn(cx, parts, vtile, key_tiles, q0, qn, scale, yT, pbuf, rsb):
    m = cx.m
    nk = len(key_tiles)
    for qb in range(q0, q0 + qn, 512):
        qw = min(512, q0 + qn - qb)
        O = m.psum[4]
        Sm = m.psum[5]

        def s_mm(i):
            sb = m.psum[i % 4]
            kt = key_tiles[i]
            for pi, (qp, kp) in enumerate(parts):
                m.matmul(sb[:, 0:qw], kp[:, kt * 128:(kt + 1) * 128], qp[:, qb:qb + qw],
                         start=(pi == 0), stop=(pi == len(parts) - 1))
        s_mm(0)
        for i in range(nk):
            if i + 1 < nk:
                s_mm(i + 1)
            P = pbuf[i % len(pbuf)]
            m.act(P[:, 0:qw], m.psum[i % 4][:, 0:qw], AF.Exp, scale=scale)
            m.matmul(O[:, 0:qw], vtile(key_tiles[i]), P[:, 0:qw], start=(i == 0), stop=(i == nk - 1))
            m.matmul(Sm[:, 0:qw], cx.ones_bf[:, 0:128], P[:, 0:qw], start=(i == 0), stop=(i == nk - 1))
        m.recip(rsb[:, 0:qw], Sm[:, 0:qw])
        m.tt(yT[:, qb:qb + qw], O[:, 0:qw], rsb[:, 0:qw], ALU.mult)


def emit_M_even(cx, io, ctx_out):
    m = cx.m
    mk = m.mark()
    LT = list(range(2, 18))
    CTt = [0, 1]
    allk = CTt + LT
    pbuf = [m.sb("pbuf%d" % i, [128, 512], BF16) for i in range(3)]
    rsb = m.sb("rsb", [128, 512], F32)
    ystage = [m.sb("ystage%d" % i, [128, TM], BF16) for i in range(2)]
    vsb = m.sb("vsb", [128, 18, 512], BF16)
    m.dma(vsb[:], io["mv"].rearrange("(t p) c -> p t c", p=128))
    kpe = m.sb("kpe", [64, TM], BF16)
    m.dma(kpe[:], io["mkpe"])
    qn_ = [m.sb("qnope%d" % i, [128, TM], BF16) for i in range(2)]
    qp_ = [m.sb("qpe%d" % i, [64, TM], BF16) for i in range(2)]
    kn_ = [m.sb("knope%d" % i, [128, TM], BF16) for i in range(2)]
    mla_scale = 192.0 ** -0.5
    c0 = 0 if ctx_out else 256
    def load_mla(h):
        m.dma(qn_[h % 2][:], io["mq"][h, 0:128, :])
        m.dma(qp_[h % 2][:], io["mq"][h, 128:192, :])
        m.dma(kn_[h % 2][:], io["mk"][h])
    load_mla(0)
    for h in range(4):
        qn, qp, kn = qn_[h % 2], qp_[h % 2], kn_[h % 2]
        if h + 1 < 4:
            load_mla(h + 1)
        ys = ystage[h % 2]
        parts = [(qn, kn), (qp, kpe)]
        vt = lambda kt, h=h: vsb[:, kt, h * 128:(h + 1) * 128]
        if ctx_out:
            dense_attn(cx, parts, vt, CTt, 0, 256, mla_scale, ys, pbuf, rsb)
        dense_attn(cx, parts, vt, allk, 256, 2048, mla_scale, ys, pbuf, rsb)
        m.dma(io["myT"][h, :, c0:TM], ys[:, c0:TM])
    m.dma(vsb[:], io["mvn"].rearrange("(t p) c -> p t c", p=128))
    mask = m.sb("namask", [128, 9 * 5 * 64], F32)
    m.dma(mask[:], io["namask"])
    bias_ = [m.sb("nabias%d" % i, [128, 9 * 5 * 64], F32) for i in range(2)]
    lg_ = [m.sb("nalg%d" % i, [128, 320], F32) for i in range(4)]
    P_ = [m.sb("naP%d" % i, [128, 448], BF16) for i in range(4)]
    nrs_ = [m.sb("nars%d" % i, [128, 64], F32) for i in range(4)]
    na_scale = 128.0 ** -0.5
    it = 0
    def load_na(h):
        m.dma(qn_[h % 2][:], io["mqn"][h * 128:(h + 1) * 128, :])
        m.dma(kn_[h % 2][:], io["mkn"][h * 128:(h + 1) * 128, :])
        m.dma(bias_[h % 2][:], io["nabias"][h])
    load_na(0)
    for h in range(4):
        qn, kn = qn_[h % 2], kn_[h % 2]
        if h + 1 < 4:
            load_na(h + 1)
        bias = bias_[h % 2]
        m.stt(bias[:], bias[:], 1.0, mask[:], ALU.mult, ALU.add)
        bv = bias[:].rearrange("p (c x) -> p c x", c=9)
        ys = ystage[h % 2]
        vt = lambda kt, h=h: vsb[:, kt, h * 128:(h + 1) * 128]
        if ctx_out:
            dense_attn(cx, [(qn, kn)], vt, CTt, 0, 256, na_scale, ys, pbuf, rsb)
        units = []
        for r in range(32):
            j0, cls, nt = na_row_info(r)
            q0 = 256 + 64 * r
            S = m.psum[it % 4]
            O = m.psum[4 + it % 4]
            lg, P, nrs = lg_[it % 4], P_[it % 4], nrs_[it % 4]
            it += 1
            tiles = [2 + j0 + j for j in range(nt)]

            def stage_a(q0=q0, S=S, lg=lg, P=P, tiles=tiles, cls=cls, nt=nt):
                for j, kt in enumerate(tiles):
                    m.matmul(S[:, 64 * j:64 * j + 64], kn[:, kt * 128:(kt + 1) * 128], qn[:, q0:q0 + 64])
                for j, kt in enumerate(CTt):
                    m.matmul(S[:, 320 + 64 * j:384 + 64 * j], kn[:, kt * 128:(kt + 1) * 128], qn[:, q0:q0 + 64])
                m.stt(lg[:, 0:64 * nt], S[:, 0:64 * nt], na_scale, bv[:, cls, 0:64 * nt], ALU.mult, ALU.add)
                m.act(P[:, 0:64 * nt], lg[:, 0:64 * nt], AF.Exp)
                m.act(P[:, 320:448], S[:, 320:448], AF.Exp, scale=na_scale)

            def stage_b(q0=q0, O=O, P=P, nrs=nrs, tiles=tiles):
                srcs = [(kt, P[:, 64 * j:64 * j + 64]) for j, kt in enumerate(tiles)]
                srcs += [(kt, P[:, 320 + 64 * j:384 + 64 * j]) for j, kt in enumerate(CTt)]
                for i, (kt, pp) in enumerate(srcs):
                    m.matmul(O[:, 0:64], vt(kt), pp, start=(i == 0), stop=(i == len(srcs) - 1), skip_group_check=True)
                    m.matmul(O[:, 64:128], cx.ones_bf[:, 0:128], pp, start=False, stop=(i == len(srcs) - 1),
                             skip_group_check=True)
                m.recip(nrs[:, :], O[:, 64:128])
                m.tt(ys[:, q0:q0 + 64], O[:, 0:64], nrs[:, :], ALU.mult)
            units.append((stage_a, stage_b))
        LOOK = 2
        for u in range(len(units) + LOOK):
            if u < len(units):
                units[u][0]()
            if u >= LOOK:
                units[u - LOOK][1]()
        m.dma(io["myT"][4 + h, :, c0:TM], ys[:, c0:TM])
    m.release(mk)


def build_Meven_program(ctx_out):
    nc = bass.Bass("TRN2", target_bir_lowering=False)
    cx = Ctx(nc)
    m = cx.m
    dr = lambda n, s, d, k="ExternalInput": m.dram(n, s, d, k)
    cx.consts(dr("ident", [128, 128], F32))
    io = {"mq": dr("mq", [4, 192, TM], BF16), "mk": dr("mk", [4, 128, TM], BF16),
          "mkpe": dr("mkpe", [64, TM], BF16), "mv": dr("mv", [TM, 512], BF16),
          "mqn": dr("mqn", [512, TM], BF16), "mkn": dr("mkn", [512, TM], BF16),
          "mvn": dr("mvn", [TM, 512], BF16),
          "nabias": dr("nabias", [4, 128, 2880], F32), "namask": dr("namask", [128, 2880], F32),
          "myT": dr("myT", [8, 128, TM], BF16, "ExternalOutput")}
    emit_M_even(cx, io, ctx_out)
    m.emit()
    return nc


def canon_cols(a0, a1):
    return np.ascontiguousarray(np.concatenate([a0[..., 1024:], a1[..., 1024:], a0[..., :1024], a1[..., :1024]], -1))


def canon_rows(a0, a1):
    return np.ascontiguousarray(np.concatenate([a0[1024:], a1[1024:], a0[:1024], a1[:1024]], 0))


_NA_TAB = None


def na_bias_host(rel_bias_e):
    global _NA_TAB
    if _NA_TAB is None:
        _NA_TAB = na_tables()
    ridx, cidx, mask = _NA_TAB
    g = rel_bias_e[:, ridx, cidx]
    g = np.ascontiguousarray(g.transpose(0, 3, 1, 2, 4).reshape(8, 128, 2880)).astype(np.float32)
    mk = np.ascontiguousarray(mask.transpose(2, 0, 1, 3).reshape(128, 2880))
    return g, mk


def assemble_Meven(o0, o1, r, nab, namask):
    hs = slice(4 * r, 4 * r + 4)
    cs = slice(512 * r, 512 * r + 512)
    return {"ident": np.eye(128, dtype=np.float32),
            "mq": canon_cols(o0["qT"][hs], o1["qT"][hs]),
            "mk": canon_cols(o0["kT"][hs], o1["kT"][hs]),
            "mkpe": canon_cols(o0["kpeT"], o1["kpeT"]),
            "mv": canon_rows(o0["v"][:, cs], o1["v"][:, cs]),
            "mqn": canon_cols(o0["qnT"][cs], o1["qnT"][cs]),
            "mkn": canon_cols(o0["knT"][cs], o1["knT"][cs]),
            "mvn": canon_rows(o0["vn"][:, cs], o1["vn"][:, cs]),
            "nabias": np.ascontiguousarray(nab[hs]), "namask": namask}


def assemble_y(m0, m1, r):
    full = np.zeros((16, 128, TM), dtype=m0["myT"].dtype)
    for rr, mm in ((0, m0), (1, m1)):
        full[4 * rr:4 * rr + 4] = mm["myT"][0:4]
        full[8 + 4 * rr:8 + 4 * rr + 4] = mm["myT"][4:8]
    loc = np.concatenate([full[:, :, 256 + 1024 * r:256 + 1024 * (r + 1)], full[:, :, 128 * r:128 * (r + 1)]], -1)
    return np.ascontiguousarray(loc.transpose(1, 0, 2))


NCH = 36


def hgrn_sigma(d, c):
    if d == 0:
        return c
    return 3 - c if c < 4 else 35 - (c - 4)


def emit_hgrn(cx, io, ctx_out, ystage):
    m = cx.m
    mk = m.mark()
    T = TM
    BIG = 2.0e17
    vtok = m.sb("hg_vtok", [128, 18, 512], BF16)
    m.dma(vtok[:], io["hv"].rearrange("(t p) c -> p t c", p=128))
    lbl = m.sb("hg_lbl", [128, 2, 2, 4], F32)
    m.dma(lbl[:], io["lbl"])
    lb = m.sb("hg_lb", [128, 2, 4], F32)
    oml = m.sb("hg_oml", [128, 2, 4], F32)
    if io["odd_idx"] == 0:
        m.memset(lb[:], 0.0)
    else:
        m.tt(lb[:], lbl[:, :, 1, :], lbl[:, :, 0, :], ALU.subtract)
        m.act(lb[:], lb[:], AF.Sigmoid)
    m.ts(oml[:], lb[:], -1.0, 1.0, ALU.mult, ALU.add)
    ng = m.sb("hg_ng", [128, 4], F32)
    m.dma(ng[:], io["ng"])
    masks = m.sb("hg_masks", [128, 4, 128], F32)
    m.dma(masks[:], io["hmasks"])
    ones_f = m.sb("hg_ones", [128, T], BF16)
    m.memset(ones_f[:], 1.0)
    qb = m.sb("hg_q", [128, T], BF16)
    gb = m.sb("hg_g", [128, T], BF16)
    Qi = [m.sb("hg_Qi%d" % d, [128, T], BF16) for d in range(2)]
    Ki = [m.sb("hg_Ki%d" % d, [128, T], BF16) for d in range(2)]
    Qo = [m.sb("hg_Qo%d" % d, [128, T], BF16) for d in range(2)]
    Ko = [m.sb("hg_Ko%d" % d, [128, T], BF16) for d in range(2)]
    Qs = [m.sb("hg_Qs%d" % d, [128, T], BF16) for d in range(2)]
    Sbf = [m.sb("hg_Sbf%d" % d, [128, NCH, 128], BF16) for d in range(2)]
    KlT = m.sb("hg_KlT", [128, T], BF16)
    Kltok = m.sb("hg_Kltok", [128, 18, 128], BF16)
    z = m.sb("hg_z", [128, T], F32)
    lf = m.sb("hg_lf", [128, T], F32)
    kk = m.sb("hg_k", [128, T], F32)
    ee = m.sb("hg_e", [128, T], F32)
    gaddr = m.mark()
    G = m.sb("hg_G", [128, T], F32)
    Gx = m.sb("hg_Gx", [128, T], F32)
    Dfull = m.sb("hg_Dfull", [128, 128 * NCH], F32, at=gaddr)
    Sst = m.sb("hg_Sst", [128, 128 * NCH], F32)
    Dsc = m.sb("hg_Dsc", [128, NCH], F32)
    Dtmp = m.sb("hg_Dtmp", [128, NCH], F32)
    Am = [m.sb("hg_Am%d" % i, [128, 128], BF16) for i in range(4)]
    T12 = [m.sb("hg_T12_%d" % i, [128, 256], F32) for i in range(4)]
    oT = lf
    rstd = ee
    sq1 = m.sb("hg_sq", [128, 1, 384], BF16)
    tmp = m.sb("hg_tmp", [128, 384], F32)

    v64 = lambda t_: t_[:].rearrange("p (c j) -> p c j", j=64)
    v32 = lambda t_: t_[:].rearrange("p (c j) -> p c j", j=32)
    bc64 = lambda t_, j: v64(t_)[:, :, j:j + 1].to_broadcast([128, NCH, 64])
    bc32 = lambda t_, j: v32(t_)[:, :, j:j + 1].to_broadcast([128, 2 * NCH, 32])
    c0 = 0 if ctx_out else 256
    ai = 0
    for h in range(4):
        m.dma(qb[:], io["hq"][h * 128:(h + 1) * 128, :])
        m.dma(gb[:], io["hg"][h * 128:(h + 1) * 128, :])
        for d in range(2):
            A = G if d == 0 else Gx
            sgn = 1.0 if d == 0 else -1.0
            HALVES = ((0, T // 2), (T // 2, T))
            for (ca, cb_) in HALVES:
                cs = slice(ca, cb_)
                nch_h = (cb_ - ca) // 64
                w64 = lambda t_: t_[:, cs].rearrange("p (c j) -> p c j", j=64)
                w32 = lambda t_: t_[:, cs].rearrange("p (c j) -> p c j", j=32)
                b64 = lambda t_, j: w64(t_)[:, :, j:j + 1].to_broadcast([128, nch_h, 64])
                b32 = lambda t_, j: w32(t_)[:, :, j:j + 1].to_broadcast([128, 2 * nch_h, 32])
                m.dma(z[:, cs], io["hz"][d, h * 128:(h + 1) * 128, cs])
                m.act(z[:, cs], z[:, cs], AF.Sigmoid)
                m.ts(z[:, cs], z[:, cs], oml[:, d, h:h + 1], lb[:, d, h:h + 1], ALU.mult, ALU.add)
                m.ts(z[:, cs], z[:, cs], 1e-30, None, ALU.max, eng="pool")
                m.act(lf[:, cs], z[:, cs], AF.Ln)
                m.ts(kk[:, cs], z[:, cs], -1.0, 1.0, ALU.mult, ALU.add, eng="pool")
                m.scan(G[:, cs], ones_f[:, cs], lf[:, cs], 0.0 if ca == 0 else G[:, ca - 1:ca], ALU.mult, ALU.add)
                m.tt(Gx[:, cs], G[:, cs], lf[:, cs], ALU.subtract, eng="pool")
                m.tt(w32(z), w32(A), b32(A, 15 if d == 0 else 16), ALU.subtract, eng="pool")
                m.act(ee[:, cs], z[:, cs], AF.Exp, scale=sgn)
                m.stt(Qi[d][:, cs], ee[:, cs], BIG, qb[:, cs], ALU.min, ALU.mult)
                m.act(ee[:, cs], z[:, cs], AF.Exp, scale=-sgn)
                m.stt(Ki[d][:, cs], ee[:, cs], BIG, kk[:, cs], ALU.min, ALU.mult)
                m.tt(w64(z), w64(A), b64(A, 31 if d == 0 else 32), ALU.subtract, eng="pool")
                m.act(ee[:, cs], z[:, cs], AF.Exp, scale=sgn)
                m.stt(Qo[d][:, cs], ee[:, cs], 1.0, qb[:, cs], ALU.min, ALU.mult)
                m.act(ee[:, cs], z[:, cs], AF.Exp, scale=-sgn)
                m.stt(Ko[d][:, cs], ee[:, cs], 1.0, kk[:, cs], ALU.min, ALU.mult)
                if d == 0:
                    m.tt(w64(z), w64(G), b64(Gx, 0), ALU.subtract, eng="pool")
                else:
                    m.tt(w64(z), w64(Gx), b64(G, 63), ALU.subtract, eng="pool")
                m.act(ee[:, cs], z[:, cs], AF.Exp, scale=sgn)
                m.stt(Qs[d][:, cs], ee[:, cs], 1.0, qb[:, cs], ALU.min, ALU.mult)
                if d == 0:
                    m.tt(w64(z), w64(G), b64(G, 63), ALU.subtract, eng="pool")
                else:
                    m.tt(w64(z), w64(Gx), b64(Gx, 0), ALU.subtract, eng="pool")
                m.act(ee[:, cs], z[:, cs], AF.Exp, scale=-sgn)
                m.stt(KlT[:, cs], ee[:, cs], 1.0, kk[:, cs], ALU.min, ALU.mult)
            m.tt(Dtmp[:], v64(G)[:, :, 63], v64(Gx)[:, :, 0], ALU.subtract)
            m.act(Dtmp[:], Dtmp[:], AF.Exp)
            if d == 0:
                m.copy(Dsc[:, 1:NCH], Dtmp[:, 1:NCH], eng="pool")
                m.memset(Dsc[:, 0:1], 0.0, eng="pool")
            else:
                for c in range(NCH):
                    s = hgrn_sigma(1, c)
                    if s == 0:
                        m.memset(Dsc[:, 0:1], 0.0, eng="pool")
                    else:
                        m.copy(Dsc[:, s:s + 1], Dtmp[:, c:c + 1], eng="pool")
            m.copy(Dfull[:].rearrange("p (e s) -> p e s", s=NCH),
                   Dsc[:].unsqueeze(1).to_broadcast([128, 128, NCH]), eng="pool")
            for t4 in range(0, 18, 4):
                nt = min(4, 18 - t4)
                pb = m.psum[6 + (t4 // 4) % 2][:].bitcast(BF16)
                for i in range(nt):
                    m.transpose(pb[:, i * 128:(i + 1) * 128], KlT[:, (t4 + i) * 128:(t4 + i + 1) * 128], cx.ident_bf[:])
                m.copy(Kltok[:, t4:t4 + nt, :], pb[:, 0:nt * 128].rearrange("p (t d) -> p t d", d=128),
                       eng=cx.evac_eng())
            S3 = Sst[:].rearrange("p (e s) -> p e s", s=NCH)
            for c in range(NCH):
                t, j = c // 2, c % 2
                ps = m.psum[c % 4]
                m.matmul(ps[:, 0:128], Kltok[64 * j:64 * j + 64, t, :], vtok[64 * j:64 * j + 64, t, h * 128:(h + 1) * 128])
                s = hgrn_sigma(d, c)
                m.copy(S3[:, :, s], ps[:, 0:128], eng=cx.evac_eng())
            m.scan(Sst[:], Dfull[:], Sst[:], 0.0, ALU.mult, ALU.add)
            m.copy(Sbf[d][:], Sst[:].rearrange("p (e s) -> p s e", s=NCH), eng="pool")
        mflat = masks[:].rearrange("p a t -> p (a t)")
        tiles_ = list(range(c0 // 128, 18))
        stA, stB = {}, {}
        for t in tiles_:
            def stage_a(t=t):
                ams = []
                tsl = slice(t * 128, (t + 1) * 128)
                for d in range(2):
                    psA = m.psum[2 * (t % 2) + d]
                    m.matmul(psA[:, 0:128], Ki[d][:, tsl], Qi[d][:, tsl])
                    m.matmul(psA[:, 128:256], Ko[d][:, tsl], Qo[d][:, tsl])
                    am = Am[2 * (t % 2) + d]
                    t12 = T12[2 * (t % 2) + d]
                    m.tt(t12[:], psA[:, 0:256], mflat[:, 256 * d:256 * d + 256], ALU.mult)
                    m.tt(am[:], t12[:, 0:128], t12[:, 128:256], ALU.add, eng="pool")
                    ams.append(am)
                stA[t] = ams

            def stage_b(t=t):
                ams = stA[t]
                psO = m.psum[4 + t % 2]
                mms = []
                for d in range(2):
                    mms.append((psO[:, 0:128], vtok[:, t, h * 128:(h + 1) * 128], ams[d][:]))
                    for j in range(2):
                        c = 2 * t + j
                        s = hgrn_sigma(d, c)
                        if s >= 1:
                            mms.append((psO[:, 64 * j:64 * j + 64], Sbf[d][:, s - 1, :], Qs[d][:, c * 64:(c + 1) * 64]))
                for i, (o_, l_, r_) in enumerate(mms):
                    m.matmul(o_, l_, r_, start=(i == 0), stop=(i == len(mms) - 1), skip_group_check=True)
                m.copy(oT[:, t * 128:(t + 1) * 128], psO[:, 0:128], eng=cx.evac_eng())
            stB[t] = stage_b
            stA[("f", t)] = stage_a
        for i in range(len(tiles_) + 1):
            if i < len(tiles_):
                stA[("f", tiles_[i])]()
            if i >= 1:
                stB[tiles_[i - 1]]()
        rstd_compute(cx, oT[:].rearrange("p (o t) -> p o t", o=1)[:, :, c0:T], 1, T - c0, 128, rstd, sq1, 7, tmp)
        ys = ystage[h % 2]
        m.stt(oT[:, c0:T], oT[:, c0:T], ng[:, h:h + 1], rstd[:, 0:T - c0], ALU.mult, ALU.mult)
        m.tt(ys[:, c0:T], oT[:, c0:T], gb[:, c0:T], ALU.mult, eng="pool")
        m.dma(io["myT"][h, :, c0:T], ys[:, c0:T])
    m.release(mk)


def sin_reduced(cx, out, x, rows, w, t1):
    m = cx.m
    PI = math.pi
    for _ in range(2):
        m.ts(t1[0:rows, 0:w], x, PI, -2 * PI, ALU.is_gt, ALU.mult)
        m.tt(x, x, t1[0:rows, 0:w], ALU.add)
        m.ts(t1[0:rows, 0:w], x, -PI, 2 * PI, ALU.is_lt, ALU.mult)
        m.tt(x, x, t1[0:rows, 0:w], ALU.add)
    m.act(out, x, AF.Sin)


def emit_hyena(cx, io, ctx_out, ystage):
    m = cx.m
    mk = m.mark()
    T = TM
    cw = m.sb("hy_cw", [128, 3, 3, 4], F32)
    cb = m.sb("hy_cb", [128, 3, 4], F32)
    skip = m.sb("hy_skip", [128, 4], F32)
    m.dma(cw[:], io["cw"])
    m.dma(cb[:], io["cb"])
    m.dma(skip[:], io["skip"])
    fw1 = m.sb("hy_w1", [33, 64], F32)
    fw2 = m.sb("hy_w2", [64, 64], F32)
    fw3 = m.sb("hy_w3", [64, 64], F32)
    fb = m.sb("hy_fb", [64, 4], F32)
    fwo = m.sb("hy_wo", [64, 2, 512], F32)
    m.dma(fw1[:], io["fw1"])
    m.dma(fw2[:], io["fw2"])
    m.dma(fw3[:], io["fw3"])
    m.dma(fb[:], io["fb"])
    m.dma(fwo[:], io["fwo"])
    delt = m.sb("hy_delt", [128, 512], F32)
    m.dma(delt[:], io["delt"])
    X0 = [m.sb("hy_X0_%d" % i, [128, T], BF16) for i in range(4)]
    VX = [m.sb("hy_VX_%d" % i, [128, T], BF16) for i in range(4)]
    vxtok = m.sb("hy_vxtok", [128, 18, 512], BF16)
    mB = m.mark()
    ub = [m.sb("hy_u%d" % i, [128, T], BF16) for i in range(2)]
    acc = [m.sb("hy_acc%d" % i, [128, T], F32) for i in range(2)]
    segs = ((0, 256), (256, T))
    for cc in range(4):
        def conv(g, a, u):
            m.dma(u[:], io["hu"][g, cc * 128:(cc + 1) * 128, :])
            m.ts(a[:], u[:], cw[:, 1, g, cc:cc + 1], cb[:, g, cc:cc + 1], ALU.mult, ALU.add)
            for (s0, s1) in segs:
                m.stt(a[:, s0 + 1:s1], u[:, s0:s1 - 1], cw[:, 0, g, cc:cc + 1], a[:, s0 + 1:s1], ALU.mult, ALU.add)
                m.stt(a[:, s0:s1 - 1], u[:, s0 + 1:s1], cw[:, 2, g, cc:cc + 1], a[:, s0:s1 - 1], ALU.mult, ALU.add)
        conv(1, acc[0], ub[0])
        conv(2, acc[1], ub[1])
        m.tt(VX[cc][:], acc[0][:], acc[1][:], ALU.mult, eng="pool")
        conv(0, acc[0], ub[0])
        m.copy(X0[cc][:], acc[0][:], eng="act")
    for t in range(18):
        pb = m.psum[6 + t % 2][:].bitcast(BF16)
        for cc in range(4):
            m.transpose(pb[:, cc * 128:(cc + 1) * 128], VX[cc][:, t * 128:(t + 1) * 128], cx.ident_bf[:])
        m.copy(vxtok[:, t, :], pb[:, 0:512], eng=cx.evac_eng())
    m.release(mB)
    for (name, n, tok0) in (("c", 256, 0), ("l", 2048, 256)):
        if name == "c" and not ctx_out:
            continue
        m.release(mB)
        ntile = n // 128
        nfc = n // 128
        Yc = m.sb("hy_Yc", [128, nfc, 512], BF16)
        Ys = m.sb("hy_Ys", [128, nfc, 512], BF16)
        p1 = m.sb("hy_p1", [128, 512], F32)
        p2 = m.sb("hy_p2", [128, 512], F32)
        mC = m.mark()
        hs = m.sb("hy_hs", [128, ntile, 512], BF16)
        hd = m.sb("hy_hd", [128, ntile, 512], BF16)
        mD = m.mark()
        zT = m.sb("hy_zT", [33, n], F32)
        m.dma(zT[:], io["zT_" + name])
        tcol = m.sb("hy_tcol", [128, ntile], F32)
        m.dma(tcol[:], io["tcol_" + name])
        hA = m.sb("hy_hA", [64, n], F32)
        hB = m.sb("hy_hB", [64, n], F32)
        t1 = m.sb("hy_t1", [64, 512], F32)
        layers = ((fw1, 33, zT, hA, 0), (fw2, 64, hA, hB, 1), (fw3, 64, hB, hA, 2))
        for (wt, kdim, src_, dst, bi) in layers:
            for b0 in range(0, n, 512):
                bw = min(512, n - b0)
                ps = m.psum[(b0 // 512) % 2]
                m.matmul(ps[0:64, 0:bw], wt[0:kdim, :], src_[0:kdim, b0:b0 + bw])
                m.ts(dst[:, b0:b0 + bw], ps[0:64, 0:bw], fb[:, bi:bi + 1], fb[:, 3:4], ALU.add, ALU.mult)
                sin_reduced(cx, dst[:, b0:b0 + bw], dst[:, b0:b0 + bw], 64, bw, t1)
        h3 = hA
        win = m.sb("hy_win", [128, 512], F32)
        hf = m.sb("hy_hf", [128, 512], F32)
        hb = m.sb("hy_hb", [128, 512], F32)
        ntc = m.sb("hy_ntc", [128, ntile], F32)
        m.ts(ntc[:], tcol[:], -1.0, None, ALU.mult)
        for t in range(ntile):
            m.act(win[:], delt[:], AF.Exp, scale=ntc[:, t:t + 1])
            m.matmul(m.psum[2][:, 0:512], h3[0:64, t * 128:(t + 1) * 128], fwo[0:64, 0, :])
            m.matmul(m.psum[3][:, 0:512], h3[0:64, t * 128:(t + 1) * 128], fwo[0:64, 1, :])
            m.tt(hf[:], m.psum[2][:, 0:512], win[:], ALU.mult)
            m.tt(hb[:], m.psum[3][:, 0:512], win[:], ALU.mult)
            if t == 0:
                m.memset(hb[0:1, :], 0.0)
            m.tt(hs[:, t, :], hf[:], hb[:], ALU.add, eng="pool")
            m.tt(hd[:, t, :], hf[:], hb[:], ALU.subtract, eng="pool")
        m.release(mD)
        nfb = max(1, nfc // 4)
        fcb = nfc // nfb
        Cb = [m.sb("hy_Cb%d" % i, [128, ntile, fcb * 128], BF16) for i in range(1)]
        Sb = [m.sb("hy_Sb%d" % i, [128, ntile, fcb * 128], BF16) for i in range(1)]
        Kc = m.sb("hy_Kc", [128, 512], F32)
        Ks = m.sb("hy_Ks", [128, 512], F32)
        dF = io["dftF_" + name]
        tt0 = tok0 // 128
        for fbk in range(nfb):
            C_, S_ = Cb[0], Sb[0]
            m.dma(C_[:], dF[fbk, 0].rearrange("p (k f) -> p k f", f=fcb * 128))
            m.dma(S_[:], dF[fbk, 1].rearrange("p (k f) -> p k f", f=fcb * 128))
            for fi in range(fcb):
                fc = fbk * fcb + fi
                fsl = slice(fi * 128, (fi + 1) * 128)
                for t in range(ntile):
                    m.matmul(m.psum[0][:, 0:512], C_[:, t, fsl], hs[:, t, :], start=(t == 0), stop=(t == ntile - 1))
                for t in range(ntile):
                    m.matmul(m.psum[1][:, 0:512], S_[:, t, fsl], hd[:, t, :], start=(t == 0), stop=(t == ntile - 1))
                m.copy(Kc[:], m.psum[0][:, 0:512], eng="act")
                m.copy(Ks[:], m.psum[1][:, 0:512], eng="act")
                pc, ps_ = m.psum[2 + 2 * (fc % 2)], m.psum[3 + 2 * (fc % 2)]
                for t in range(ntile):
                    m.matmul(pc[:, 0:512], C_[:, t, fsl], vxtok[:, tt0 + t, :], start=(t == 0), stop=(t == ntile - 1))
                for t in range(ntile):
                    m.matmul(ps_[:, 0:512], S_[:, t, fsl], vxtok[:, tt0 + t, :], start=(t == 0), stop=(t == ntile - 1))
                m.tt(p1[:], pc[:, 0:512], Kc[:], ALU.mult)
                m.tt(p2[:], ps_[:, 0:512], Ks[:], ALU.mult)
                m.tt(Yc[:, fc, :], p1[:], p2[:], ALU.subtract, eng="pool")
                m.tt(p1[:], pc[:, 0:512], Ks[:], ALU.mult)
                m.tt(p2[:], ps_[:, 0:512], Kc[:], ALU.mult)
                m.tt(Ys[:, fc, :], p1[:], p2[:], ALU.add, eng="pool")
        m.release(mC)
        dI = io["dftI_" + name]
        tbw = min(512, n)
        Ci = [m.sb("hy_Ci%d" % i, [128, nfc, tbw], BF16) for i in range(2)]
        Si = [m.sb("hy_Si%d" % i, [128, nfc, tbw], BF16) for i in range(2)]
        for tb in range(n // tbw):
            C_, S_ = Ci[tb % 2], Si[tb % 2]
            m.dma(C_[:], dI[tb, 0].rearrange("p (k t) -> p k t", t=tbw))
            m.dma(S_[:], dI[tb, 1].rearrange("p (k t) -> p k t", t=tbw))
            for cc in range(4):
                po = m.psum[cc % 2]
                csl = slice(cc * 128, (cc + 1) * 128)
                for fc in range(nfc):
                    m.matmul(po[:, 0:tbw], Yc[:, fc, csl], C_[:, fc, :], start=(fc == 0), stop=False)
                for fc in range(nfc):
                    m.matmul(po[:, 0:tbw], Ys[:, fc, csl], S_[:, fc, :], start=False, stop=(fc == nfc - 1))
                a0 = tok0 + tb * tbw
                ys = ystage[cc % 2]
                m.stt(p1[:, 0:tbw], VX[cc][:, a0:a0 + tbw], skip[:, cc:cc + 1], po[:, 0:tbw], ALU.mult, ALU.add)
                m.tt(ys[:, 0:tbw], p1[:, 0:tbw], X0[cc][:, a0:a0 + tbw], ALU.mult, eng="pool")
                m.dma(io["myT"][4 + cc, :, a0:a0 + tbw], ys[:, 0:tbw])
    m.release(mk)


def build_Modd_program(odd_idx, ctx_out):
    nc = bass.Bass("TRN2", target_bir_lowering=False)
    cx = Ctx(nc)
    m = cx.m
    dr = lambda n, s, d, k="ExternalInput": m.dram(n, s, d, k)
    cx.consts(dr("ident", [128, 128], F32))
    cx.eps_col = m.sb("eps", [128, 1], F32)
    m.memset(cx.eps_col[:], EPS)
    io = {"odd_idx": odd_idx,
          "hq": dr("hq", [512, TM], BF16), "hz": dr("hz", [2, 512, TM], F32), "hv": dr("hv", [TM, 512], BF16),
          "hg": dr("hg", [512, TM], BF16), "hu": dr("hu", [3, 512, TM], BF16),
          "lbl": dr("lbl", [128, 2, 2, 4], F32), "ng": dr("ng", [128, 4], F32),
          "hmasks": dr("hmasks", [128, 4, 128], F32),
          "cw": dr("cw", [128, 3, 3, 4], F32), "cb": dr("cb", [128, 3, 4], F32), "skip": dr("skip", [128, 4], F32),
          "fw1": dr("fw1", [33, 64], F32), "fw2": dr("fw2", [64, 64], F32), "fw3": dr("fw3", [64, 64], F32),
          "fb": dr("fb", [64, 4], F32), "fwo": dr("fwo", [64, 2, 512], F32), "delt": dr("delt", [128, 512], F32),
          "zT_l": dr("zT_l", [33, 2048], F32), "tcol_l": dr("tcol_l", [128, 16], F32),
          "dftF_l": dr("dftF_l", [2048, 2, 2048], BF16), "dftI_l": dr("dftI_l", [2048, 2, 2048], BF16),
          "myT": dr("myT", [8, 128, TM], BF16, "ExternalOutput")}
    if ctx_out:
        io.update({"zT_c": dr("zT_c", [33, 256], F32), "tcol_c": dr("tcol_c", [128, 2], F32),
                   "dftF_c": dr("dftF_c", [256, 2, 256], BF16), "dftI_c": dr("dftI_c", [256, 2, 256], BF16)})
    ystage = [m.sb("ystage%d" % i, [128, TM], BF16) for i in range(2)]
    emit_hgrn(cx, io, ctx_out, ystage)
    emit_hyena(cx, io, ctx_out, ystage)
    m.emit()
    return nc


_HY_CONST = {}


def hyena_consts(n):
    if n in _HY_CONST:
        return _HY_CONST[n]
    pos = np.arange(n, dtype=np.float32)
    t = (pos / np.float32(max(n - 1, 1))).astype(np.float32)
    bands = np.linspace(1e-4, 15, 16, dtype=np.float32)
    ang = (np.float32(2.0 * math.pi / n) * pos[:, None] * bands[None, :]).astype(np.float32)
    z = np.concatenate([t[:, None], np.cos(ang), -np.sin(ang)], -1).astype(np.float32)
    zT = np.ascontiguousarray(z.T)
    tcol = np.ascontiguousarray(t.reshape(-1, 128).T)
    f = np.arange(n, dtype=np.float64) + 0.5
    M = np.arange(n, dtype=np.float64)[:, None] * (2.0 * math.pi * f[None, :] / (2.0 * n))
    c, s = np.cos(M), np.sin(M)
    dftF = np.stack([c, s], 1).astype(NPBF)
    dftI = np.stack([c.T / n, s.T / n], 1).astype(NPBF)
    nt = n // 128
    nfb = max(1, nt // 4)
    fw = (nt // nfb) * 128
    tbw = min(512, n)
    dftF = np.ascontiguousarray(dftF.reshape(nt, 128, 2, nfb, fw).transpose(3, 2, 1, 0, 4)).reshape(nfb, 2, 128, nt * fw)
    dftI = np.ascontiguousarray(dftI.reshape(nt, 128, 2, n // tbw, tbw).transpose(3, 2, 1, 0, 4)).reshape(n // tbw, 2, 128, nt * tbw)
    _HY_CONST[n] = (zT, tcol, dftF, dftI)
    return _HY_CONST[n]


def hgrn_masks():
    i = np.arange(128)
    s, t = i[:, None], i[None, :]
    same32 = (s // 32) == (t // 32)
    same64 = (s // 64) == (t // 64)
    first = lambda x: (x % 64) < 32
    m1f = same32 & (s <= t)
    m2f = same64 & first(s) & ~first(t)
    m1b = same32 & (s >= t)
    m2b = same64 & ~first(s) & first(t)
    return np.ascontiguousarray(np.stack([m1f, m2f, m1b, m2b], 1)).astype(np.float32)


def hyena_deltas(r):
    mx = math.log(1e-2) / 0.3
    mn = math.log(1e-2) / 1.5
    d = np.abs(np.linspace(mn, mx, 1024, dtype=np.float32))
    return np.ascontiguousarray(np.broadcast_to(d[512 * r:512 * r + 512][None, :], (128, 512))).astype(np.float32)


def assemble_Modd(o0, o1, r, inp, o, ctx_out):
    cs = slice(512 * r, 512 * r + 512)
    hz = np.stack([canon_cols(o0["zT"][d * 1024 + 512 * r:d * 1024 + 512 * r + 512],
                              o1["zT"][d * 1024 + 512 * r:d * 1024 + 512 * r + 512]) for d in range(2)], 0)
    hu = np.stack([canon_cols(o0["uT"][g * 1024 + 512 * r:g * 1024 + 512 * r + 512],
                              o1["uT"][g * 1024 + 512 * r:g * 1024 + 512 * r + 512]) for g in range(3)], 0)
    lbl = np.stack([np.stack([fm_vec(inp["hgrn_lb_logits"][d, oo, cs]) for oo in range(2)], 1) for d in range(2)], 1)
    cwv = inp["hy_conv_w"][o]
    cw = np.stack([np.stack([fm_vec(cwv[tap, g * 1024 + 512 * r:g * 1024 + 512 * r + 512]) for g in range(3)], 1)
                   for tap in range(3)], 1)
    cbv = inp["hy_conv_b"][o]
    cb = np.stack([fm_vec(cbv[g * 1024 + 512 * r:g * 1024 + 512 * r + 512]) for g in range(3)], 1)
    wo = inp["hy_filt_wout"][o]
    zl, tl, fl, il = hyena_consts(2048)
    d = {"ident": np.eye(128, dtype=np.float32),
         "hq": canon_cols(o0["qT"][cs], o1["qT"][cs]), "hz": np.ascontiguousarray(hz),
         "hv": canon_rows(o0["vi"][:, cs], o1["vi"][:, cs]),
         "hg": canon_cols(o0["gT"][cs], o1["gT"][cs]), "hu": np.ascontiguousarray(hu),
         "lbl": np.ascontiguousarray(lbl).astype(np.float32), "ng": fm_vec(inp["hgrn_norm_g"][o][cs]),
         "hmasks": hgrn_masks(),
         "cw": np.ascontiguousarray(cw).astype(np.float32), "cb": np.ascontiguousarray(cb).astype(np.float32),
         "skip": fm_vec(inp["hy_skip"][o][cs]),
         "fw1": np.ascontiguousarray(inp["hy_filt_w1"][o]), "fw2": np.ascontiguousarray(inp["hy_filt_w2"][o]),
         "fw3": np.ascontiguousarray(inp["hy_filt_w3"][o]),
         "fb": np.ascontiguousarray(np.stack([inp["hy_filt_b1"][o], inp["hy_filt_b2"][o], inp["hy_filt_b3"][o],
                                              inp["hy_filt_freq"][o]], 1)).astype(np.float32),
         "fwo": np.ascontiguousarray(np.stack([wo[:, cs], wo[:, 1024 + 512 * r:1024 + 512 * r + 512]], 1)),
         "delt": hyena_deltas(r), "zT_l": zl, "tcol_l": tl, "dftF_l": fl, "dftI_l": il}
    if ctx_out:
        zc, tc, fc_, ic = hyena_consts(256)
        d.update({"zT_c": zc, "tcol_c": tc, "dftF_c": fc_, "dftI_c": ic})
    return d


def build_fused_program(depth=DEPTH):
    nc = bass.Bass("TRN2", target_bir_lowering=False)
    cx = Ctx(nc)
    m = cx.m
    ext = lambda n, s, d: m.dram(n, s, d, "ExternalInput")
    itn = lambda n, s, d: m.dram(n, s, d, "Internal")
    ident_d = ext("ident", [128, 128], F32)
    cx.consts(ident_d)
    cx.eps_col = m.sb("eps", [128, 1], F32)
    m.memset(cx.eps_col[:], EPS)
    h0_d = ext("h0", [2, 128, KC, TL], F32)
    cT_d = ext("cT", [128, KC, 2], F32)
    ada_w_d = ext("ada_w", [DEPTH, 24, 128, KC * 512], F32)
    ada_b_d = ext("ada_b", [DEPTH, 6 * D], F32)
    gmix_d = ext("gmix", [128, DEPTH, KC], F32)
    gmlp_d = ext("gmlp", [128, DEPTH, KC], F32)
    w_out_d = ext("w_out", [DEPTH, 8, 128, 4096], F32)
    w1_d = ext("mlp_w1", [DEPTH, 32, 128, 4096], F32)
    w2_d = ext("mlp_w2", [DEPTH, 32, 128, 4096], F32)
    evw_d = ext("ev_w_in", [2, EV_NEXT // 256, 128, 4096], F32)
    ukv_d = ext("w_ukv", [2, 2, 128, 4096], F32)
    odw_d = ext("od_w_in", [2, 32, 128, 4096], F32)
    cos_d = ext("cosT", [2, 64, TL], F32)
    sin_d = ext("sinT", [2, 64, TL], F32)
    kvg_d = ext("kvg", [2, 128, 4], F32)
    fg_d = ext("fg", [128, KC], F32)
    nab_d = ext("nabias", [2, 8, 128, 2880], F32)
    namask_d = ext("namask", [128, 2880], F32)
    lbl_d = ext("lbl", [2, 128, 2, 2, 4], F32)
    ng_d = ext("ng", [2, 2, 128, 4], F32)
    hmasks_d = ext("hmasks", [128, 4, 128], F32)
    cw_d = ext("cw", [2, 2, 128, 3, 3, 4], F32)
    cb_d = ext("cb", [2, 2, 128, 3, 4], F32)
    skip_d = ext("skip", [2, 2, 128, 4], F32)
    fw1_d = ext("fw1", [2, 33, 64], F32)
    fw2_d = ext("fw2", [2, 64, 64], F32)
    fw3_d = ext("fw3", [2, 64, 64], F32)
    fb_d = ext("fb", [2, 64, 4], F32)
    fwo_d = ext("fwo", [2, 2, 64, 2, 512], F32)
    delt_d = ext("delt", [2, 128, 512], F32)
    hyc = {"zT_l": ext("zT_l", [33, 2048], F32), "tcol_l": ext("tcol_l", [128, 16], F32),
           "dftF_l": ext("dftF_l", [4, 2, 128, 8192], BF16), "dftI_l": ext("dftI_l", [4, 2, 128, 8192], BF16),
           "zT_c": ext("zT_c", [33, 256], F32), "tcol_c": ext("tcol_c", [128, 2], F32),
           "dftF_c": ext("dftF_c", [1, 2, 128, 512], BF16), "dftI_c": ext("dftI_c", [1, 2, 128, 512], BF16)}
    outT_d = m.dram("outT", [2, 128, KC, 1024], F32, "ExternalOutput")
    hT_d = [itn("hT_%d" % v, [128, KC, TL], F32) for v in range(2)]
    RE = [{"qT": itn("re_qT%d" % v, [8, 192, TL], BF16), "kT": itn("re_kT%d" % v, [8, 128, TL], BF16),
           "kpeT": itn("re_kpeT%d" % v, [64, TL], BF16), "v": itn("re_v%d" % v, [TL, 1024], BF16),
           "qnT": itn("re_qnT%d" % v, [1024, TL], BF16), "knT": itn("re_knT%d" % v, [1024, TL], BF16),
           "vn": itn("re_vn%d" % v, [TL, 1024], BF16)} for v in range(2)]
    RO = [{"qT": itn("ro_qT%d" % v, [1024, TL], BF16), "vi": itn("ro_vi%d" % v, [TL, 1024], BF16),
           "zT": itn("ro_zT%d" % v, [2048, TL], F32), "gT": itn("ro_gT%d" % v, [1024, TL], BF16),
           "uT": itn("ro_uT%d" % v, [3072, TL], BF16)} for v in range(2)]
    ME = [{"mq": itn("me_mq%d" % v, [4, 192, TM], BF16), "mk": itn("me_mk%d" % v, [4, 128, TM], BF16),
           "mkpe": itn("me_mkpe%d" % v, [64, TM], BF16), "mv": itn("me_mv%d" % v, [TM, 512], BF16),
           "mqn": itn("me_mqn%d" % v, [512, TM], BF16), "mkn": itn("me_mkn%d" % v, [512, TM], BF16),
           "mvn": itn("me_mvn%d" % v, [TM, 512], BF16)} for v in range(2)]
    MO = [{"hq": itn("mo_hq%d" % v, [512, TM], BF16), "hz": itn("mo_hz%d" % v, [2, 512, TM], F32),
           "hv": itn("mo_hv%d" % v, [TM, 512], BF16), "hg": itn("mo_hg%d" % v, [512, TM], BF16),
           "hu": itn("mo_hu%d" % v, [3, 512, TM], BF16)} for v in range(2)]
    myT_d = [itn("myT_%d" % v, [8, 128, TM], BF16) for v in range(2)]

    def pcols(dst, s, sv):
        if len(dst.shape) == 3:
            m.dma(dst[:, :, 128 * sv:128 * sv + 128], s[:, :, 1024:TL])
            m.dma(dst[:, :, 256 + 1024 * sv:1280 + 1024 * sv], s[:, :, 0:1024])
        else:
            m.dma(dst[:, 128 * sv:128 * sv + 128], s[:, 1024:TL])
            m.dma(dst[:, 256 + 1024 * sv:1280 + 1024 * sv], s[:, 0:1024])

    def prows(dst, s, sv):
        m.dma(dst[128 * sv:128 * sv + 128], s[1024:TL])
        m.dma(dst[256 + 1024 * sv:1280 + 1024 * sv], s[0:1024])

    def assemble_from(l, sv):
        for tv in range(2):
            hs = slice(4 * tv, 4 * tv + 4)
            cs = slice(512 * tv, 512 * tv + 512)
            if l % 2 == 0:
                pcols(ME[tv]["mq"], RE[sv]["qT"][hs], sv)
                pcols(ME[tv]["mk"], RE[sv]["kT"][hs], sv)
                pcols(ME[tv]["mkpe"], RE[sv]["kpeT"], sv)
                prows(ME[tv]["mv"], RE[sv]["v"][:, cs], sv)
                pcols(ME[tv]["mqn"], RE[sv]["qnT"][cs], sv)
                pcols(ME[tv]["mkn"], RE[sv]["knT"][cs], sv)
                prows(ME[tv]["mvn"], RE[sv]["vn"][:, cs], sv)
            else:
                pcols(MO[tv]["hq"], RO[sv]["qT"][cs], sv)
                for d in range(2):
                    zs = slice(d * 1024 + 512 * tv, d * 1024 + 512 * tv + 512)
                    pcols(MO[tv]["hz"][d], RO[sv]["zT"][zs], sv)
                prows(MO[tv]["hv"], RO[sv]["vi"][:, cs], sv)
                pcols(MO[tv]["hg"], RO[sv]["gT"][cs], sv)
                for g in range(3):
                    us = slice(g * 1024 + 512 * tv, g * 1024 + 512 * tv + 512)
                    pcols(MO[tv]["hu"][g], RO[sv]["uT"][us], sv)

    modall = m.sb("modall", [128, DEPTH, 128, 2], F32)
    emit_mod(cx, modall[:, :, 0:96, :], cT_d, ada_w_d, ada_b_d, 96, 2)
    emit_gsc(cx, modall, gmix_d, gmlp_d)
    base_mark = m.mark()
    for l in range(depth + 1):
        last = (l == depth)
        for v in range(2):
            m.release(base_mark)
            hT = m.sb("hT", [128, KC, TL], F32)
            m.dma(hT[:], h0_d[v] if l == 0 else hT_d[v])
            actT = m.sb("actT", [128, KC, TL], BF16)
            rstd = m.sb("rstd", [128, TL], F32)
            cx.init_wring(4)
            if l >= 1:
                def load_y(actT_, v=v):
                    for f in range(16):
                        own, loc = (f // 4, f % 4) if f < 8 else ((f - 8) // 4, 4 + (f - 8) % 4)
                        srcT = myT_d[own][loc]
                        m.dma(actT_[:, f, 0:1024], srcT[:, 256 + 1024 * v:256 + 1024 * (v + 1)])
                        m.dma(actT_[:, f, 1024:TL], srcT[:, 128 * v:128 * (v + 1)])
                io = {"load_y": load_y, "w_out": w_out_d[l - 1], "w1": w1_d[l - 1], "w2": w2_d[l - 1]}
                emit_post(cx, l - 1, hT, actT, rstd, modall, io, TL if not last or depth < DEPTH else 1024)
            if not last and l % 2 == 0:
                e = l // 2
                io = dict(RE[v])
                io.update({"w_in": evw_d[e], "w_ukv": ukv_d[e], "cosT": cos_d[v], "sinT": sin_d[v], "kvg": kvg_d[e]})
                emit_pre_even(cx, l, hT, actT, rstd, modall, io)
            elif not last:
                io = dict(RO[v])
                io.update({"w_in": odw_d[l // 2]})
                emit_pre_odd(cx, l, hT, actT, rstd, modall, io)
            if not last:
                m.dma(hT_d[v], hT[:])
                assemble_from(l, v)
            else:
                emit_final(cx, hT, rstd, {"fg": fg_d, "outT": outT_d[v]})
        if last:
            break
        ctx_out = l < DEPTH - 1
        for v in range(2):
            m.release(base_mark)
            hs = slice(4 * v, 4 * v + 4)
            cs = slice(512 * v, 512 * v + 512)
            if l % 2 == 0:
                e = l // 2
                io = dict(ME[v])
                io.update({"nabias": nab_d[e, hs], "namask": namask_d, "myT": myT_d[v]})
                emit_M_even(cx, io, ctx_out)
            else:
                o = l // 2
                io = dict(MO[v])
                io.update(hyc)
                io.update({"odd_idx": o, "lbl": lbl_d[v], "ng": ng_d[o, v], "hmasks": hmasks_d,
                           "cw": cw_d[o, v], "cb": cb_d[o, v], "skip": skip_d[o, v],
                           "fw1": fw1_d[o], "fw2": fw2_d[o], "fw3": fw3_d[o], "fb": fb_d[o],
                           "fwo": fwo_d[o, v], "delt": delt_d[v], "myT": myT_d[v]})
                ystage = [m.sb("ystage%d" % i, [128, TM], BF16) for i in range(2)]
                emit_hgrn(cx, io, ctx_out, ystage)
                emit_hyena(cx, io, ctx_out, ystage)
    m.emit()
    return nc


_FUSED = {}


def host_inputs(inp, b):
    x, ctx, c, c_ctx = inp["x"], inp["ctx"], inp["c"], inp["c_ctx"]
    h0 = np.stack([fm_tokens(np.concatenate([x[b, r * 1024:(r + 1) * 1024], ctx[b, r * 128:(r + 1) * 128]], 0))
                   for r in range(2)], 0)
    d = {"h0": np.ascontiguousarray(h0), "cT": np.ascontiguousarray(np.stack([fm_vec(c[b]), fm_vec(c_ctx)], -1))}
    return d


def slotify(W, nkc, k_blocks=1):
    K, N = W.shape
    ncb = 4096 // nkc
    assert K == k_blocks * nkc * 128 and N % ncb == 0
    a = W.reshape(k_blocks, nkc, 128, N // ncb, ncb).transpose(0, 3, 2, 1, 4)
    return np.ascontiguousarray(a).reshape(k_blocks * (N // ncb), 128, nkc * ncb)


def host_shared(inp):
    adaw = inp["ada_w"].reshape(DEPTH, KC, 128, 24, 512).transpose(0, 3, 2, 1, 4)
    sh = {"ident": np.eye(128, dtype=np.float32),
          "ada_w": np.ascontiguousarray(adaw).reshape(DEPTH, 24, 128, KC * 512), "ada_b": inp["ada_b"],
          "gmix": np.ascontiguousarray(np.stack([fm_vec(inp["norm_mix_g"][l]) for l in range(DEPTH)], 1)),
          "gmlp": np.ascontiguousarray(np.stack([fm_vec(inp["norm_mlp_g"][l]) for l in range(DEPTH)], 1)),
          "w_out": np.stack([slotify(inp["w_out"][l], KC) for l in range(DEPTH)], 0),
          "mlp_w1": np.stack([slotify(inp["mlp_w1"][l], KC) for l in range(DEPTH)], 0),
          "mlp_w2": np.stack([slotify(inp["mlp_w2"][l], 4, 16) for l in range(DEPTH)], 0),
          "ev_w_in": np.stack([slotify(ev_w_in_ext(inp["ev_w_in"][e]), KC) for e in range(2)], 0),
          "w_ukv": np.stack([slotify(ukv_ext(inp["mla_w_ukv"][e]), 4) for e in range(2)], 0),
          "od_w_in": np.stack([slotify(inp["od_w_in"][o], KC) for o in range(2)], 0),
          "kvg": np.stack([fm_vec(inp["mla_kv_norm_g"][e]) for e in range(2)], 0),
          "fg": fm_vec(inp["final_norm_g"])}
    rt = [rope_tables(r) for r in range(2)]
    sh["cosT"] = np.stack([rt[0][0], rt[1][0]], 0)
    sh["sinT"] = np.stack([rt[0][1], rt[1][1]], 0)
    nabs = [na_bias_host(inp["na_rel_bias"][e]) for e in range(2)]
    sh["nabias"] = np.ascontiguousarray(np.stack([nabs[0][0], nabs[1][0]], 0))
    sh["namask"] = nabs[0][1]
    f32 = lambda a: np.ascontiguousarray(a).astype(np.float32)
    lg = inp["hgrn_lb_logits"]
    sh["lbl"] = f32(np.stack([np.stack([np.stack([fm_vec(lg[d, oo, 512 * v:512 * v + 512]) for oo in range(2)], 1)
                                        for d in range(2)], 1) for v in range(2)], 0))
    sh["ng"] = f32(np.stack([np.stack([fm_vec(inp["hgrn_norm_g"][o][512 * v:512 * v + 512]) for v in range(2)], 0)
                             for o in range(2)], 0))
    sh["hmasks"] = hgrn_masks()
    cwl, cbl, skl, fwol = [], [], [], []
    for o in range(2):
        cwv, cbv, wo = inp["hy_conv_w"][o], inp["hy_conv_b"][o], inp["hy_filt_wout"][o]
        cwl.append(np.stack([np.stack([np.stack([fm_vec(cwv[tap, g * 1024 + 512 * v:g * 1024 + 512 * v + 512])
                                                 for g in range(3)], 1) for tap in range(3)], 1) for v in range(2)], 0))
        cbl.append(np.stack([np.stack([fm_vec(cbv[g * 1024 + 512 * v:g * 1024 + 512 * v + 512]) for g in range(3)], 1)
                             for v in range(2)], 0))
        skl.append(np.stack([fm_vec(inp["hy_skip"][o][512 * v:512 * v + 512]) for v in range(2)], 0))
        fwol.append(np.stack([np.stack([wo[:, 512 * v:512 * v + 512], wo[:, 1024 + 512 * v:1024 + 512 * v + 512]], 1)
                              for v in range(2)], 0))
    sh["cw"], sh["cb"], sh["skip"], sh["fwo"] = f32(np.stack(cwl, 0)), f32(np.stack(cbl, 0)), f32(np.stack(skl, 0)), f32(np.stack(fwol, 0))
    sh["fw1"], sh["fw2"], sh["fw3"] = f32(inp["hy_filt_w1"]), f32(inp["hy_filt_w2"]), f32(inp["hy_filt_w3"])
    sh["fb"] = f32(np.stack([inp["hy_filt_b1"], inp["hy_filt_b2"], inp["hy_filt_b3"], inp["hy_filt_freq"]], -1))
    sh["delt"] = np.stack([hyena_deltas(v) for v in range(2)], 0)
    zl, tl, fl, il = hyena_consts(2048)
    zc, tc, fc_, ic = hyena_consts(256)
    sh.update({"zT_l": zl, "tcol_l": tl, "dftF_l": fl, "dftI_l": il, "zT_c": zc, "tcol_c": tc, "dftF_c": fc_, "dftI_c": ic})
    return sh


def kernel(**inp):
    inp = {k: np.asarray(v) for k, v in inp.items()}
    if "nc" not in _FUSED:
        _FUSED["nc"] = build_fused_program()
    sh = host_shared(inp)
    per_b = [host_inputs(inp, b) for b in range(4)]
    maps = []
    for core in range(8):
        d = dict(sh)
        d.update(per_b[core % 4])
        maps.append(d)
    res = run_bass_kernel_spmd(_FUSED["nc"], maps, core_ids=list(range(8)))
    out = np.zeros((4, SEQ, D), dtype=np.float32)
    for b in range(4):
        oT = np.asarray(res.results[b]["outT"])
        for r in range(2):
            out[b, r * 1024:(r + 1) * 1024] = oT[r].transpose(2, 1, 0).reshape(1024, D)
    return out
```
